# Optimizing a Trainium2 kernel written in Bass

```python
import math
import jax
import jax.numpy as jnp
from jax import lax
import numpy as np

D_MODEL = 2048
BATCH = 4
SEQ = 8192
DEPTH = 4

GRID_W = 64
CTX_LEN = 256
N_BRANCH = 4
BRANCH_W = D_MODEL // 4
HEAD_DIM = 64
A_HEADS = BRANCH_W // HEAD_DIM
A_KV = A_HEADS // 4
WINDOW = 128
Q_BLOCK = 128
B_HEADS = BRANCH_W // HEAD_DIM
B_LORA = 64
B_DECAY_SCALE = 0.606531
B_GN_EPS = 64e-5
C_HEADS = BRANCH_W // HEAD_DIM
C_KV = C_HEADS // 4
D_HD = 128
D_HEADS = BRANCH_W // D_HD
D_CONV = 5
D_CHUNK = 64
GATE_RANK = D_MODEL // 8
ROPE_THETA = 10000.0
EPS = 1e-6
NEG = -1e30
F32 = jnp.float32

W_A = 2 * BRANCH_W + 2 * A_KV * HEAD_DIM
W_B = 4 * BRANCH_W + 2 * B_LORA
W_C = 2 * BRANCH_W + 2 * C_KV * HEAD_DIM
W_DM = 4 * BRANCH_W + 4 * D_HEADS
OFF_A = 0
OFF_B = OFF_A + W_A
OFF_C = OFF_B + W_B
OFF_D = OFF_C + W_C
OFF_G = OFF_D + W_DM
PROJ_W = OFF_G + GATE_RANK

kernel_name = 'hybrid_parallel_mixer_dit'


def _rms(x, g, eps=EPS):
    xf = x.astype(F32)
    y = xf * lax.rsqrt(jnp.mean(xf * xf, axis=-1, keepdims=True) + eps)
    return (y * g.astype(F32)).astype(x.dtype)


def _l2n(x, eps=EPS):
    xf = x.astype(F32)
    return xf * lax.rsqrt(jnp.maximum(jnp.sum(xf * xf, axis=-1, keepdims=True), eps * eps))


def _heads(t, hd):
    return t.reshape(t.shape[:-1] + (t.shape[-1] // hd, hd))


def _rope_tables(rows):
    row = jnp.repeat(jnp.arange(rows, dtype=F32), GRID_W)
    col = jnp.tile(jnp.arange(GRID_W, dtype=F32), rows)
    half = HEAD_DIM // 2
    inv = ROPE_THETA ** (-jnp.arange(0, half, 2, dtype=F32) / half)
    ar = row[:, None] * inv
    ac = col[:, None] * inv
    return (jnp.cos(ar), jnp.sin(ar), jnp.cos(ac), jnp.sin(ac))


def _rot_half(x, cos, sin):
    x1, x2 = jnp.split(x, 2, axis=-1)
    cos = cos[:, None, :]
    sin = sin[:, None, :]
    return jnp.concatenate([x1 * cos - x2 * sin, x1 * sin + x2 * cos], axis=-1)


def _rope2d(x, rope):
    cr, sr, cc, sc = rope
    xr, xcol = jnp.split(x.astype(F32), 2, axis=-1)
    return jnp.concatenate([_rot_half(xr, cr, sr), _rot_half(xcol, cc, sc)], axis=-1).astype(x.dtype)


def _attn_split(p, n_kv):
    kvw = n_kv * HEAD_DIM
    q = _heads(p[..., :BRANCH_W], HEAD_DIM)
    k = _heads(p[..., BRANCH_W:BRANCH_W + kvw], HEAD_DIM)
    v = _heads(p[..., BRANCH_W + kvw:BRANCH_W + 2 * kvw], HEAD_DIM)
    return q, k, v, p[..., BRANCH_W + 2 * kvw:]


def _dense_attn(q, k, v, n_kv, sink=None):
    b, m, h, hd = q.shape
    grp = h // n_kv
    ns = k.shape[1]
    s = jnp.einsum('bqkgd,bskd->bkgqs', q.reshape(b, m, n_kv, grp, hd), k).astype(F32) * (hd ** -0.5)
    if sink is not None:
        snk = jnp.broadcast_to(sink.reshape(n_kv, grp, 1, 1).astype(F32), s.shape[:-1] + (1,))
        s = jnp.concatenate([s, snk], axis=-1)
    pr = jax.nn.softmax(s, axis=-1)[..., :ns].astype(v.dtype)
    return jnp.einsum('bkgqs,bskd->bqkgd', pr, v).reshape(b, m, h * hd)


def _mixer_a(pa, pac, sink, rope, need_ctx):
    q, k, v, g = _attn_split(pa, A_KV)
    qc, kc, vc, gc = _attn_split(pac, A_KV)
    q = _rope2d(q, rope)
    k = _rope2d(k, rope)
    b, n = q.shape[:2]
    m = kc.shape[1]
    grp = A_HEADS // A_KV
    nb = n // Q_BLOCK
    span = Q_BLOCK + 2 * WINDOW
    scale = HEAD_DIM ** -0.5
    pad = ((0, 0), (WINDOW, WINDOW), (0, 0), (0, 0))
    kp = jnp.pad(k, pad)
    vp = jnp.pad(v, pad)
    snk = sink.reshape(A_KV, grp, 1, 1).astype(F32)
    qb = jnp.swapaxes(q.reshape(b, nb, Q_BLOCK, A_KV, grp, HEAD_DIM), 0, 1)

    def block(args):
        i, qi = args
        start = i * Q_BLOCK
        ki = lax.dynamic_slice_in_dim(kp, start, span, axis=1)
        vi = lax.dynamic_slice_in_dim(vp, start, span, axis=1)
        kpos = start - WINDOW + jnp.arange(span)
        qpos = start + jnp.arange(Q_BLOCK)
        ok = (jnp.abs(qpos[:, None] - kpos[None, :]) <= WINDOW) & (kpos >= 0) & (kpos < n)
        s_loc = jnp.einsum('bqkgd,bskd->bkgqs', qi, ki).astype(F32) * scale
        s_loc = jnp.where(ok, s_loc, NEG)
        s_ctx = jnp.einsum('bqkgd,bskd->bkgqs', qi, kc).astype(F32) * scale
        s_snk = jnp.broadcast_to(snk, s_ctx.shape[:-1] + (1,))
        pr = jax.nn.softmax(jnp.concatenate([s_loc, s_ctx, s_snk], axis=-1), axis=-1).astype(v.dtype)
        o = (jnp.einsum('bkgqs,bskd->bqkgd', pr[..., :span], vi)
             + jnp.einsum('bkgqs,bskd->bqkgd', pr[..., span:span + m], vc))
        return o.reshape(o.shape[0], Q_BLOCK, BRANCH_W)

    o = lax.map(block, (jnp.arange(nb), qb))
    y = jnp.swapaxes(o, 0, 1).reshape(b, n, BRANCH_W) * jax.nn.silu(g)
    yc = _dense_attn(qc, kc, vc, A_KV, sink) * jax.nn.silu(gc) if need_ctx else None
    return y, yc


def _mixer_c(pc_, pcc, qn, kn, rope, need_ctx):
    q, k, v, g = _attn_split(pc_, C_KV)
    qc, kc, vc, gc = _attn_split(pcc, C_KV)
    q = _rope2d(_rms(q, qn), rope)
    k = _rope2d(_rms(k, kn), rope)
    qc = _rms(qc, qn)
    kc = _rms(kc, kn)
    b, n = q.shape[:2]
    nb = n // Q_BLOCK
    k_all = jnp.concatenate([k, kc], axis=1)
    v_all = jnp.concatenate([v, vc], axis=1)
    qb = jnp.swapaxes(q.reshape(b, nb, Q_BLOCK, C_HEADS, HEAD_DIM), 0, 1)
    o = lax.map(lambda qi: _dense_attn(qi, k_all, v_all, C_KV), qb)
    y = jnp.swapaxes(o, 0, 1).reshape(b, n, BRANCH_W) * jax.nn.silu(g)
    yc = _dense_attn(qc, kc, vc, C_KV) * jax.nn.silu(gc) if need_ctx else None
    return y, yc


def _token_shift(z, mu):
    prev = jnp.pad(z[:, :-1], ((0, 0), (1, 0), (0, 0)))
    nxt = jnp.pad(z[:, 1:], ((0, 0), (0, 1), (0, 0)))
    return z + mu[0] * (prev - z) + mu[1] * (nxt - z)


def _rwkv_feats(pb, mu, w0, wup, a0, aup, k_k, k_a):
    W, L = BRANCH_W, B_LORA
    z = _token_shift(pb[..., :3 * W + 2 * L], mu)
    r, k, v = z[..., :W], z[..., W:2 * W], z[..., 2 * W:3 * W]
    wl = jnp.tanh(z[..., 3 * W:3 * W + L])
    al = z[..., 3 * W + L:]
    kk = _l2n(_heads(k * k_k, HEAD_DIM))
    dirs = []
    for d in range(2):
        w = jnp.exp(-B_DECAY_SCALE * jax.nn.sigmoid((w0[d] + wl @ wup[d]).astype(F32)))
        a = jax.nn.sigmoid((a0[d] + al @ aup[d]).astype(F32))
        kt = k.astype(F32) * (1.0 + (a - 1.0) * k_a)
        dirs.append((_heads(w, HEAD_DIM), _heads(a, HEAD_DIM), _heads(kt, HEAD_DIM)))
    r = _heads(r, HEAD_DIM).astype(F32)
    v = _heads(v, HEAD_DIM).astype(F32)
    return r, v, kk, pb[..., 3 * W + 2 * L:], dirs


def _rwkv_scan(s0, r, w, k, v, kk, a, reverse):
    xs = tuple(jnp.moveaxis(t.astype(F32), 1, 0) for t in (r, w, k, v, kk, a))

    def step(s, inp):
        r_t, w_t, k_t, v_t, kk_t, a_t = inp
        sa = jnp.einsum('bhij,bhj->bhi', s, -kk_t)
        s = (s * w_t[:, :, None, :] + sa[..., None] * (kk_t * a_t)[:, :, None, :]
             + v_t[..., None] * k_t[:, :, None, :])
        return s, jnp.einsum('bhij,bhj->bhi', s, r_t)

    s, y = lax.scan(step, s0, xs, reverse=reverse)
    return s, jnp.moveaxis(y, 0, 1)


def _rwkv_out(wkv, bonus, g, ln_g, ln_b):
    b, n = wkv.shape[:2]
    mu = jnp.mean(wkv, axis=-1, keepdims=True)
    var = jnp.mean(jnp.square(wkv - mu), axis=-1, keepdims=True)
    gn = ((wkv - mu) * lax.rsqrt(var + B_GN_EPS)).reshape(b, n, BRANCH_W) * ln_g + ln_b
    return ((gn + bonus.reshape(b, n, BRANCH_W)) * jax.nn.silu(g.astype(F32))).astype(g.dtype)


def _mixer_b(pb, pbc, mu, w0, wup, a0, aup, k_k, k_a, r_k, ln_g, ln_b, need_ctx):
    r, v, kk, g, dirs = _rwkv_feats(pb, mu, w0, wup, a0, aup, k_k, k_a)
    rc, vc, kkc, gc, dirsc = _rwkv_feats(pbc, mu, w0, wup, a0, aup, k_k, k_a)
    s0 = jnp.zeros((pb.shape[0], B_HEADS, HEAD_DIM, HEAD_DIM), F32)
    rk = r_k.astype(F32)
    ys, bon, ycs, bonc = [], [], [], []
    for d in range(2):
        w, a, kt = dirs[d]
        wc, ac, ktc = dirsc[d]
        sc, yc_d = _rwkv_scan(s0, rc, wc, ktc, vc, kkc, ac, d == 1)
        _, y_d = _rwkv_scan(sc, r, w, kt, v, kk, a, d == 1)
        ys.append(y_d)
        bon.append(jnp.sum(r * kt * rk, axis=-1, keepdims=True) * v)
        if need_ctx:
            ycs.append(yc_d)
            bonc.append(jnp.sum(rc * ktc * rk, axis=-1, keepdims=True) * vc)
    y = _rwkv_out(ys[0] + ys[1], bon[0] + bon[1], g, ln_g, ln_b)
    yc = _rwkv_out(ycs[0] + ycs[1], bonc[0] + bonc[1], gc, ln_g, ln_b) if need_ctx else None
    return y, yc


def _short_conv(z, w):
    return lax.conv_general_dilated(z, w[:, None, :].astype(z.dtype), window_strides=(1,),
                                    padding=[(D_CONV // 2, D_CONV // 2)],
                                    dimension_numbers=('NWC', 'WIO', 'NWC'),
                                    feature_group_count=z.shape[-1])


def _gdn_feats(pd, conv_w, alog, dtb):
    W, H = BRANCH_W, D_HEADS
    qkv = jax.nn.silu(_short_conv(pd[..., :3 * W], conv_w)).astype(F32)
    q = _l2n(_heads(qkv[..., :W], D_HD)) * (D_HD ** -0.5)
    k = _l2n(_heads(qkv[..., W:2 * W], D_HD))
    v = _heads(qkv[..., 2 * W:], D_HD)
    ab = pd[..., 3 * W:3 * W + 4 * H].astype(F32)
    dirs = []
    for d in range(2):
        a_raw = ab[..., 2 * d * H:(2 * d + 1) * H]
        b_raw = ab[..., (2 * d + 1) * H:(2 * d + 2) * H]
        lg = -jnp.exp(alog[d].astype(F32)) * jax.nn.softplus(a_raw + dtb[d])
        dirs.append((lg, jax.nn.sigmoid(b_raw)))
    return q, k, v, pd[..., 3 * W + 4 * H:], dirs


def _gdn_chunked(s0, q, k, v, lg, beta):
    b, n, h, dk = k.shape
    dv = v.shape[-1]
    c = D_CHUNK
    nc = n // c

    def blk(t):
        return jnp.moveaxis(t.reshape((b, nc, c) + t.shape[2:]), 2, 3)

    q, k, v, lg, beta = blk(q), blk(k), blk(v), blk(lg), blk(beta)
    gcum = jnp.cumsum(lg, axis=-1)
    incl = jnp.tril(jnp.ones((c, c), dtype=bool))
    strict = jnp.tril(jnp.ones((c, c), dtype=bool), -1)
    diff = gcum[..., :, None] - gcum[..., None, :]
    dmask = jnp.where(incl, jnp.exp(jnp.where(incl, diff, 0.0)), 0.0)
    kb = k * beta[..., None]
    a_mat = jnp.where(strict, jnp.einsum('bnhid,bnhjd->bnhij', kb, k) * dmask, 0.0)
    t_mat = a_mat + jnp.eye(c, dtype=a_mat.dtype)
    rhs = jnp.concatenate([v * beta[..., None], kb * jnp.exp(gcum)[..., None]], axis=-1)
    sol = lax.linalg.triangular_solve(t_mat, rhs, left_side=True, lower=True, unit_diagonal=True)
    u, wk = sol[..., :dv], sol[..., dv:]
    qk = jnp.einsum('bnhid,bnhjd->bnhij', q, k) * dmask
    qg = q * jnp.exp(gcum)[..., None]
    glast = gcum[..., -1]
    kd = k * jnp.exp(glast[..., None] - gcum)[..., None]
    xs = tuple(jnp.moveaxis(t, 1, 0) for t in (qg, qk, u, wk, kd, jnp.exp(glast)))

    def step(s, inp):
        qg_c, qk_c, u_c, wk_c, kd_c, dec = inp
        vnew = u_c - jnp.einsum('bhcd,bhdv->bhcv', wk_c, s)
        o = jnp.einsum('bhcd,bhdv->bhcv', qg_c, s) + jnp.einsum('bhij,bhjv->bhiv', qk_c, vnew)
        s = s * dec[..., None, None] + jnp.einsum('bhcd,bhcv->bhdv', kd_c, vnew)
        return s, o

    s, o = lax.scan(step, s0, xs)
    o = jnp.moveaxis(jnp.moveaxis(o, 0, 1), 2, 3).reshape(b, n, h, dv)
    return s, o


def _gdn_dir(s0, q, k, v, lg, beta, rev):
    if rev:
        s, o = _gdn_chunked(s0, jnp.flip(q, 1), jnp.flip(k, 1), jnp.flip(v, 1),
                            jnp.flip(lg, 1), jnp.flip(beta, 1))
        return s, jnp.flip(o, 1)
    return _gdn_chunked(s0, q, k, v, lg, beta)


def _mixer_d(pd, pdc, conv_w, alog, dtb, norm_g, need_ctx):
    q, k, v, g, dirs = _gdn_feats(pd, conv_w, alog, dtb)
    qc, kc, vc, gc, dirsc = _gdn_feats(pdc, conv_w, alog, dtb)
    b, n = pd.shape[:2]
    m = pdc.shape[1]
    s0 = jnp.zeros((b, D_HEADS, D_HD, D_HD), F32)
    os_, ocs = [], []
    for d in range(2):
        sc, oc_d = _gdn_dir(s0, qc, kc, vc, dirsc[d][0], dirsc[d][1], d == 1)
        _, o_d = _gdn_dir(sc, q, k, v, dirs[d][0], dirs[d][1], d == 1)
        os_.append(o_d)
        ocs.append(oc_d)
    y = (_rms(os_[0] + os_[1], norm_g).reshape(b, n, BRANCH_W) * jax.nn.silu(g.astype(F32))).astype(g.dtype)
    yc = None
    if need_ctx:
        yc = (_rms(ocs[0] + ocs[1], norm_g).reshape(b, m, BRANCH_W) * jax.nn.silu(gc.astype(F32))).astype(gc.dtype)
    return y, yc


def _merge(ys, pm, g_up, g_b, w_br, w_out):
    acc = jax.nn.sigmoid(pm @ g_up[0] + g_b[0]) * (ys[0] @ w_br[0])
    for i in range(1, N_BRANCH):
        acc = acc + jax.nn.sigmoid(pm @ g_up[i] + g_b[i]) * (ys[i] @ w_br[i])
    return acc @ w_out


def _layer(x, xc, c, c_ctx, rope, need_ctx, norm_g, w_mod, b_mod, w_in, a_sink,
           b_mu, b_w0, b_wup, b_a0, b_aup, b_kk, b_ka, b_rk, b_lng, b_lnb,
           c_qn, c_kn, d_conv, d_alog, d_dtb, d_norm, g_up, g_b, w_br, w_out):
    shift, scale, gate = jnp.split(jax.nn.silu(c) @ w_mod + b_mod, 3, axis=-1)
    sh_c, sc_c, gt_c = jnp.split(jax.nn.silu(c_ctx) @ w_mod + b_mod, 3, axis=-1)
    h = _rms(x, norm_g) * (1.0 + scale[:, None]) + shift[:, None]
    hc = _rms(xc, norm_g) * (1.0 + sc_c) + sh_c
    p = h @ w_in
    pc = hc @ w_in
    y_a, z_a = _mixer_a(p[..., OFF_A:OFF_B], pc[..., OFF_A:OFF_B], a_sink, rope, need_ctx)
    y_b, z_b = _mixer_b(p[..., OFF_B:OFF_C], pc[..., OFF_B:OFF_C], b_mu, b_w0, b_wup, b_a0, b_aup,
                        b_kk, b_ka, b_rk, b_lng, b_lnb, need_ctx)
    y_c, z_c = _mixer_c(p[..., OFF_C:OFF_D], pc[..., OFF_C:OFF_D], c_qn, c_kn, rope, need_ctx)
    y_d, z_d = _mixer_d(p[..., OFF_D:OFF_G], pc[..., OFF_D:OFF_G], d_conv, d_alog, d_dtb, d_norm, need_ctx)
    x = x + gate[:, None] * _merge([y_a, y_b, y_c, y_d], p[..., OFF_G:], g_up, g_b, w_br, w_out)
    if need_ctx:
        xc = xc + gt_c * _merge([z_a, z_b, z_c, z_d], pc[..., OFF_G:], g_up, g_b, w_br, w_out)
    return x, xc


def setup_inputs(seed: int = 0) -> dict:
    key = jax.random.key(seed)
    ks = iter(jax.random.split(key, 32))
    L, D, W = DEPTH, D_MODEL, BRANCH_W

    def nrm(shape, s):
        return jax.random.normal(next(ks), shape, F32) * s

    def uni(shape, lo, hi):
        return jax.random.uniform(next(ks), shape, F32, lo, hi)

    x = nrm((BATCH, SEQ, D), 1.0)
    c = nrm((BATCH, D), 1.0)
    ctx = nrm((BATCH, CTX_LEN, D), 1.0)
    c_ctx = nrm((D,), 1.0)
    norm_g = 1.0 + nrm((L, D), 0.02)
    w_mod = nrm((L, D, 3 * D), 0.5 * D ** -0.5)
    b_mod = nrm((L, 3 * D), 0.02)
    w_in = nrm((L, D, PROJ_W), D ** -0.5)
    a_sink = nrm((L, A_HEADS), 0.5)
    b_mu = uni((L, 2, 3 * W + 2 * B_LORA), 0.0, 0.5)
    b_w0 = nrm((L, 2, W), 1.0)
    b_wup = nrm((L, 2, B_LORA, W), 0.1)
    b_a0 = nrm((L, 2, W), 0.5)
    b_aup = nrm((L, 2, B_LORA, W), 0.1)
    b_kk = 0.85 + nrm((L, W), 0.02)
    b_ka = 1.0 + nrm((L, W), 0.02)
    b_rk = nrm((L, B_HEADS, HEAD_DIM), 0.1)
    b_lng = 1.0 + nrm((L, W), 0.02)
    b_lnb = nrm((L, W), 0.02)
    c_qn = 1.0 + nrm((L, HEAD_DIM), 0.02)
    c_kn = 1.0 + nrm((L, HEAD_DIM), 0.02)
    d_conv = nrm((L, D_CONV, 3 * W), D_CONV ** -0.5)
    d_alog = jnp.log(uni((L, 2, D_HEADS), 1.0, 16.0))
    dt = jnp.exp(uni((L, 2, D_HEADS), math.log(1e-3), math.log(1e-1)))
    d_dtb = dt + jnp.log(-jnp.expm1(-dt))
    d_norm = 1.0 + nrm((L, D_HD), 0.02)
    g_up = nrm((L, N_BRANCH, GATE_RANK, D), GATE_RANK ** -0.5)
    g_b = nrm((L, N_BRANCH, D), 0.02)
    w_br = nrm((L, N_BRANCH, W, D), W ** -0.5)
    w_out = nrm((L, D, D), D ** -0.5)
    final_g = 1.0 + nrm((D,), 0.02)
    return {'x': x, 'c': c, 'ctx': ctx, 'c_ctx': c_ctx, 'norm_g': norm_g, 'w_mod': w_mod,
            'b_mod': b_mod, 'w_in': w_in, 'a_sink': a_sink, 'b_mu': b_mu, 'b_w0': b_w0,
            'b_wup': b_wup, 'b_a0': b_a0, 'b_aup': b_aup, 'b_kk': b_kk, 'b_ka': b_ka,
            'b_rk': b_rk, 'b_lng': b_lng, 'b_lnb': b_lnb, 'c_qn': c_qn, 'c_kn': c_kn,
            'd_conv': d_conv, 'd_alog': d_alog, 'd_dtb': d_dtb, 'd_norm': d_norm,
            'g_up': g_up, 'g_b': g_b, 'w_br': w_br, 'w_out': w_out, 'final_g': final_g}


def reference(x, c, ctx, c_ctx, norm_g, w_mod, b_mod, w_in, a_sink, b_mu, b_w0, b_wup, b_a0,
              b_aup, b_kk, b_ka, b_rk, b_lng, b_lnb, c_qn, c_kn, d_conv, d_alog, d_dtb, d_norm,
              g_up, g_b, w_br, w_out, final_g):
    n = x.shape[1]
    rows = n // GRID_W
    rope = _rope_tables(rows)
    xc = ctx
    for l in range(DEPTH):
        x, xc = _layer(x, xc, c, c_ctx, rope, l < DEPTH - 1, norm_g[l], w_mod[l], b_mod[l],
                       w_in[l], a_sink[l], b_mu[l], b_w0[l], b_wup[l], b_a0[l], b_aup[l],
                       b_kk[l], b_ka[l], b_rk[l], b_lng[l], b_lnb[l], c_qn[l], c_kn[l],
                       d_conv[l], d_alog[l], d_dtb[l], d_norm[l], g_up[l], g_b[l], w_br[l],
                       w_out[l])
    return _rms(x, final_g)
```

```python
import math
from contextlib import ExitStack

import numpy as np
import concourse.bass as bass
import concourse.mybir as mybir
from concourse.bass_utils import run_bass_kernel_spmd

F32 = mybir.dt.float32
BF16 = mybir.dt.bfloat16
AF = mybir.ActivationFunctionType
ALU = mybir.AluOpType

D_MODEL = 2048
CTX = 256
W = 512
KC = D_MODEL // 128

OFF_A, OFF_B, OFF_C, OFF_D, OFF_G = 0, 1280, 3456, 4736, 6800
SEGS = [
    ("Aq", OFF_A + 0, 512, False), ("Aqp", OFF_A + 0, 512, True),
    ("Ak", OFF_A + 512, 128, False), ("Akp", OFF_A + 512, 128, True),
    ("Ag", OFF_A + 768, 512, False),
    ("Bz", OFF_B + 0, 1664, False), ("Bg", OFF_B + 1664, 512, False),
    ("Cq", OFF_C + 0, 512, False), ("Cqp", OFF_C + 0, 512, True),
    ("Ck", OFF_C + 512, 128, False), ("Ckp", OFF_C + 512, 128, True),
    ("Cg", OFF_C + 768, 512, False),
    ("Dqkv", OFF_D + 0, 1536, False), ("Dg", OFF_D + 1552, 512, False),
    ("pm", OFF_G, 256, False),
    ("Dab", OFF_D + 1536, 16, False),
]
SEG_OFF = {}
_o = 0
for _n, _s, _c, _p in SEGS:
    SEG_OFF[_n] = _o
    _o += _c
NFM = _o
NFM_PAD = 8192
VSEGS = [("Av", OFF_A + 640, 128), ("Cv", OFF_C + 640, 128)]
NCT = NFM_PAD // 128 + len(VSEGS)


def _perm64():
    p = np.arange(64)
    blk, r = p // 32, p % 32
    return blk * 32 + (r + 16) % 32


def w_in_columns():
    cols = []
    for n, s, c, perm in SEGS:
        idx = np.arange(s, s + c)
        if perm:
            idx = idx.reshape(-1, 64)[:, _perm64()].reshape(-1)
        cols.append(idx)
    cols = np.concatenate(cols)
    pad = np.zeros(NFM_PAD - NFM, dtype=np.int64)
    vcols = np.concatenate([np.arange(s, s + c) for _, s, c in VSEGS])
    return np.concatenate([cols, pad, vcols])


class Tile:
    __slots__ = ("h", "w", "r", "name")

    def __init__(self, h, name):
        self.h = h
        self.w = None
        self.r = []
        self.name = name

    def __getitem__(self, idx):
        return self.h[idx]


_WRITE_KW = ("out", "ap", "accum_out")


class _RowSplit:
    def __init__(self, a, b, half):
        self.a, self.b, self.half = a, b, half

    def ap(self):
        return self

    def __getitem__(self, idx):
        r, c = idx
        if r.start >= self.half:
            return self.b.ap()[r.start - self.half:r.stop - self.half, c]
        assert r.stop <= self.half
        return self.a.ap()[r, c]


class _View:
    def __init__(self, h, lo, w):
        self.h, self.lo, self.w = h, lo, w

    def __getitem__(self, idx):
        if not isinstance(idx, tuple):
            idx = (idx, slice(None))
        p, c = idx[0], idx[1]
        a, bnd, _ = c.indices(self.w)
        return self.h[p, self.lo + a:self.lo + bnd]


class Builder:
    ENGS = ("pe", "dve", "act", "pool", "sp")

    def __init__(self, nc, es):
        self.nc = nc
        self.es = es
        self.ops = {e: [] for e in self.ENGS}
        self.cnt = {e: 0 for e in self.ENGS}
        self.sem = {}
        for e in ("pe", "dve", "act", "pool"):
            self.sem[e] = es.enter_context(nc.semaphore("s_" + e))
        self.waited = {e: {} for e in self.ENGS}
        NQ = 12
        self.dsem = {}
        self.dnext = {}
        for q in ("sp", "pool", "act"):
            self.dsem[q] = [es.enter_context(nc.semaphore("d_%s%d" % (q, i))) for i in range(NQ)]
            self.dnext[q] = 0
        self.dcum = {}
        self.n_tiles = 0
        self.reg = {}

    def scope(self):
        b = self

        class _S:
            def __enter__(s2):
                s2.old = b.es
                s2.st = ExitStack()
                b.es = s2.st
                return s2

            def __exit__(s2, *a):
                b.barrier()
                b.es = s2.old
                s2.st.close()
                return False
        return _S()

    def sb(self, shape, dtype, name=None, psum=False):
        self.n_tiles += 1
        name = "%s_%d" % (name or "t", self.n_tiles)
        mk = self.nc.psum_tensor if psum else self.nc.sbuf_tensor
        h = self.es.enter_context(mk(name, list(shape), dtype))
        t = Tile(h, name)
        self.reg[h.name] = t
        return t

    def ps(self, shape, dtype=F32, name=None):
        return self.sb(shape, dtype, name, psum=True)

    def _need(self, eng, rec, waits):
        if rec is None:
            return
        if rec[0] == "E":
            key, idx = rec[1], rec[2]
            if key == eng and eng == "pe":
                return
        else:
            key, idx = rec[1], rec[2]
        if self.waited[eng].get(key, 0) >= idx:
            return
        self.waited[eng][key] = idx
        waits.append((key, idx))

    def _deps(self, eng, reads, writes):
        waits = []
        for t in reads:
            self._need(eng, t.w, waits)
        for t in writes:
            self._need(eng, t.w, waits)
            for r in t.r:
                self._need(eng, r, waits)
        return waits

    def _split(self, kw):
        reads, writes = [], []
        for k, v in kw.items():
            if hasattr(v, "tensor") and hasattr(v, "ap"):
                t = self.reg.get(v.tensor.name)
                if isinstance(t, tuple):
                    t = t[1][(v.offset % 512) // t[0]]
                if t is not None:
                    (writes if k in _WRITE_KW else reads).append(t)
        return reads, writes

    def subtiles(self, tile, width):
        subs = []
        for j in range(512 // width):
            st = Tile(_View(tile.h, j * width, width), "%s_s%d" % (tile.name, j))
            subs.append(st)
        self.reg[tile.h.name] = (width, subs)
        return subs

    def _mark(self, rec, reads, writes):
        for t in reads:
            t.r.append(rec)
        for t in writes:
            t.w = rec
            t.r = []

    def I(self, eng, method, **kw):
        reads, writes = self._split(kw)
        waits = self._deps(eng, reads, writes)
        self.cnt[eng] += 1
        idx = self.cnt[eng]
        self.ops[eng].append((waits, method, kw, None))
        self._mark(("E", eng, idx), reads, writes)

    def dma(self, q, out, in_, **kw):
        reads, writes = self._split(dict(out=out, in_=in_))
        waits = self._deps(q, reads, writes)
        i = self.dnext[q]
        self.dnext[q] = (i + 1) % len(self.dsem[q])
        key = (q, i)
        prev = self.dcum.get(key, 0)
        if prev:
            self._need(q, ("D", key, prev), waits)
        val = prev + 16
        self.dcum[key] = val
        kw = dict(kw, out=out, in_=in_)
        self.ops[q].append((waits, "dma_start", kw, self.dsem[q][i]))
        self._mark(("D", key, val), reads, writes)

    def barrier(self):
        for e in self.ENGS:
            waits = []
            for x in ("pe", "dve", "act", "pool"):
                if self.cnt[x] and not (x == e and e == "pe"):
                    self._need(e, ("E", x, self.cnt[x]), waits)
            for key, val in self.dcum.items():
                self._need(e, ("D", key, val), waits)
            if waits:
                self.ops[e].append((waits, None, None, None))

    def _semof(self, key):
        if isinstance(key, tuple):
            return self.dsem[key[0]][key[1]]
        return self.sem[key]

    def emit(self):
        nc = self.nc
        with nc.Block() as block:
            def mk(ename):
                def body(e):
                    own = self.sem.get(ename)
                    for waits, method, kw, dsem in self.ops[ename]:
                        for key, val in waits:
                            e.wait_ge(self._semof(key), val)
                        if method is None:
                            continue
                        ins = getattr(e, method)(**kw)
                        if dsem is not None:
                            ins.then_inc(dsem, 16)
                        else:
                            ins.then_inc(own, 1)
                return body
            block.tensor(mk("pe"))
            block.vector(mk("dve"))
            block.scalar(mk("act"))
            block.gpsimd(mk("pool"))
            block.sync(mk("sp"))

    def mm(self, out, lhsT, rhs, start=True, stop=True):
        self.I("pe", "matmul", out=out, lhsT=lhsT, rhs=rhs, start=start, stop=stop)

    def tr(self, out, in_, identity):
        self.I("pe", "transpose", out=out, in_=in_, identity=identity)

    def act(self, out, in_, func, scale=1.0, bias=0.0):
        self.I("act", "activation", out=out, in_=in_, func=func, scale=scale, bias=bias)

    def tt(self, eng, out, in0, in1, op):
        self.I(eng, "tensor_tensor", out=out, in0=in0, in1=in1, op=op)

    def ts(self, eng, out, in0, s1, op0, s2=None, op1=None):
        if op1 is None:
            self.I(eng, "tensor_scalar", out=out, in0=in0, scalar1=s1, scalar2=None, op0=op0)
        else:
            self.I(eng, "tensor_scalar", out=out, in0=in0, scalar1=s1, scalar2=s2, op0=op0, op1=op1)

    def stt(self, out, in0, scalar, in1, op0, op1):
        self.I("dve", "scalar_tensor_tensor", out=out, in0=in0, scalar=scalar, in1=in1, op0=op0, op1=op1)

    def cp(self, eng, out, in_):
        if eng == "act":
            self.I("act", "activation", out=out, in_=in_, func=AF.Copy)
        else:
            self.I(eng, "tensor_copy", out=out, in_=in_)

    def rcp(self, out, in_):
        self.I("dve", "reciprocal", out=out, in_=in_)

    def ms(self, eng, ap, val):
        self.I(eng, "memset", ap=ap, constant=val)


class Pool:
    def __init__(self, b, n, shape, dtype, name, psum=False):
        self.t = [b.sb(shape, dtype, name, psum=psum) for _ in range(n)]
        self.i = 0

    def get(self):
        t = self.t[self.i]
        self.i = (self.i + 1) % len(self.t)
        return t


class Prog:
    def __init__(self, n_lat, depth, debug=(), mixers=(0, 1, 2, 3)):
        self.n = n_lat
        self.NT = CTX + n_lat
        self.depth = depth
        self.debug = set(debug)
        self.mixers = mixers
        self.stages = ("prep", "units", "post")
        self.cut = 0
        self.do_merge = True
        self.blocks = [(0, CTX, True)]
        t = CTX
        while t < self.NT:
            s = min(512, self.NT - t)
            self.blocks.append((t, s, False))
            t += s

    def build(self):
        nc = bass.Bass("TRN2", target_bir_lowering=False)
        self.nc = nc
        L, NT = self.depth, self.NT
        with ExitStack() as es:
            b = Builder(nc, es)
            self.b = b
            dk = lambda name: ("ExternalOutput" if name in self.debug else "Internal")
            I = {}

            def inp(name, shape, dt=F32):
                I[name] = nc.dram_tensor(name, list(shape), dt, kind="ExternalInput")
            inp("xT", [D_MODEL, NT])
            inp("cc", [128, KC, 2])
            inp("norm_g", [L, 128, KC])
            inp("w_mod", [L, D_MODEL, 3 * D_MODEL])
            inp("b_mod", [L, 128, 48])
            inp("w_in", [L, NCT, 128, KC, 128])
            inp("final_g", [128, KC])
            inp("ropec", [128, NT])
            inp("ropes", [128, NT])
            inp("m3", [128, 384])
            inp("ident", [128, 128])
            inp("a_sink", [L, 128, 8])
            inp("c_qk", [L, 128, 4])
            inp("bd2", [128, 2, 64])
            inp("rmask", [128, 4, 128])
            inp("b_mu", [L, 128, 13, 2])
            inp("b_w0a0", [L, 128, 2, 2, 4])
            inp("b_aw", [L, 128, 2, 512])
            inp("b_vec", [L, 128, 4, 4])
            inp("b_rkblk", [L, 128, 4, 128])
            inp("gmask", [128, 4, 128])
            inp("sel8", [8, 8, 128])
            inp("d_conv", [L, 128, 12, 5])
            inp("d_ab", [L, 8, 4])
            inp("d_vec", [L, 128, 4, 4])
            inp("wm", [L, 16, 128, 24, 128])
            inp("wo", [L, 16, 128, 16, 128])
            inp("g_b", [L, 128, 4, 16])
            self.I = I
            self.out = nc.dram_tensor("outT", [D_MODEL, self.n], F32, kind="ExternalOutput")
            self.xs = nc.dram_tensor("xs", [D_MODEL, NT], F32, kind=dk("xs"))
            self.pT = _RowSplit(nc.dram_tensor("pTa", [NFM_PAD // 2, NT], F32, kind=dk("pTa")),
                                nc.dram_tensor("pTb", [NFM_PAD // 2, NT], F32, kind=dk("pTb")), NFM_PAD // 2)
            self.vtm = nc.dram_tensor("vtm", [2, NT, 128], BF16, kind=dk("vtm"))
            self.wbf = nc.dram_tensor("wbf", [NCT, 128, KC, 128], BF16, kind="Internal")
            self.yT = nc.dram_tensor("yT", [4, W, NT], BF16, kind=dk("yT"))
            self.UB = nc.dram_tensor("UB", [2, 4, 128, 7, NT], F32, kind=dk("UB"))
            self.GAMB = nc.dram_tensor("GAMB", [2, 4, 128, NT // 64], F32, kind=dk("GAMB"))
            self.bon = nc.dram_tensor("bon", [W, NT], F32, kind=dk("bon"))
            self.ytm = nc.dram_tensor("ytm", [2, NT, W], F32, kind=dk("ytm"))
            self.UD = nc.dram_tensor("UD", [4, 128, 3, NT], F32, kind=dk("UD"))
            self.RD = nc.dram_tensor("RD", [8, 6, NT], F32, kind=dk("RD"))
            self.EGTD = nc.dram_tensor("EGTD", [8, NT // 128], F32, kind=dk("EGTD"))
            self.wmb = nc.dram_tensor("wmb", [16, 128, 24, 128], BF16, kind="Internal")
            self.wob = nc.dram_tensor("wob", [16, 128, 16, 128], BF16, kind="Internal")
            self.ones_bf = b.sb([128, 128], BF16, "ones")
            b.ms("dve", self.ones_bf[:], 1.0)
            self.modv = b.sb([128, L, KC, 6], F32, "modv")

            with b.scope():
                self.phase_mod()
            for l in range(L):
                with b.scope():
                    self.phase_wcast(l)
                with b.scope():
                    self.phase_inproj(l)
                need_ctx = l < L - 1
                for mi in self.mixers:
                    with b.scope():
                        if mi in (0, 2):
                            self.phase_attn(l, mi, need_ctx)
                        elif mi == 1 and "prep" in self.stages:
                            self.phase_b_prep(l)
                    if mi in (1, 3):
                        if mi == 3 and "prep" in self.stages:
                            with b.scope():
                                self.phase_d_prep(l)
                        if "units" in self.stages:
                            with b.scope():
                                self.phase_units(l, mi)
                        if "post" in self.stages:
                            with b.scope():
                                self.phase_post(l, mi, need_ctx)
                if self.do_merge:
                    with b.scope():
                        self.phase_mcast(l)
                    with b.scope():
                        self.phase_merge(l, need_ctx)
            if self.do_merge:
                with b.scope():
                    self.phase_final()
            b.barrier()
            b.emit()
        return nc

    def phase_mod(self):
        b, I, L = self.b, self.I, self.depth
        psum = Pool(b, 4, [128, 512], F32, "ps", psum=True)
        cc = b.sb([128, KC, 2], F32, "cc")
        b.dma("sp", cc[:], I["cc"].ap())
        sc = b.sb([128, KC, 2], F32, "sc")
        b.act(sc[:], cc[:], AF.Silu)
        wpool = Pool(b, 2, [128, KC, 512], F32, "wmod")
        raw = b.sb([128, 48, 2], F32, "modraw")
        bm = b.sb([128, 48], F32, "bmod")
        ng = b.sb([128, KC], F32, "ng")
        mv = self.modv
        for l in range(L):
            b.dma("sp", bm[:], I["b_mod"].ap()[l])
            b.dma("sp", ng[:], I["norm_g"].ap()[l])
            for cg in range(12):
                wt = wpool.get()
                src = I["w_mod"].ap()[l, :, cg * 512:(cg + 1) * 512].rearrange("(k p) c -> p k c", p=128)
                b.dma("sp", wt[:], src)
                for j in range(4):
                    ct = cg * 4 + j
                    ps = psum.get()
                    for k in range(KC):
                        b.mm(ps[:, 0:2], wt[:, k, j * 128:(j + 1) * 128], sc[:, k, :], k == 0, k == KC - 1)
                    b.ts("dve", raw[:, ct, :], ps[:, 0:2], bm[:, ct:ct + 1], ALU.add)
            for v in range(2):
                b.stt(mv[:, l, :, 3 * v + 0], raw[:, 16:32, v], 1.0, ng[:], ALU.add, ALU.mult)
                b.cp("dve", mv[:, l, :, 3 * v + 1], raw[:, 0:16, v])
                b.cp("dve", mv[:, l, :, 3 * v + 2], raw[:, 32:48, v])

    def phase_wcast(self, l):
        b, I = self.b, self.I
        pf = Pool(b, 3, [128, KC, 128], F32, "wcf")
        pb = Pool(b, 3, [128, KC, 128], BF16, "wcb")
        for ct in range(NCT):
            f, g = pf.get(), pb.get()
            b.dma("sp", f[:], I["w_in"].ap()[l, ct])
            b.cp(("dve", "pool")[ct % 2], g[:], f[:])
            b.dma("sp", self.wbf.ap()[ct], g[:])

    def phase_inproj(self, l):
        b, I = self.b, self.I
        mv = self.modv
        psum = Pool(b, 6, [128, 512], F32, "ps", psum=True)
        Pxt = Pool(b, 2, [128, KC, 512], F32, "xt")
        Psq = Pool(b, 1, [128, KC, 512], BF16, "sq")
        PhT = Pool(b, 2, [128, KC, 512], BF16, "hT")
        Prs = Pool(b, 2, [128, 512], F32, "rs")
        Pw = Pool(b, 3, [128, KC, 128], BF16, "wip")
        Pev = Pool(b, 3, [128, 512], F32, "ev")
        Pevb = Pool(b, 2, [128, 128], BF16, "evb")
        src_x = I["xT"] if l == 0 else self.xs
        for (t0, T, is_ctx) in self.blocks:
            v = 1 if is_ctx else 0
            xt = Pxt.get()
            for k in range(KC):
                b.dma("sp", xt[:, k, 0:T], src_x.ap()[k * 128:(k + 1) * 128, t0:t0 + T])
            sq = Psq.get()
            b.act(sq[:, :, 0:T], xt[:, :, 0:T], AF.Square)
            ps = psum.get()
            for k in range(KC):
                b.mm(ps[:, 0:T], self.ones_bf[:], sq[:, k, 0:T], k == 0, k == KC - 1)
            rs = Prs.get()
            b.act(rs[:, 0:T], ps[:, 0:T], AF.Sqrt, scale=1.0 / D_MODEL, bias=1e-6)
            b.rcp(rs[:, 0:T], rs[:, 0:T])
            hT = PhT.get()
            for k in range(KC):
                b.tt("dve", xt[:, k, 0:T], xt[:, k, 0:T], rs[:, 0:T], ALU.mult)
                b.ts(("pool", "dve")[k % 2], hT[:, k, 0:T], xt[:, k, 0:T], mv[:, l, k, 3 * v:3 * v + 1], ALU.mult,
                     mv[:, l, k, 3 * v + 1:3 * v + 2], ALU.add)
            for ct in range((NFM + 127) // 128):
                wt = Pw.get()
                b.dma("sp", wt[:], self.wbf.ap()[ct])
                ps = psum.get()
                for k in range(KC):
                    b.mm(ps[:, 0:T], wt[:, k, :], hT[:, k, 0:T], k == 0, k == KC - 1)
                ev = Pev.get()
                b.cp(("act", "dve")[ct % 2], ev[:, 0:T], ps[:, 0:T])
                b.dma("pool", self.pT.ap()[ct * 128:(ct + 1) * 128, t0:t0 + T], ev[:, 0:T])
            for vi in range(2):
                wt = Pw.get()
                b.dma("sp", wt[:], self.wbf.ap()[NFM_PAD // 128 + vi])
                for sbk in range(T // 128):
                    ps = psum.get()
                    for k in range(KC):
                        b.mm(ps[:, 0:128], hT[:, k, sbk * 128:(sbk + 1) * 128], wt[:, k, :], k == 0, k == KC - 1)
                    evb = Pevb.get()
                    b.cp("dve", evb[:], ps[:, 0:128])
                    r0 = t0 + sbk * 128
                    b.dma("pool", self.vtm.ap()[vi, r0:r0 + 128, :], evb[:])

    def phase_attn(self, l, mi, need_ctx):
        b, I, NT = self.b, self.I, self.NT
        isC = (mi == 2)
        pre = "C" if isC else "A"
        oq, oqp, ok_, okp, og = (SEG_OFF[pre + x] for x in ("q", "qp", "k", "kp", "g"))
        pT = self.pT.ap()
        NCH = NT // 128
        psS = Pool(b, 4, [128, 512], F32, "psS", psum=True)
        psO = Pool(b, 2, [128, 512], F32, "psO", psum=True)
        psB = Pool(b, 2, [128, 512], F32, "psB", psum=True)
        KT = b.sb([128, 2, NT], BF16, "KT")
        VE = b.sb([128, NCH, 2, 65], BF16, "VE")
        cst = b.sb([128, 16], F32, "acst")
        m3 = b.sb([128, 384], F32, "m3")
        ones_f = b.sb([128, 64], F32, "ones_f")
        oblk = b.sb([128, 128], BF16, "oblk")
        b.ms("dve", ones_f[:], 1.0)
        b.ms("dve", oblk[:], 0.0)
        b.ms("dve", oblk[0:64, 0:64], 1.0)
        b.ms("dve", oblk[64:128, 64:128], 1.0)
        b.dma("sp", m3[:], I["m3"].ap())
        b.dma("sp", cst[:, 0:8], I["a_sink"].ap()[l])
        b.dma("sp", cst[:, 8:12], I["c_qk"].ap()[l])
        b.act(cst[:, 0:8], cst[:, 0:8], AF.Exp)
        b.ms("dve", VE[:, :, :, 64:65], 1.0)
        vsrc = self.vtm.ap()[1 if isC else 0].rearrange("(c p) (g d) -> p c g d", p=128, g=2)
        for c0 in range(0, NCH, 8):
            c1 = min(NCH, c0 + 8)
            for g in range(2):
                b.dma("sp", VE[:, c0:c1, g, 0:64], vsrc[:, c0:c1, g, :])

        ld = Pool(b, 4, [128, 512], F32, "ald")
        tb = Pool(b, 2, [128, 512], F32, "atb")
        tmp = Pool(b, 4, [128, 512], F32, "atmp")
        sqp = Pool(b, 2, [128, 512], BF16, "asq")
        rsp = Pool(b, 2, [128, 512], F32, "ars")

        def rope(dst_ap, rows, rowsp, t0, T, cosb, sinb, nidx, dup):
            x, xp = ld.get(), ld.get()
            if dup:
                for hh in range(2):
                    b.dma("sp", x[hh * 64:(hh + 1) * 64, 0:T], pT[rows:rows + 64, t0:t0 + T])
                    b.dma("sp", xp[hh * 64:(hh + 1) * 64, 0:T], pT[rowsp:rowsp + 64, t0:t0 + T])
            else:
                b.dma("sp", x[:, 0:T], pT[rows:rows + 128, t0:t0 + T])
                b.dma("sp", xp[:, 0:T], pT[rowsp:rowsp + 128, t0:t0 + T])
            t1, t2 = tmp.get(), tmp.get()
            if isC:
                sq = sqp.get()
                b.act(sq[:, 0:T], x[:, 0:T], AF.Square)
                ps = psB.get()
                b.mm(ps[:, 0:T], oblk[:], sq[:, 0:T])
                rs = rsp.get()
                b.act(rs[:, 0:T], ps[:, 0:T], AF.Sqrt, scale=1.0 / 64, bias=1e-6)
                b.rcp(rs[:, 0:T], rs[:, 0:T])
                b.stt(t1[:, 0:T], x[:, 0:T], cst[:, nidx:nidx + 1], cosb[:, 0:T], ALU.mult, ALU.mult)
                b.stt(t2[:, 0:T], xp[:, 0:T], cst[:, nidx + 1:nidx + 2], sinb[:, 0:T], ALU.mult, ALU.mult)
                b.tt("pool", t1[:, 0:T], t1[:, 0:T], t2[:, 0:T], ALU.add)
                b.tt("dve", dst_ap, t1[:, 0:T], rs[:, 0:T], ALU.mult)
            else:
                b.tt("dve", t1[:, 0:T], x[:, 0:T], cosb[:, 0:T], ALU.mult)
                b.tt("pool", t2[:, 0:T], xp[:, 0:T], sinb[:, 0:T], ALU.mult)
                b.tt("dve", dst_ap, t1[:, 0:T], t2[:, 0:T], ALU.add)

        def tables(t0, T):
            cosb, sinb = tb.get(), tb.get()
            b.dma("sp", cosb[:, 0:T], I["ropec"].ap()[:, t0:t0 + T])
            b.dma("sp", sinb[:, 0:T], I["ropes"].ap()[:, t0:t0 + T])
            return cosb, sinb

        for (t0, T, is_ctx) in self.blocks:
            cosb, sinb = tables(t0, T)
            for g in range(2):
                rope(KT[:, g, t0:t0 + T], ok_ + g * 64, okp + g * 64, t0, T, cosb, sinb, 10, True)

        QT = Pool(b, 2, [128, 4, 512], BF16, "QT")
        PTp = Pool(b, 4, [128, 512], BF16, "PT")
        gp = Pool(b, 3, [64, 512], F32, "ag")
        ep = Pool(b, 3, [64, 512], F32, "ae")
        drp = Pool(b, 2, [128, 512], F32, "adr")
        bcp = Pool(b, 2, [64, 512], F32, "abc")
        yp = Pool(b, 3, [64, 512], BF16, "ay")
        nlat_ch = self.n // 128
        for (t0, T, is_ctx) in self.blocks:
            if is_ctx and not need_ctx:
                continue
            cosb, sinb = tables(t0, T)
            qt = QT.get()
            for pr in range(4):
                rope(qt[:, pr, 0:T], oq + pr * 128, oqp + pr * 128, t0, T, cosb, sinb, 8, False)
            chunks = [(0, 0, T, None), (1, 0, T, None)]
            if not is_ctx:
                lb = (t0 - CTX) // 128
                nq = T // 128
                if isC:
                    chunks += [(2 + c, 0, T, None) for c in range(nlat_ch)]
                else:
                    for c in range(max(0, lb - 1), min(nlat_ch, lb + nq + 1)):
                        qlo, qhi = max(lb, c - 1), min(lb + nq - 1, c + 1)
                        chunks.append((2 + c, (qlo - lb) * 128, (qhi - lb + 1) * 128, (qlo - c + 1) * 128))
            for h in range(8):
                g, pr, po = h // 4, h // 2, (h % 2) * 64
                gt = gp.get()
                b.dma("sp", gt[:, 0:T], pT[og + h * 64:og + (h + 1) * 64, t0:t0 + T])
                O = psO.get()
                for ci, (ch, lo, hi, mlo) in enumerate(chunks):
                    S = psS.get()
                    b.mm(S[:, lo:hi], KT[po:po + 64, g, ch * 128:(ch + 1) * 128], qt[po:po + 64, pr, lo:hi])
                    pt = PTp.get()
                    b.act(pt[:, lo:hi], S[:, lo:hi], AF.Exp, scale=0.125)
                    if mlo is not None:
                        b.tt("dve", pt[:, lo:hi], pt[:, lo:hi], m3[:, mlo:mlo + hi - lo], ALU.mult)
                    b.mm(O[0:65, lo:hi], VE[:, ch, g, :], pt[:, lo:hi], ci == 0, ci == len(chunks) - 1)
                dr = drp.get()
                if isC:
                    b.rcp(dr[64:65, 0:T], O[64:65, 0:T])
                else:
                    b.ts("dve", dr[64:65, 0:T], O[64:65, 0:T], cst[64:65, h:h + 1], ALU.add)
                    b.rcp(dr[64:65, 0:T], dr[64:65, 0:T])
                B = psB.get()
                b.mm(B[0:64, 0:T], ones_f[64:65, 0:64], dr[64:65, 0:T])
                bc = bcp.get()
                b.cp("act", bc[:, 0:T], B[0:64, 0:T])
                et = ep.get()
                b.act(et[:, 0:T], gt[:, 0:T], AF.Exp, scale=-1.0)
                b.ts("pool", et[:, 0:T], et[:, 0:T], 1.0, ALU.add)
                b.rcp(et[:, 0:T], et[:, 0:T])
                b.tt("pool", et[:, 0:T], et[:, 0:T], gt[:, 0:T], ALU.mult)
                b.tt("dve", bc[:, 0:T], O[0:64, 0:T], bc[:, 0:T], ALU.mult)
                y = yp.get()
                b.tt("dve", y[:, 0:T], bc[:, 0:T], et[:, 0:T], ALU.mult)
                b.dma("pool", self.yT.ap()[mi, h * 64:(h + 1) * 64, t0:t0 + T], y[:, 0:T])

    def seg_bounds(self, t0):
        return (0, CTX) if t0 < CTX else (CTX, self.NT)

    def phase_b_prep(self, l):
        b, I, NT = self.b, self.I, self.NT
        pT = self.pT.ap()
        oz = SEG_OFF["Bz"]
        psum = Pool(b, 4, [128, 512], F32, "ps", psum=True)
        psB = Pool(b, 2, [128, 512], F32, "psb", psum=True)
        mu = b.sb([128, 13, 2], F32, "mu")
        cmu = b.sb([128, 13], F32, "cmu")
        w0a0 = b.sb([128, 2, 2, 4], F32, "w0a0")
        aw = b.sb([128, 2, 512], F32, "aw")
        vec = b.sb([128, 4, 4], F32, "bvec")
        rkb = b.sb([128, 4, 128], F32, "rkb")
        oblk = b.sb([128, 128], BF16, "oblk")
        onesF = b.sb([128, 512], F32, "onesF")
        b.ms("dve", onesF[:], 1.0)
        b.ms("dve", oblk[:], 0.0)
        b.ms("dve", oblk[0:64, 0:64], 1.0)
        b.ms("dve", oblk[64:128, 64:128], 1.0)
        b.dma("sp", mu[:], I["b_mu"].ap()[l])
        b.dma("sp", w0a0[:], I["b_w0a0"].ap()[l])
        b.dma("sp", aw[:], I["b_aw"].ap()[l])
        b.dma("sp", vec[:], I["b_vec"].ap()[l])
        b.dma("sp", rkb[:], I["b_rkblk"].ap()[l])
        b.ts("dve", cmu[:], mu[:, :, 0], -1.0, ALU.mult, 1.0, ALU.add)
        b.tt("dve", cmu[:], cmu[:], mu[:, :, 1], ALU.subtract)
        ptp = Pool(b, 3, [128, 514], F32, "bpt")
        zall = b.sb([128, 13, 512], F32, "zall")
        kkT = b.sb([128, 4, 512], F32, "kkT")
        t2k = Pool(b, 12, [128, 512], F32, "bt")
        sqp = Pool(b, 2, [128, 512], BF16, "bsq")
        Sx = b.sb([128, 513], F32, "Sx")
        b.ms("dve", Sx[:, 0:1], 0.0)
        o7p = Pool(b, 2, [128, 7, 512], F32, "o7")
        gtp = Pool(b, 2, [128, 8], F32, "egt")
        for (t0, T, is_ctx) in self.blocks:
            lo, hi = self.seg_bounds(t0)
            nch = T // 64
            for i in range(13):
                pt = ptp.get()
                a0, a1 = max(t0 - 1, lo), min(t0 + T + 1, hi)
                if a0 > t0 - 1:
                    b.ms("pool", pt[:, 0:1], 0.0)
                if a1 < t0 + T + 1:
                    b.ms("pool", pt[:, T + 1:T + 2], 0.0)
                b.dma("sp", pt[:, a0 - (t0 - 1):a1 - (t0 - 1)], pT[oz + i * 128:oz + (i + 1) * 128, a0:a1])
                z = zall[:, i, 0:T]
                b.ts("dve", z, pt[:, 1:T + 1], cmu[:, i:i + 1], ALU.mult)
                b.stt(z, pt[:, 0:T], mu[:, i, 0:1], z, ALU.mult, ALU.add)
                b.stt(z, pt[:, 2:T + 2], mu[:, i, 1:2], z, ALU.mult, ALU.add)
            b.act(zall[0:64, 12, 0:T], zall[0:64, 12, 0:T], AF.Tanh)
            for pr in range(4):
                kx = t2k.get()
                b.ts("dve", kx[:, 0:T], zall[:, 4 + pr, 0:T], vec[:, 0, pr:pr + 1], ALU.mult)
                sq = sqp.get()
                b.act(sq[:, 0:T], kx[:, 0:T], AF.Square)
                ps = psum.get()
                b.mm(ps[:, 0:T], oblk[:], sq[:, 0:T])
                rn = t2k.get()
                b.act(rn[:, 0:T], ps[:, 0:T], AF.Sqrt, bias=1e-12)
                b.rcp(rn[:, 0:T], rn[:, 0:T])
                b.tt("dve", kkT[:, pr, 0:T], kx[:, 0:T], rn[:, 0:T], ALU.mult)
            for pr in range(4):
                psb = psB.get()
                rT, kT_, vT_ = zall[:, pr, 0:T], zall[:, 4 + pr, 0:T], zall[:, 8 + pr, 0:T]
                for d in range(2):
                    pw = psum.get()
                    b.mm(pw[:, 0:T], aw[0:64, d, pr * 128:(pr + 1) * 128], zall[0:64, 12, 0:T])
                    lw = t2k.get()
                    b.act(lw[:, 0:T], pw[:, 0:T], AF.Sigmoid, bias=w0a0[:, 0, d, pr:pr + 1])
                    b.ts("pool", lw[:, 0:T], lw[:, 0:T], -0.606531, ALU.mult)
                    pa = psum.get()
                    b.mm(pa[:, 0:T], aw[64:128, d, pr * 128:(pr + 1) * 128], zall[64:128, 12, 0:T])
                    a = t2k.get()
                    b.act(a[:, 0:T], pa[:, 0:T], AF.Sigmoid, bias=w0a0[:, 1, d, pr:pr + 1])
                    kt = t2k.get()
                    b.ts("dve", kt[:, 0:T], a[:, 0:T], -1.0, ALU.add, vec[:, 1, pr:pr + 1], ALU.mult)
                    b.stt(kt[:, 0:T], kt[:, 0:T], 1.0, kT_, ALU.add, ALU.mult)
                    bb = t2k.get()
                    b.tt("pool", bb[:, 0:T], kkT[:, pr, 0:T], a[:, 0:T], ALU.mult)
                    b.I("dve", "tensor_tensor_scan", out=Sx[:, 1:T + 1], data0=onesF[:, 0:T], data1=lw[:, 0:T],
                        initial=0.0, op0=ALU.mult, op1=ALU.add)
                    gc = t2k.get()
                    v3 = lambda ap: ap.rearrange("p (c t) -> p c t", t=64)
                    b.tt("dve", v3(gc[:, 0:T]), v3(Sx[:, 1:T + 1]), v3(Sx[:, 0:T])[:, :, 0:1].broadcast_to([128, nch, 64]),
                         ALU.subtract)
                    gtot_bc = v3(gc[:, 0:T])[:, :, 63:64].broadcast_to([128, nch, 64])
                    ge = t2k.get()
                    if d == 0:
                        gi = gc
                        b.tt("dve", ge[:, 0:T], gc[:, 0:T], lw[:, 0:T], ALU.subtract)
                    else:
                        gi = t2k.get()
                        b.tt("dve", v3(ge[:, 0:T]), gtot_bc, v3(gc[:, 0:T]), ALU.subtract)
                        b.tt("pool", gi[:, 0:T], ge[:, 0:T], lw[:, 0:T], ALU.add)
                    egt = gtp.get()
                    b.act(egt[:, 0:nch], v3(gc[:, 0:T])[:, :, 63], AF.Exp)
                    ege, egi, engi = t2k.get(), t2k.get(), t2k.get()
                    b.act(ege[:, 0:T], ge[:, 0:T], AF.Exp)
                    b.act(egi[:, 0:T], gi[:, 0:T], AF.Exp)
                    b.act(engi[:, 0:T], gi[:, 0:T], AF.Exp, scale=-1.0)
                    o7 = o7p.get()
                    b.tt("dve", o7[:, 0, 0:T], kkT[:, pr, 0:T], ege[:, 0:T], ALU.mult)
                    b.tt("pool", o7[:, 1, 0:T], rT, egi[:, 0:T], ALU.mult)
                    b.tt("dve", o7[:, 2, 0:T], bb[:, 0:T], engi[:, 0:T], ALU.mult)
                    b.tt("pool", o7[:, 3, 0:T], kt[:, 0:T], engi[:, 0:T], ALU.mult)
                    ebc = egt[:, 0:nch].unsqueeze(2).broadcast_to([128, nch, 64])
                    b.tt("dve", v3(o7[:, 4, 0:T]), v3(o7[:, 3, 0:T]), ebc, ALU.mult)
                    b.tt("dve", v3(o7[:, 5, 0:T]), v3(o7[:, 2, 0:T]), ebc, ALU.mult)
                    b.cp("pool", o7[:, 6, 0:T], vT_)
                    b.dma("pool", self.UB.ap()[d, pr, :, :, t0:t0 + T], o7[:, :, 0:T])
                    b.dma("pool", self.GAMB.ap()[d, pr, :, t0 // 64:t0 // 64 + nch], egt[:, 0:nch])
                    rkt = t2k.get()
                    b.tt("dve", rkt[:, 0:T], rT, kt[:, 0:T], ALU.mult)
                    b.mm(psb[:, 0:T], rkb[:, pr, :], rkt[:, 0:T], d == 0, d == 1)
                bo = t2k.get()
                b.tt("dve", bo[:, 0:T], psb[:, 0:T], vT_, ALU.mult)
                b.dma("pool", self.bon.ap()[pr * 128:(pr + 1) * 128, t0:t0 + T], bo[:, 0:T])

    def phase_units(self, l, mi):
        b, I, NT = self.b, self.I, self.NT
        isB = (mi == 1)
        nlev = 5 if isB else 6
        CT = 64 if isB else 128
        NCH = NT // CT
        cch = CTX // CT
        order = {0: list(range(NCH)), 1: list(range(cch - 1, -1, -1)) + list(range(NCH - 1, cch - 1, -1))}
        chains = [(u, d) for u in range(4) for d in range(2)]
        NC = len(chains)
        psG = Pool(b, 2, [128, 512], F32, "psG", psum=True)
        psA = Pool(b, 2, [128, 512], F32, "psA", psum=True)
        sm = Pool(b, 4, [128, 512], F32, "psm", psum=True)
        ident = b.sb([128, 128], F32, "ident")
        b.dma("sp", ident[:], I["ident"].ap())
        rmask = b.sb([128, 4, 128], F32, "rmask")
        b.dma("sp", rmask[:], I["rmask" if isB else "gmask"].ap())
        bd2 = b.sb([128, 2, 64], F32, "bd2")
        b.dma("sp", bd2[:], I["bd2"].ap())
        NB = 12
        XXp = Pool(b, NB, [128, 4, 128], BF16, "XX")
        TMp = Pool(b, NB, [128, 3, 128], BF16, "TM3")
        A3p = Pool(b, NB, [128, 3, 128], BF16, "A3")
        Pmp = Pool(b, 20, [128, 2, 128], F32, "Pm")
        XTp = Pool(b, 20, [128, 128], F32, "XT")
        AVp = Pool(b, NB, [128, 128], F32, "AVs")
        Wp = Pool(b, NB, [128, 128], F32, "Wt")
        NUp = Pool(b, NB, [128, 128], BF16, "negU")
        Yop = Pool(b, 6, [128, 128], F32, "Yo")
        H = [b.sb([128, 128], F32, "H") for _ in chains]
        Hb = [b.sb([128, 128], BF16, "Hb") for _ in chains]
        for i in range(NC):
            b.ms("dve", H[i][:], 0.0)
            b.ms("pool", Hb[i][:], 0.0)
        gam = [b.sb([128, NCH], F32, "gam") for _ in chains]
        if isB:
            ldp = Pool(b, NB, [128, 7, 64], F32, "ld7")
            F3p = Pool(b, 8, [128, 3, 128], F32, "F3")
            for i, (u, d) in enumerate(chains):
                b.dma("sp", gam[i][:], self.GAMB.ap()[d, u])
        else:
            self.gdn_unit_setup(l, gam)
            self.gdn_gam(gam, sm, chains)

        for step in range(NCH):
            units = []
            for i, (u, d) in enumerate(chains):
                c = order[d][step]
                tk = c * CT
                U = dict(i=i, u=u, d=d, c=c, tk=tk)
                XX, TM3 = XXp.get(), TMp.get()
                if isB:
                    ld = ldp.get()
                    b.dma("sp", ld[:], self.UB.ap()[d, u, :, :, tk:tk + 64])
                    bdb = bd2[:].unsqueeze(1).broadcast_to([128, 4, 2, 64])
                    b.tt("dve", XX[:].rearrange("p k (h t) -> p k h t", h=2),
                         ld[:, 0:4, :].unsqueeze(2).broadcast_to([128, 4, 2, 64]), bdb, ALU.mult)
                    F3 = F3p.get()
                    b.tt("pool", F3[:].rearrange("p k (h t) -> p k h t", h=2),
                         ld[:, 4:7, :].unsqueeze(2).broadcast_to([128, 3, 2, 64]),
                         bd2[:].unsqueeze(1).broadcast_to([128, 3, 2, 64]), ALU.mult)
                    tp = sm.get()
                    for k in range(3):
                        b.tr(tp[:, k * 128:(k + 1) * 128], F3[:, k, :], ident[:])
                    b.cp(("act", "dve")[i % 2], TM3[:].rearrange("p k t -> p (k t)"), tp[:, 0:384])
                    if d == 0:
                        WsT, WiT, Ws = rmask[:, 0, :], rmask[:, 1, :], rmask[:, 2, :]
                    else:
                        WsT, WiT, Ws = rmask[:, 2, :], rmask[:, 3, :], rmask[:, 0, :]
                    gcol = gam[i][:, c:c + 1]
                else:
                    WsT, WiT, Ws, XH = self.gdn_unit_prep(l, U, XX, TM3, sm, rmask, ident)
                    gcol = gam[i][:, c:c + 1]
                    U.update(XkH=XH[:, 0, :], XrH=XH[:, 1, :])
                if isB:
                    U.update(XkH=XX[:, 0, :], XrH=XX[:, 1, :])
                U.update(XX=XX, TM3=TM3, gcol=gcol)
                if self.cut == 1:
                    continue
                G = psG.get()
                xkr = XX[:, 0:2, :].rearrange("p k t -> p (k t)")
                b.mm(G[:, 0:256], XX[:, 2, :], xkr)
                b.mm(G[:, 256:512], XX[:, 3, :], xkr)
                g3 = sm.get()
                b.mm(g3[:, 0:128], XX[:, 0, :], XX[:, 2, :])
                Pm = Pmp.get()
                b.stt(Pm[:, 1, :], G[:, 0:128], -1.0, WsT, ALU.mult, ALU.mult)
                b.stt(Pm[:, 0, :], g3[:, 0:128], -1.0, Ws, ALU.mult, ALU.mult)
                A3 = A3p.get()
                b.tt("dve", A3[:, 0, :], G[:, 128:256], WiT, ALU.mult)
                b.tt("dve", A3[:, 1, :], G[:, 256:384], WsT, ALU.mult)
                b.tt("dve", A3[:, 2, :], G[:, 384:512], WiT, ALU.mult)
                XT = XTp.get()
                b.tt("pool", XT[:], Pm[:, 1, :], ident[:], ALU.add)
                U.update(A3=A3)
                U["Pm"], U["XT"] = Pm, XT
                units.append(U)
            if self.cut in (1, 2):
                return
            for lev in range(nlev):
                last = lev == nlev - 1
                for U in units:
                    Pm, XT = U["Pm"], U["XT"]
                    p2 = sm.get()
                    b.mm(p2[:, 0:128], Pm[:, 1, :], Pm[:, 0, :])
                    Pn = Pmp.get()
                    if not last:
                        b.mm(p2[:, 128:256], Pm[:, 0, :], Pm[:, 1, :])
                        b.cp(("act", "dve")[U["i"] % 2], Pn[:].rearrange("p k t -> p (k t)"), p2[:, 0:256])
                    else:
                        b.cp("act", Pn[:, 0, :], p2[:, 0:128])
                    U["Pn"] = Pn
                for U in units:
                    Pn, XT = U["Pn"], U["XT"]
                    xu = sm.get()
                    b.mm(xu[:, 0:128], Pn[:, 0, :], XT[:])
                    XTn = XTp.get()
                    b.tt("dve", XTn[:], xu[:, 0:128], XT[:], ALU.add)
                    U["Pm"], U["XT"] = Pn, XTn
            if self.cut == 3:
                return
            for U in units:
                av = sm.get()
                b.mm(av[:, 0:128], U["A3"][:, 1, :], U["TM3"][:, 2, :])
                AVs = AVp.get()
                b.cp("act", AVs[:], av[:, 0:128])
                U["AVs"] = AVs
            if self.cut == 4:
                return
            for U in units:
                kh = sm.get()
                b.mm(kh[:, 0:128], U["XkH"], Hb[U["i"]][:])
                Wt = Wp.get()
                b.tt("dve", Wt[:], kh[:, 0:128], U["AVs"][:], ALU.add)
                U["Wt"] = Wt
            for U in units:
                uu = sm.get()
                b.mm(uu[:, 0:128], U["XT"][:], U["Wt"][:])
                nu = NUp.get()
                b.act(nu[:], uu[:, 0:128], AF.Copy, scale=-1.0)
                U["nu"] = nu
            for U in units:
                i = U["i"]
                Y = psA.get()
                b.mm(Y[:, 0:128], U["XrH"], Hb[i][:], True, False)
                b.mm(Y[:, 0:128], U["A3"][:, 0, :], U["nu"][:], False, False)
                b.mm(Y[:, 0:128], U["A3"][:, 2, :], U["TM3"][:, 2, :], False, True)
                Yo = Yop.get()
                b.cp("act", Yo[:], Y[:, 0:128])
                dst = self.ytm.ap()[U["d"]]
                if isB:
                    for hh in range(2):
                        b.dma("pool", dst[U["tk"]:U["tk"] + 64, (2 * U["u"] + hh) * 64:(2 * U["u"] + hh + 1) * 64],
                              Yo[hh * 64:(hh + 1) * 64, hh * 64:(hh + 1) * 64])
                else:
                    b.dma("pool", dst[U["tk"]:U["tk"] + 128, U["u"] * 128:(U["u"] + 1) * 128], Yo[:])
            for U in units:
                i = U["i"]
                Hn = psA.get()
                b.mm(Hn[:, 0:128], U["TM3"][:, 1, :], U["nu"][:], True, False)
                b.mm(Hn[:, 0:128], U["TM3"][:, 0, :], U["TM3"][:, 2, :], False, True)
                b.stt(H[i][:], H[i][:], U["gcol"], Hn[:, 0:128], ALU.mult, ALU.add)
                b.cp("pool", Hb[i][:], H[i][:])
            if self.cut == 5:
                return

    def phase_post(self, l, mi, need_ctx):
        b, I, NT = self.b, self.I, self.NT
        isB = (mi == 1)
        pT = self.pT.ap()
        og = SEG_OFF["Bg" if isB else "Dg"]
        psum = Pool(b, 3, [128, 512], F32, "ps", psum=True)
        ident = b.sb([128, 128], F32, "ident")
        b.dma("sp", ident[:], I["ident"].ap())
        vec = b.sb([128, 4, 4], F32, "pvec")
        b.dma("sp", vec[:], I["b_vec" if isB else "d_vec"].ap()[l])
        yp = Pool(b, 4, [128, 512], F32, "py")
        stp = Pool(b, 8, [128, 8], F32, "pst")
        gp = Pool(b, 4, [128, 128], F32, "pg")
        ep = Pool(b, 4, [128, 128], F32, "pe_")
        bp = Pool(b, 4, [128, 128], F32, "pb")
        op_ = Pool(b, 4, [128, 128], BF16, "po")
        NH, HD = (8, 64) if isB else (4, 128)
        for t0 in range(0 if need_ctx else CTX, NT, 128):
            y0, y1 = yp.get(), yp.get()
            b.dma("sp", y0[:], self.ytm.ap()[0, t0:t0 + 128, :])
            b.dma("sp", y1[:], self.ytm.ap()[1, t0:t0 + 128, :])
            b.tt("pool", y0[:], y0[:], y1[:], ALU.add)
            y3 = y0[:].rearrange("p (h d) -> p h d", h=NH)
            st = stp.get()
            if isB:
                b.I("dve", "tensor_reduce", out=st[:, 0:NH], in_=y3, axis=mybir.AxisListType.X, op=ALU.add)
                b.ts("dve", st[:, 0:NH], st[:, 0:NH], 1.0 / HD, ALU.mult)
                b.tt("dve", y3, y3, st[:, 0:NH].unsqueeze(2).broadcast_to([128, NH, HD]), ALU.subtract)
            sq = yp.get()
            b.tt("pool", sq[:], y0[:], y0[:], ALU.mult)
            s2 = stp.get()
            b.I("dve", "tensor_reduce", out=s2[:, 0:NH], in_=sq[:].rearrange("p (h d) -> p h d", h=NH),
                axis=mybir.AxisListType.X, op=ALU.add)
            b.act(s2[:, 0:NH], s2[:, 0:NH], AF.Sqrt, scale=1.0 / HD, bias=(64e-5 if isB else 1e-6))
            b.rcp(s2[:, 0:NH], s2[:, 0:NH])
            b.tt("dve", y3, y3, s2[:, 0:NH].unsqueeze(2).broadcast_to([128, NH, HD]), ALU.mult)
            tpb = psum.get()
            for ct in range(4):
                b.tr(tpb[:, ct * 128:(ct + 1) * 128], y0[:, ct * 128:(ct + 1) * 128], ident[:])
            for ct in range(4):
                tp = _View(tpb.h, ct * 128, 128)
                g = gp.get()
                b.dma("sp", g[:], pT[og + ct * 128:og + (ct + 1) * 128, t0:t0 + 128])
                e = ep.get()
                b.act(e[:], g[:], AF.Exp, scale=-1.0)
                b.ts("pool", e[:], e[:], 1.0, ALU.add)
                b.rcp(e[:], e[:])
                b.tt("pool", e[:], e[:], g[:], ALU.mult)
                o = op_.get()
                if isB:
                    bo = bp.get()
                    b.dma("sp", bo[:], self.bon.ap()[ct * 128:(ct + 1) * 128, t0:t0 + 128])
                    t = gp.get()
                    b.ts("dve", t[:], tp[:], vec[:, 2, ct:ct + 1], ALU.mult, vec[:, 3, ct:ct + 1], ALU.add)
                    b.tt("pool", t[:], t[:], bo[:], ALU.add)
                    b.tt("dve", o[:], t[:], e[:], ALU.mult)
                else:
                    b.stt(o[:], tp[:], vec[:, 0, ct:ct + 1], e[:], ALU.mult, ALU.mult)
                b.dma("pool", self.yT.ap()[mi, ct * 128:(ct + 1) * 128, t0:t0 + 128], o[:])


    def phase_d_prep(self, l):
        b, I, NT = self.b, self.I, self.NT
        pT = self.pT.ap()
        oq, oab = SEG_OFF["Dqkv"], SEG_OFF["Dab"]
        psum = Pool(b, 4, [128, 512], F32, "ps", psum=True)
        cw = b.sb([128, 12, 5], F32, "cw")
        b.dma("sp", cw[:], I["d_conv"].ap()[l])
        abp = b.sb([8, 4], F32, "abp")
        b.dma("sp", abp[:], I["d_ab"].ap()[l])
        nexpA = b.sb([8, 1], F32, "nexpA")
        b.act(nexpA[:], abp[:, 1:2], AF.Exp)
        b.ts("dve", nexpA[:], nexpA[:], -1.0, ALU.mult)
        onesF = b.sb([8, 512], F32, "onesF")
        b.ms("dve", onesF[:], 1.0)
        Sx = b.sb([8, 513], F32, "Sx")
        b.ms("dve", Sx[:, 0:1], 0.0)
        ptp = Pool(b, 3, [128, 516], F32, "dpt")
        tp = Pool(b, 6, [128, 512], F32, "dt")
        sqp = Pool(b, 2, [128, 512], BF16, "dsq")
        o3p = Pool(b, 2, [128, 3, 512], F32, "o3")
        rp = Pool(b, 12, [8, 512], F32, "dr")
        o6p = Pool(b, 2, [8, 6, 512], F32, "o6")
        egp = Pool(b, 2, [8, 4], F32, "deg")
        for (t0, T, is_ctx) in self.blocks:
            lo, hi = self.seg_bounds(t0)
            nch = T // 128
            for h in range(4):
                o3 = o3p.get()
                for kind in range(3):
                    i = kind * 4 + h
                    pt = ptp.get()
                    a0, a1 = max(t0 - 2, lo), min(t0 + T + 2, hi)
                    if a0 > t0 - 2:
                        b.ms("pool", pt[:, 0:2], 0.0)
                    if a1 < t0 + T + 2:
                        b.ms("pool", pt[:, T + 2:T + 4], 0.0)
                    b.dma("sp", pt[:, a0 - (t0 - 2):a1 - (t0 - 2)], pT[oq + i * 128:oq + (i + 1) * 128, a0:a1])
                    acc = tp.get()
                    b.ts("dve", acc[:, 0:T], pt[:, 0:T], cw[:, i, 0:1], ALU.mult)
                    for j in range(1, 5):
                        b.stt(acc[:, 0:T], pt[:, j:j + T], cw[:, i, j:j + 1], acc[:, 0:T], ALU.mult, ALU.add)
                    e = tp.get()
                    b.act(e[:, 0:T], acc[:, 0:T], AF.Exp, scale=-1.0)
                    b.ts("pool", e[:, 0:T], e[:, 0:T], 1.0, ALU.add)
                    b.rcp(e[:, 0:T], e[:, 0:T])
                    if kind == 2:
                        b.tt("pool", o3[:, 2, 0:T], acc[:, 0:T], e[:, 0:T], ALU.mult)
                        continue
                    b.tt("pool", acc[:, 0:T], acc[:, 0:T], e[:, 0:T], ALU.mult)
                    sq = sqp.get()
                    b.act(sq[:, 0:T], acc[:, 0:T], AF.Square)
                    ps = psum.get()
                    b.mm(ps[:, 0:T], self.ones_bf[:], sq[:, 0:T])
                    rn = tp.get()
                    b.act(rn[:, 0:T], ps[:, 0:T], AF.Sqrt, bias=1e-12)
                    b.rcp(rn[:, 0:T], rn[:, 0:T])
                    if kind == 0:
                        b.stt(o3[:, 0, 0:T], acc[:, 0:T], 128.0 ** -0.5, rn[:, 0:T], ALU.mult, ALU.mult)
                    else:
                        b.tt("dve", o3[:, 1, 0:T], acc[:, 0:T], rn[:, 0:T], ALU.mult)
                b.dma("pool", self.UD.ap()[h, :, :, t0:t0 + T], o3[:, :, 0:T])
            lgr, br = rp.get(), rp.get()
            for d in range(2):
                b.dma("sp", lgr[d * 4:(d + 1) * 4, 0:T], pT[oab + d * 8:oab + d * 8 + 4, t0:t0 + T])
                b.dma("sp", br[d * 4:(d + 1) * 4, 0:T], pT[oab + d * 8 + 4:oab + d * 8 + 8, t0:t0 + T])
            lg = rp.get()
            b.act(lg[:, 0:T], lgr[:, 0:T], AF.Exp, bias=abp[:, 0:1])
            b.act(lg[:, 0:T], lg[:, 0:T], AF.Ln, bias=1.0)
            b.ts("dve", lg[:, 0:T], lg[:, 0:T], nexpA[:, 0:1], ALU.mult)
            beta = rp.get()
            b.act(beta[:, 0:T], br[:, 0:T], AF.Sigmoid)
            b.I("dve", "tensor_tensor_scan", out=Sx[:, 1:T + 1], data0=onesF[:, 0:T], data1=lg[:, 0:T],
                initial=0.0, op0=ALU.mult, op1=ALU.add)
            v3 = lambda ap: ap.rearrange("p (c t) -> p c t", t=128)
            gc = rp.get()
            b.tt("dve", v3(gc[:, 0:T]), v3(Sx[:, 1:T + 1]), v3(Sx[:, 0:T])[:, :, 0:1].broadcast_to([8, nch, 128]), ALU.subtract)
            gtot_bc = v3(gc[:, 0:T])[:, :, 127:128].broadcast_to([8, nch, 128])
            o6 = o6p.get()
            t1 = rp.get()
            b.ts("dve", t1[:, 0:T], gc[:, 0:T], abp[:, 2:3], ALU.mult)
            b.stt(v3(t1[:, 0:T]), gtot_bc, abp[:, 3:4], v3(t1[:, 0:T]), ALU.mult, ALU.add)
            b.stt(o6[:, 3, 0:T], lg[:, 0:T], abp[:, 3:4], t1[:, 0:T], ALU.mult, ALU.add)
            b.tt("dve", o6[:, 2, 0:T], o6[:, 3, 0:T], lg[:, 0:T], ALU.subtract)
            elg = rp.get()
            b.act(elg[:, 0:T], lg[:, 0:T], AF.Exp)
            b.tt("dve", o6[:, 0, 0:T], beta[:, 0:T], elg[:, 0:T], ALU.mult)
            b.cp("dve", o6[:, 1, 0:T], beta[:, 0:T])
            t2 = rp.get()
            b.tt("dve", v3(t2[:, 0:T]), gtot_bc, v3(o6[:, 3, 0:T]), ALU.subtract)
            b.act(t2[:, 0:T], t2[:, 0:T], AF.Exp)
            b.tt("dve", o6[:, 4, 0:T], beta[:, 0:T], t2[:, 0:T], ALU.mult)
            b.tt("dve", o6[:, 5, 0:T], o6[:, 4, 0:T], elg[:, 0:T], ALU.mult)
            eg = egp.get()
            b.act(eg[:, 0:nch], v3(gc[:, 0:T])[:, :, 127], AF.Exp)
            b.dma("pool", self.RD.ap()[:, :, t0:t0 + T], o6[:, :, 0:T])
            b.dma("pool", self.EGTD.ap()[:, t0 // 128:t0 // 128 + nch], eg[:, 0:nch])

    def gdn_unit_setup(self, l, gam):
        b, I, NT = self.b, self.I, self.NT
        NCH = NT // 128
        sel8 = b.sb([8, 8, 128], F32, "sel8")
        b.dma("sp", sel8[:], I["sel8"].ap())
        egt = b.sb([8, NCH + (NCH % 2)], F32, "egtall")
        b.ms("dve", egt[:], 0.0)
        b.dma("sp", egt[:, 0:NCH], self.EGTD.ap())
        ps = Pool(b, 1, [128, 512], F32, "psg0", psum=False)
        self._gd = dict(
            sel8=sel8,
            ld3=Pool(b, 10, [128, 3, 128], F32, "ld3"),
            rows=Pool(b, 4, [8, 6, 128], F32, "grow"),
            cols=Pool(b, 4, [128, 4, 8], F32, "gcol"),
            Wt=Pool(b, 10, [128, 3, 128], F32, "gW"),
            xa=Pool(b, 6, [128, 128], F32, "gxa"),
            eb=Pool(b, 4, [128, 256], F32, "geb"),
            XH=Pool(b, 12, [128, 2, 128], BF16, "gXH"),
            cache={},
        )
        self._gd_egt = egt

    def gdn_gam(self, gam, sm, chains):
        b = self.b
        NCH = self.NT // 128
        n2 = NCH + (NCH % 2)
        for i, (u, d) in enumerate(chains):
            p = sm.get()
            b.mm(p[:, 0:n2], self._gd["sel8"][:, d * 4 + u, :], self._gd_egt[:, 0:n2])
            b.cp("act", gam[i][:], p[:, 0:NCH])

    def gdn_unit_prep(self, l, U, XX, TM3, sm, gmask, ident):
        b = self.b
        G = self._gd
        h, d, c, tk = U["u"], U["d"], U["c"], U["tk"]
        r = d * 4 + h
        key = (c, d)
        if key not in G["cache"]:
            rows = G["rows"].get()
            b.dma("sp", rows[:], self.RD.ap()[:, :, tk:tk + 128])
            tp = sm.get()
            for j, kind in enumerate((3, 2, 4, 5)):
                b.tr(tp[:, j * 8:(j + 1) * 8], rows[:, kind, :], ident[0:8, 0:8])
            cols = G["cols"].get()
            b.cp("act", cols[:].rearrange("p k r -> p (k r)"), tp[:, 0:32])
            G["cache"] = {kk: vv for kk, vv in G["cache"].items() if kk[1] != d}
            G["cache"][key] = (rows, cols)
        rows, cols = G["cache"][key]
        ld = G["ld3"].get()
        b.dma("sp", ld[:], self.UD.ap()[h, :, :, tk:tk + 128])
        bc = sm.get()
        for j, kind in enumerate((0, 1, 2, 3)):
            b.mm(bc[:, j * 128:(j + 1) * 128], G["sel8"][:, r, :], rows[:, kind, :])
        b.cp("act", XX[:, 0, :], ld[:, 1, :])
        b.cp("pool", XX[:, 1, :], ld[:, 0, :])
        b.tt("dve", XX[:, 2, :], bc[:, 0:128], ld[:, 1, :], ALU.mult)
        b.tt("dve", XX[:, 3, :], bc[:, 128:256], ld[:, 1, :], ALU.mult)
        tp = sm.get()
        b.tr(tp[:, 0:128], ld[:, 1, :], ident[:])
        b.tr(tp[:, 128:256], ld[:, 2, :], ident[:])
        b.ts("dve", TM3[:, 0, :], tp[:, 0:128], cols[:, 2, r:r + 1], ALU.mult)
        b.ts("dve", TM3[:, 1, :], tp[:, 0:128], cols[:, 3, r:r + 1], ALU.mult)
        b.cp("act", TM3[:, 2, :], tp[:, 128:256])
        Wt = G["Wt"].get()
        mk = (0, 1, 2) if d == 0 else (2, 3, 0)
        xa = G["xa"].get()
        b.stt(xa[:], bc[:, 256:384], cols[:, 0, r:r + 1], gmask[:, mk[0], :], ALU.subtract, ALU.add)
        b.act(Wt[:, 0, :], xa[:], AF.Exp)
        xb = G["xa"].get()
        b.stt(xb[:], bc[:, 384:512], cols[:, 0, r:r + 1], gmask[:, mk[1], :], ALU.subtract, ALU.add)
        b.act(Wt[:, 1, :], xb[:], AF.Exp)
        xc = G["xa"].get()
        b.stt(xc[:], bc[:, 384:512], cols[:, 1, r:r + 1], gmask[:, mk[2], :], ALU.subtract, ALU.subtract)
        b.act(Wt[:, 2, :], xc[:], AF.Exp, scale=-1.0)
        eb = G["eb"].get()
        b.act(eb[:], bc[:, 256:512], AF.Exp)
        XH = G["XH"].get()
        b.tt("dve", XH[:, 0, :], eb[:, 0:128], ld[:, 1, :], ALU.mult)
        b.tt("pool", XH[:, 1, :], eb[:, 128:256], ld[:, 0, :], ALU.mult)
        return Wt[:, 0, :], Wt[:, 1, :], Wt[:, 2, :], XH

    def phase_mcast(self, l):
        b, I = self.b, self.I
        fa = Pool(b, 2, [128, 24, 128], F32, "mcf")
        ba = Pool(b, 2, [128, 24, 128], BF16, "mcb")
        for dt in range(16):
            f, g = fa.get(), ba.get()
            b.dma("sp", f[:], I["wm"].ap()[l, dt])
            b.cp(("dve", "pool")[dt % 2], g[:], f[:])
            b.dma("sp", self.wmb.ap()[dt], g[:])
            f, g = fa.get(), ba.get()
            b.dma("sp", f[:, 0:16, :], I["wo"].ap()[l, dt])
            b.cp(("pool", "dve")[dt % 2], g[:, 0:16, :], f[:, 0:16, :])
            b.dma("sp", self.wob.ap()[dt], g[:, 0:16, :])

    def phase_merge(self, l, need_ctx):
        b, I, NT = self.b, self.I, self.NT
        mv = self.modv
        pT = self.pT.ap()
        opm = SEG_OFF["pm"]
        psG = Pool(b, 2, [128, 512], F32, "psg", psum=True)
        psB = Pool(b, 3, [128, 512], F32, "psb", psum=True)
        psO = Pool(b, 2, [128, 512], F32, "pso", psum=True)
        gb = b.sb([128, 4, 16], F32, "gb")
        b.dma("sp", gb[:], I["g_b"].ap()[l])
        pmf = Pool(b, 1, [128, 2, 512], F32, "pmf")
        pmb = Pool(b, 2, [128, 2, 512], BF16, "pmb")
        ybp = Pool(b, 2, [128, 16, 512], BF16, "yb")
        accT = Pool(b, 2, [128, 16, 512], BF16, "accT")
        wmp = Pool(b, 2, [128, 24, 128], BF16, "wmt")
        wop = Pool(b, 2, [128, 16, 128], BF16, "wot")
        gtp = Pool(b, 3, [128, 512], F32, "mg")
        acp = Pool(b, 2, [128, 512], F32, "macc")
        tmp = Pool(b, 3, [128, 512], F32, "mtmp")
        xtp = Pool(b, 3, [128, 512], F32, "mx")
        src_x = I["xT"] if l == 0 else self.xs
        for (t0, T, is_ctx) in self.blocks:
            if is_ctx and not need_ctx:
                continue
            v = 1 if is_ctx else 0
            pf, pb = pmf.get(), pmb.get()
            b.dma("sp", pf[:, :, 0:T], pT[opm:opm + 256, t0:t0 + T].rearrange("(k p) t -> p k t", p=128))
            b.cp("pool", pb[:, :, 0:T], pf[:, :, 0:T])
            yb = ybp.get()
            for mi in range(4):
                b.dma("sp", yb[:, mi * 4:(mi + 1) * 4, 0:T], self.yT.ap()[mi, :, t0:t0 + T].rearrange("(k p) t -> p k t", p=128))
            aT = accT.get()
            for dt in range(16):
                wt = wmp.get()
                b.dma("sp", wt[:], self.wmb.ap()[dt])
                acc = acp.get()
                for i in range(4):
                    pg = psG.get()
                    for rc in range(2):
                        b.mm(pg[:, 0:T], wt[:, i * 2 + rc, :], pb[:, rc, 0:T], rc == 0, rc == 1)
                    gt = gtp.get()
                    b.act(gt[:, 0:T], pg[:, 0:T], AF.Sigmoid, bias=gb[:, i, dt:dt + 1])
                    pbr = psB.get()
                    for cc in range(4):
                        b.mm(pbr[:, 0:T], wt[:, 8 + i * 4 + cc, :], yb[:, i * 4 + cc, 0:T], cc == 0, cc == 3)
                    if i == 0:
                        b.tt("dve", acc[:, 0:T], pbr[:, 0:T], gt[:, 0:T], ALU.mult)
                    else:
                        tm = tmp.get()
                        b.tt("dve", tm[:, 0:T], pbr[:, 0:T], gt[:, 0:T], ALU.mult)
                        b.tt("pool", acc[:, 0:T], acc[:, 0:T], tm[:, 0:T], ALU.add)
                b.cp("act", aT[:, dt, 0:T], acc[:, 0:T])
            for dt in range(16):
                wo = wop.get()
                b.dma("sp", wo[:], self.wob.ap()[dt])
                po = psO.get()
                for k in range(16):
                    b.mm(po[:, 0:T], wo[:, k, :], aT[:, k, 0:T], k == 0, k == 15)
                xt = xtp.get()
                b.dma("sp", xt[:, 0:T], src_x.ap()[dt * 128:(dt + 1) * 128, t0:t0 + T])
                b.stt(xt[:, 0:T], po[:, 0:T], mv[:, l, dt, 3 * v + 2:3 * v + 3], xt[:, 0:T], ALU.mult, ALU.add)
                b.dma("pool", self.xs.ap()[dt * 128:(dt + 1) * 128, t0:t0 + T], xt[:, 0:T])

    def phase_final(self):
        b, I = self.b, self.I
        psum = Pool(b, 2, [128, 512], F32, "ps", psum=True)
        fg = b.sb([128, KC], F32, "fg")
        b.dma("sp", fg[:], I["final_g"].ap())
        Pxt = Pool(b, 2, [128, KC, 512], F32, "xt")
        Psq = Pool(b, 1, [128, KC, 512], BF16, "sq")
        Prs = Pool(b, 2, [128, 512], F32, "rs")
        for (t0, T, is_ctx) in self.blocks:
            if is_ctx:
                continue
            xt = Pxt.get()
            for k in range(KC):
                b.dma("sp", xt[:, k, 0:T], self.xs.ap()[k * 128:(k + 1) * 128, t0:t0 + T])
            sq = Psq.get()
            b.act(sq[:, :, 0:T], xt[:, :, 0:T], AF.Square)
            ps = psum.get()
            for k in range(KC):
                b.mm(ps[:, 0:T], self.ones_bf[:], sq[:, k, 0:T], k == 0, k == KC - 1)
            rs = Prs.get()
            b.act(rs[:, 0:T], ps[:, 0:T], AF.Sqrt, scale=1.0 / D_MODEL, bias=1e-6)
            b.rcp(rs[:, 0:T], rs[:, 0:T])
            for k in range(KC):
                b.stt(xt[:, k, 0:T], xt[:, k, 0:T], fg[:, k:k + 1], rs[:, 0:T], ALU.mult, ALU.mult)
                b.dma("pool", self.out.ap()[k * 128:(k + 1) * 128, t0 - CTX:t0 - CTX + T], xt[:, k, 0:T])


def _fm(v):
    v = np.asarray(v)
    c = v.shape[-1]
    return np.ascontiguousarray(np.swapaxes(v.reshape(v.shape[:-1] + (c // 128, 128)), -1, -2))


def host_inputs(inp, bi, n_lat, depth):
    L = depth
    d = {}
    xcat = np.concatenate([inp["ctx"][bi], inp["x"][bi][:n_lat]], axis=0)
    d["xT"] = np.ascontiguousarray(xcat.T)
    d["cc"] = np.ascontiguousarray(np.stack([_fm(inp["c"][bi]), _fm(inp["c_ctx"])], axis=-1))
    d["norm_g"] = _fm(inp["norm_g"][:L])
    d["w_mod"] = np.ascontiguousarray(inp["w_mod"][:L])
    d["b_mod"] = _fm(inp["b_mod"][:L])
    cols = w_in_columns()
    w = inp["w_in"][:L][:, :, cols]
    w = w.reshape(L, KC, 128, NCT, 128).transpose(0, 3, 2, 1, 4)
    d["w_in"] = np.ascontiguousarray(w)
    d["final_g"] = _fm(inp["final_g"])
    NT = CTX + n_lat
    p = np.arange(128)
    dd = p % 64
    half, r = dd // 32, dd % 32
    inv = 10000.0 ** (-np.arange(0, 32, 2, dtype=np.float32) / np.float32(32))
    f = inv[r % 16].astype(np.float32)
    t = np.arange(n_lat)
    pos = np.where(half[:, None] == 0, (t // 64)[None, :], (t % 64)[None, :]).astype(np.float32)
    ang = (pos * f[:, None]).astype(np.float32)
    sign = np.where(r < 16, -1.0, 1.0).astype(np.float32)
    rc = np.ones((128, NT), np.float32)
    rsn = np.zeros((128, NT), np.float32)
    rc[:, CTX:] = np.cos(ang)
    rsn[:, CTX:] = np.sin(ang) * sign[:, None]
    d["ropec"], d["ropes"] = rc, rsn
    j = np.arange(128)[:, None]
    i = np.arange(128)[None, :]
    d["m3"] = np.concatenate([(j <= i), np.ones((128, 128), bool), (i <= j)], axis=1).astype(np.float32)
    d["ident"] = np.eye(128, dtype=np.float32)
    d["a_sink"] = np.ascontiguousarray(np.broadcast_to(inp["a_sink"][:L, None, :], (L, 128, 8)))
    pm = _perm64()
    qn, kn = inp["c_qn"][:L], inp["c_kn"][:L]
    cq = np.stack([qn, qn[:, pm], kn, kn[:, pm]], axis=-1)
    d["c_qk"] = np.ascontiguousarray(np.concatenate([cq, cq], axis=1))
    pp = np.arange(128)
    d["bd2"] = np.ascontiguousarray(np.broadcast_to((pp[:, None] // 64 == np.arange(2)[None, :])[:, :, None], (128, 2, 64))).astype(np.float32)
    row, col = pp[:, None], pp[None, :]
    same = (row // 64) == (col // 64)
    tri = np.stack([row < col, row <= col, row > col, row >= col], axis=1)
    d["rmask"] = (tri & same[:, None, :]).astype(np.float32)
    d["gmask"] = np.where(tri, 0.0, -1.0e4).astype(np.float32)
    d["b_mu"] = np.ascontiguousarray(_fm(inp["b_mu"][:L]).transpose(0, 2, 3, 1))
    w0 = _fm(inp["b_w0"][:L]).transpose(0, 2, 1, 3)
    a0 = _fm(inp["b_a0"][:L]).transpose(0, 2, 1, 3)
    d["b_w0a0"] = np.ascontiguousarray(np.stack([w0, a0], axis=2))
    d["b_aw"] = np.ascontiguousarray(np.concatenate([inp["b_wup"][:L], inp["b_aup"][:L]], axis=2).transpose(0, 2, 1, 3))
    d["b_vec"] = np.ascontiguousarray(np.stack([_fm(inp[k][:L]) for k in ("b_kk", "b_ka", "b_lng", "b_lnb")], axis=2))
    rk = inp["b_rk"][:L]
    blk = np.zeros((L, 128, 4, 128), np.float32)
    for pr in range(4):
        for hh in range(2):
            blk[:, hh * 64:(hh + 1) * 64, pr, hh * 64:(hh + 1) * 64] = rk[:, 2 * pr + hh, :, None]
    d["b_rkblk"] = blk
    sel = np.zeros((8, 8, 128), np.float32)
    for r_ in range(8):
        sel[r_, r_, :] = 1.0
    d["sel8"] = sel
    d["d_conv"] = np.ascontiguousarray(_fm(inp["d_conv"][:L]).transpose(0, 2, 3, 1))
    ab = np.zeros((L, 8, 4), np.float32)
    ab[:, :, 0] = inp["d_dtb"][:L].reshape(L, 8)
    ab[:, :, 1] = inp["d_alog"][:L].reshape(L, 8)
    ab[:, 0:4, 2], ab[:, 4:8, 2] = 1.0, -1.0
    ab[:, 4:8, 3] = 1.0
    d["d_ab"] = ab
    dv = np.zeros((L, 128, 4, 4), np.float32)
    dv[:, :, 0, :] = inp["d_norm"][:L][:, :, None]
    d["d_vec"] = dv
    gu = inp["g_up"][:L].reshape(L, 4, 2, 128, 16, 128)
    wb = inp["w_br"][:L].reshape(L, 4, 4, 128, 16, 128)
    wm = np.concatenate([gu.transpose(0, 4, 3, 1, 2, 5).reshape(L, 16, 128, 8, 128),
                         wb.transpose(0, 4, 3, 1, 2, 5).reshape(L, 16, 128, 16, 128)], axis=3)
    d["wm"] = np.ascontiguousarray(wm)
    wo = inp["w_out"][:L].reshape(L, 16, 128, 16, 128)
    d["wo"] = np.ascontiguousarray(wo.transpose(0, 3, 2, 1, 4))
    d["g_b"] = np.ascontiguousarray(_fm(inp["g_b"][:L]).transpose(0, 2, 1, 3))
    return d


N_CORES = 8
_PROG_CACHE = {}


def run_model(inp, n_lat, depth):
    inp = {k: np.asarray(v) for k, v in inp.items()}
    B = inp["x"].shape[0]
    key = (n_lat, depth)
    if key not in _PROG_CACHE:
        _PROG_CACHE[key] = Prog(n_lat, depth).build()
    nc = _PROG_CACHE[key]
    per_b = [host_inputs(inp, bi, n_lat, depth) for bi in range(B)]
    in_maps = [per_b[i % B] for i in range(N_CORES)]
    res = run_bass_kernel_spmd(nc, in_maps, core_ids=list(range(N_CORES)))
    out = np.stack([np.ascontiguousarray(res.results[bi]["outT"].T) for bi in range(B)], axis=0)
    return out.astype(np.float32)


def kernel(**inputs):
    return run_model(inputs, 8192, 4)
```

```python
import math
from contextlib import ExitStack

import numpy as np
import concourse.bass as bass
import concourse.mybir as mybir
from concourse.bass_utils import run_bass_kernel_spmd

F32 = mybir.dt.float32
BF16 = mybir.dt.bfloat16
AF = mybir.ActivationFunctionType
ALU = mybir.AluOpType

D_MODEL = 2048
CTX = 256
W = 512
KC = D_MODEL // 128

OFF_A, OFF_B, OFF_C, OFF_D, OFF_G = 0, 1280, 3456, 4736, 6800
SEGS = [
    ("Aq", OFF_A + 0, 512, False), ("Aqp", OFF_A + 0, 512, True),
    ("Ak", OFF_A + 512, 128, False), ("Akp", OFF_A + 512, 128, True),
    ("Ag", OFF_A + 768, 512, False),
    ("Bz", OFF_B + 0, 1664, False), ("Bg", OFF_B + 1664, 512, False),
    ("Cq", OFF_C + 0, 512, False), ("Cqp", OFF_C + 0, 512, True),
    ("Ck", OFF_C + 512, 128, False), ("Ckp", OFF_C + 512, 128, True),
    ("Cg", OFF_C + 768, 512, False),
    ("Dqkv", OFF_D + 0, 1536, False), ("Dg", OFF_D + 1552, 512, False),
    ("pm", OFF_G, 256, False),
    ("Dab", OFF_D + 1536, 16, False),
]
SEG_OFF = {}
_o = 0
for _n, _s, _c, _p in SEGS:
    SEG_OFF[_n] = _o
    _o += _c
NFM = _o
NFM_PAD = 8192
VSEGS = [("Av", OFF_A + 640, 128), ("Cv", OFF_C + 640, 128)]
NCT = NFM_PAD // 128 + len(VSEGS)


def _perm64():
    p = np.arange(64)
    blk, r = p // 32, p % 32
    return blk * 32 + (r + 16) % 32


def w_in_columns():
    cols = []
    for n, s, c, perm in SEGS:
        idx = np.arange(s, s + c)
        if perm:
            idx = idx.reshape(-1, 64)[:, _perm64()].reshape(-1)
        cols.append(idx)
    cols = np.concatenate(cols)
    pad = np.zeros(NFM_PAD - NFM, dtype=np.int64)
    vcols = np.concatenate([np.arange(s, s + c) for _, s, c in VSEGS])
    return np.concatenate([cols, pad, vcols])


class Tile:
    __slots__ = ("h", "w", "r", "name")

    def __init__(self, h, name):
        self.h = h
        self.w = None
        self.r = []
        self.name = name

    def __getitem__(self, idx):
        return self.h[idx]


_WRITE_KW = ("out", "ap", "accum_out")


class _RowSplit:
    def __init__(self, a, b, half):
        self.a, self.b, self.half = a, b, half

    def ap(self):
        return self

    def __getitem__(self, idx):
        r, c = idx
        if r.start >= self.half:
            return self.b.ap()[r.start - self.half:r.stop - self.half, c]
        assert r.stop <= self.half
        return self.a.ap()[r, c]


class _View:
    def __init__(self, h, lo, w):
        self.h, self.lo, self.w = h, lo, w

    def __getitem__(self, idx):
        if not isinstance(idx, tuple):
            idx = (idx, slice(None))
        p, c = idx[0], idx[1]
        a, bnd, _ = c.indices(self.w)
        return self.h[p, self.lo + a:self.lo + bnd]


class Builder:
    ENGS = ("pe", "dve", "act", "pool", "sp")

    def __init__(self, nc, es):
        self.nc = nc
        self.es = es
        self.ops = {e: [] for e in self.ENGS}
        self.cnt = {e: 0 for e in self.ENGS}
        self.sem = {}
        for e in ("pe", "dve", "act", "pool"):
            self.sem[e] = es.enter_context(nc.semaphore("s_" + e))
        self.waited = {e: {} for e in self.ENGS}
        NQ = 12
        self.dsem = {}
        self.dnext = {}
        for q in ("sp", "pool", "act"):
            self.dsem[q] = [es.enter_context(nc.semaphore("d_%s%d" % (q, i))) for i in range(NQ)]
            self.dnext[q] = 0
        self.dcum = {}
        self.n_tiles = 0
        self.reg = {}

    def scope(self):
        b = self

        class _S:
            def __enter__(s2):
                s2.old = b.es
                s2.st = ExitStack()
                b.es = s2.st
                return s2

            def __exit__(s2, *a):
                b.barrier()
                b.es = s2.old
                s2.st.close()
                return False
        return _S()

    def sb(self, shape, dtype, name=None, psum=False):
        self.n_tiles += 1
        name = "%s_%d" % (name or "t", self.n_tiles)
        mk = self.nc.psum_tensor if psum else self.nc.sbuf_tensor
        h = self.es.enter_context(mk(name, list(shape), dtype))
        t = Tile(h, name)
        self.reg[h.name] = t
        return t

    def ps(self, shape, dtype=F32, name=None):
        return self.sb(shape, dtype, name, psum=True)

    def _need(self, eng, rec, waits):
        if rec is None:
            return
        if rec[0] == "E":
            key, idx = rec[1], rec[2]
            if key == eng and eng == "pe":
                return
        else:
            key, idx = rec[1], rec[2]
        if self.waited[eng].get(key, 0) >= idx:
            return
        self.waited[eng][key] = idx
        waits.append((key, idx))

    def _deps(self, eng, reads, writes):
        waits = []
        for t in reads:
            self._need(eng, t.w, waits)
        for t in writes:
            self._need(eng, t.w, waits)
            for r in t.r:
                self._need(eng, r, waits)
        return waits

    def _split(self, kw):
        reads, writes = [], []
        for k, v in kw.items():
            if hasattr(v, "tensor") and hasattr(v, "ap"):
                t = self.reg.get(v.tensor.name)
                if isinstance(t, tuple):
                    t = t[1][(v.offset % 512) // t[0]]
                if t is not None:
                    (writes if k in _WRITE_KW else reads).append(t)
        return reads, writes

    def subtiles(self, tile, width):
        subs = []
        for j in range(512 // width):
            st = Tile(_View(tile.h, j * width, width), "%s_s%d" % (tile.name, j))
            subs.append(st)
        self.reg[tile.h.name] = (width, subs)
        return subs

    def _mark(self, rec, reads, writes):
        for t in reads:
            t.r.append(rec)
        for t in writes:
            t.w = rec
            t.r = []

    def I(self, eng, method, **kw):
        reads, writes = self._split(kw)
        waits = self._deps(eng, reads, writes)
        self.cnt[eng] += 1
        idx = self.cnt[eng]
        self.ops[eng].append((waits, method, kw, None))
        self._mark(("E", eng, idx), reads, writes)

    def dma(self, q, out, in_, **kw):
        reads, writes = self._split(dict(out=out, in_=in_))
        waits = self._deps(q, reads, writes)
        i = self.dnext[q]
        self.dnext[q] = (i + 1) % len(self.dsem[q])
        key = (q, i)
        prev = self.dcum.get(key, 0)
        if prev:
            self._need(q, ("D", key, prev), waits)
        val = prev + 16
        self.dcum[key] = val
        kw = dict(kw, out=out, in_=in_)
        self.ops[q].append((waits, "dma_start", kw, self.dsem[q][i]))
        self._mark(("D", key, val), reads, writes)

    def barrier(self):
        for e in self.ENGS:
            waits = []
            for x in ("pe", "dve", "act", "pool"):
                if self.cnt[x] and not (x == e and e == "pe"):
                    self._need(e, ("E", x, self.cnt[x]), waits)
            for key, val in self.dcum.items():
                self._need(e, ("D", key, val), waits)
            if waits:
                self.ops[e].append((waits, None, None, None))

    def _semof(self, key):
        if isinstance(key, tuple):
            return self.dsem[key[0]][key[1]]
        return self.sem[key]

    def emit(self):
        nc = self.nc
        with nc.Block() as block:
            def mk(ename):
                def body(e):
                    own = self.sem.get(ename)
                    for waits, method, kw, dsem in self.ops[ename]:
                        for key, val in waits:
                            e.wait_ge(self._semof(key), val)
                        if method is None:
                            continue
                        ins = getattr(e, method)(**kw)
                        if dsem is not None:
                            ins.then_inc(dsem, 16)
                        else:
                            ins.then_inc(own, 1)
                return body
            block.tensor(mk("pe"))
            block.vector(mk("dve"))
            block.scalar(mk("act"))
            block.gpsimd(mk("pool"))
            block.sync(mk("sp"))

    def mm(self, out, lhsT, rhs, start=True, stop=True):
        self.I("pe", "matmul", out=out, lhsT=lhsT, rhs=rhs, start=start, stop=stop)

    def tr(self, out, in_, identity):
        self.I("pe", "transpose", out=out, in_=in_, identity=identity)

    def act(self, out, in_, func, scale=1.0, bias=0.0):
        self.I("act", "activation", out=out, in_=in_, func=func, scale=scale, bias=bias)

    def tt(self, eng, out, in0, in1, op):
        self.I(eng, "tensor_tensor", out=out, in0=in0, in1=in1, op=op)

    def ts(self, eng, out, in0, s1, op0, s2=None, op1=None):
        if op1 is None:
            self.I(eng, "tensor_scalar", out=out, in0=in0, scalar1=s1, scalar2=None, op0=op0)
        else:
            self.I(eng, "tensor_scalar", out=out, in0=in0, scalar1=s1, scalar2=s2, op0=op0, op1=op1)

    def stt(self, out, in0, scalar, in1, op0, op1):
        self.I("dve", "scalar_tensor_tensor", out=out, in0=in0, scalar=scalar, in1=in1, op0=op0, op1=op1)

    def cp(self, eng, out, in_):
        if eng == "act":
            self.I("act", "activation", out=out, in_=in_, func=AF.Copy)
        else:
            self.I(eng, "tensor_copy", out=out, in_=in_)

    def rcp(self, out, in_):
        self.I("dve", "reciprocal", out=out, in_=in_)

    def ms(self, eng, ap, val):
        self.I(eng, "memset", ap=ap, constant=val)


class Pool:
    def __init__(self, b, n, shape, dtype, name, psum=False):
        self.t = [b.sb(shape, dtype, name, psum=psum) for _ in range(n)]
        self.i = 0

    def get(self):
        t = self.t[self.i]
        self.i = (self.i + 1) % len(self.t)
        return t


class Prog:
    def __init__(self, n_lat, depth, debug=(), mixers=(0, 1, 2, 3)):
        self.n = n_lat
        self.NT = CTX + n_lat
        self.depth = depth
        self.debug = set(debug)
        self.mixers = mixers
        self.stages = ("prep", "units", "post")
        self.cut = 0
        self.do_merge = True
        self.no_inter = False
        self.blocks = [(0, CTX, True)]
        t = CTX
        while t < self.NT:
            s = min(512, self.NT - t)
            self.blocks.append((t, s, False))
            t += s

    def build(self):
        nc = bass.Bass("TRN2", target_bir_lowering=False)
        self.nc = nc
        L, NT = self.depth, self.NT
        with ExitStack() as es:
            b = Builder(nc, es)
            self.b = b
            dk = lambda name: ("ExternalOutput" if name in self.debug else "Internal")
            I = {}

            def inp(name, shape, dt=F32):
                I[name] = nc.dram_tensor(name, list(shape), dt, kind="ExternalInput")
            inp("xT", [D_MODEL, NT])
            inp("cc", [128, KC, 2])
            inp("norm_g", [L, 128, KC])
            inp("w_mod", [L, D_MODEL, 3 * D_MODEL])
            inp("b_mod", [L, 128, 48])
            inp("w_in", [L, NCT, 128, KC, 128])
            inp("final_g", [128, KC])
            inp("ropec", [128, NT])
            inp("ropes", [128, NT])
            inp("m3", [128, 384])
            inp("ident", [128, 128])
            inp("a_sink", [L, 128, 8])
            inp("c_qk", [L, 128, 4])
            inp("bd2", [128, 2, 64])
            inp("rmask", [128, 4, 128])
            inp("b_mu", [L, 128, 13, 2])
            inp("b_w0a0", [L, 128, 2, 2, 4])
            inp("b_aw", [L, 128, 2, 512])
            inp("b_vec", [L, 128, 4, 4])
            inp("b_rkblk", [L, 128, 4, 128])
            inp("gmask", [128, 4, 128])
            inp("sel8", [8, 8, 128])
            inp("d_conv", [L, 128, 12, 5])
            inp("d_ab", [L, 8, 4])
            inp("d_vec", [L, 128, 4, 4])
            inp("wm", [L, 16, 128, 24, 128])
            inp("wo", [L, 16, 128, 16, 128])
            inp("g_b", [L, 128, 4, 16])
            self.I = I
            self.out = nc.dram_tensor("outT", [D_MODEL, self.n], F32, kind="ExternalOutput")
            self.xs = nc.dram_tensor("xs", [D_MODEL, NT], F32, kind=dk("xs"))
            self.pT = _RowSplit(nc.dram_tensor("pTa", [NFM_PAD // 2, NT], F32, kind=dk("pTa")),
                                nc.dram_tensor("pTb", [NFM_PAD // 2, NT], F32, kind=dk("pTb")), NFM_PAD // 2)
            self.vtm = nc.dram_tensor("vtm", [2, NT, 128], BF16, kind=dk("vtm"))
            self.wbf = nc.dram_tensor("wbf", [NCT, 128, KC, 128], BF16, kind="Internal")
            self.yT = nc.dram_tensor("yT", [4, W, NT], BF16, kind=dk("yT"))
            self.UB = nc.dram_tensor("UB", [2, 4, 128, 7, NT], F32, kind=dk("UB"))
            self.GAMB = nc.dram_tensor("GAMB", [2, 4, 128, NT // 64], F32, kind=dk("GAMB"))
            self.bon = nc.dram_tensor("bon", [W, NT], F32, kind=dk("bon"))
            self.ytm = nc.dram_tensor("ytm", [2, NT, W], F32, kind=dk("ytm"))
            self.UD = nc.dram_tensor("UD", [4, 128, 3, NT], F32, kind=dk("UD"))
            self.RD = nc.dram_tensor("RD", [8, 6, NT], F32, kind=dk("RD"))
            self.EGTD = nc.dram_tensor("EGTD", [8, NT // 128], F32, kind=dk("EGTD"))
            self.wmb = nc.dram_tensor("wmb", [16, 128, 24, 128], BF16, kind="Internal")
            self.wob = nc.dram_tensor("wob", [16, 128, 16, 128], BF16, kind="Internal")
            self.ones_bf = b.sb([128, 128], BF16, "ones")
            b.ms("dve", self.ones_bf[:], 1.0)
            self.modv = b.sb([128, L, KC, 6], F32, "modv")

            with b.scope():
                self.phase_mod()
            for l in range(L):
                with b.scope():
                    self.phase_wcast(l)
                with b.scope():
                    self.phase_inproj(l)
                need_ctx = l < L - 1
                for mi in self.mixers:
                    with b.scope():
                        if mi in (0, 2):
                            self.phase_attn(l, mi, need_ctx)
                        elif mi == 1 and "prep" in self.stages:
                            self.phase_b_prep(l)
                    if mi in (1, 3):
                        if mi == 3 and "prep" in self.stages:
                            with b.scope():
                                self.phase_d_prep(l)
                        if "units" in self.stages:
                            with b.scope():
                                self.phase_units(l, mi)
                        if "post" in self.stages:
                            with b.scope():
                                self.phase_post(l, mi, need_ctx)
                if self.do_merge:
                    with b.scope():
                        self.phase_mcast(l)
                    with b.scope():
                        self.phase_merge(l, need_ctx)
            if self.do_merge:
                with b.scope():
                    self.phase_final()
            b.barrier()
            b.emit()
        return nc

    def phase_mod(self):
        b, I, L = self.b, self.I, self.depth
        psum = Pool(b, 4, [128, 512], F32, "ps", psum=True)
        cc = b.sb([128, KC, 2], F32, "cc")
        b.dma("sp", cc[:], I["cc"].ap())
        sc = b.sb([128, KC, 2], F32, "sc")
        b.act(sc[:], cc[:], AF.Silu)
        wpool = Pool(b, 2, [128, KC, 512], F32, "wmod")
        raw = b.sb([128, 48, 2], F32, "modraw")
        bm = b.sb([128, 48], F32, "bmod")
        ng = b.sb([128, KC], F32, "ng")
        mv = self.modv
        for l in range(L):
            b.dma("sp", bm[:], I["b_mod"].ap()[l])
            b.dma("sp", ng[:], I["norm_g"].ap()[l])
            for cg in range(12):
                wt = wpool.get()
                src = I["w_mod"].ap()[l, :, cg * 512:(cg + 1) * 512].rearrange("(k p) c -> p k c", p=128)
                b.dma("sp", wt[:], src)
                for j in range(4):
                    ct = cg * 4 + j
                    ps = psum.get()
                    for k in range(KC):
                        b.mm(ps[:, 0:2], wt[:, k, j * 128:(j + 1) * 128], sc[:, k, :], k == 0, k == KC - 1)
                    b.ts("dve", raw[:, ct, :], ps[:, 0:2], bm[:, ct:ct + 1], ALU.add)
            for v in range(2):
                b.stt(mv[:, l, :, 3 * v + 0], raw[:, 16:32, v], 1.0, ng[:], ALU.add, ALU.mult)
                b.cp("dve", mv[:, l, :, 3 * v + 1], raw[:, 0:16, v])
                b.cp("dve", mv[:, l, :, 3 * v + 2], raw[:, 32:48, v])

    def phase_wcast(self, l):
        b, I = self.b, self.I
        pf = Pool(b, 3, [128, KC, 128], F32, "wcf")
        pb = Pool(b, 3, [128, KC, 128], BF16, "wcb")
        for ct in range(NCT):
            f, g = pf.get(), pb.get()
            b.dma("sp", f[:], I["w_in"].ap()[l, ct])
            b.cp(("dve", "pool")[ct % 2], g[:], f[:])
            b.dma("sp", self.wbf.ap()[ct], g[:])

    def phase_inproj(self, l):
        b, I = self.b, self.I
        mv = self.modv
        psum = Pool(b, 6, [128, 512], F32, "ps", psum=True)
        Pxt = Pool(b, 2, [128, KC, 512], F32, "xt")
        Psq = Pool(b, 1, [128, KC, 512], BF16, "sq")
        PhT = Pool(b, 2, [128, KC, 512], BF16, "hT")
        Prs = Pool(b, 2, [128, 512], F32, "rs")
        Pw = Pool(b, 3, [128, KC, 128], BF16, "wip")
        Pev = Pool(b, 3, [128, 512], F32, "ev")
        Pevb = Pool(b, 2, [128, 128], BF16, "evb")
        src_x = I["xT"] if l == 0 else self.xs
        for (t0, T, is_ctx) in self.blocks:
            v = 1 if is_ctx else 0
            xt = Pxt.get()
            for k in range(KC):
                b.dma("sp", xt[:, k, 0:T], src_x.ap()[k * 128:(k + 1) * 128, t0:t0 + T])
            sq = Psq.get()
            b.act(sq[:, :, 0:T], xt[:, :, 0:T], AF.Square)
            ps = psum.get()
            for k in range(KC):
                b.mm(ps[:, 0:T], self.ones_bf[:], sq[:, k, 0:T], k == 0, k == KC - 1)
            rs = Prs.get()
            b.act(rs[:, 0:T], ps[:, 0:T], AF.Sqrt, scale=1.0 / D_MODEL, bias=1e-6)
            b.rcp(rs[:, 0:T], rs[:, 0:T])
            hT = PhT.get()
            for k in range(KC):
                b.tt("dve", xt[:, k, 0:T], xt[:, k, 0:T], rs[:, 0:T], ALU.mult)
                b.ts(("pool", "dve")[k % 2], hT[:, k, 0:T], xt[:, k, 0:T], mv[:, l, k, 3 * v:3 * v + 1], ALU.mult,
                     mv[:, l, k, 3 * v + 1:3 * v + 2], ALU.add)
            for ct in range((NFM + 127) // 128):
                wt = Pw.get()
                b.dma("sp", wt[:], self.wbf.ap()[ct])
                ps = psum.get()
                for k in range(KC):
                    b.mm(ps[:, 0:T], wt[:, k, :], hT[:, k, 0:T], k == 0, k == KC - 1)
                ev = Pev.get()
                b.cp(("act", "dve")[ct % 2], ev[:, 0:T], ps[:, 0:T])
                b.dma("pool", self.pT.ap()[ct * 128:(ct + 1) * 128, t0:t0 + T], ev[:, 0:T])
            for vi in range(2):
                wt = Pw.get()
                b.dma("sp", wt[:], self.wbf.ap()[NFM_PAD // 128 + vi])
                for sbk in range(T // 128):
                    ps = psum.get()
                    for k in range(KC):
                        b.mm(ps[:, 0:128], hT[:, k, sbk * 128:(sbk + 1) * 128], wt[:, k, :], k == 0, k == KC - 1)
                    evb = Pevb.get()
                    b.cp("dve", evb[:], ps[:, 0:128])
                    r0 = t0 + sbk * 128
                    b.dma("pool", self.vtm.ap()[vi, r0:r0 + 128, :], evb[:])

    def phase_attn(self, l, mi, need_ctx):
        b, I, NT = self.b, self.I, self.NT
        isC = (mi == 2)
        pre = "C" if isC else "A"
        oq, oqp, ok_, okp, og = (SEG_OFF[pre + x] for x in ("q", "qp", "k", "kp", "g"))
        pT = self.pT.ap()
        NCH = NT // 128
        psS = Pool(b, 4, [128, 512], F32, "psS", psum=True)
        psO = Pool(b, 2, [128, 512], F32, "psO", psum=True)
        psB = Pool(b, 2, [128, 512], F32, "psB", psum=True)
        KT = b.sb([128, 2, NT], BF16, "KT")
        VE = b.sb([128, NCH, 2, 65], BF16, "VE")
        cst = b.sb([128, 16], F32, "acst")
        m3 = b.sb([128, 384], F32, "m3")
        ones_f = b.sb([128, 64], F32, "ones_f")
        oblk = b.sb([128, 128], BF16, "oblk")
        b.ms("dve", ones_f[:], 1.0)
        b.ms("dve", oblk[:], 0.0)
        b.ms("dve", oblk[0:64, 0:64], 1.0)
        b.ms("dve", oblk[64:128, 64:128], 1.0)
        b.dma("sp", m3[:], I["m3"].ap())
        b.dma("sp", cst[:, 0:8], I["a_sink"].ap()[l])
        b.dma("sp", cst[:, 8:12], I["c_qk"].ap()[l])
        b.act(cst[:, 0:8], cst[:, 0:8], AF.Exp)
        b.ms("dve", VE[:, :, :, 64:65], 1.0)
        vsrc = self.vtm.ap()[1 if isC else 0].rearrange("(c p) (g d) -> p c g d", p=128, g=2)
        for c0 in range(0, NCH, 8):
            c1 = min(NCH, c0 + 8)
            for g in range(2):
                b.dma("sp", VE[:, c0:c1, g, 0:64], vsrc[:, c0:c1, g, :])

        ld = Pool(b, 4, [128, 512], F32, "ald")
        tb = Pool(b, 2, [128, 512], F32, "atb")
        tmp = Pool(b, 4, [128, 512], F32, "atmp")
        sqp = Pool(b, 2, [128, 512], BF16, "asq")
        rsp = Pool(b, 2, [128, 512], F32, "ars")

        def rope(dst_ap, rows, rowsp, t0, T, cosb, sinb, nidx, dup):
            x, xp = ld.get(), ld.get()
            if dup:
                for hh in range(2):
                    b.dma("sp", x[hh * 64:(hh + 1) * 64, 0:T], pT[rows:rows + 64, t0:t0 + T])
                    b.dma("sp", xp[hh * 64:(hh + 1) * 64, 0:T], pT[rowsp:rowsp + 64, t0:t0 + T])
            else:
                b.dma("sp", x[:, 0:T], pT[rows:rows + 128, t0:t0 + T])
                b.dma("sp", xp[:, 0:T], pT[rowsp:rowsp + 128, t0:t0 + T])
            t1, t2 = tmp.get(), tmp.get()
            if isC:
                sq = sqp.get()
                b.act(sq[:, 0:T], x[:, 0:T], AF.Square)
                ps = psB.get()
                b.mm(ps[:, 0:T], oblk[:], sq[:, 0:T])
                rs = rsp.get()
                b.act(rs[:, 0:T], ps[:, 0:T], AF.Sqrt, scale=1.0 / 64, bias=1e-6)
                b.rcp(rs[:, 0:T], rs[:, 0:T])
                b.stt(t1[:, 0:T], x[:, 0:T], cst[:, nidx:nidx + 1], cosb[:, 0:T], ALU.mult, ALU.mult)
                b.stt(t2[:, 0:T], xp[:, 0:T], cst[:, nidx + 1:nidx + 2], sinb[:, 0:T], ALU.mult, ALU.mult)
                b.tt("pool", t1[:, 0:T], t1[:, 0:T], t2[:, 0:T], ALU.add)
                b.tt("dve", dst_ap, t1[:, 0:T], rs[:, 0:T], ALU.mult)
            else:
                b.tt("dve", t1[:, 0:T], x[:, 0:T], cosb[:, 0:T], ALU.mult)
                b.tt("pool", t2[:, 0:T], xp[:, 0:T], sinb[:, 0:T], ALU.mult)
                b.tt("dve", dst_ap, t1[:, 0:T], t2[:, 0:T], ALU.add)

        def tables(t0, T):
            cosb, sinb = tb.get(), tb.get()
            b.dma("sp", cosb[:, 0:T], I["ropec"].ap()[:, t0:t0 + T])
            b.dma("sp", sinb[:, 0:T], I["ropes"].ap()[:, t0:t0 + T])
            return cosb, sinb

        for (t0, T, is_ctx) in self.blocks:
            cosb, sinb = tables(t0, T)
            for g in range(2):
                rope(KT[:, g, t0:t0 + T], ok_ + g * 64, okp + g * 64, t0, T, cosb, sinb, 10, True)

        QT = Pool(b, 2, [128, 4, 512], BF16, "QT")
        PTp = Pool(b, 4, [128, 512], BF16, "PT")
        gp = Pool(b, 3, [64, 512], F32, "ag")
        ep = Pool(b, 3, [64, 512], F32, "ae")
        drp = Pool(b, 2, [128, 512], F32, "adr")
        bcp = Pool(b, 2, [64, 512], F32, "abc")
        yp = Pool(b, 3, [64, 512], BF16, "ay")
        nlat_ch = self.n // 128
        for (t0, T, is_ctx) in self.blocks:
            if is_ctx and not need_ctx:
                continue
            cosb, sinb = tables(t0, T)
            qt = QT.get()
            for pr in range(4):
                rope(qt[:, pr, 0:T], oq + pr * 128, oqp + pr * 128, t0, T, cosb, sinb, 8, False)
            chunks = [(0, 0, T, None), (1, 0, T, None)]
            if not is_ctx:
                lb = (t0 - CTX) // 128
                nq = T // 128
                if isC:
                    chunks += [(2 + c, 0, T, None) for c in range(nlat_ch)]
                else:
                    for c in range(max(0, lb - 1), min(nlat_ch, lb + nq + 1)):
                        qlo, qhi = max(lb, c - 1), min(lb + nq - 1, c + 1)
                        chunks.append((2 + c, (qlo - lb) * 128, (qhi - lb + 1) * 128, (qlo - c + 1) * 128))
            for h in range(8):
                g, pr, po = h // 4, h // 2, (h % 2) * 64
                gt = gp.get()
                b.dma("sp", gt[:, 0:T], pT[og + h * 64:og + (h + 1) * 64, t0:t0 + T])
                O = psO.get()
                LA = 2
                Sq = []

                def issue_s(cj):
                    ch_, lo_, hi_, _m = chunks[cj]
                    S_ = psS.get()
                    b.mm(S_[:, lo_:hi_], KT[po:po + 64, g, ch_ * 128:(ch_ + 1) * 128], qt[po:po + 64, pr, lo_:hi_])
                    Sq.append(S_)
                for cj in range(min(LA, len(chunks))):
                    issue_s(cj)
                for ci, (ch, lo, hi, mlo) in enumerate(chunks):
                    if ci + LA < len(chunks):
                        issue_s(ci + LA)
                    S = Sq[ci]
                    pt = PTp.get()
                    b.act(pt[:, lo:hi], S[:, lo:hi], AF.Exp, scale=0.125)
                    if mlo is not None:
                        b.tt("dve", pt[:, lo:hi], pt[:, lo:hi], m3[:, mlo:mlo + hi - lo], ALU.mult)
                    b.mm(O[0:65, lo:hi], VE[:, ch, g, :], pt[:, lo:hi], ci == 0, ci == len(chunks) - 1)
                dr = drp.get()
                if isC:
                    b.rcp(dr[64:65, 0:T], O[64:65, 0:T])
                else:
                    b.ts("dve", dr[64:65, 0:T], O[64:65, 0:T], cst[64:65, h:h + 1], ALU.add)
                    b.rcp(dr[64:65, 0:T], dr[64:65, 0:T])
                B = psB.get()
                b.mm(B[0:64, 0:T], ones_f[64:65, 0:64], dr[64:65, 0:T])
                bc = bcp.get()
                b.cp("act", bc[:, 0:T], B[0:64, 0:T])
                et = ep.get()
                b.act(et[:, 0:T], gt[:, 0:T], AF.Exp, scale=-1.0)
                b.ts("pool", et[:, 0:T], et[:, 0:T], 1.0, ALU.add)
                b.rcp(et[:, 0:T], et[:, 0:T])
                b.tt("pool", et[:, 0:T], et[:, 0:T], gt[:, 0:T], ALU.mult)
                b.tt("dve", bc[:, 0:T], O[0:64, 0:T], bc[:, 0:T], ALU.mult)
                y = yp.get()
                b.tt("dve", y[:, 0:T], bc[:, 0:T], et[:, 0:T], ALU.mult)
                b.dma("pool", self.yT.ap()[mi, h * 64:(h + 1) * 64, t0:t0 + T], y[:, 0:T])

    def seg_bounds(self, t0):
        return (0, CTX) if t0 < CTX else (CTX, self.NT)

    def phase_b_prep(self, l):
        b, I, NT = self.b, self.I, self.NT
        pT = self.pT.ap()
        oz = SEG_OFF["Bz"]
        psum = Pool(b, 4, [128, 512], F32, "ps", psum=True)
        psB = Pool(b, 2, [128, 512], F32, "psb", psum=True)
        mu = b.sb([128, 13, 2], F32, "mu")
        cmu = b.sb([128, 13], F32, "cmu")
        w0a0 = b.sb([128, 2, 2, 4], F32, "w0a0")
        aw = b.sb([128, 2, 512], F32, "aw")
        vec = b.sb([128, 4, 4], F32, "bvec")
        rkb = b.sb([128, 4, 128], F32, "rkb")
        oblk = b.sb([128, 128], BF16, "oblk")
        onesF = b.sb([128, 512], F32, "onesF")
        b.ms("dve", onesF[:], 1.0)
        b.ms("dve", oblk[:], 0.0)
        b.ms("dve", oblk[0:64, 0:64], 1.0)
        b.ms("dve", oblk[64:128, 64:128], 1.0)
        b.dma("sp", mu[:], I["b_mu"].ap()[l])
        b.dma("sp", w0a0[:], I["b_w0a0"].ap()[l])
        b.dma("sp", aw[:], I["b_aw"].ap()[l])
        b.dma("sp", vec[:], I["b_vec"].ap()[l])
        b.dma("sp", rkb[:], I["b_rkblk"].ap()[l])
        b.ts("dve", cmu[:], mu[:, :, 0], -1.0, ALU.mult, 1.0, ALU.add)
        b.tt("dve", cmu[:], cmu[:], mu[:, :, 1], ALU.subtract)
        ptp = Pool(b, 3, [128, 514], F32, "bpt")
        zall = b.sb([128, 13, 512], F32, "zall")
        kkT = b.sb([128, 4, 512], F32, "kkT")
        t2k = Pool(b, 12, [128, 512], F32, "bt")
        sqp = Pool(b, 2, [128, 512], BF16, "bsq")
        Sx = b.sb([128, 513], F32, "Sx")
        b.ms("dve", Sx[:, 0:1], 0.0)
        o7p = Pool(b, 2, [128, 7, 512], F32, "o7")
        gtp = Pool(b, 2, [128, 8], F32, "egt")
        for (t0, T, is_ctx) in self.blocks:
            lo, hi = self.seg_bounds(t0)
            nch = T // 64
            for i in range(13):
                pt = ptp.get()
                a0, a1 = max(t0 - 1, lo), min(t0 + T + 1, hi)
                if a0 > t0 - 1:
                    b.ms("pool", pt[:, 0:1], 0.0)
                if a1 < t0 + T + 1:
                    b.ms("pool", pt[:, T + 1:T + 2], 0.0)
                b.dma("sp", pt[:, a0 - (t0 - 1):a1 - (t0 - 1)], pT[oz + i * 128:oz + (i + 1) * 128, a0:a1])
                z = zall[:, i, 0:T]
                b.ts("dve", z, pt[:, 1:T + 1], cmu[:, i:i + 1], ALU.mult)
                b.stt(z, pt[:, 0:T], mu[:, i, 0:1], z, ALU.mult, ALU.add)
                b.stt(z, pt[:, 2:T + 2], mu[:, i, 1:2], z, ALU.mult, ALU.add)
            b.act(zall[0:64, 12, 0:T], zall[0:64, 12, 0:T], AF.Tanh)
            for pr in range(4):
                kx = t2k.get()
                b.ts("dve", kx[:, 0:T], zall[:, 4 + pr, 0:T], vec[:, 0, pr:pr + 1], ALU.mult)
                sq = sqp.get()
                b.act(sq[:, 0:T], kx[:, 0:T], AF.Square)
                ps = psum.get()
                b.mm(ps[:, 0:T], oblk[:], sq[:, 0:T])
                rn = t2k.get()
                b.act(rn[:, 0:T], ps[:, 0:T], AF.Sqrt, bias=1e-12)
                b.rcp(rn[:, 0:T], rn[:, 0:T])
                b.tt("dve", kkT[:, pr, 0:T], kx[:, 0:T], rn[:, 0:T], ALU.mult)
            for pr in range(4):
                psb = psB.get()
                rT, kT_, vT_ = zall[:, pr, 0:T], zall[:, 4 + pr, 0:T], zall[:, 8 + pr, 0:T]
                for d in range(2):
                    pw = psum.get()
                    b.mm(pw[:, 0:T], aw[0:64, d, pr * 128:(pr + 1) * 128], zall[0:64, 12, 0:T])
                    lw = t2k.get()
                    b.act(lw[:, 0:T], pw[:, 0:T], AF.Sigmoid, bias=w0a0[:, 0, d, pr:pr + 1])
                    b.ts("pool", lw[:, 0:T], lw[:, 0:T], -0.606531, ALU.mult)
                    pa = psum.get()
                    b.mm(pa[:, 0:T], aw[64:128, d, pr * 128:(pr + 1) * 128], zall[64:128, 12, 0:T])
                    a = t2k.get()
                    b.act(a[:, 0:T], pa[:, 0:T], AF.Sigmoid, bias=w0a0[:, 1, d, pr:pr + 1])
                    kt = t2k.get()
                    b.ts("dve", kt[:, 0:T], a[:, 0:T], -1.0, ALU.add, vec[:, 1, pr:pr + 1], ALU.mult)
                    b.stt(kt[:, 0:T], kt[:, 0:T], 1.0, kT_, ALU.add, ALU.mult)
                    bb = t2k.get()
                    b.tt("pool", bb[:, 0:T], kkT[:, pr, 0:T], a[:, 0:T], ALU.mult)
                    b.I("dve", "tensor_tensor_scan", out=Sx[:, 1:T + 1], data0=onesF[:, 0:T], data1=lw[:, 0:T],
                        initial=0.0, op0=ALU.mult, op1=ALU.add)
                    gc = t2k.get()
                    v3 = lambda ap: ap.rearrange("p (c t) -> p c t", t=64)
                    b.tt("dve", v3(gc[:, 0:T]), v3(Sx[:, 1:T + 1]), v3(Sx[:, 0:T])[:, :, 0:1].broadcast_to([128, nch, 64]),
                         ALU.subtract)
                    gtot_bc = v3(gc[:, 0:T])[:, :, 63:64].broadcast_to([128, nch, 64])
                    ge = t2k.get()
                    if d == 0:
                        gi = gc
                        b.tt("dve", ge[:, 0:T], gc[:, 0:T], lw[:, 0:T], ALU.subtract)
                    else:
                        gi = t2k.get()
                        b.tt("dve", v3(ge[:, 0:T]), gtot_bc, v3(gc[:, 0:T]), ALU.subtract)
                        b.tt("pool", gi[:, 0:T], ge[:, 0:T], lw[:, 0:T], ALU.add)
                    egt = gtp.get()
                    b.act(egt[:, 0:nch], v3(gc[:, 0:T])[:, :, 63], AF.Exp)
                    ege, egi, engi = t2k.get(), t2k.get(), t2k.get()
                    b.act(ege[:, 0:T], ge[:, 0:T], AF.Exp)
                    b.act(egi[:, 0:T], gi[:, 0:T], AF.Exp)
                    b.act(engi[:, 0:T], gi[:, 0:T], AF.Exp, scale=-1.0)
                    o7 = o7p.get()
                    b.tt("dve", o7[:, 0, 0:T], kkT[:, pr, 0:T], ege[:, 0:T], ALU.mult)
                    b.tt("pool", o7[:, 1, 0:T], rT, egi[:, 0:T], ALU.mult)
                    b.tt("dve", o7[:, 2, 0:T], bb[:, 0:T], engi[:, 0:T], ALU.mult)
                    b.tt("pool", o7[:, 3, 0:T], kt[:, 0:T], engi[:, 0:T], ALU.mult)
                    ebc = egt[:, 0:nch].unsqueeze(2).broadcast_to([128, nch, 64])
                    b.tt("dve", v3(o7[:, 4, 0:T]), v3(o7[:, 3, 0:T]), ebc, ALU.mult)
                    b.tt("dve", v3(o7[:, 5, 0:T]), v3(o7[:, 2, 0:T]), ebc, ALU.mult)
                    b.cp("pool", o7[:, 6, 0:T], vT_)
                    b.dma("pool", self.UB.ap()[d, pr, :, :, t0:t0 + T], o7[:, :, 0:T])
                    b.dma("pool", self.GAMB.ap()[d, pr, :, t0 // 64:t0 // 64 + nch], egt[:, 0:nch])
                    rkt = t2k.get()
                    b.tt("dve", rkt[:, 0:T], rT, kt[:, 0:T], ALU.mult)
                    b.mm(psb[:, 0:T], rkb[:, pr, :], rkt[:, 0:T], d == 0, d == 1)
                bo = t2k.get()
                b.tt("dve", bo[:, 0:T], psb[:, 0:T], vT_, ALU.mult)
                b.dma("pool", self.bon.ap()[pr * 128:(pr + 1) * 128, t0:t0 + T], bo[:, 0:T])

    def phase_units(self, l, mi):
        b, I, NT = self.b, self.I, self.NT
        isB = (mi == 1)
        nlev = 5 if isB else 6
        CT = 64 if isB else 128
        NCH = NT // CT
        cch = CTX // CT
        order = {0: list(range(NCH)), 1: list(range(cch - 1, -1, -1)) + list(range(NCH - 1, cch - 1, -1))}
        chains = [(u, d) for u in range(4) for d in range(2)]
        NC = len(chains)
        psG = Pool(b, 2, [128, 512], F32, "psG", psum=True)
        psA = Pool(b, 2, [128, 512], F32, "psA", psum=True)
        sm = Pool(b, 4, [128, 512], F32, "psm", psum=True)
        ident = b.sb([128, 128], F32, "ident")
        b.dma("sp", ident[:], I["ident"].ap())
        rmask = b.sb([128, 4, 128], F32, "rmask")
        b.dma("sp", rmask[:], I["rmask" if isB else "gmask"].ap())
        bd2 = b.sb([128, 2, 64], F32, "bd2")
        b.dma("sp", bd2[:], I["bd2"].ap())
        NB = 16
        XXp = Pool(b, NB, [128, 4, 128], BF16, "XX")
        TMp = Pool(b, NB, [128, 3, 128], BF16, "TM3")
        A3p = Pool(b, NB, [128, 3, 128], BF16, "A3")
        Pmp = Pool(b, 18, [128, 2, 128], F32, "Pm")
        XTp = Pool(b, 17, [128, 128], F32, "XT")
        XTf = Pool(b, 16, [128, 128], F32, "XTf")
        AVp = Pool(b, NB, [128, 128], F32, "AVs")
        Wp = Pool(b, 9, [128, 128], F32, "Wt")
        NUp = Pool(b, 9, [128, 128], BF16, "negU")
        Yop = Pool(b, 6, [128, 128], F32, "Yo")
        H = [b.sb([128, 128], F32, "H") for _ in chains]
        Hb = [b.sb([128, 128], BF16, "Hb") for _ in chains]
        for i in range(NC):
            b.ms("dve", H[i][:], 0.0)
            b.ms("pool", Hb[i][:], 0.0)
        gam = [b.sb([128, NCH], F32, "gam") for _ in chains]
        if isB:
            ldp = Pool(b, 10, [128, 7, 64], F32, "ld7")
            F3p = Pool(b, 8, [128, 3, 128], F32, "F3")
            for i, (u, d) in enumerate(chains):
                b.dma("sp", gam[i][:], self.GAMB.ap()[d, u])
        else:
            self.gdn_unit_setup(l, gam)
            self.gdn_gam(gam, sm, chains)

        def pre_stages(step):
            units = []
            st = []

            def s_prep():
                for i, (u, d) in enumerate(chains):
                    c = order[d][step]
                    tk = c * CT
                    U = dict(i=i, u=u, d=d, c=c, tk=tk)
                    XX, TM3 = XXp.get(), TMp.get()
                    if isB:
                        ld = ldp.get()
                        b.dma("sp", ld[:], self.UB.ap()[d, u, :, :, tk:tk + 64])
                        bdb = bd2[:].unsqueeze(1).broadcast_to([128, 4, 2, 64])
                        b.tt("dve", XX[:].rearrange("p k (h t) -> p k h t", h=2),
                             ld[:, 0:4, :].unsqueeze(2).broadcast_to([128, 4, 2, 64]), bdb, ALU.mult)
                        F3 = F3p.get()
                        b.tt("pool", F3[:].rearrange("p k (h t) -> p k h t", h=2),
                             ld[:, 4:7, :].unsqueeze(2).broadcast_to([128, 3, 2, 64]),
                             bd2[:].unsqueeze(1).broadcast_to([128, 3, 2, 64]), ALU.mult)
                        tp = sm.get()
                        for k in range(3):
                            b.tr(tp[:, k * 128:(k + 1) * 128], F3[:, k, :], ident[:])
                        b.cp(("act", "dve")[i % 2], TM3[:].rearrange("p k t -> p (k t)"), tp[:, 0:384])
                        if d == 0:
                            WsT, WiT, Ws = rmask[:, 0, :], rmask[:, 1, :], rmask[:, 2, :]
                        else:
                            WsT, WiT, Ws = rmask[:, 2, :], rmask[:, 3, :], rmask[:, 0, :]
                        U.update(XkH=XX[:, 0, :], XrH=XX[:, 1, :])
                    else:
                        WsT, WiT, Ws, XH = self.gdn_unit_prep(l, U, XX, TM3, sm, rmask, ident)
                        U.update(XkH=XH[:, 0, :], XrH=XH[:, 1, :])
                    U.update(XX=XX, TM3=TM3, gcol=gam[i][:, c:c + 1], W=(WsT, WiT, Ws))
                    units.append(U)

            def s_gram():
                for U in units:
                    XX = U["XX"]
                    WsT, WiT, Ws = U["W"]
                    G = psG.get()
                    xkr = XX[:, 0:2, :].rearrange("p k t -> p (k t)")
                    b.mm(G[:, 0:256], XX[:, 2, :], xkr)
                    b.mm(G[:, 256:512], XX[:, 3, :], xkr)
                    g3 = sm.get()
                    b.mm(g3[:, 0:128], XX[:, 0, :], XX[:, 2, :])
                    Pm = Pmp.get()
                    b.stt(Pm[:, 1, :], G[:, 0:128], -1.0, WsT, ALU.mult, ALU.mult)
                    b.stt(Pm[:, 0, :], g3[:, 0:128], -1.0, Ws, ALU.mult, ALU.mult)
                    A3 = A3p.get()
                    b.tt("dve", A3[:, 0, :], G[:, 128:256], WiT, ALU.mult)
                    b.tt("dve", A3[:, 1, :], G[:, 256:384], WsT, ALU.mult)
                    b.tt("dve", A3[:, 2, :], G[:, 384:512], WiT, ALU.mult)
                    XT = XTp.get()
                    b.tt("pool", XT[:], Pm[:, 1, :], ident[:], ALU.add)
                    U.update(A3=A3, Pm=Pm, XT=XT)

            def mk_lev_a(lev):
                def f():
                    last = lev == nlev - 1
                    for U in units:
                        Pm = U["Pm"]
                        p2 = sm.get()
                        b.mm(p2[:, 0:128], Pm[:, 1, :], Pm[:, 0, :])
                        Pn = Pmp.get()
                        if not last:
                            b.mm(p2[:, 128:256], Pm[:, 0, :], Pm[:, 1, :])
                            b.cp(("act", "dve")[U["i"] % 2], Pn[:].rearrange("p k t -> p (k t)"), p2[:, 0:256])
                        else:
                            b.cp("act", Pn[:, 0, :], p2[:, 0:128])
                        U["Pn"] = Pn
                return f

            def mk_lev_b(lev):
                def f():
                    for U in units:
                        Pn, XT = U["Pn"], U["XT"]
                        xu = sm.get()
                        b.mm(xu[:, 0:128], Pn[:, 0, :], XT[:])
                        XTn = (XTf if lev == nlev - 1 else XTp).get()
                        b.tt("dve", XTn[:], xu[:, 0:128], XT[:], ALU.add)
                        U["Pm"], U["XT"] = Pn, XTn
                return f

            def s_av():
                for U in units:
                    av = sm.get()
                    b.mm(av[:, 0:128], U["A3"][:, 1, :], U["TM3"][:, 2, :])
                    AVs = AVp.get()
                    b.cp("act", AVs[:], av[:, 0:128])
                    U["AVs"] = AVs

            st.append(s_prep)
            st.append(s_gram)
            for lev in range(nlev):
                st.append(mk_lev_a(lev))
                st.append(mk_lev_b(lev))
            st.append(s_av)
            return st, units

        def seq_stages(units):
            def s5():
                for U in units:
                    kh = sm.get()
                    b.mm(kh[:, 0:128], U["XkH"], Hb[U["i"]][:])
                    Wt = Wp.get()
                    b.tt("dve", Wt[:], kh[:, 0:128], U["AVs"][:], ALU.add)
                    U["Wt"] = Wt

            def s6():
                for U in units:
                    uu = sm.get()
                    b.mm(uu[:, 0:128], U["XT"][:], U["Wt"][:])
                    nu = NUp.get()
                    b.act(nu[:], uu[:, 0:128], AF.Copy, scale=-1.0)
                    U["nu"] = nu

            def s7():
                for U in units:
                    i = U["i"]
                    Y = psA.get()
                    b.mm(Y[:, 0:128], U["XrH"], Hb[i][:], True, False)
                    b.mm(Y[:, 0:128], U["A3"][:, 0, :], U["nu"][:], False, False)
                    b.mm(Y[:, 0:128], U["A3"][:, 2, :], U["TM3"][:, 2, :], False, True)
                    Yo = Yop.get()
                    b.cp("act", Yo[:], Y[:, 0:128])
                    dst = self.ytm.ap()[U["d"]]
                    if isB:
                        for hh in range(2):
                            b.dma("pool", dst[U["tk"]:U["tk"] + 64, (2 * U["u"] + hh) * 64:(2 * U["u"] + hh + 1) * 64],
                                  Yo[hh * 64:(hh + 1) * 64, hh * 64:(hh + 1) * 64])
                    else:
                        b.dma("pool", dst[U["tk"]:U["tk"] + 128, U["u"] * 128:(U["u"] + 1) * 128], Yo[:])

            def s8():
                for U in units:
                    i = U["i"]
                    Hn = psA.get()
                    b.mm(Hn[:, 0:128], U["TM3"][:, 1, :], U["nu"][:], True, False)
                    b.mm(Hn[:, 0:128], U["TM3"][:, 0, :], U["TM3"][:, 2, :], False, True)
                    b.stt(H[i][:], H[i][:], U["gcol"], Hn[:, 0:128], ALU.mult, ALU.add)
                    b.cp("pool", Hb[i][:], H[i][:])
            return [s5, s6, s7, s8]

        st, units = pre_stages(0)
        for f in st:
            f()
        for step in range(NCH):
            seq = seq_stages(units)
            if step + 1 < NCH:
                pre, nunits = pre_stages(step + 1)
            else:
                pre, nunits = [], None
            npre = len(pre)
            marks = {((k + 1) * npre) // 5: k for k in range(4)} if npre else {}
            if self.no_inter:
                for k in range(4):
                    seq[k]()
                marks = {}
                done_all = True
            else:
                done_all = False
            done = set(range(4)) if done_all else set()
            for j, f in enumerate(pre):
                if j in marks and marks[j] not in done:
                    seq[marks[j]]()
                    done.add(marks[j])
                f()
            for k in range(4):
                if k not in done:
                    seq[k]()
                    done.add(k)
            units = nunits

    def phase_post(self, l, mi, need_ctx):
        b, I, NT = self.b, self.I, self.NT
        isB = (mi == 1)
        pT = self.pT.ap()
        og = SEG_OFF["Bg" if isB else "Dg"]
        psum = Pool(b, 3, [128, 512], F32, "ps", psum=True)
        ident = b.sb([128, 128], F32, "ident")
        b.dma("sp", ident[:], I["ident"].ap())
        vec = b.sb([128, 4, 4], F32, "pvec")
        b.dma("sp", vec[:], I["b_vec" if isB else "d_vec"].ap()[l])
        yp = Pool(b, 4, [128, 512], F32, "py")
        stp = Pool(b, 8, [128, 8], F32, "pst")
        gp = Pool(b, 4, [128, 128], F32, "pg")
        ep = Pool(b, 4, [128, 128], F32, "pe_")
        bp = Pool(b, 4, [128, 128], F32, "pb")
        op_ = Pool(b, 4, [128, 128], BF16, "po")
        NH, HD = (8, 64) if isB else (4, 128)
        for t0 in range(0 if need_ctx else CTX, NT, 128):
            y0, y1 = yp.get(), yp.get()
            b.dma("sp", y0[:], self.ytm.ap()[0, t0:t0 + 128, :])
            b.dma("sp", y1[:], self.ytm.ap()[1, t0:t0 + 128, :])
            b.tt("pool", y0[:], y0[:], y1[:], ALU.add)
            y3 = y0[:].rearrange("p (h d) -> p h d", h=NH)
            st = stp.get()
            if isB:
                b.I("dve", "tensor_reduce", out=st[:, 0:NH], in_=y3, axis=mybir.AxisListType.X, op=ALU.add)
                b.ts("dve", st[:, 0:NH], st[:, 0:NH], 1.0 / HD, ALU.mult)
                b.tt("dve", y3, y3, st[:, 0:NH].unsqueeze(2).broadcast_to([128, NH, HD]), ALU.subtract)
            sq = yp.get()
            b.tt("pool", sq[:], y0[:], y0[:], ALU.mult)
            s2 = stp.get()
            b.I("dve", "tensor_reduce", out=s2[:, 0:NH], in_=sq[:].rearrange("p (h d) -> p h d", h=NH),
                axis=mybir.AxisListType.X, op=ALU.add)
            b.act(s2[:, 0:NH], s2[:, 0:NH], AF.Sqrt, scale=1.0 / HD, bias=(64e-5 if isB else 1e-6))
            b.rcp(s2[:, 0:NH], s2[:, 0:NH])
            b.tt("dve", y3, y3, s2[:, 0:NH].unsqueeze(2).broadcast_to([128, NH, HD]), ALU.mult)
            tpb = psum.get()
            for ct in range(4):
                b.tr(tpb[:, ct * 128:(ct + 1) * 128], y0[:, ct * 128:(ct + 1) * 128], ident[:])
            for ct in range(4):
                tp = _View(tpb.h, ct * 128, 128)
                g = gp.get()
                b.dma("sp", g[:], pT[og + ct * 128:og + (ct + 1) * 128, t0:t0 + 128])
                e = ep.get()
                b.act(e[:], g[:], AF.Exp, scale=-1.0)
                b.ts("pool", e[:], e[:], 1.0, ALU.add)
                b.rcp(e[:], e[:])
                b.tt("pool", e[:], e[:], g[:], ALU.mult)
                o = op_.get()
                if isB:
                    bo = bp.get()
                    b.dma("sp", bo[:], self.bon.ap()[ct * 128:(ct + 1) * 128, t0:t0 + 128])
                    t = gp.get()
                    b.ts("dve", t[:], tp[:], vec[:, 2, ct:ct + 1], ALU.mult, vec[:, 3, ct:ct + 1], ALU.add)
                    b.tt("pool", t[:], t[:], bo[:], ALU.add)
                    b.tt("dve", o[:], t[:], e[:], ALU.mult)
                else:
                    b.stt(o[:], tp[:], vec[:, 0, ct:ct + 1], e[:], ALU.mult, ALU.mult)
                b.dma("pool", self.yT.ap()[mi, ct * 128:(ct + 1) * 128, t0:t0 + 128], o[:])


    def phase_d_prep(self, l):
        b, I, NT = self.b, self.I, self.NT
        pT = self.pT.ap()
        oq, oab = SEG_OFF["Dqkv"], SEG_OFF["Dab"]
        psum = Pool(b, 4, [128, 512], F32, "ps", psum=True)
        cw = b.sb([128, 12, 5], F32, "cw")
        b.dma("sp", cw[:], I["d_conv"].ap()[l])
        abp = b.sb([8, 4], F32, "abp")
        b.dma("sp", abp[:], I["d_ab"].ap()[l])
        nexpA = b.sb([8, 1], F32, "nexpA")
        b.act(nexpA[:], abp[:, 1:2], AF.Exp)
        b.ts("dve", nexpA[:], nexpA[:], -1.0, ALU.mult)
        onesF = b.sb([8, 512], F32, "onesF")
        b.ms("dve", onesF[:], 1.0)
        Sx = b.sb([8, 513], F32, "Sx")
        b.ms("dve", Sx[:, 0:1], 0.0)
        ptp = Pool(b, 3, [128, 516], F32, "dpt")
        tp = Pool(b, 6, [128, 512], F32, "dt")
        sqp = Pool(b, 2, [128, 512], BF16, "dsq")
        o3p = Pool(b, 2, [128, 3, 512], F32, "o3")
        rp = Pool(b, 12, [8, 512], F32, "dr")
        o6p = Pool(b, 2, [8, 6, 512], F32, "o6")
        egp = Pool(b, 2, [8, 4], F32, "deg")
        for (t0, T, is_ctx) in self.blocks:
            lo, hi = self.seg_bounds(t0)
            nch = T // 128
            for h in range(4):
                o3 = o3p.get()
                for kind in range(3):
                    i = kind * 4 + h
                    pt = ptp.get()
                    a0, a1 = max(t0 - 2, lo), min(t0 + T + 2, hi)
                    if a0 > t0 - 2:
                        b.ms("pool", pt[:, 0:2], 0.0)
                    if a1 < t0 + T + 2:
                        b.ms("pool", pt[:, T + 2:T + 4], 0.0)
                    b.dma("sp", pt[:, a0 - (t0 - 2):a1 - (t0 - 2)], pT[oq + i * 128:oq + (i + 1) * 128, a0:a1])
                    acc = tp.get()
                    b.ts("dve", acc[:, 0:T], pt[:, 0:T], cw[:, i, 0:1], ALU.mult)
                    for j in range(1, 5):
                        b.stt(acc[:, 0:T], pt[:, j:j + T], cw[:, i, j:j + 1], acc[:, 0:T], ALU.mult, ALU.add)
                    e = tp.get()
                    b.act(e[:, 0:T], acc[:, 0:T], AF.Exp, scale=-1.0)
                    b.ts("pool", e[:, 0:T], e[:, 0:T], 1.0, ALU.add)
                    b.rcp(e[:, 0:T], e[:, 0:T])
                    if kind == 2:
                        b.tt("pool", o3[:, 2, 0:T], acc[:, 0:T], e[:, 0:T], ALU.mult)
                        continue
                    b.tt("pool", acc[:, 0:T], acc[:, 0:T], e[:, 0:T], ALU.mult)
                    sq = sqp.get()
                    b.act(sq[:, 0:T], acc[:, 0:T], AF.Square)
                    ps = psum.get()
                    b.mm(ps[:, 0:T], self.ones_bf[:], sq[:, 0:T])
                    rn = tp.get()
                    b.act(rn[:, 0:T], ps[:, 0:T], AF.Sqrt, bias=1e-12)
                    b.rcp(rn[:, 0:T], rn[:, 0:T])
                    if kind == 0:
                        b.stt(o3[:, 0, 0:T], acc[:, 0:T], 128.0 ** -0.5, rn[:, 0:T], ALU.mult, ALU.mult)
                    else:
                        b.tt("dve", o3[:, 1, 0:T], acc[:, 0:T], rn[:, 0:T], ALU.mult)
                b.dma("pool", self.UD.ap()[h, :, :, t0:t0 + T], o3[:, :, 0:T])
            lgr, br = rp.get(), rp.get()
            for d in range(2):
                b.dma("sp", lgr[d * 4:(d + 1) * 4, 0:T], pT[oab + d * 8:oab + d * 8 + 4, t0:t0 + T])
                b.dma("sp", br[d * 4:(d + 1) * 4, 0:T], pT[oab + d * 8 + 4:oab + d * 8 + 8, t0:t0 + T])
            lg = rp.get()
            b.act(lg[:, 0:T], lgr[:, 0:T], AF.Exp, bias=abp[:, 0:1])
            b.act(lg[:, 0:T], lg[:, 0:T], AF.Ln, bias=1.0)
            b.ts("dve", lg[:, 0:T], lg[:, 0:T], nexpA[:, 0:1], ALU.mult)
            beta = rp.get()
            b.act(beta[:, 0:T], br[:, 0:T], AF.Sigmoid)
            b.I("dve", "tensor_tensor_scan", out=Sx[:, 1:T + 1], data0=onesF[:, 0:T], data1=lg[:, 0:T],
                initial=0.0, op0=ALU.mult, op1=ALU.add)
            v3 = lambda ap: ap.rearrange("p (c t) -> p c t", t=128)
            gc = rp.get()
            b.tt("dve", v3(gc[:, 0:T]), v3(Sx[:, 1:T + 1]), v3(Sx[:, 0:T])[:, :, 0:1].broadcast_to([8, nch, 128]), ALU.subtract)
            gtot_bc = v3(gc[:, 0:T])[:, :, 127:128].broadcast_to([8, nch, 128])
            o6 = o6p.get()
            t1 = rp.get()
            b.ts("dve", t1[:, 0:T], gc[:, 0:T], abp[:, 2:3], ALU.mult)
            b.stt(v3(t1[:, 0:T]), gtot_bc, abp[:, 3:4], v3(t1[:, 0:T]), ALU.mult, ALU.add)
            b.stt(o6[:, 3, 0:T], lg[:, 0:T], abp[:, 3:4], t1[:, 0:T], ALU.mult, ALU.add)
            b.tt("dve", o6[:, 2, 0:T], o6[:, 3, 0:T], lg[:, 0:T], ALU.subtract)
            elg = rp.get()
            b.act(elg[:, 0:T], lg[:, 0:T], AF.Exp)
            b.tt("dve", o6[:, 0, 0:T], beta[:, 0:T], elg[:, 0:T], ALU.mult)
            b.cp("dve", o6[:, 1, 0:T], beta[:, 0:T])
            t2 = rp.get()
            b.tt("dve", v3(t2[:, 0:T]), gtot_bc, v3(o6[:, 3, 0:T]), ALU.subtract)
            b.act(t2[:, 0:T], t2[:, 0:T], AF.Exp)
            b.tt("dve", o6[:, 4, 0:T], beta[:, 0:T], t2[:, 0:T], ALU.mult)
            b.tt("dve", o6[:, 5, 0:T], o6[:, 4, 0:T], elg[:, 0:T], ALU.mult)
            eg = egp.get()
            b.act(eg[:, 0:nch], v3(gc[:, 0:T])[:, :, 127], AF.Exp)
            b.dma("pool", self.RD.ap()[:, :, t0:t0 + T], o6[:, :, 0:T])
            b.dma("pool", self.EGTD.ap()[:, t0 // 128:t0 // 128 + nch], eg[:, 0:nch])

    def gdn_unit_setup(self, l, gam):
        b, I, NT = self.b, self.I, self.NT
        NCH = NT // 128
        sel8 = b.sb([8, 8, 128], F32, "sel8")
        b.dma("sp", sel8[:], I["sel8"].ap())
        egt = b.sb([8, NCH + (NCH % 2)], F32, "egtall")
        b.ms("dve", egt[:], 0.0)
        b.dma("sp", egt[:, 0:NCH], self.EGTD.ap())
        self._gd = dict(
            sel8=sel8,
            ld3=Pool(b, 12, [128, 3, 128], F32, "ld3"),
            rows=Pool(b, 4, [8, 6, 128], F32, "grow"),
            cols=Pool(b, 4, [128, 4, 8], F32, "gcol"),
            Wt=Pool(b, 12, [128, 3, 128], F32, "gW"),
            xa=Pool(b, 6, [128, 128], F32, "gxa"),
            eb=Pool(b, 4, [128, 256], F32, "geb"),
            XH=Pool(b, 17, [128, 2, 128], BF16, "gXH"),
            cache={},
        )
        self._gd_egt = egt

    def gdn_gam(self, gam, sm, chains):
        b = self.b
        NCH = self.NT // 128
        n2 = NCH + (NCH % 2)
        for i, (u, d) in enumerate(chains):
            p = sm.get()
            b.mm(p[:, 0:n2], self._gd["sel8"][:, d * 4 + u, :], self._gd_egt[:, 0:n2])
            b.cp("act", gam[i][:], p[:, 0:NCH])

    def gdn_unit_prep(self, l, U, XX, TM3, sm, gmask, ident):
        b = self.b
        G = self._gd
        h, d, c, tk = U["u"], U["d"], U["c"], U["tk"]
        r = d * 4 + h
        key = (c, d)
        if key not in G["cache"]:
            rows = G["rows"].get()
            b.dma("sp", rows[:], self.RD.ap()[:, :, tk:tk + 128])
            tp = sm.get()
            for j, kind in enumerate((3, 2, 4, 5)):
                b.tr(tp[:, j * 8:(j + 1) * 8], rows[:, kind, :], ident[0:8, 0:8])
            cols = G["cols"].get()
            b.cp("act", cols[:].rearrange("p k r -> p (k r)"), tp[:, 0:32])
            G["cache"] = {kk: vv for kk, vv in G["cache"].items() if kk[1] != d}
            G["cache"][key] = (rows, cols)
        rows, cols = G["cache"][key]
        ld = G["ld3"].get()
        b.dma("sp", ld[:], self.UD.ap()[h, :, :, tk:tk + 128])
        bc = sm.get()
        for j, kind in enumerate((0, 1, 2, 3)):
            b.mm(bc[:, j * 128:(j + 1) * 128], G["sel8"][:, r, :], rows[:, kind, :])
        b.cp("act", XX[:, 0, :], ld[:, 1, :])
        b.cp("pool", XX[:, 1, :], ld[:, 0, :])
        b.tt("dve", XX[:, 2, :], bc[:, 0:128], ld[:, 1, :], ALU.mult)
        b.tt("dve", XX[:, 3, :], bc[:, 128:256], ld[:, 1, :], ALU.mult)
        tp = sm.get()
        b.tr(tp[:, 0:128], ld[:, 1, :], ident[:])
        b.tr(tp[:, 128:256], ld[:, 2, :], ident[:])
        b.ts("dve", TM3[:, 0, :], tp[:, 0:128], cols[:, 2, r:r + 1], ALU.mult)
        b.ts("dve", TM3[:, 1, :], tp[:, 0:128], cols[:, 3, r:r + 1], ALU.mult)
        b.cp("act", TM3[:, 2, :], tp[:, 128:256])
        Wt = G["Wt"].get()
        mk = (0, 1, 2) if d == 0 else (2, 3, 0)
        xa = G["xa"].get()
        b.stt(xa[:], bc[:, 256:384], cols[:, 0, r:r + 1], gmask[:, mk[0], :], ALU.subtract, ALU.add)
        b.act(Wt[:, 0, :], xa[:], AF.Exp)
        xb = G["xa"].get()
        b.stt(xb[:], bc[:, 384:512], cols[:, 0, r:r + 1], gmask[:, mk[1], :], ALU.subtract, ALU.add)
        b.act(Wt[:, 1, :], xb[:], AF.Exp)
        xc = G["xa"].get()
        b.stt(xc[:], bc[:, 384:512], cols[:, 1, r:r + 1], gmask[:, mk[2], :], ALU.subtract, ALU.subtract)
        b.act(Wt[:, 2, :], xc[:], AF.Exp, scale=-1.0)
        eb = G["eb"].get()
        b.act(eb[:], bc[:, 256:512], AF.Exp)
        XH = G["XH"].get()
        b.tt("dve", XH[:, 0, :], eb[:, 0:128], ld[:, 1, :], ALU.mult)
        b.tt("pool", XH[:, 1, :], eb[:, 128:256], ld[:, 0, :], ALU.mult)
        return Wt[:, 0, :], Wt[:, 1, :], Wt[:, 2, :], XH

    def phase_mcast(self, l):
        b, I = self.b, self.I
        fa = Pool(b, 2, [128, 24, 128], F32, "mcf")
        ba = Pool(b, 2, [128, 24, 128], BF16, "mcb")
        for dt in range(16):
            f, g = fa.get(), ba.get()
            b.dma("sp", f[:], I["wm"].ap()[l, dt])
            b.cp(("dve", "pool")[dt % 2], g[:], f[:])
            b.dma("sp", self.wmb.ap()[dt], g[:])
            f, g = fa.get(), ba.get()
            b.dma("sp", f[:, 0:16, :], I["wo"].ap()[l, dt])
            b.cp(("pool", "dve")[dt % 2], g[:, 0:16, :], f[:, 0:16, :])
            b.dma("sp", self.wob.ap()[dt], g[:, 0:16, :])

    def phase_merge(self, l, need_ctx):
        b, I, NT = self.b, self.I, self.NT
        mv = self.modv
        pT = self.pT.ap()
        opm = SEG_OFF["pm"]
        psG = Pool(b, 2, [128, 512], F32, "psg", psum=True)
        psB = Pool(b, 3, [128, 512], F32, "psb", psum=True)
        psO = Pool(b, 2, [128, 512], F32, "pso", psum=True)
        gb = b.sb([128, 4, 16], F32, "gb")
        b.dma("sp", gb[:], I["g_b"].ap()[l])
        pmf = Pool(b, 1, [128, 2, 512], F32, "pmf")
        pmb = Pool(b, 2, [128, 2, 512], BF16, "pmb")
        ybp = Pool(b, 2, [128, 16, 512], BF16, "yb")
        accT = Pool(b, 2, [128, 16, 512], BF16, "accT")
        wmp = Pool(b, 2, [128, 24, 128], BF16, "wmt")
        wop = Pool(b, 2, [128, 16, 128], BF16, "wot")
        gtp = Pool(b, 3, [128, 512], F32, "mg")
        acp = Pool(b, 2, [128, 512], F32, "macc")
        tmp = Pool(b, 3, [128, 512], F32, "mtmp")
        xtp = Pool(b, 3, [128, 512], F32, "mx")
        src_x = I["xT"] if l == 0 else self.xs
        for (t0, T, is_ctx) in self.blocks:
            if is_ctx and not need_ctx:
                continue
            v = 1 if is_ctx else 0
            pf, pb = pmf.get(), pmb.get()
            b.dma("sp", pf[:, :, 0:T], pT[opm:opm + 256, t0:t0 + T].rearrange("(k p) t -> p k t", p=128))
            b.cp("pool", pb[:, :, 0:T], pf[:, :, 0:T])
            yb = ybp.get()
            for mi in range(4):
                b.dma("sp", yb[:, mi * 4:(mi + 1) * 4, 0:T], self.yT.ap()[mi, :, t0:t0 + T].rearrange("(k p) t -> p k t", p=128))
            aT = accT.get()
            for dt in range(16):
                wt = wmp.get()
                b.dma("sp", wt[:], self.wmb.ap()[dt])
                acc = acp.get()
                for i in range(4):
                    pg = psG.get()
                    for rc in range(2):
                        b.mm(pg[:, 0:T], wt[:, i * 2 + rc, :], pb[:, rc, 0:T], rc == 0, rc == 1)
                    gt = gtp.get()
                    b.act(gt[:, 0:T], pg[:, 0:T], AF.Sigmoid, bias=gb[:, i, dt:dt + 1])
                    pbr = psB.get()
                    for cc in range(4):
                        b.mm(pbr[:, 0:T], wt[:, 8 + i * 4 + cc, :], yb[:, i * 4 + cc, 0:T], cc == 0, cc == 3)
                    if i == 0:
                        b.tt("dve", acc[:, 0:T], pbr[:, 0:T], gt[:, 0:T], ALU.mult)
                    else:
                        tm = tmp.get()
                        b.tt("dve", tm[:, 0:T], pbr[:, 0:T], gt[:, 0:T], ALU.mult)
                        b.tt("pool", acc[:, 0:T], acc[:, 0:T], tm[:, 0:T], ALU.add)
                b.cp("act", aT[:, dt, 0:T], acc[:, 0:T])
            for dt in range(16):
                wo = wop.get()
                b.dma("sp", wo[:], self.wob.ap()[dt])
                po = psO.get()
                for k in range(16):
                    b.mm(po[:, 0:T], wo[:, k, :], aT[:, k, 0:T], k == 0, k == 15)
                xt = xtp.get()
                b.dma("sp", xt[:, 0:T], src_x.ap()[dt * 128:(dt + 1) * 128, t0:t0 + T])
                b.stt(xt[:, 0:T], po[:, 0:T], mv[:, l, dt, 3 * v + 2:3 * v + 3], xt[:, 0:T], ALU.mult, ALU.add)
                b.dma("pool", self.xs.ap()[dt * 128:(dt + 1) * 128, t0:t0 + T], xt[:, 0:T])

    def phase_final(self):
        b, I = self.b, self.I
        psum = Pool(b, 2, [128, 512], F32, "ps", psum=True)
        fg = b.sb([128, KC], F32, "fg")
        b.dma("sp", fg[:], I["final_g"].ap())
        Pxt = Pool(b, 2, [128, KC, 512], F32, "xt")
        Psq = Pool(b, 1, [128, KC, 512], BF16, "sq")
        Prs = Pool(b, 2, [128, 512], F32, "rs")
        for (t0, T, is_ctx) in self.blocks:
            if is_ctx:
                continue
            xt = Pxt.get()
            for k in range(KC):
                b.dma("sp", xt[:, k, 0:T], self.xs.ap()[k * 128:(k + 1) * 128, t0:t0 + T])
            sq = Psq.get()
            b.act(sq[:, :, 0:T], xt[:, :, 0:T], AF.Square)
            ps = psum.get()
            for k in range(KC):
                b.mm(ps[:, 0:T], self.ones_bf[:], sq[:, k, 0:T], k == 0, k == KC - 1)
            rs = Prs.get()
            b.act(rs[:, 0:T], ps[:, 0:T], AF.Sqrt, scale=1.0 / D_MODEL, bias=1e-6)
            b.rcp(rs[:, 0:T], rs[:, 0:T])
            for k in range(KC):
                b.stt(xt[:, k, 0:T], xt[:, k, 0:T], fg[:, k:k + 1], rs[:, 0:T], ALU.mult, ALU.mult)
                b.dma("pool", self.out.ap()[k * 128:(k + 1) * 128, t0 - CTX:t0 - CTX + T], xt[:, k, 0:T])


def _fm(v):
    v = np.asarray(v)
    c = v.shape[-1]
    return np.ascontiguousarray(np.swapaxes(v.reshape(v.shape[:-1] + (c // 128, 128)), -1, -2))


def host_inputs(inp, bi, n_lat, depth):
    L = depth
    d = {}
    xcat = np.concatenate([inp["ctx"][bi], inp["x"][bi][:n_lat]], axis=0)
    d["xT"] = np.ascontiguousarray(xcat.T)
    d["cc"] = np.ascontiguousarray(np.stack([_fm(inp["c"][bi]), _fm(inp["c_ctx"])], axis=-1))
    d["norm_g"] = _fm(inp["norm_g"][:L])
    d["w_mod"] = np.ascontiguousarray(inp["w_mod"][:L])
    d["b_mod"] = _fm(inp["b_mod"][:L])
    cols = w_in_columns()
    w = inp["w_in"][:L][:, :, cols]
    w = w.reshape(L, KC, 128, NCT, 128).transpose(0, 3, 2, 1, 4)
    d["w_in"] = np.ascontiguousarray(w)
    d["final_g"] = _fm(inp["final_g"])
    NT = CTX + n_lat
    p = np.arange(128)
    dd = p % 64
    half, r = dd // 32, dd % 32
    inv = 10000.0 ** (-np.arange(0, 32, 2, dtype=np.float32) / np.float32(32))
    f = inv[r % 16].astype(np.float32)
    t = np.arange(n_lat)
    pos = np.where(half[:, None] == 0, (t // 64)[None, :], (t % 64)[None, :]).astype(np.float32)
    ang = (pos * f[:, None]).astype(np.float32)
    sign = np.where(r < 16, -1.0, 1.0).astype(np.float32)
    rc = np.ones((128, NT), np.float32)
    rsn = np.zeros((128, NT), np.float32)
    rc[:, CTX:] = np.cos(ang)
    rsn[:, CTX:] = np.sin(ang) * sign[:, None]
    d["ropec"], d["ropes"] = rc, rsn
    j = np.arange(128)[:, None]
    i = np.arange(128)[None, :]
    d["m3"] = np.concatenate([(j <= i), np.ones((128, 128), bool), (i <= j)], axis=1).astype(np.float32)
    d["ident"] = np.eye(128, dtype=np.float32)
    d["a_sink"] = np.ascontiguousarray(np.broadcast_to(inp["a_sink"][:L, None, :], (L, 128, 8)))
    pm = _perm64()
    qn, kn = inp["c_qn"][:L], inp["c_kn"][:L]
    cq = np.stack([qn, qn[:, pm], kn, kn[:, pm]], axis=-1)
    d["c_qk"] = np.ascontiguousarray(np.concatenate([cq, cq], axis=1))
    pp = np.arange(128)
    d["bd2"] = np.ascontiguousarray(np.broadcast_to((pp[:, None] // 64 == np.arange(2)[None, :])[:, :, None], (128, 2, 64))).astype(np.float32)
    row, col = pp[:, None], pp[None, :]
    same = (row // 64) == (col // 64)
    tri = np.stack([row < col, row <= col, row > col, row >= col], axis=1)
    d["rmask"] = (tri & same[:, None, :]).astype(np.float32)
    d["gmask"] = np.where(tri, 0.0, -1.0e4).astype(np.float32)
    d["b_mu"] = np.ascontiguousarray(_fm(inp["b_mu"][:L]).transpose(0, 2, 3, 1))
    w0 = _fm(inp["b_w0"][:L]).transpose(0, 2, 1, 3)
    a0 = _fm(inp["b_a0"][:L]).transpose(0, 2, 1, 3)
    d["b_w0a0"] = np.ascontiguousarray(np.stack([w0, a0], axis=2))
    d["b_aw"] = np.ascontiguousarray(np.concatenate([inp["b_wup"][:L], inp["b_aup"][:L]], axis=2).transpose(0, 2, 1, 3))
    d["b_vec"] = np.ascontiguousarray(np.stack([_fm(inp[k][:L]) for k in ("b_kk", "b_ka", "b_lng", "b_lnb")], axis=2))
    rk = inp["b_rk"][:L]
    blk = np.zeros((L, 128, 4, 128), np.float32)
    for pr in range(4):
        for hh in range(2):
            blk[:, hh * 64:(hh + 1) * 64, pr, hh * 64:(hh + 1) * 64] = rk[:, 2 * pr + hh, :, None]
    d["b_rkblk"] = blk
    sel = np.zeros((8, 8, 128), np.float32)
    for r_ in range(8):
        sel[r_, r_, :] = 1.0
    d["sel8"] = sel
    d["d_conv"] = np.ascontiguousarray(_fm(inp["d_conv"][:L]).transpose(0, 2, 3, 1))
    ab = np.zeros((L, 8, 4), np.float32)
    ab[:, :, 0] = inp["d_dtb"][:L].reshape(L, 8)
    ab[:, :, 1] = inp["d_alog"][:L].reshape(L, 8)
    ab[:, 0:4, 2], ab[:, 4:8, 2] = 1.0, -1.0
    ab[:, 4:8, 3] = 1.0
    d["d_ab"] = ab
    dv = np.zeros((L, 128, 4, 4), np.float32)
    dv[:, :, 0, :] = inp["d_norm"][:L][:, :, None]
    d["d_vec"] = dv
    gu = inp["g_up"][:L].reshape(L, 4, 2, 128, 16, 128)
    wb = inp["w_br"][:L].reshape(L, 4, 4, 128, 16, 128)
    wm = np.concatenate([gu.transpose(0, 4, 3, 1, 2, 5).reshape(L, 16, 128, 8, 128),
                         wb.transpose(0, 4, 3, 1, 2, 5).reshape(L, 16, 128, 16, 128)], axis=3)
    d["wm"] = np.ascontiguousarray(wm)
    wo = inp["w_out"][:L].reshape(L, 16, 128, 16, 128)
    d["wo"] = np.ascontiguousarray(wo.transpose(0, 3, 2, 1, 4))
    d["g_b"] = np.ascontiguousarray(_fm(inp["g_b"][:L]).transpose(0, 2, 1, 3))
    return d


N_CORES = 8
_PROG_CACHE = {}


def run_model(inp, n_lat, depth):
    inp = {k: np.asarray(v) for k, v in inp.items()}
    B = inp["x"].shape[0]
    key = (n_lat, depth)
    if key not in _PROG_CACHE:
        _PROG_CACHE[key] = Prog(n_lat, depth).build()
    nc = _PROG_CACHE[key]
    per_b = [host_inputs(inp, bi, n_lat, depth) for bi in range(B)]
    in_maps = [per_b[i % B] for i in range(N_CORES)]
    res = run_bass_kernel_spmd(nc, in_maps, core_ids=list(range(N_CORES)))
    out = np.stack([np.ascontiguousarray(res.results[bi]["outT"].T) for bi in range(B)], axis=0)
    return out.astype(np.float32)


def kernel(**inputs):
    return run_model(inputs, 8192, 4)
```

```python
import math
from contextlib import ExitStack

import numpy as np
import concourse.bass as bass
import concourse.mybir as mybir
from concourse.bass_utils import run_bass_kernel_spmd

F32 = mybir.dt.float32
BF16 = mybir.dt.bfloat16
AF = mybir.ActivationFunctionType
ALU = mybir.AluOpType

D_MODEL = 2048
CTX = 256
W = 512
KC = D_MODEL // 128

OFF_A, OFF_B, OFF_C, OFF_D, OFF_G = 0, 1280, 3456, 4736, 6800
SEGS = [
    ("Aq", OFF_A + 0, 512, False), ("Aqp", OFF_A + 0, 512, True),
    ("Ak", OFF_A + 512, 128, False), ("Akp", OFF_A + 512, 128, True),
    ("Ag", OFF_A + 768, 512, False),
    ("Bz", OFF_B + 0, 1664, False), ("Bg", OFF_B + 1664, 512, False),
    ("Cq", OFF_C + 0, 512, False), ("Cqp", OFF_C + 0, 512, True),
    ("Ck", OFF_C + 512, 128, False), ("Ckp", OFF_C + 512, 128, True),
    ("Cg", OFF_C + 768, 512, False),
    ("Dqkv", OFF_D + 0, 1536, False), ("Dg", OFF_D + 1552, 512, False),
    ("pm", OFF_G, 256, False),
    ("Dab", OFF_D + 1536, 16, False),
]
SEG_OFF = {}
_o = 0
for _n, _s, _c, _p in SEGS:
    SEG_OFF[_n] = _o
    _o += _c
NFM = _o
NFM_PAD = 8192
VSEGS = [("Av", OFF_A + 640, 128), ("Cv", OFF_C + 640, 128)]
NCT = NFM_PAD // 128 + len(VSEGS)


def _perm64():
    p = np.arange(64)
    blk, r = p // 32, p % 32
    return blk * 32 + (r + 16) % 32


def w_in_columns():
    cols = []
    for n, s, c, perm in SEGS:
        idx = np.arange(s, s + c)
        if perm:
            idx = idx.reshape(-1, 64)[:, _perm64()].reshape(-1)
        cols.append(idx)
    cols = np.concatenate(cols)
    pad = np.zeros(NFM_PAD - NFM, dtype=np.int64)
    vcols = np.concatenate([np.arange(s, s + c) for _, s, c in VSEGS])
    return np.concatenate([cols, pad, vcols])


class Tile:
    __slots__ = ("h", "w", "r", "name")

    def __init__(self, h, name):
        self.h = h
        self.w = None
        self.r = []
        self.name = name

    def __getitem__(self, idx):
        return self.h[idx]


_WRITE_KW = ("out", "ap", "accum_out")


class _RowSplit:
    def __init__(self, a, b, half):
        self.a, self.b, self.half = a, b, half

    def ap(self):
        return self

    def __getitem__(self, idx):
        r, c = idx
        if r.start >= self.half:
            return self.b.ap()[r.start - self.half:r.stop - self.half, c]
        assert r.stop <= self.half
        return self.a.ap()[r, c]


class _View3:
    def __init__(self, h, k):
        self.h, self.k = h, k

    def __getitem__(self, idx):
        return self.h[:, self.k, :]


class _View:
    def __init__(self, h, lo, w):
        self.h, self.lo, self.w = h, lo, w

    def __getitem__(self, idx):
        if not isinstance(idx, tuple):
            idx = (idx, slice(None))
        p, c = idx[0], idx[1]
        a, bnd, _ = c.indices(self.w)
        return self.h[p, self.lo + a:self.lo + bnd]


class Builder:
    ENGS = ("pe", "dve", "act", "pool", "sp")

    def __init__(self, nc, es):
        self.nc = nc
        self.es = es
        self.ops = {e: [] for e in self.ENGS}
        self.cnt = {e: 0 for e in self.ENGS}
        self.sem = {}
        for e in ("pe", "dve", "act", "pool"):
            self.sem[e] = es.enter_context(nc.semaphore("s_" + e))
        self.waited = {e: {} for e in self.ENGS}
        NQ = 12
        self.dsem = {}
        self.dnext = {}
        for q in ("sp", "pool", "act"):
            self.dsem[q] = [es.enter_context(nc.semaphore("d_%s%d" % (q, i))) for i in range(NQ)]
            self.dnext[q] = 0
        self.dcum = {}
        self.n_tiles = 0
        self.reg = {}

    def scope(self):
        b = self

        class _S:
            def __enter__(s2):
                s2.old = b.es
                s2.st = ExitStack()
                b.es = s2.st
                return s2

            def __exit__(s2, *a):
                b.barrier()
                b.es = s2.old
                s2.st.close()
                return False
        return _S()

    def sb(self, shape, dtype, name=None, psum=False):
        self.n_tiles += 1
        name = "%s_%d" % (name or "t", self.n_tiles)
        mk = self.nc.psum_tensor if psum else self.nc.sbuf_tensor
        h = self.es.enter_context(mk(name, list(shape), dtype))
        t = Tile(h, name)
        self.reg[h.name] = t
        return t

    def ps(self, shape, dtype=F32, name=None):
        return self.sb(shape, dtype, name, psum=True)

    def _need(self, eng, rec, waits):
        if rec is None:
            return
        if rec[0] == "E":
            key, idx = rec[1], rec[2]
            if key == eng and eng == "pe":
                return
        else:
            key, idx = rec[1], rec[2]
        if self.waited[eng].get(key, 0) >= idx:
            return
        self.waited[eng][key] = idx
        waits.append((key, idx))

    def _deps(self, eng, reads, writes):
        waits = []
        for t in reads:
            self._need(eng, t.w, waits)
        for t in writes:
            self._need(eng, t.w, waits)
            for r in t.r:
                self._need(eng, r, waits)
        return waits

    def _split(self, kw):
        reads, writes = [], []
        for k, v in kw.items():
            if hasattr(v, "tensor") and hasattr(v, "ap"):
                t = self.reg.get(v.tensor.name)
                if isinstance(t, tuple):
                    t = t[1][(v.offset % 512) // t[0]]
                if t is not None:
                    (writes if k in _WRITE_KW else reads).append(t)
        return reads, writes

    def subtiles(self, tile, width):
        subs = []
        for j in range(512 // width):
            st = Tile(_View(tile.h, j * width, width), "%s_s%d" % (tile.name, j))
            subs.append(st)
        self.reg[tile.h.name] = (width, subs)
        return subs

    def _mark(self, rec, reads, writes):
        for t in reads:
            t.r.append(rec)
        for t in writes:
            t.w = rec
            t.r = []

    def I(self, eng, method, **kw):
        reads, writes = self._split(kw)
        waits = self._deps(eng, reads, writes)
        self.cnt[eng] += 1
        idx = self.cnt[eng]
        self.ops[eng].append((waits, method, kw, None))
        self._mark(("E", eng, idx), reads, writes)

    def dma(self, q, out, in_, **kw):
        reads, writes = self._split(dict(out=out, in_=in_))
        waits = self._deps(q, reads, writes)
        i = self.dnext[q]
        self.dnext[q] = (i + 1) % len(self.dsem[q])
        key = (q, i)
        prev = self.dcum.get(key, 0)
        if prev:
            self._need(q, ("D", key, prev), waits)
        val = prev + 16
        self.dcum[key] = val
        kw = dict(kw, out=out, in_=in_)
        self.ops[q].append((waits, "dma_start", kw, self.dsem[q][i]))
        self._mark(("D", key, val), reads, writes)

    def barrier(self):
        for e in self.ENGS:
            waits = []
            for x in ("pe", "dve", "act", "pool"):
                if self.cnt[x] and not (x == e and e == "pe"):
                    self._need(e, ("E", x, self.cnt[x]), waits)
            for key, val in self.dcum.items():
                self._need(e, ("D", key, val), waits)
            if waits:
                self.ops[e].append((waits, None, None, None))

    def _semof(self, key):
        if isinstance(key, tuple):
            return self.dsem[key[0]][key[1]]
        return self.sem[key]

    def emit(self):
        nc = self.nc
        with nc.Block() as block:
            def mk(ename):
                def body(e):
                    own = self.sem.get(ename)
                    for waits, method, kw, dsem in self.ops[ename]:
                        for key, val in waits:
                            e.wait_ge(self._semof(key), val)
                        if method is None:
                            continue
                        ins = getattr(e, method)(**kw)
                        if dsem is not None:
                            ins.then_inc(dsem, 16)
                        else:
                            ins.then_inc(own, 1)
                return body
            block.tensor(mk("pe"))
            block.vector(mk("dve"))
            block.scalar(mk("act"))
            block.gpsimd(mk("pool"))
            block.sync(mk("sp"))

    def mm(self, out, lhsT, rhs, start=True, stop=True):
        self.I("pe", "matmul", out=out, lhsT=lhsT, rhs=rhs, start=start, stop=stop)

    def tr(self, out, in_, identity):
        self.I("pe", "transpose", out=out, in_=in_, identity=identity)

    def act(self, out, in_, func, scale=1.0, bias=0.0):
        self.I("act", "activation", out=out, in_=in_, func=func, scale=scale, bias=bias)

    def tt(self, eng, out, in0, in1, op):
        self.I(eng, "tensor_tensor", out=out, in0=in0, in1=in1, op=op)

    def ts(self, eng, out, in0, s1, op0, s2=None, op1=None):
        if op1 is None:
            self.I(eng, "tensor_scalar", out=out, in0=in0, scalar1=s1, scalar2=None, op0=op0)
        else:
            self.I(eng, "tensor_scalar", out=out, in0=in0, scalar1=s1, scalar2=s2, op0=op0, op1=op1)

    def stt(self, out, in0, scalar, in1, op0, op1):
        self.I("dve", "scalar_tensor_tensor", out=out, in0=in0, scalar=scalar, in1=in1, op0=op0, op1=op1)

    def cp(self, eng, out, in_):
        if eng == "act":
            self.I("act", "activation", out=out, in_=in_, func=AF.Copy)
        else:
            self.I(eng, "tensor_copy", out=out, in_=in_)

    def rcp(self, out, in_):
        self.I("dve", "reciprocal", out=out, in_=in_)

    def ms(self, eng, ap, val):
        self.I(eng, "memset", ap=ap, constant=val)


class Pool:
    def __init__(self, b, n, shape, dtype, name, psum=False):
        self.t = [b.sb(shape, dtype, name, psum=psum) for _ in range(n)]
        self.i = 0

    def get(self):
        t = self.t[self.i]
        self.i = (self.i + 1) % len(self.t)
        return t


class Prog:
    def __init__(self, n_lat, depth, debug=(), mixers=(0, 1, 2, 3)):
        self.n = n_lat
        self.NT = CTX + n_lat
        self.depth = depth
        self.debug = set(debug)
        self.mixers = mixers
        self.stages = ("prep", "units", "post")
        self.cut = 0
        self.do_merge = True
        self.no_inter = False
        self.blocks = [(0, CTX, True)]
        t = CTX
        while t < self.NT:
            s = min(512, self.NT - t)
            self.blocks.append((t, s, False))
            t += s

    def build(self):
        nc = bass.Bass("TRN2", target_bir_lowering=False)
        self.nc = nc
        L, NT = self.depth, self.NT
        with ExitStack() as es:
            b = Builder(nc, es)
            self.b = b
            dk = lambda name: ("ExternalOutput" if name in self.debug else "Internal")
            I = {}

            def inp(name, shape, dt=F32):
                I[name] = nc.dram_tensor(name, list(shape), dt, kind="ExternalInput")
            inp("xT", [D_MODEL, NT])
            inp("cc", [128, KC, 2])
            inp("norm_g", [L, 128, KC])
            inp("w_mod", [L, D_MODEL, 3 * D_MODEL])
            inp("b_mod", [L, 128, 48])
            inp("w_in", [L, NCT, 128, KC, 128])
            inp("final_g", [128, KC])
            inp("ropec", [128, NT])
            inp("ropes", [128, NT])
            inp("m3", [128, 384])
            inp("ident", [128, 128])
            inp("a_sink", [L, 128, 8])
            inp("c_qk", [L, 128, 4])
            inp("bd2", [128, 2, 64])
            inp("rmask", [128, 4, 128])
            inp("b_mu", [L, 128, 13, 2])
            inp("b_w0a0", [L, 128, 2, 2, 4])
            inp("b_aw", [L, 128, 2, 512])
            inp("b_vec", [L, 128, 4, 4])
            inp("b_rkblk", [L, 128, 4, 128])
            inp("gmask", [128, 4, 128])
            inp("sel8", [8, 8, 128])
            inp("d_conv", [L, 128, 12, 5])
            inp("d_ab", [L, 8, 4])
            inp("d_vec", [L, 128, 4, 4])
            inp("wm", [L, 16, 128, 24, 128])
            inp("wo", [L, 16, 128, 16, 128])
            inp("g_b", [L, 128, 4, 16])
            self.I = I
            self.out = nc.dram_tensor("outT", [D_MODEL, self.n], F32, kind="ExternalOutput")
            self.xs = nc.dram_tensor("xs", [D_MODEL, NT], F32, kind=dk("xs"))
            self.pT = _RowSplit(nc.dram_tensor("pTa", [NFM_PAD // 2, NT], F32, kind=dk("pTa")),
                                nc.dram_tensor("pTb", [NFM_PAD // 2, NT], F32, kind=dk("pTb")), NFM_PAD // 2)
            self.vtm = nc.dram_tensor("vtm", [2, NT, 128], BF16, kind=dk("vtm"))
            self.wbf = nc.dram_tensor("wbf", [NCT, 128, KC, 128], BF16, kind="Internal")
            self.yT = nc.dram_tensor("yT", [4, W, NT], BF16, kind=dk("yT"))
            self.UB = nc.dram_tensor("UB", [2, 4, 128, 7, NT], F32, kind=dk("UB"))
            self.GAMB = nc.dram_tensor("GAMB", [2, 4, 128, NT // 64], F32, kind=dk("GAMB"))
            self.bon = nc.dram_tensor("bon", [W, NT], F32, kind=dk("bon"))
            self.ytm = nc.dram_tensor("ytm", [2, NT, W], F32, kind=dk("ytm"))
            self.UD = nc.dram_tensor("UD", [4, 128, 3, NT], F32, kind=dk("UD"))
            self.RD = nc.dram_tensor("RD", [8, 6, NT], F32, kind=dk("RD"))
            self.EGTD = nc.dram_tensor("EGTD", [8, NT // 128], F32, kind=dk("EGTD"))
            self.wmb = nc.dram_tensor("wmb", [16, 128, 24, 128], BF16, kind="Internal")
            self.wob = nc.dram_tensor("wob", [16, 128, 16, 128], BF16, kind="Internal")
            self.ones_bf = b.sb([128, 128], BF16, "ones")
            b.ms("dve", self.ones_bf[:], 1.0)
            self.modv = b.sb([128, L, KC, 6], F32, "modv")

            with b.scope():
                self.phase_mod()
            for l in range(L):
                with b.scope():
                    self.phase_wcast(l)
                with b.scope():
                    self.phase_inproj(l)
                need_ctx = l < L - 1
                for mi in self.mixers:
                    with b.scope():
                        if mi in (0, 2):
                            self.phase_attn(l, mi, need_ctx)
                        elif mi == 1 and "prep" in self.stages:
                            self.phase_b_prep(l)
                    if mi in (1, 3):
                        if mi == 3 and "prep" in self.stages:
                            with b.scope():
                                self.phase_d_prep(l)
                        if "units" in self.stages:
                            with b.scope():
                                self.phase_units(l, mi)
                        if "post" in self.stages:
                            with b.scope():
                                self.phase_post(l, mi, need_ctx)
                if self.do_merge:
                    with b.scope():
                        self.phase_mcast(l)
                    with b.scope():
                        self.phase_merge(l, need_ctx)
            if self.do_merge:
                with b.scope():
                    self.phase_final()
            b.barrier()
            b.emit()
        return nc

    def phase_mod(self):
        b, I, L = self.b, self.I, self.depth
        psum = Pool(b, 4, [128, 512], F32, "ps", psum=True)
        cc = b.sb([128, KC, 2], F32, "cc")
        b.dma("sp", cc[:], I["cc"].ap())
        sc = b.sb([128, KC, 2], F32, "sc")
        b.act(sc[:], cc[:], AF.Silu)
        wpool = Pool(b, 2, [128, KC, 512], F32, "wmod")
        raw = b.sb([128, 48, 2], F32, "modraw")
        bm = b.sb([128, 48], F32, "bmod")
        ng = b.sb([128, KC], F32, "ng")
        mv = self.modv
        for l in range(L):
            b.dma("sp", bm[:], I["b_mod"].ap()[l])
            b.dma("sp", ng[:], I["norm_g"].ap()[l])
            for cg in range(12):
                wt = wpool.get()
                src = I["w_mod"].ap()[l, :, cg * 512:(cg + 1) * 512].rearrange("(k p) c -> p k c", p=128)
                b.dma("sp", wt[:], src)
                for j in range(4):
                    ct = cg * 4 + j
                    ps = psum.get()
                    for k in range(KC):
                        b.mm(ps[:, 0:2], wt[:, k, j * 128:(j + 1) * 128], sc[:, k, :], k == 0, k == KC - 1)
                    b.ts("dve", raw[:, ct, :], ps[:, 0:2], bm[:, ct:ct + 1], ALU.add)
            for v in range(2):
                b.stt(mv[:, l, :, 3 * v + 0], raw[:, 16:32, v], 1.0, ng[:], ALU.add, ALU.mult)
                b.cp("dve", mv[:, l, :, 3 * v + 1], raw[:, 0:16, v])
                b.cp("dve", mv[:, l, :, 3 * v + 2], raw[:, 32:48, v])

    def phase_wcast(self, l):
        b, I = self.b, self.I
        pf = Pool(b, 3, [128, KC, 128], F32, "wcf")
        pb = Pool(b, 3, [128, KC, 128], BF16, "wcb")
        for ct in range(NCT):
            f, g = pf.get(), pb.get()
            b.dma("sp", f[:], I["w_in"].ap()[l, ct])
            b.cp(("dve", "pool")[ct % 2], g[:], f[:])
            b.dma("sp", self.wbf.ap()[ct], g[:])

    def phase_inproj(self, l):
        b, I = self.b, self.I
        mv = self.modv
        psum = Pool(b, 6, [128, 512], F32, "ps", psum=True)
        Pxt = Pool(b, 2, [128, KC, 512], F32, "xt")
        Psq = Pool(b, 1, [128, KC, 512], BF16, "sq")
        PhT = Pool(b, 2, [128, KC, 512], BF16, "hT")
        Prs = Pool(b, 2, [128, 512], F32, "rs")
        Pw = Pool(b, 3, [128, KC, 128], BF16, "wip")
        Pev = Pool(b, 3, [128, 512], F32, "ev")
        Pevb = Pool(b, 2, [128, 128], BF16, "evb")
        src_x = I["xT"] if l == 0 else self.xs
        for (t0, T, is_ctx) in self.blocks:
            v = 1 if is_ctx else 0
            xt = Pxt.get()
            for k in range(KC):
                b.dma("sp", xt[:, k, 0:T], src_x.ap()[k * 128:(k + 1) * 128, t0:t0 + T])
            sq = Psq.get()
            b.act(sq[:, :, 0:T], xt[:, :, 0:T], AF.Square)
            ps = psum.get()
            for k in range(KC):
                b.mm(ps[:, 0:T], self.ones_bf[:], sq[:, k, 0:T], k == 0, k == KC - 1)
            rs = Prs.get()
            b.act(rs[:, 0:T], ps[:, 0:T], AF.Sqrt, scale=1.0 / D_MODEL, bias=1e-6)
            b.rcp(rs[:, 0:T], rs[:, 0:T])
            hT = PhT.get()
            for k in range(KC):
                b.tt("dve", xt[:, k, 0:T], xt[:, k, 0:T], rs[:, 0:T], ALU.mult)
                b.ts(("pool", "dve")[k % 2], hT[:, k, 0:T], xt[:, k, 0:T], mv[:, l, k, 3 * v:3 * v + 1], ALU.mult,
                     mv[:, l, k, 3 * v + 1:3 * v + 2], ALU.add)
            for ct in range((NFM + 127) // 128):
                wt = Pw.get()
                b.dma("sp", wt[:], self.wbf.ap()[ct])
                ps = psum.get()
                for k in range(KC):
                    b.mm(ps[:, 0:T], wt[:, k, :], hT[:, k, 0:T], k == 0, k == KC - 1)
                ev = Pev.get()
                b.cp(("act", "dve")[ct % 2], ev[:, 0:T], ps[:, 0:T])
                b.dma("pool", self.pT.ap()[ct * 128:(ct + 1) * 128, t0:t0 + T], ev[:, 0:T])
            for vi in range(2):
                wt = Pw.get()
                b.dma("sp", wt[:], self.wbf.ap()[NFM_PAD // 128 + vi])
                for sbk in range(T // 128):
                    ps = psum.get()
                    for k in range(KC):
                        b.mm(ps[:, 0:128], hT[:, k, sbk * 128:(sbk + 1) * 128], wt[:, k, :], k == 0, k == KC - 1)
                    evb = Pevb.get()
                    b.cp("dve", evb[:], ps[:, 0:128])
                    r0 = t0 + sbk * 128
                    b.dma("pool", self.vtm.ap()[vi, r0:r0 + 128, :], evb[:])

    def phase_attn(self, l, mi, need_ctx):
        b, I, NT = self.b, self.I, self.NT
        isC = (mi == 2)
        pre = "C" if isC else "A"
        oq, oqp, ok_, okp, og = (SEG_OFF[pre + x] for x in ("q", "qp", "k", "kp", "g"))
        pT = self.pT.ap()
        NCH = NT // 128
        psS = Pool(b, 4, [128, 512], F32, "psS", psum=True)
        psO = Pool(b, 2, [128, 512], F32, "psO", psum=True)
        psB = Pool(b, 2, [128, 512], F32, "psB", psum=True)
        KT = b.sb([128, 2, NT], BF16, "KT")
        VE = b.sb([128, NCH, 2, 65], BF16, "VE")
        cst = b.sb([128, 16], F32, "acst")
        m3 = b.sb([128, 384], F32, "m3")
        ones_f = b.sb([128, 64], F32, "ones_f")
        oblk = b.sb([128, 128], BF16, "oblk")
        b.ms("dve", ones_f[:], 1.0)
        b.ms("dve", oblk[:], 0.0)
        b.ms("dve", oblk[0:64, 0:64], 1.0)
        b.ms("dve", oblk[64:128, 64:128], 1.0)
        b.dma("sp", m3[:], I["m3"].ap())
        b.dma("sp", cst[:, 0:8], I["a_sink"].ap()[l])
        b.dma("sp", cst[:, 8:12], I["c_qk"].ap()[l])
        b.act(cst[:, 0:8], cst[:, 0:8], AF.Exp)
        b.ms("dve", VE[:, :, :, 64:65], 1.0)
        vsrc = self.vtm.ap()[1 if isC else 0].rearrange("(c p) (g d) -> p c g d", p=128, g=2)
        for c0 in range(0, NCH, 8):
            c1 = min(NCH, c0 + 8)
            for g in range(2):
                b.dma("sp", VE[:, c0:c1, g, 0:64], vsrc[:, c0:c1, g, :])

        ld = Pool(b, 4, [128, 512], F32, "ald")
        tb = Pool(b, 2, [128, 512], F32, "atb")
        tmp = Pool(b, 4, [128, 512], F32, "atmp")
        sqp = Pool(b, 2, [128, 512], BF16, "asq")
        rsp = Pool(b, 2, [128, 512], F32, "ars")

        def rope(dst_ap, rows, rowsp, t0, T, cosb, sinb, nidx, dup):
            x, xp = ld.get(), ld.get()
            if dup:
                for hh in range(2):
                    b.dma("sp", x[hh * 64:(hh + 1) * 64, 0:T], pT[rows:rows + 64, t0:t0 + T])
                    b.dma("sp", xp[hh * 64:(hh + 1) * 64, 0:T], pT[rowsp:rowsp + 64, t0:t0 + T])
            else:
                b.dma("sp", x[:, 0:T], pT[rows:rows + 128, t0:t0 + T])
                b.dma("sp", xp[:, 0:T], pT[rowsp:rowsp + 128, t0:t0 + T])
            t1, t2 = tmp.get(), tmp.get()
            if isC:
                sq = sqp.get()
                b.act(sq[:, 0:T], x[:, 0:T], AF.Square)
                ps = psB.get()
                b.mm(ps[:, 0:T], oblk[:], sq[:, 0:T])
                rs = rsp.get()
                b.act(rs[:, 0:T], ps[:, 0:T], AF.Sqrt, scale=1.0 / 64, bias=1e-6)
                b.rcp(rs[:, 0:T], rs[:, 0:T])
                b.stt(t1[:, 0:T], x[:, 0:T], cst[:, nidx:nidx + 1], cosb[:, 0:T], ALU.mult, ALU.mult)
                b.stt(t2[:, 0:T], xp[:, 0:T], cst[:, nidx + 1:nidx + 2], sinb[:, 0:T], ALU.mult, ALU.mult)
                b.tt("pool", t1[:, 0:T], t1[:, 0:T], t2[:, 0:T], ALU.add)
                b.tt("dve", dst_ap, t1[:, 0:T], rs[:, 0:T], ALU.mult)
            else:
                b.tt("dve", t1[:, 0:T], x[:, 0:T], cosb[:, 0:T], ALU.mult)
                b.tt("pool", t2[:, 0:T], xp[:, 0:T], sinb[:, 0:T], ALU.mult)
                b.tt("dve", dst_ap, t1[:, 0:T], t2[:, 0:T], ALU.add)

        def tables(t0, T):
            cosb, sinb = tb.get(), tb.get()
            b.dma("sp", cosb[:, 0:T], I["ropec"].ap()[:, t0:t0 + T])
            b.dma("sp", sinb[:, 0:T], I["ropes"].ap()[:, t0:t0 + T])
            return cosb, sinb

        for (t0, T, is_ctx) in self.blocks:
            cosb, sinb = tables(t0, T)
            for g in range(2):
                rope(KT[:, g, t0:t0 + T], ok_ + g * 64, okp + g * 64, t0, T, cosb, sinb, 10, True)

        QT = Pool(b, 2, [128, 4, 512], BF16, "QT")
        PTp = Pool(b, 4, [128, 512], BF16, "PT")
        gp = Pool(b, 2, [64, 8, 512], F32, "ag")
        ep = Pool(b, 2, [64, 8, 512], F32, "ae")
        drp = Pool(b, 2, [128, 512], F32, "adr")
        bcp = Pool(b, 2, [64, 512], F32, "abc")
        yp = Pool(b, 3, [64, 512], BF16, "ay")
        nlat_ch = self.n // 128
        for (t0, T, is_ctx) in self.blocks:
            if is_ctx and not need_ctx:
                continue
            cosb, sinb = tables(t0, T)
            qt = QT.get()
            for pr in range(4):
                rope(qt[:, pr, 0:T], oq + pr * 128, oqp + pr * 128, t0, T, cosb, sinb, 8, False)
            chunks = [(0, 0, T, None), (1, 0, T, None)]
            if not is_ctx:
                lb = (t0 - CTX) // 128
                nq = T // 128
                if isC:
                    chunks += [(2 + c, 0, T, None) for c in range(nlat_ch)]
                else:
                    for c in range(max(0, lb - 1), min(nlat_ch, lb + nq + 1)):
                        qlo, qhi = max(lb, c - 1), min(lb + nq - 1, c + 1)
                        chunks.append((2 + c, (qlo - lb) * 128, (qhi - lb + 1) * 128, (qlo - c + 1) * 128))
            gall, sgall = gp.get(), ep.get()
            for h in range(8):
                b.dma("sp", gall[:, h, 0:T], pT[og + h * 64:og + (h + 1) * 64, t0:t0 + T])
            b.act(sgall[:, :, 0:T], gall[:, :, 0:T], AF.Sigmoid)
            b.tt("dve", sgall[:, :, 0:T], sgall[:, :, 0:T], gall[:, :, 0:T], ALU.mult)
            for h in range(8):
                g, pr, po = h // 4, h // 2, (h % 2) * 64
                O = psO.get()
                LA = 2
                Sq = []

                def issue_s(cj):
                    ch_, lo_, hi_, _m = chunks[cj]
                    S_ = psS.get()
                    b.mm(S_[:, lo_:hi_], KT[po:po + 64, g, ch_ * 128:(ch_ + 1) * 128], qt[po:po + 64, pr, lo_:hi_])
                    Sq.append(S_)
                for cj in range(min(LA, len(chunks))):
                    issue_s(cj)
                for ci, (ch, lo, hi, mlo) in enumerate(chunks):
                    if ci + LA < len(chunks):
                        issue_s(ci + LA)
                    S = Sq[ci]
                    pt = PTp.get()
                    b.act(pt[:, lo:hi], S[:, lo:hi], AF.Exp, scale=0.125)
                    if mlo is not None:
                        b.tt("dve", pt[:, lo:hi], pt[:, lo:hi], m3[:, mlo:mlo + hi - lo], ALU.mult)
                    b.mm(O[0:65, lo:hi], VE[:, ch, g, :], pt[:, lo:hi], ci == 0, ci == len(chunks) - 1)
                dr = drp.get()
                if isC:
                    b.rcp(dr[64:65, 0:T], O[64:65, 0:T])
                else:
                    b.ts("dve", dr[64:65, 0:T], O[64:65, 0:T], cst[64:65, h:h + 1], ALU.add)
                    b.rcp(dr[64:65, 0:T], dr[64:65, 0:T])
                B = psB.get()
                b.mm(B[0:64, 0:T], ones_f[64:65, 0:64], dr[64:65, 0:T])
                bc = bcp.get()
                b.cp("act", bc[:, 0:T], B[0:64, 0:T])
                b.tt("dve", bc[:, 0:T], O[0:64, 0:T], bc[:, 0:T], ALU.mult)
                y = yp.get()
                b.tt("dve", y[:, 0:T], bc[:, 0:T], sgall[:, h, 0:T], ALU.mult)
                b.dma("pool", self.yT.ap()[mi, h * 64:(h + 1) * 64, t0:t0 + T], y[:, 0:T])

    def seg_bounds(self, t0):
        return (0, CTX) if t0 < CTX else (CTX, self.NT)

    def phase_b_prep(self, l):
        b, I, NT = self.b, self.I, self.NT
        pT = self.pT.ap()
        oz = SEG_OFF["Bz"]
        psum = Pool(b, 4, [128, 512], F32, "ps", psum=True)
        psB = Pool(b, 2, [128, 512], F32, "psb", psum=True)
        mu = b.sb([128, 13, 2], F32, "mu")
        cmu = b.sb([128, 13], F32, "cmu")
        w0a0 = b.sb([128, 2, 2, 4], F32, "w0a0")
        aw = b.sb([128, 2, 512], F32, "aw")
        vec = b.sb([128, 4, 4], F32, "bvec")
        rkb = b.sb([128, 4, 128], F32, "rkb")
        oblk = b.sb([128, 128], BF16, "oblk")
        onesF = b.sb([128, 512], F32, "onesF")
        b.ms("dve", onesF[:], 1.0)
        b.ms("dve", oblk[:], 0.0)
        b.ms("dve", oblk[0:64, 0:64], 1.0)
        b.ms("dve", oblk[64:128, 64:128], 1.0)
        b.dma("sp", mu[:], I["b_mu"].ap()[l])
        b.dma("sp", w0a0[:], I["b_w0a0"].ap()[l])
        b.dma("sp", aw[:], I["b_aw"].ap()[l])
        b.dma("sp", vec[:], I["b_vec"].ap()[l])
        b.dma("sp", rkb[:], I["b_rkblk"].ap()[l])
        b.ts("dve", cmu[:], mu[:, :, 0], -1.0, ALU.mult, 1.0, ALU.add)
        b.tt("dve", cmu[:], cmu[:], mu[:, :, 1], ALU.subtract)
        ptp = Pool(b, 3, [128, 514], F32, "bpt")
        zall = b.sb([128, 13, 512], F32, "zall")
        kkT = b.sb([128, 4, 512], F32, "kkT")
        t2k = Pool(b, 12, [128, 512], F32, "bt")
        sqp = Pool(b, 2, [128, 512], BF16, "bsq")
        Sx = b.sb([128, 513], F32, "Sx")
        b.ms("dve", Sx[:, 0:1], 0.0)
        o7p = Pool(b, 2, [128, 7, 512], F32, "o7")
        gtp = Pool(b, 2, [128, 8], F32, "egt")
        for (t0, T, is_ctx) in self.blocks:
            lo, hi = self.seg_bounds(t0)
            nch = T // 64
            for i in range(13):
                pt = ptp.get()
                a0, a1 = max(t0 - 1, lo), min(t0 + T + 1, hi)
                if a0 > t0 - 1:
                    b.ms("pool", pt[:, 0:1], 0.0)
                if a1 < t0 + T + 1:
                    b.ms("pool", pt[:, T + 1:T + 2], 0.0)
                b.dma("sp", pt[:, a0 - (t0 - 1):a1 - (t0 - 1)], pT[oz + i * 128:oz + (i + 1) * 128, a0:a1])
                z = zall[:, i, 0:T]
                b.ts("dve", z, pt[:, 1:T + 1], cmu[:, i:i + 1], ALU.mult)
                b.stt(z, pt[:, 0:T], mu[:, i, 0:1], z, ALU.mult, ALU.add)
                b.stt(z, pt[:, 2:T + 2], mu[:, i, 1:2], z, ALU.mult, ALU.add)
            b.act(zall[0:64, 12, 0:T], zall[0:64, 12, 0:T], AF.Tanh)
            for pr in range(4):
                kx = t2k.get()
                b.ts("dve", kx[:, 0:T], zall[:, 4 + pr, 0:T], vec[:, 0, pr:pr + 1], ALU.mult)
                sq = sqp.get()
                b.act(sq[:, 0:T], kx[:, 0:T], AF.Square)
                ps = psum.get()
                b.mm(ps[:, 0:T], oblk[:], sq[:, 0:T])
                rn = t2k.get()
                b.act(rn[:, 0:T], ps[:, 0:T], AF.Sqrt, bias=1e-12)
                b.rcp(rn[:, 0:T], rn[:, 0:T])
                b.tt("dve", kkT[:, pr, 0:T], kx[:, 0:T], rn[:, 0:T], ALU.mult)
            for pr in range(4):
                psb = psB.get()
                rT, kT_, vT_ = zall[:, pr, 0:T], zall[:, 4 + pr, 0:T], zall[:, 8 + pr, 0:T]
                for d in range(2):
                    pw = psum.get()
                    b.mm(pw[:, 0:T], aw[0:64, d, pr * 128:(pr + 1) * 128], zall[0:64, 12, 0:T])
                    lw = t2k.get()
                    b.act(lw[:, 0:T], pw[:, 0:T], AF.Sigmoid, bias=w0a0[:, 0, d, pr:pr + 1])
                    b.ts("pool", lw[:, 0:T], lw[:, 0:T], -0.606531, ALU.mult)
                    pa = psum.get()
                    b.mm(pa[:, 0:T], aw[64:128, d, pr * 128:(pr + 1) * 128], zall[64:128, 12, 0:T])
                    a = t2k.get()
                    b.act(a[:, 0:T], pa[:, 0:T], AF.Sigmoid, bias=w0a0[:, 1, d, pr:pr + 1])
                    kt = t2k.get()
                    b.ts("dve", kt[:, 0:T], a[:, 0:T], -1.0, ALU.add, vec[:, 1, pr:pr + 1], ALU.mult)
                    b.stt(kt[:, 0:T], kt[:, 0:T], 1.0, kT_, ALU.add, ALU.mult)
                    bb = t2k.get()
                    b.tt("pool", bb[:, 0:T], kkT[:, pr, 0:T], a[:, 0:T], ALU.mult)
                    b.I("dve", "tensor_tensor_scan", out=Sx[:, 1:T + 1], data0=onesF[:, 0:T], data1=lw[:, 0:T],
                        initial=0.0, op0=ALU.mult, op1=ALU.add)
                    gc = t2k.get()
                    v3 = lambda ap: ap.rearrange("p (c t) -> p c t", t=64)
                    b.tt("dve", v3(gc[:, 0:T]), v3(Sx[:, 1:T + 1]), v3(Sx[:, 0:T])[:, :, 0:1].broadcast_to([128, nch, 64]),
                         ALU.subtract)
                    gtot_bc = v3(gc[:, 0:T])[:, :, 63:64].broadcast_to([128, nch, 64])
                    ge = t2k.get()
                    if d == 0:
                        gi = gc
                        b.tt("dve", ge[:, 0:T], gc[:, 0:T], lw[:, 0:T], ALU.subtract)
                    else:
                        gi = t2k.get()
                        b.tt("dve", v3(ge[:, 0:T]), gtot_bc, v3(gc[:, 0:T]), ALU.subtract)
                        b.tt("pool", gi[:, 0:T], ge[:, 0:T], lw[:, 0:T], ALU.add)
                    egt = gtp.get()
                    b.act(egt[:, 0:nch], v3(gc[:, 0:T])[:, :, 63], AF.Exp)
                    ege, egi, engi = t2k.get(), t2k.get(), t2k.get()
                    b.act(ege[:, 0:T], ge[:, 0:T], AF.Exp)
                    b.act(egi[:, 0:T], gi[:, 0:T], AF.Exp)
                    b.act(engi[:, 0:T], gi[:, 0:T], AF.Exp, scale=-1.0)
                    o7 = o7p.get()
                    b.tt("dve", o7[:, 0, 0:T], kkT[:, pr, 0:T], ege[:, 0:T], ALU.mult)
                    b.tt("pool", o7[:, 1, 0:T], rT, egi[:, 0:T], ALU.mult)
                    b.tt("dve", o7[:, 2, 0:T], bb[:, 0:T], engi[:, 0:T], ALU.mult)
                    b.tt("pool", o7[:, 3, 0:T], kt[:, 0:T], engi[:, 0:T], ALU.mult)
                    ebc = egt[:, 0:nch].unsqueeze(2).broadcast_to([128, nch, 64])
                    b.tt("dve", v3(o7[:, 4, 0:T]), v3(o7[:, 3, 0:T]), ebc, ALU.mult)
                    b.tt("dve", v3(o7[:, 5, 0:T]), v3(o7[:, 2, 0:T]), ebc, ALU.mult)
                    b.cp("pool", o7[:, 6, 0:T], vT_)
                    b.dma("pool", self.UB.ap()[d, pr, :, :, t0:t0 + T], o7[:, :, 0:T])
                    b.dma("pool", self.GAMB.ap()[d, pr, :, t0 // 64:t0 // 64 + nch], egt[:, 0:nch])
                    rkt = t2k.get()
                    b.tt("dve", rkt[:, 0:T], rT, kt[:, 0:T], ALU.mult)
                    b.mm(psb[:, 0:T], rkb[:, pr, :], rkt[:, 0:T], d == 0, d == 1)
                bo = t2k.get()
                b.tt("dve", bo[:, 0:T], psb[:, 0:T], vT_, ALU.mult)
                b.dma("pool", self.bon.ap()[pr * 128:(pr + 1) * 128, t0:t0 + T], bo[:, 0:T])

    def phase_units(self, l, mi):
        b, I, NT = self.b, self.I, self.NT
        isB = (mi == 1)
        nlev = 5 if isB else 6
        CT = 64 if isB else 128
        NCH = NT // CT
        cch = CTX // CT
        order = {0: list(range(NCH)), 1: list(range(cch - 1, -1, -1)) + list(range(NCH - 1, cch - 1, -1))}
        chains = [(u, d) for u in range(4) for d in range(2)]
        NC = len(chains)
        psG = Pool(b, 2, [128, 512], F32, "psG", psum=True)
        psA = Pool(b, 2, [128, 512], F32, "psA", psum=True)
        sm = Pool(b, 4, [128, 512], F32, "psm", psum=True)
        ident = b.sb([128, 128], F32, "ident")
        b.dma("sp", ident[:], I["ident"].ap())
        rmask = b.sb([128, 4, 128], F32, "rmask")
        b.dma("sp", rmask[:], I["rmask" if isB else "gmask"].ap())
        bd2 = b.sb([128, 2, 64], F32, "bd2")
        b.dma("sp", bd2[:], I["bd2"].ap())
        NB = 16
        XXp = Pool(b, NB, [128, 4, 128], BF16, "XX")
        TMp = Pool(b, NB, [128, 3, 128], BF16, "TM3")
        A3p = Pool(b, NB, [128, 3, 128], BF16, "A3")
        Pmp = Pool(b, 10 if isB else 18, [128, 2, 128], F32, "Pm")
        XTp = Pool(b, 17, [128, 128], F32, "XT")
        XTf = Pool(b, 16, [128, 128], F32, "XTf")
        if isB:
            P2p = Pool(b, 10, [128, 128], F32, "P2f")
            Pbp = Pool(b, 18, [128, 2, 128], BF16, "Pb")
            XTbp = Pool(b, 17, [128, 128], BF16, "XTb")
        AVp = Pool(b, NB, [128, 128], F32, "AVs")
        Wp = Pool(b, 9, [128, 128], F32, "Wt")
        NUp = Pool(b, 9, [128, 128], BF16, "negU")
        Yop = Pool(b, 6, [128, 128], F32, "Yo")
        H = [b.sb([128, 128], F32, "H") for _ in chains]
        Hb = [b.sb([128, 128], BF16, "Hb") for _ in chains]
        for i in range(NC):
            b.ms("dve", H[i][:], 0.0)
            b.ms("pool", Hb[i][:], 0.0)
        gam = [b.sb([128, NCH], F32, "gam") for _ in chains]
        if isB:
            ldp = Pool(b, 10, [128, 7, 64], F32, "ld7")
            F3p = Pool(b, 8, [128, 3, 128], F32, "F3")
            for i, (u, d) in enumerate(chains):
                b.dma("sp", gam[i][:], self.GAMB.ap()[d, u])
        else:
            self.gdn_unit_setup(l, gam)
            self.gdn_gam(gam, sm, chains)

        def pre_stages(step):
            units = []
            st = []

            def s_prep():
                for i, (u, d) in enumerate(chains):
                    c = order[d][step]
                    tk = c * CT
                    U = dict(i=i, u=u, d=d, c=c, tk=tk)
                    XX, TM3 = XXp.get(), TMp.get()
                    if isB:
                        ld = ldp.get()
                        b.dma("sp", ld[:], self.UB.ap()[d, u, :, :, tk:tk + 64])
                        bdb = bd2[:].unsqueeze(1).broadcast_to([128, 4, 2, 64])
                        b.tt("dve", XX[:].rearrange("p k (h t) -> p k h t", h=2),
                             ld[:, 0:4, :].unsqueeze(2).broadcast_to([128, 4, 2, 64]), bdb, ALU.mult)
                        F3 = F3p.get()
                        b.tt("pool", F3[:].rearrange("p k (h t) -> p k h t", h=2),
                             ld[:, 4:7, :].unsqueeze(2).broadcast_to([128, 3, 2, 64]),
                             bd2[:].unsqueeze(1).broadcast_to([128, 3, 2, 64]), ALU.mult)
                        tp = sm.get()
                        for k in range(3):
                            b.tr(tp[:, k * 128:(k + 1) * 128], F3[:, k, :], ident[:])
                        b.cp(("act", "dve")[i % 2], TM3[:].rearrange("p k t -> p (k t)"), tp[:, 0:384])
                        if d == 0:
                            WsT, WiT, Ws = rmask[:, 0, :], rmask[:, 1, :], rmask[:, 2, :]
                        else:
                            WsT, WiT, Ws = rmask[:, 2, :], rmask[:, 3, :], rmask[:, 0, :]
                        U.update(XkH=XX[:, 0, :], XrH=XX[:, 1, :])
                    else:
                        WsT, WiT, Ws, XH = self.gdn_unit_prep(l, U, XX, TM3, sm, rmask, ident)
                        U.update(XkH=XH[:, 0, :], XrH=XH[:, 1, :])
                    U.update(XX=XX, TM3=TM3, gcol=gam[i][:, c:c + 1], W=(WsT, WiT, Ws))
                    units.append(U)

            def s_gram():
                for U in units:
                    XX = U["XX"]
                    WsT, WiT, Ws = U["W"]
                    G = psG.get()
                    xkr = XX[:, 0:2, :].rearrange("p k t -> p (k t)")
                    b.mm(G[:, 0:256], XX[:, 2, :], xkr)
                    b.mm(G[:, 256:512], XX[:, 3, :], xkr)
                    g3 = sm.get()
                    b.mm(g3[:, 0:128], XX[:, 0, :], XX[:, 2, :])
                    Pm = Pmp.get()
                    b.stt(Pm[:, 1, :], G[:, 0:128], -1.0, WsT, ALU.mult, ALU.mult)
                    b.stt(Pm[:, 0, :], g3[:, 0:128], -1.0, Ws, ALU.mult, ALU.mult)
                    A3 = A3p.get()
                    b.tt("dve", A3[:, 0, :], G[:, 128:256], WiT, ALU.mult)
                    b.tt("dve", A3[:, 1, :], G[:, 256:384], WsT, ALU.mult)
                    b.tt("dve", A3[:, 2, :], G[:, 384:512], WiT, ALU.mult)
                    XT = XTp.get()
                    b.tt("pool", XT[:], Pm[:, 1, :], ident[:], ALU.add)
                    U.update(A3=A3, Pm=Pm, XT=XT)

            def mk_lev_a(lev):
                def f():
                    last = lev == nlev - 1
                    for U in units:
                        p2 = sm.get()
                        if not isB:
                            Pm = U["Pm"]
                            b.mm(p2[:, 0:128], Pm[:, 1, :], Pm[:, 0, :])
                            Pn = Pmp.get()
                            if not last:
                                b.mm(p2[:, 128:256], Pm[:, 0, :], Pm[:, 1, :])
                                b.cp(("act", "dve")[U["i"] % 2], Pn[:].rearrange("p k t -> p (k t)"), p2[:, 0:256])
                            else:
                                b.cp("act", Pn[:, 0, :], p2[:, 0:128])
                            U["Pn"] = Pn
                        elif lev == 0:
                            Pm = U["Pm"]
                            b.mm(p2[:, 0:128], Pm[:, 1, :], Pm[:, 0, :])
                            b.mm(p2[:, 128:256], Pm[:, 0, :], Pm[:, 1, :])
                            P2f = P2p.get()
                            Pb = Pbp.get()
                            eng = ("act", "dve")[U["i"] % 2]
                            b.cp(eng, P2f[:], p2[:, 0:128])
                            b.cp(eng, Pb[:].rearrange("p k t -> p (k t)"), p2[:, 0:256])
                            U["P2f"], U["Pb"] = P2f, Pb
                        else:
                            Pb = U["Pb"]
                            b.mm(p2[:, 0:128], Pb[:, 1, :], Pb[:, 0, :])
                            Pn = Pbp.get()
                            if not last:
                                b.mm(p2[:, 128:256], Pb[:, 0, :], Pb[:, 1, :])
                                b.cp(("act", "dve")[U["i"] % 2], Pn[:].rearrange("p k t -> p (k t)"), p2[:, 0:256])
                            else:
                                b.cp("act", Pn[:, 0, :], p2[:, 0:128])
                            U["Pb"] = Pn
                return f

            def mk_lev_b(lev):
                def f():
                    last = lev == nlev - 1
                    for U in units:
                        XT = U["XT"]
                        xu = sm.get()
                        if not isB:
                            b.mm(xu[:, 0:128], U["Pn"][:, 0, :], XT[:])
                            U["Pm"] = U["Pn"]
                        elif lev == 0:
                            b.mm(xu[:, 0:128], U["P2f"][:], XT[:])
                        else:
                            b.mm(xu[:, 0:128], U["Pb"][:, 0, :], U["XTb"][:])
                        XTn = (XTf if last else XTp).get()
                        b.tt("dve", XTn[:], xu[:, 0:128], XT[:], ALU.add)
                        if isB and not last:
                            XTb = XTbp.get()
                            b.cp("pool", XTb[:], XTn[:])
                            U["XTb"] = XTb
                        U["XT"] = XTn
                return f

            def s_av():
                for U in units:
                    av = sm.get()
                    b.mm(av[:, 0:128], U["A3"][:, 1, :], U["TM3"][:, 2, :])
                    AVs = AVp.get()
                    b.cp("act", AVs[:], av[:, 0:128])
                    U["AVs"] = AVs

            st.append(s_prep)
            st.append(s_gram)
            for lev in range(nlev):
                st.append(mk_lev_a(lev))
                st.append(mk_lev_b(lev))
            st.append(s_av)
            return st, units

        def seq_stages(units):
            def s5():
                for U in units:
                    kh = sm.get()
                    b.mm(kh[:, 0:128], U["XkH"], Hb[U["i"]][:])
                    Wt = Wp.get()
                    b.tt("dve", Wt[:], kh[:, 0:128], U["AVs"][:], ALU.add)
                    U["Wt"] = Wt

            def s6():
                for U in units:
                    uu = sm.get()
                    b.mm(uu[:, 0:128], U["XT"][:], U["Wt"][:])
                    nu = NUp.get()
                    b.act(nu[:], uu[:, 0:128], AF.Copy, scale=-1.0)
                    U["nu"] = nu

            def s7():
                for U in units:
                    i = U["i"]
                    Y = psA.get()
                    b.mm(Y[:, 0:128], U["XrH"], Hb[i][:], True, False)
                    b.mm(Y[:, 0:128], U["A3"][:, 0, :], U["nu"][:], False, False)
                    b.mm(Y[:, 0:128], U["A3"][:, 2, :], U["TM3"][:, 2, :], False, True)
                    Yo = Yop.get()
                    b.cp("act", Yo[:], Y[:, 0:128])
                    dst = self.ytm.ap()[U["d"]]
                    if isB:
                        for hh in range(2):
                            b.dma("pool", dst[U["tk"]:U["tk"] + 64, (2 * U["u"] + hh) * 64:(2 * U["u"] + hh + 1) * 64],
                                  Yo[hh * 64:(hh + 1) * 64, hh * 64:(hh + 1) * 64])
                    else:
                        b.dma("pool", dst[U["tk"]:U["tk"] + 128, U["u"] * 128:(U["u"] + 1) * 128], Yo[:])

            def s8():
                for U in units:
                    i = U["i"]
                    Hn = psA.get()
                    b.mm(Hn[:, 0:128], U["TM3"][:, 1, :], U["nu"][:], True, False)
                    b.mm(Hn[:, 0:128], U["TM3"][:, 0, :], U["TM3"][:, 2, :], False, True)
                    b.stt(H[i][:], H[i][:], U["gcol"], Hn[:, 0:128], ALU.mult, ALU.add)
                    b.cp("pool", Hb[i][:], H[i][:])
            return [s5, s6, s7, s8]

        st, units = pre_stages(0)
        for f in st:
            f()
        for step in range(NCH):
            seq = seq_stages(units)
            if step + 1 < NCH:
                pre, nunits = pre_stages(step + 1)
            else:
                pre, nunits = [], None
            npre = len(pre)
            marks = {((k + 1) * npre) // 5: k for k in range(4)} if npre else {}
            if self.no_inter:
                for k in range(4):
                    seq[k]()
                marks = {}
                done_all = True
            else:
                done_all = False
            done = set(range(4)) if done_all else set()
            for j, f in enumerate(pre):
                if j in marks and marks[j] not in done:
                    seq[marks[j]]()
                    done.add(marks[j])
                f()
            for k in range(4):
                if k not in done:
                    seq[k]()
                    done.add(k)
            units = nunits

    def phase_post(self, l, mi, need_ctx):
        b, I, NT = self.b, self.I, self.NT
        isB = (mi == 1)
        pT = self.pT.ap()
        og = SEG_OFF["Bg" if isB else "Dg"]
        psum = Pool(b, 3, [128, 512], F32, "ps", psum=True)
        ident = b.sb([128, 128], F32, "ident")
        b.dma("sp", ident[:], I["ident"].ap())
        vec = b.sb([128, 4, 4], F32, "pvec")
        b.dma("sp", vec[:], I["b_vec" if isB else "d_vec"].ap()[l])
        yp = Pool(b, 4, [128, 512], F32, "py")
        stp = Pool(b, 8, [128, 8], F32, "pst")
        gp = Pool(b, 4, [128, 128], F32, "pg")
        g4p = Pool(b, 2, [128, 4, 128], F32, "pg4")
        e4p = Pool(b, 2, [128, 4, 128], F32, "pe4")
        bp = Pool(b, 4, [128, 128], F32, "pb")
        op_ = Pool(b, 4, [128, 128], BF16, "po")
        NH, HD = (8, 64) if isB else (4, 128)
        for t0 in range(0 if need_ctx else CTX, NT, 128):
            y0, y1 = yp.get(), yp.get()
            b.dma("sp", y0[:], self.ytm.ap()[0, t0:t0 + 128, :])
            b.dma("sp", y1[:], self.ytm.ap()[1, t0:t0 + 128, :])
            b.tt("pool", y0[:], y0[:], y1[:], ALU.add)
            y3 = y0[:].rearrange("p (h d) -> p h d", h=NH)
            st = stp.get()
            if isB:
                b.I("dve", "tensor_reduce", out=st[:, 0:NH], in_=y3, axis=mybir.AxisListType.X, op=ALU.add)
                b.ts("dve", st[:, 0:NH], st[:, 0:NH], 1.0 / HD, ALU.mult)
                b.tt("dve", y3, y3, st[:, 0:NH].unsqueeze(2).broadcast_to([128, NH, HD]), ALU.subtract)
            sq = yp.get()
            b.tt("pool", sq[:], y0[:], y0[:], ALU.mult)
            s2 = stp.get()
            b.I("dve", "tensor_reduce", out=s2[:, 0:NH], in_=sq[:].rearrange("p (h d) -> p h d", h=NH),
                axis=mybir.AxisListType.X, op=ALU.add)
            b.act(s2[:, 0:NH], s2[:, 0:NH], AF.Sqrt, scale=1.0 / HD, bias=(64e-5 if isB else 1e-6))
            b.rcp(s2[:, 0:NH], s2[:, 0:NH])
            b.tt("dve", y3, y3, s2[:, 0:NH].unsqueeze(2).broadcast_to([128, NH, HD]), ALU.mult)
            tpb = psum.get()
            for ct in range(4):
                b.tr(tpb[:, ct * 128:(ct + 1) * 128], y0[:, ct * 128:(ct + 1) * 128], ident[:])
            g4, e4 = g4p.get(), e4p.get()
            b.dma("sp", g4[:], pT[og:og + 512, t0:t0 + 128].rearrange("(k p) t -> p k t", p=128))
            b.act(e4[:], g4[:], AF.Sigmoid)
            b.tt("pool", e4[:], e4[:], g4[:], ALU.mult)
            for ct in range(4):
                tp = _View(tpb.h, ct * 128, 128)
                e = _View3(e4.h, ct)
                o = op_.get()
                if isB:
                    bo = bp.get()
                    b.dma("sp", bo[:], self.bon.ap()[ct * 128:(ct + 1) * 128, t0:t0 + 128])
                    t = gp.get()
                    b.ts("dve", t[:], tp[:], vec[:, 2, ct:ct + 1], ALU.mult, vec[:, 3, ct:ct + 1], ALU.add)
                    b.tt("pool", t[:], t[:], bo[:], ALU.add)
                    b.tt("dve", o[:], t[:], e[:], ALU.mult)
                else:
                    b.stt(o[:], tp[:], vec[:, 0, ct:ct + 1], e[:], ALU.mult, ALU.mult)
                b.dma("pool", self.yT.ap()[mi, ct * 128:(ct + 1) * 128, t0:t0 + 128], o[:])


    def phase_d_prep(self, l):
        b, I, NT = self.b, self.I, self.NT
        pT = self.pT.ap()
        oq, oab = SEG_OFF["Dqkv"], SEG_OFF["Dab"]
        psum = Pool(b, 4, [128, 512], F32, "ps", psum=True)
        cw = b.sb([128, 12, 5], F32, "cw")
        b.dma("sp", cw[:], I["d_conv"].ap()[l])
        abp = b.sb([8, 4], F32, "abp")
        b.dma("sp", abp[:], I["d_ab"].ap()[l])
        nexpA = b.sb([8, 1], F32, "nexpA")
        b.act(nexpA[:], abp[:, 1:2], AF.Exp)
        b.ts("dve", nexpA[:], nexpA[:], -1.0, ALU.mult)
        onesF = b.sb([8, 512], F32, "onesF")
        b.ms("dve", onesF[:], 1.0)
        Sx = b.sb([8, 513], F32, "Sx")
        b.ms("dve", Sx[:, 0:1], 0.0)
        ptp = Pool(b, 3, [128, 516], F32, "dpt")
        tp = Pool(b, 6, [128, 512], F32, "dt")
        sqp = Pool(b, 2, [128, 512], BF16, "dsq")
        o3p = Pool(b, 2, [128, 3, 512], F32, "o3")
        rp = Pool(b, 12, [8, 512], F32, "dr")
        o6p = Pool(b, 2, [8, 6, 512], F32, "o6")
        egp = Pool(b, 2, [8, 4], F32, "deg")
        for (t0, T, is_ctx) in self.blocks:
            lo, hi = self.seg_bounds(t0)
            nch = T // 128
            for h in range(4):
                o3 = o3p.get()
                for kind in range(3):
                    i = kind * 4 + h
                    pt = ptp.get()
                    a0, a1 = max(t0 - 2, lo), min(t0 + T + 2, hi)
                    if a0 > t0 - 2:
                        b.ms("pool", pt[:, 0:2], 0.0)
                    if a1 < t0 + T + 2:
                        b.ms("pool", pt[:, T + 2:T + 4], 0.0)
                    b.dma("sp", pt[:, a0 - (t0 - 2):a1 - (t0 - 2)], pT[oq + i * 128:oq + (i + 1) * 128, a0:a1])
                    acc = tp.get()
                    b.ts("dve", acc[:, 0:T], pt[:, 0:T], cw[:, i, 0:1], ALU.mult)
                    for j in range(1, 5):
                        b.stt(acc[:, 0:T], pt[:, j:j + T], cw[:, i, j:j + 1], acc[:, 0:T], ALU.mult, ALU.add)
                    e = tp.get()
                    b.act(e[:, 0:T], acc[:, 0:T], AF.Sigmoid)
                    if kind == 2:
                        b.tt("pool", o3[:, 2, 0:T], acc[:, 0:T], e[:, 0:T], ALU.mult)
                        continue
                    b.tt("pool", acc[:, 0:T], acc[:, 0:T], e[:, 0:T], ALU.mult)
                    sq = sqp.get()
                    b.act(sq[:, 0:T], acc[:, 0:T], AF.Square)
                    ps = psum.get()
                    b.mm(ps[:, 0:T], self.ones_bf[:], sq[:, 0:T])
                    rn = tp.get()
                    b.act(rn[:, 0:T], ps[:, 0:T], AF.Sqrt, bias=1e-12)
                    b.rcp(rn[:, 0:T], rn[:, 0:T])
                    if kind == 0:
                        b.stt(o3[:, 0, 0:T], acc[:, 0:T], 128.0 ** -0.5, rn[:, 0:T], ALU.mult, ALU.mult)
                    else:
                        b.tt("dve", o3[:, 1, 0:T], acc[:, 0:T], rn[:, 0:T], ALU.mult)
                b.dma("pool", self.UD.ap()[h, :, :, t0:t0 + T], o3[:, :, 0:T])
            lgr, br = rp.get(), rp.get()
            for d in range(2):
                b.dma("sp", lgr[d * 4:(d + 1) * 4, 0:T], pT[oab + d * 8:oab + d * 8 + 4, t0:t0 + T])
                b.dma("sp", br[d * 4:(d + 1) * 4, 0:T], pT[oab + d * 8 + 4:oab + d * 8 + 8, t0:t0 + T])
            lg = rp.get()
            b.act(lg[:, 0:T], lgr[:, 0:T], AF.Exp, bias=abp[:, 0:1])
            b.act(lg[:, 0:T], lg[:, 0:T], AF.Ln, bias=1.0)
            b.ts("dve", lg[:, 0:T], lg[:, 0:T], nexpA[:, 0:1], ALU.mult)
            beta = rp.get()
            b.act(beta[:, 0:T], br[:, 0:T], AF.Sigmoid)
            b.I("dve", "tensor_tensor_scan", out=Sx[:, 1:T + 1], data0=onesF[:, 0:T], data1=lg[:, 0:T],
                initial=0.0, op0=ALU.mult, op1=ALU.add)
            v3 = lambda ap: ap.rearrange("p (c t) -> p c t", t=128)
            gc = rp.get()
            b.tt("dve", v3(gc[:, 0:T]), v3(Sx[:, 1:T + 1]), v3(Sx[:, 0:T])[:, :, 0:1].broadcast_to([8, nch, 128]), ALU.subtract)
            gtot_bc = v3(gc[:, 0:T])[:, :, 127:128].broadcast_to([8, nch, 128])
            o6 = o6p.get()
            t1 = rp.get()
            b.ts("dve", t1[:, 0:T], gc[:, 0:T], abp[:, 2:3], ALU.mult)
            b.stt(v3(t1[:, 0:T]), gtot_bc, abp[:, 3:4], v3(t1[:, 0:T]), ALU.mult, ALU.add)
            b.stt(o6[:, 3, 0:T], lg[:, 0:T], abp[:, 3:4], t1[:, 0:T], ALU.mult, ALU.add)
            b.tt("dve", o6[:, 2, 0:T], o6[:, 3, 0:T], lg[:, 0:T], ALU.subtract)
            elg = rp.get()
            b.act(elg[:, 0:T], lg[:, 0:T], AF.Exp)
            b.tt("dve", o6[:, 0, 0:T], beta[:, 0:T], elg[:, 0:T], ALU.mult)
            b.cp("dve", o6[:, 1, 0:T], beta[:, 0:T])
            t2 = rp.get()
            b.tt("dve", v3(t2[:, 0:T]), gtot_bc, v3(o6[:, 3, 0:T]), ALU.subtract)
            b.act(t2[:, 0:T], t2[:, 0:T], AF.Exp)
            b.tt("dve", o6[:, 4, 0:T], beta[:, 0:T], t2[:, 0:T], ALU.mult)
            b.tt("dve", o6[:, 5, 0:T], o6[:, 4, 0:T], elg[:, 0:T], ALU.mult)
            eg = egp.get()
            b.act(eg[:, 0:nch], v3(gc[:, 0:T])[:, :, 127], AF.Exp)
            b.dma("pool", self.RD.ap()[:, :, t0:t0 + T], o6[:, :, 0:T])
            b.dma("pool", self.EGTD.ap()[:, t0 // 128:t0 // 128 + nch], eg[:, 0:nch])

    def gdn_unit_setup(self, l, gam):
        b, I, NT = self.b, self.I, self.NT
        NCH = NT // 128
        sel8 = b.sb([8, 8, 128], F32, "sel8")
        b.dma("sp", sel8[:], I["sel8"].ap())
        egt = b.sb([8, NCH + (NCH % 2)], F32, "egtall")
        b.ms("dve", egt[:], 0.0)
        b.dma("sp", egt[:, 0:NCH], self.EGTD.ap())
        self._gd = dict(
            sel8=sel8,
            ld3=Pool(b, 12, [128, 3, 128], F32, "ld3"),
            rows=Pool(b, 4, [8, 6, 128], F32, "grow"),
            cols=Pool(b, 4, [128, 4, 8], F32, "gcol"),
            Wt=Pool(b, 12, [128, 3, 128], F32, "gW"),
            xa=Pool(b, 6, [128, 128], F32, "gxa"),
            eb=Pool(b, 4, [128, 256], F32, "geb"),
            bcS=Pool(b, 10, [128, 4, 128], F32, "gbc"),
            XH=Pool(b, 17, [128, 2, 128], BF16, "gXH"),
            cache={},
        )
        self._gd_egt = egt

    def gdn_gam(self, gam, sm, chains):
        b = self.b
        NCH = self.NT // 128
        n2 = NCH + (NCH % 2)
        for i, (u, d) in enumerate(chains):
            p = sm.get()
            b.mm(p[:, 0:n2], self._gd["sel8"][:, d * 4 + u, :], self._gd_egt[:, 0:n2])
            b.cp("act", gam[i][:], p[:, 0:NCH])

    def gdn_unit_prep(self, l, U, XX, TM3, sm, gmask, ident):
        b = self.b
        G = self._gd
        h, d, c, tk = U["u"], U["d"], U["c"], U["tk"]
        r = d * 4 + h
        key = (c, d)
        if key not in G["cache"]:
            rows = G["rows"].get()
            b.dma("sp", rows[:], self.RD.ap()[:, :, tk:tk + 128])
            tp = sm.get()
            for j, kind in enumerate((3, 2, 4, 5)):
                b.tr(tp[:, j * 8:(j + 1) * 8], rows[:, kind, :], ident[0:8, 0:8])
            cols = G["cols"].get()
            b.cp("act", cols[:].rearrange("p k r -> p (k r)"), tp[:, 0:32])
            G["cache"] = {kk: vv for kk, vv in G["cache"].items() if kk[1] != d}
            G["cache"][key] = (rows, cols)
        rows, cols = G["cache"][key]
        ld = G["ld3"].get()
        b.dma("sp", ld[:], self.UD.ap()[h, :, :, tk:tk + 128])
        bc = G["bcS"].get()
        b.dma("sp", bc[:], self.RD.ap()[r:r + 1, 0:4, tk:tk + 128].broadcast_to([128, 4, 128]))
        b.cp("act", XX[:, 0, :], ld[:, 1, :])
        b.cp("pool", XX[:, 1, :], ld[:, 0, :])
        b.tt("dve", XX[:, 2, :], bc[:, 0, :], ld[:, 1, :], ALU.mult)
        b.tt("pool", XX[:, 3, :], bc[:, 1, :], ld[:, 1, :], ALU.mult)
        tp = sm.get()
        b.tr(tp[:, 0:128], ld[:, 1, :], ident[:])
        b.tr(tp[:, 128:256], ld[:, 2, :], ident[:])
        b.ts("dve", TM3[:, 0, :], tp[:, 0:128], cols[:, 2, r:r + 1], ALU.mult)
        b.ts("dve", TM3[:, 1, :], tp[:, 0:128], cols[:, 3, r:r + 1], ALU.mult)
        b.cp("dve", TM3[:, 2, :], tp[:, 128:256])
        Wt = G["Wt"].get()
        mk = (0, 1, 2) if d == 0 else (2, 3, 0)
        xa = G["xa"].get()
        b.stt(xa[:], bc[:, 2, :], cols[:, 0, r:r + 1], gmask[:, mk[0], :], ALU.subtract, ALU.add)
        b.act(Wt[:, 0, :], xa[:], AF.Exp)
        xb = G["xa"].get()
        b.stt(xb[:], bc[:, 3, :], cols[:, 0, r:r + 1], gmask[:, mk[1], :], ALU.subtract, ALU.add)
        b.act(Wt[:, 1, :], xb[:], AF.Exp)
        xc = G["xa"].get()
        b.stt(xc[:], bc[:, 3, :], cols[:, 1, r:r + 1], gmask[:, mk[2], :], ALU.subtract, ALU.subtract)
        b.act(Wt[:, 2, :], xc[:], AF.Exp, scale=-1.0)
        eb = G["eb"].get()
        b.act(eb[:], bc[:, 2:4, :].rearrange("p k t -> p (k t)"), AF.Exp)
        XH = G["XH"].get()
        b.tt("dve", XH[:, 0, :], eb[:, 0:128], ld[:, 1, :], ALU.mult)
        b.tt("pool", XH[:, 1, :], eb[:, 128:256], ld[:, 0, :], ALU.mult)
        return Wt[:, 0, :], Wt[:, 1, :], Wt[:, 2, :], XH

    def phase_mcast(self, l):
        b, I = self.b, self.I
        fa = Pool(b, 2, [128, 24, 128], F32, "mcf")
        ba = Pool(b, 2, [128, 24, 128], BF16, "mcb")
        for dt in range(16):
            f, g = fa.get(), ba.get()
            b.dma("sp", f[:], I["wm"].ap()[l, dt])
            b.cp(("dve", "pool")[dt % 2], g[:], f[:])
            b.dma("sp", self.wmb.ap()[dt], g[:])
            f, g = fa.get(), ba.get()
            b.dma("sp", f[:, 0:16, :], I["wo"].ap()[l, dt])
            b.cp(("pool", "dve")[dt % 2], g[:, 0:16, :], f[:, 0:16, :])
            b.dma("sp", self.wob.ap()[dt], g[:, 0:16, :])

    def phase_merge(self, l, need_ctx):
        b, I, NT = self.b, self.I, self.NT
        mv = self.modv
        pT = self.pT.ap()
        opm = SEG_OFF["pm"]
        psG = Pool(b, 2, [128, 512], F32, "psg", psum=True)
        psB = Pool(b, 3, [128, 512], F32, "psb", psum=True)
        psO = Pool(b, 2, [128, 512], F32, "pso", psum=True)
        gb = b.sb([128, 4, 16], F32, "gb")
        b.dma("sp", gb[:], I["g_b"].ap()[l])
        pmf = Pool(b, 1, [128, 2, 512], F32, "pmf")
        pmb = Pool(b, 2, [128, 2, 512], BF16, "pmb")
        ybp = Pool(b, 2, [128, 16, 512], BF16, "yb")
        accT = Pool(b, 2, [128, 16, 512], BF16, "accT")
        wmp = Pool(b, 2, [128, 24, 128], BF16, "wmt")
        wop = Pool(b, 2, [128, 16, 128], BF16, "wot")
        gtp = Pool(b, 3, [128, 512], F32, "mg")
        acp = Pool(b, 2, [128, 512], F32, "macc")
        tmp = Pool(b, 3, [128, 512], F32, "mtmp")
        xtp = Pool(b, 3, [128, 512], F32, "mx")
        src_x = I["xT"] if l == 0 else self.xs
        for (t0, T, is_ctx) in self.blocks:
            if is_ctx and not need_ctx:
                continue
            v = 1 if is_ctx else 0
            pf, pb = pmf.get(), pmb.get()
            b.dma("sp", pf[:, :, 0:T], pT[opm:opm + 256, t0:t0 + T].rearrange("(k p) t -> p k t", p=128))
            b.cp("pool", pb[:, :, 0:T], pf[:, :, 0:T])
            yb = ybp.get()
            for mi in range(4):
                b.dma("sp", yb[:, mi * 4:(mi + 1) * 4, 0:T], self.yT.ap()[mi, :, t0:t0 + T].rearrange("(k p) t -> p k t", p=128))
            aT = accT.get()
            for dt in range(16):
                wt = wmp.get()
                b.dma("sp", wt[:], self.wmb.ap()[dt])
                acc = acp.get()
                for i in range(4):
                    pg = psG.get()
                    for rc in range(2):
                        b.mm(pg[:, 0:T], wt[:, i * 2 + rc, :], pb[:, rc, 0:T], rc == 0, rc == 1)
                    gt = gtp.get()
                    b.act(gt[:, 0:T], pg[:, 0:T], AF.Sigmoid, bias=gb[:, i, dt:dt + 1])
                    pbr = psB.get()
                    for cc in range(4):
                        b.mm(pbr[:, 0:T], wt[:, 8 + i * 4 + cc, :], yb[:, i * 4 + cc, 0:T], cc == 0, cc == 3)
                    if i == 0:
                        b.tt("dve", acc[:, 0:T], pbr[:, 0:T], gt[:, 0:T], ALU.mult)
                    else:
                        tm = tmp.get()
                        b.tt("dve", tm[:, 0:T], pbr[:, 0:T], gt[:, 0:T], ALU.mult)
                        b.tt("pool", acc[:, 0:T], acc[:, 0:T], tm[:, 0:T], ALU.add)
                b.cp("act", aT[:, dt, 0:T], acc[:, 0:T])
            for dt in range(16):
                wo = wop.get()
                b.dma("sp", wo[:], self.wob.ap()[dt])
                po = psO.get()
                for k in range(16):
                    b.mm(po[:, 0:T], wo[:, k, :], aT[:, k, 0:T], k == 0, k == 15)
                xt = xtp.get()
                b.dma("sp", xt[:, 0:T], src_x.ap()[dt * 128:(dt + 1) * 128, t0:t0 + T])
                b.stt(xt[:, 0:T], po[:, 0:T], mv[:, l, dt, 3 * v + 2:3 * v + 3], xt[:, 0:T], ALU.mult, ALU.add)
                b.dma("pool", self.xs.ap()[dt * 128:(dt + 1) * 128, t0:t0 + T], xt[:, 0:T])

    def phase_final(self):
        b, I = self.b, self.I
        psum = Pool(b, 2, [128, 512], F32, "ps", psum=True)
        fg = b.sb([128, KC], F32, "fg")
        b.dma("sp", fg[:], I["final_g"].ap())
        Pxt = Pool(b, 2, [128, KC, 512], F32, "xt")
        Psq = Pool(b, 1, [128, KC, 512], BF16, "sq")
        Prs = Pool(b, 2, [128, 512], F32, "rs")
        for (t0, T, is_ctx) in self.blocks:
            if is_ctx:
                continue
            xt = Pxt.get()
            for k in range(KC):
                b.dma("sp", xt[:, k, 0:T], self.xs.ap()[k * 128:(k + 1) * 128, t0:t0 + T])
            sq = Psq.get()
            b.act(sq[:, :, 0:T], xt[:, :, 0:T], AF.Square)
            ps = psum.get()
            for k in range(KC):
                b.mm(ps[:, 0:T], self.ones_bf[:], sq[:, k, 0:T], k == 0, k == KC - 1)
            rs = Prs.get()
            b.act(rs[:, 0:T], ps[:, 0:T], AF.Sqrt, scale=1.0 / D_MODEL, bias=1e-6)
            b.rcp(rs[:, 0:T], rs[:, 0:T])
            for k in range(KC):
                b.stt(xt[:, k, 0:T], xt[:, k, 0:T], fg[:, k:k + 1], rs[:, 0:T], ALU.mult, ALU.mult)
                b.dma("pool", self.out.ap()[k * 128:(k + 1) * 128, t0 - CTX:t0 - CTX + T], xt[:, k, 0:T])


def _fm(v):
    v = np.asarray(v)
    c = v.shape[-1]
    return np.ascontiguousarray(np.swapaxes(v.reshape(v.shape[:-1] + (c // 128, 128)), -1, -2))


def host_inputs(inp, bi, n_lat, depth):
    L = depth
    d = {}
    xcat = np.concatenate([inp["ctx"][bi], inp["x"][bi][:n_lat]], axis=0)
    d["xT"] = np.ascontiguousarray(xcat.T)
    d["cc"] = np.ascontiguousarray(np.stack([_fm(inp["c"][bi]), _fm(inp["c_ctx"])], axis=-1))
    d["norm_g"] = _fm(inp["norm_g"][:L])
    d["w_mod"] = np.ascontiguousarray(inp["w_mod"][:L])
    d["b_mod"] = _fm(inp["b_mod"][:L])
    cols = w_in_columns()
    w = inp["w_in"][:L][:, :, cols]
    w = w.reshape(L, KC, 128, NCT, 128).transpose(0, 3, 2, 1, 4)
    d["w_in"] = np.ascontiguousarray(w)
    d["final_g"] = _fm(inp["final_g"])
    NT = CTX + n_lat
    p = np.arange(128)
    dd = p % 64
    half, r = dd // 32, dd % 32
    inv = 10000.0 ** (-np.arange(0, 32, 2, dtype=np.float32) / np.float32(32))
    f = inv[r % 16].astype(np.float32)
    t = np.arange(n_lat)
    pos = np.where(half[:, None] == 0, (t // 64)[None, :], (t % 64)[None, :]).astype(np.float32)
    ang = (pos * f[:, None]).astype(np.float32)
    sign = np.where(r < 16, -1.0, 1.0).astype(np.float32)
    rc = np.ones((128, NT), np.float32)
    rsn = np.zeros((128, NT), np.float32)
    rc[:, CTX:] = np.cos(ang)
    rsn[:, CTX:] = np.sin(ang) * sign[:, None]
    d["ropec"], d["ropes"] = rc, rsn
    j = np.arange(128)[:, None]
    i = np.arange(128)[None, :]
    d["m3"] = np.concatenate([(j <= i), np.ones((128, 128), bool), (i <= j)], axis=1).astype(np.float32)
    d["ident"] = np.eye(128, dtype=np.float32)
    d["a_sink"] = np.ascontiguousarray(np.broadcast_to(inp["a_sink"][:L, None, :], (L, 128, 8)))
    pm = _perm64()
    qn, kn = inp["c_qn"][:L], inp["c_kn"][:L]
    cq = np.stack([qn, qn[:, pm], kn, kn[:, pm]], axis=-1)
    d["c_qk"] = np.ascontiguousarray(np.concatenate([cq, cq], axis=1))
    pp = np.arange(128)
    d["bd2"] = np.ascontiguousarray(np.broadcast_to((pp[:, None] // 64 == np.arange(2)[None, :])[:, :, None], (128, 2, 64))).astype(np.float32)
    row, col = pp[:, None], pp[None, :]
    same = (row // 64) == (col // 64)
    tri = np.stack([row < col, row <= col, row > col, row >= col], axis=1)
    d["rmask"] = (tri & same[:, None, :]).astype(np.float32)
    d["gmask"] = np.where(tri, 0.0, -1.0e4).astype(np.float32)
    d["b_mu"] = np.ascontiguousarray(_fm(inp["b_mu"][:L]).transpose(0, 2, 3, 1))
    w0 = _fm(inp["b_w0"][:L]).transpose(0, 2, 1, 3)
    a0 = _fm(inp["b_a0"][:L]).transpose(0, 2, 1, 3)
    d["b_w0a0"] = np.ascontiguousarray(np.stack([w0, a0], axis=2))
    d["b_aw"] = np.ascontiguousarray(np.concatenate([inp["b_wup"][:L], inp["b_aup"][:L]], axis=2).transpose(0, 2, 1, 3))
    d["b_vec"] = np.ascontiguousarray(np.stack([_fm(inp[k][:L]) for k in ("b_kk", "b_ka", "b_lng", "b_lnb")], axis=2))
    rk = inp["b_rk"][:L]
    blk = np.zeros((L, 128, 4, 128), np.float32)
    for pr in range(4):
        for hh in range(2):
            blk[:, hh * 64:(hh + 1) * 64, pr, hh * 64:(hh + 1) * 64] = rk[:, 2 * pr + hh, :, None]
    d["b_rkblk"] = blk
    sel = np.zeros((8, 8, 128), np.float32)
    for r_ in range(8):
        sel[r_, r_, :] = 1.0
    d["sel8"] = sel
    d["d_conv"] = np.ascontiguousarray(_fm(inp["d_conv"][:L]).transpose(0, 2, 3, 1))
    ab = np.zeros((L, 8, 4), np.float32)
    ab[:, :, 0] = inp["d_dtb"][:L].reshape(L, 8)
    ab[:, :, 1] = inp["d_alog"][:L].reshape(L, 8)
    ab[:, 0:4, 2], ab[:, 4:8, 2] = 1.0, -1.0
    ab[:, 4:8, 3] = 1.0
    d["d_ab"] = ab
    dv = np.zeros((L, 128, 4, 4), np.float32)
    dv[:, :, 0, :] = inp["d_norm"][:L][:, :, None]
    d["d_vec"] = dv
    gu = inp["g_up"][:L].reshape(L, 4, 2, 128, 16, 128)
    wb = inp["w_br"][:L].reshape(L, 4, 4, 128, 16, 128)
    wm = np.concatenate([gu.transpose(0, 4, 3, 1, 2, 5).reshape(L, 16, 128, 8, 128),
                         wb.transpose(0, 4, 3, 1, 2, 5).reshape(L, 16, 128, 16, 128)], axis=3)
    d["wm"] = np.ascontiguousarray(wm)
    wo = inp["w_out"][:L].reshape(L, 16, 128, 16, 128)
    d["wo"] = np.ascontiguousarray(wo.transpose(0, 3, 2, 1, 4))
    d["g_b"] = np.ascontiguousarray(_fm(inp["g_b"][:L]).transpose(0, 2, 1, 3))
    return d


N_CORES = 8
_PROG_CACHE = {}


def run_model(inp, n_lat, depth):
    inp = {k: np.asarray(v) for k, v in inp.items()}
    B = inp["x"].shape[0]
    key = (n_lat, depth)
    if key not in _PROG_CACHE:
        _PROG_CACHE[key] = Prog(n_lat, depth).build()
    nc = _PROG_CACHE[key]
    per_b = [host_inputs(inp, bi, n_lat, depth) for bi in range(B)]
    in_maps = [per_b[i % B] for i in range(N_CORES)]
    res = run_bass_kernel_spmd(nc, in_maps, core_ids=list(range(N_CORES)))
    out = np.stack([np.ascontiguousarray(res.results[bi]["outT"].T) for bi in range(B)], axis=0)
    return out.astype(np.float32)


def kernel(**inputs):
    return run_model(inputs, 8192, 4)
```

```python
import math
from contextlib import ExitStack

import numpy as np
import concourse.bass as bass
import concourse.mybir as mybir
from concourse.bass_utils import run_bass_kernel_spmd

F32 = mybir.dt.float32
BF16 = mybir.dt.bfloat16
AF = mybir.ActivationFunctionType
ALU = mybir.AluOpType

D_MODEL = 2048
CTX = 256
W = 512
KC = D_MODEL // 128

OFF_A, OFF_B, OFF_C, OFF_D, OFF_G = 0, 1280, 3456, 4736, 6800
SEGS = [
    ("Aq", OFF_A + 0, 512, False), ("Aqp", OFF_A + 0, 512, True),
    ("Ak", OFF_A + 512, 128, False), ("Akp", OFF_A + 512, 128, True),
    ("Ag", OFF_A + 768, 512, False),
    ("Bz", OFF_B + 0, 1664, False), ("Bg", OFF_B + 1664, 512, False),
    ("Cq", OFF_C + 0, 512, False), ("Cqp", OFF_C + 0, 512, True),
    ("Ck", OFF_C + 512, 128, False), ("Ckp", OFF_C + 512, 128, True),
    ("Cg", OFF_C + 768, 512, False),
    ("Dqkv", OFF_D + 0, 1536, False), ("Dg", OFF_D + 1552, 512, False),
    ("pm", OFF_G, 256, False),
    ("Dab", OFF_D + 1536, 16, False),
]
SEG_OFF = {}
_o = 0
for _n, _s, _c, _p in SEGS:
    SEG_OFF[_n] = _o
    _o += _c
NFM = _o
NFM_PAD = 8192
VSEGS = [("Av", OFF_A + 640, 128), ("Cv", OFF_C + 640, 128)]
NCT = NFM_PAD // 128 + len(VSEGS)


def _perm64():
    p = np.arange(64)
    blk, r = p // 32, p % 32
    return blk * 32 + (r + 16) % 32


def w_in_columns():
    cols = []
    for n, s, c, perm in SEGS:
        idx = np.arange(s, s + c)
        if perm:
            idx = idx.reshape(-1, 64)[:, _perm64()].reshape(-1)
        cols.append(idx)
    cols = np.concatenate(cols)
    pad = np.zeros(NFM_PAD - NFM, dtype=np.int64)
    vcols = np.concatenate([np.arange(s, s + c) for _, s, c in VSEGS])
    return np.concatenate([cols, pad, vcols])


class Tile:
    __slots__ = ("h", "w", "r", "name")

    def __init__(self, h, name):
        self.h = h
        self.w = None
        self.r = []
        self.name = name

    def __getitem__(self, idx):
        return self.h[idx]


_WRITE_KW = ("out", "ap", "accum_out")


class _RowSplit:
    def __init__(self, a, b, half):
        self.a, self.b, self.half = a, b, half

    def ap(self):
        return self

    def __getitem__(self, idx):
        r, c = idx
        if r.start >= self.half:
            return self.b.ap()[r.start - self.half:r.stop - self.half, c]
        assert r.stop <= self.half
        return self.a.ap()[r, c]


class _View3:
    def __init__(self, h, k):
        self.h, self.k = h, k

    def __getitem__(self, idx):
        return self.h[:, self.k, :]


class _View:
    def __init__(self, h, lo, w):
        self.h, self.lo, self.w = h, lo, w

    def __getitem__(self, idx):
        if not isinstance(idx, tuple):
            idx = (idx, slice(None))
        p, c = idx[0], idx[1]
        a, bnd, _ = c.indices(self.w)
        return self.h[p, self.lo + a:self.lo + bnd]


class Builder:
    ENGS = ("pe", "dve", "act", "pool", "sp")

    def __init__(self, nc, es):
        self.nc = nc
        self.es = es
        self.ops = {e: [] for e in self.ENGS}
        self.cnt = {e: 0 for e in self.ENGS}
        self.sem = {}
        for e in ("pe", "dve", "act", "pool"):
            self.sem[e] = es.enter_context(nc.semaphore("s_" + e))
        self.waited = {e: {} for e in self.ENGS}
        NQ = 12
        self.dsem = {}
        self.dnext = {}
        for q in ("sp", "pool", "act"):
            self.dsem[q] = [es.enter_context(nc.semaphore("d_%s%d" % (q, i))) for i in range(NQ)]
            self.dnext[q] = 0
        self.dcum = {}
        self.n_tiles = 0
        self.reg = {}

    def scope(self):
        b = self

        class _S:
            def __enter__(s2):
                s2.old = b.es
                s2.st = ExitStack()
                b.es = s2.st
                return s2

            def __exit__(s2, *a):
                b.barrier()
                b.es = s2.old
                s2.st.close()
                return False
        return _S()

    def sb(self, shape, dtype, name=None, psum=False):
        self.n_tiles += 1
        name = "%s_%d" % (name or "t", self.n_tiles)
        mk = self.nc.psum_tensor if psum else self.nc.sbuf_tensor
        h = self.es.enter_context(mk(name, list(shape), dtype))
        t = Tile(h, name)
        self.reg[h.name] = t
        return t

    def ps(self, shape, dtype=F32, name=None):
        return self.sb(shape, dtype, name, psum=True)

    def _need(self, eng, rec, waits):
        if rec is None:
            return
        if rec[0] == "E":
            key, idx = rec[1], rec[2]
            if key == eng and eng == "pe":
                return
        else:
            key, idx = rec[1], rec[2]
        if self.waited[eng].get(key, 0) >= idx:
            return
        self.waited[eng][key] = idx
        waits.append((key, idx))

    def _deps(self, eng, reads, writes):
        waits = []
        for t in reads:
            self._need(eng, t.w, waits)
        for t in writes:
            self._need(eng, t.w, waits)
            for r in t.r:
                self._need(eng, r, waits)
        return waits

    def _split(self, kw):
        reads, writes = [], []
        for k, v in kw.items():
            if hasattr(v, "tensor") and hasattr(v, "ap"):
                t = self.reg.get(v.tensor.name)
                if isinstance(t, tuple):
                    t = t[1][(v.offset % 512) // t[0]]
                if t is not None:
                    (writes if k in _WRITE_KW else reads).append(t)
        return reads, writes

    def subtiles(self, tile, width):
        subs = []
        for j in range(512 // width):
            st = Tile(_View(tile.h, j * width, width), "%s_s%d" % (tile.name, j))
            subs.append(st)
        self.reg[tile.h.name] = (width, subs)
        return subs

    def _mark(self, rec, reads, writes):
        for t in reads:
            t.r.append(rec)
        for t in writes:
            t.w = rec
            t.r = []

    def I(self, eng, method, **kw):
        reads, writes = self._split(kw)
        waits = self._deps(eng, reads, writes)
        self.cnt[eng] += 1
        idx = self.cnt[eng]
        self.ops[eng].append((waits, method, kw, None))
        self._mark(("E", eng, idx), reads, writes)

    def dma(self, q, out, in_, **kw):
        reads, writes = self._split(dict(out=out, in_=in_))
        waits = self._deps(q, reads, writes)
        i = self.dnext[q]
        self.dnext[q] = (i + 1) % len(self.dsem[q])
        key = (q, i)
        prev = self.dcum.get(key, 0)
        if prev:
            self._need(q, ("D", key, prev), waits)
        val = prev + 16
        self.dcum[key] = val
        kw = dict(kw, out=out, in_=in_)
        self.ops[q].append((waits, "dma_start", kw, self.dsem[q][i]))
        self._mark(("D", key, val), reads, writes)

    def barrier(self):
        for e in self.ENGS:
            waits = []
            for x in ("pe", "dve", "act", "pool"):
                if self.cnt[x] and not (x == e and e == "pe"):
                    self._need(e, ("E", x, self.cnt[x]), waits)
            for key, val in self.dcum.items():
                self._need(e, ("D", key, val), waits)
            if waits:
                self.ops[e].append((waits, None, None, None))

    def _semof(self, key):
        if isinstance(key, tuple):
            return self.dsem[key[0]][key[1]]
        return self.sem[key]

    def emit(self):
        nc = self.nc
        with nc.Block() as block:
            def mk(ename):
                def body(e):
                    own = self.sem.get(ename)
                    for waits, method, kw, dsem in self.ops[ename]:
                        for key, val in waits:
                            e.wait_ge(self._semof(key), val)
                        if method is None:
                            continue
                        ins = getattr(e, method)(**kw)
                        if dsem is not None:
                            ins.then_inc(dsem, 16)
                        else:
                            ins.then_inc(own, 1)
                return body
            block.tensor(mk("pe"))
            block.vector(mk("dve"))
            block.scalar(mk("act"))
            block.gpsimd(mk("pool"))
            block.sync(mk("sp"))

    def mm(self, out, lhsT, rhs, start=True, stop=True):
        self.I("pe", "matmul", out=out, lhsT=lhsT, rhs=rhs, start=start, stop=stop)

    def tr(self, out, in_, identity):
        self.I("pe", "transpose", out=out, in_=in_, identity=identity)

    def act(self, out, in_, func, scale=1.0, bias=0.0):
        self.I("act", "activation", out=out, in_=in_, func=func, scale=scale, bias=bias)

    def tt(self, eng, out, in0, in1, op):
        self.I(eng, "tensor_tensor", out=out, in0=in0, in1=in1, op=op)

    def ts(self, eng, out, in0, s1, op0, s2=None, op1=None):
        if op1 is None:
            self.I(eng, "tensor_scalar", out=out, in0=in0, scalar1=s1, scalar2=None, op0=op0)
        else:
            self.I(eng, "tensor_scalar", out=out, in0=in0, scalar1=s1, scalar2=s2, op0=op0, op1=op1)

    def stt(self, out, in0, scalar, in1, op0, op1):
        self.I("dve", "scalar_tensor_tensor", out=out, in0=in0, scalar=scalar, in1=in1, op0=op0, op1=op1)

    def cp(self, eng, out, in_):
        if eng == "act":
            self.I("act", "activation", out=out, in_=in_, func=AF.Copy)
        else:
            self.I(eng, "tensor_copy", out=out, in_=in_)

    def rcp(self, out, in_):
        self.I("dve", "reciprocal", out=out, in_=in_)

    def ms(self, eng, ap, val):
        self.I(eng, "memset", ap=ap, constant=val)


class Pool:
    def __init__(self, b, n, shape, dtype, name, psum=False):
        self.t = [b.sb(shape, dtype, name, psum=psum) for _ in range(n)]
        self.i = 0

    def get(self):
        t = self.t[self.i]
        self.i = (self.i + 1) % len(self.t)
        return t


class Prog:
    def __init__(self, n_lat, depth, debug=(), mixers=(0, 1, 2, 3)):
        self.n = n_lat
        self.NT = CTX + n_lat
        self.depth = depth
        self.debug = set(debug)
        self.mixers = mixers
        self.stages = ("prep", "units", "post")
        self.cut = 0
        self.do_merge = True
        self.no_inter = False
        self.blocks = [(0, CTX, True)]
        t = CTX
        while t < self.NT:
            s = min(512, self.NT - t)
            self.blocks.append((t, s, False))
            t += s

    def build(self):
        nc = bass.Bass("TRN2", target_bir_lowering=False)
        self.nc = nc
        L, NT = self.depth, self.NT
        with ExitStack() as es:
            b = Builder(nc, es)
            self.b = b
            dk = lambda name: ("ExternalOutput" if name in self.debug else "Internal")
            I = {}

            def inp(name, shape, dt=F32):
                I[name] = nc.dram_tensor(name, list(shape), dt, kind="ExternalInput")
            inp("xT", [D_MODEL, NT])
            inp("cc", [128, KC, 2])
            inp("norm_g", [L, 128, KC])
            inp("w_mod", [L, D_MODEL, 3 * D_MODEL])
            inp("b_mod", [L, 128, 48])
            inp("w_in", [L, NCT, 128, KC, 128])
            inp("final_g", [128, KC])
            inp("ropec", [128, NT])
            inp("ropes", [128, NT])
            inp("m3", [128, 384])
            inp("ident", [128, 128])
            inp("a_sink", [L, 128, 8])
            inp("c_qk", [L, 128, 4])
            inp("bd2", [128, 2, 64])
            inp("rmask", [128, 4, 128])
            inp("b_mu", [L, 128, 13, 2])
            inp("b_w0a0", [L, 128, 2, 2, 4])
            inp("b_aw", [L, 128, 2, 512])
            inp("b_vec", [L, 128, 4, 4])
            inp("b_rkblk", [L, 128, 4, 128])
            inp("gmask", [128, 4, 128])
            inp("sel8", [8, 8, 128])
            inp("d_conv", [L, 128, 12, 5])
            inp("d_ab", [L, 8, 4])
            inp("d_vec", [L, 128, 4, 4])
            inp("wm", [L, 16, 128, 24, 128])
            inp("wo", [L, 16, 128, 16, 128])
            inp("g_b", [L, 128, 4, 16])
            self.I = I
            self.out = nc.dram_tensor("outT", [D_MODEL, self.n], F32, kind="ExternalOutput")
            self.xs = nc.dram_tensor("xs", [D_MODEL, NT], F32, kind=dk("xs"))
            self.pT = _RowSplit(nc.dram_tensor("pTa", [NFM_PAD // 2, NT], F32, kind=dk("pTa")),
                                nc.dram_tensor("pTb", [NFM_PAD // 2, NT], F32, kind=dk("pTb")), NFM_PAD // 2)
            self.vtm = nc.dram_tensor("vtm", [2, NT, 128], BF16, kind=dk("vtm"))
            self.wbf = nc.dram_tensor("wbf", [NCT, 128, KC, 128], BF16, kind="Internal")
            self.yT = nc.dram_tensor("yT", [4, W, NT], BF16, kind=dk("yT"))
            self.UB = nc.dram_tensor("UB", [2, 4, 128, 7, NT], F32, kind=dk("UB"))
            self.GAMB = nc.dram_tensor("GAMB", [2, 4, 128, NT // 64], F32, kind=dk("GAMB"))
            self.bon = nc.dram_tensor("bon", [W, NT], F32, kind=dk("bon"))
            self.ytm = nc.dram_tensor("ytm", [2, NT, W], F32, kind=dk("ytm"))
            self.UD = nc.dram_tensor("UD", [4, 128, 3, NT], F32, kind=dk("UD"))
            self.RD = nc.dram_tensor("RD", [8, 6, NT], F32, kind=dk("RD"))
            self.EGTD = nc.dram_tensor("EGTD", [8, NT // 128], F32, kind=dk("EGTD"))
            self.wmb = nc.dram_tensor("wmb", [16, 128, 24, 128], BF16, kind="Internal")
            self.wob = nc.dram_tensor("wob", [16, 128, 16, 128], BF16, kind="Internal")
            self.ones_bf = b.sb([128, 128], BF16, "ones")
            b.ms("dve", self.ones_bf[:], 1.0)
            self.modv = b.sb([128, L, KC, 6], F32, "modv")

            with b.scope():
                self.phase_mod()
            for l in range(L):
                with b.scope():
                    self.phase_wcast(l)
                with b.scope():
                    self.phase_inproj(l)
                need_ctx = l < L - 1
                for mi in self.mixers:
                    with b.scope():
                        if mi in (0, 2):
                            self.phase_attn(l, mi, need_ctx)
                        elif mi == 1 and "prep" in self.stages:
                            self.phase_b_prep(l)
                    if mi in (1, 3):
                        if mi == 3 and "prep" in self.stages:
                            with b.scope():
                                self.phase_d_prep(l)
                        if "units" in self.stages:
                            with b.scope():
                                self.phase_units(l, mi)
                        if "post" in self.stages:
                            with b.scope():
                                self.phase_post(l, mi, need_ctx)
                if self.do_merge:
                    with b.scope():
                        self.phase_mcast(l)
                    with b.scope():
                        self.phase_merge(l, need_ctx)
            if self.do_merge:
                with b.scope():
                    self.phase_final()
            b.barrier()
            b.emit()
        return nc

    def phase_mod(self):
        b, I, L = self.b, self.I, self.depth
        psum = Pool(b, 4, [128, 512], F32, "ps", psum=True)
        cc = b.sb([128, KC, 2], F32, "cc")
        b.dma("sp", cc[:], I["cc"].ap())
        sc = b.sb([128, KC, 2], F32, "sc")
        b.act(sc[:], cc[:], AF.Silu)
        wpool = Pool(b, 2, [128, KC, 512], F32, "wmod")
        raw = b.sb([128, 48, 2], F32, "modraw")
        bm = b.sb([128, 48], F32, "bmod")
        ng = b.sb([128, KC], F32, "ng")
        mv = self.modv
        for l in range(L):
            b.dma("sp", bm[:], I["b_mod"].ap()[l])
            b.dma("sp", ng[:], I["norm_g"].ap()[l])
            for cg in range(12):
                wt = wpool.get()
                src = I["w_mod"].ap()[l, :, cg * 512:(cg + 1) * 512].rearrange("(k p) c -> p k c", p=128)
                b.dma("sp", wt[:], src)
                for j in range(4):
                    ct = cg * 4 + j
                    ps = psum.get()
                    for k in range(KC):
                        b.mm(ps[:, 0:2], wt[:, k, j * 128:(j + 1) * 128], sc[:, k, :], k == 0, k == KC - 1)
                    b.ts("dve", raw[:, ct, :], ps[:, 0:2], bm[:, ct:ct + 1], ALU.add)
            for v in range(2):
                b.stt(mv[:, l, :, 3 * v + 0], raw[:, 16:32, v], 1.0, ng[:], ALU.add, ALU.mult)
                b.cp("dve", mv[:, l, :, 3 * v + 1], raw[:, 0:16, v])
                b.cp("dve", mv[:, l, :, 3 * v + 2], raw[:, 32:48, v])

    def phase_wcast(self, l):
        b, I = self.b, self.I
        pf = Pool(b, 3, [128, KC, 128], F32, "wcf")
        pb = Pool(b, 3, [128, KC, 128], BF16, "wcb")
        for ct in range(NCT):
            f, g = pf.get(), pb.get()
            b.dma("sp", f[:], I["w_in"].ap()[l, ct])
            b.cp(("dve", "pool")[ct % 2], g[:], f[:])
            b.dma("sp", self.wbf.ap()[ct], g[:])

    def phase_inproj(self, l):
        b, I = self.b, self.I
        mv = self.modv
        psum = Pool(b, 6, [128, 512], F32, "ps", psum=True)
        Pxt = Pool(b, 2, [128, KC, 512], F32, "xt")
        Psq = Pool(b, 1, [128, KC, 512], BF16, "sq")
        PhT = Pool(b, 2, [128, KC, 1024], BF16, "hT")
        Prs = Pool(b, 2, [128, 512], F32, "rs")
        Pw = Pool(b, 3, [128, KC, 128], BF16, "wip")
        Pev = Pool(b, 4, [128, 512], F32, "ev")
        Pevb = Pool(b, 2, [128, 128], BF16, "evb")
        src_x = I["xT"] if l == 0 else self.xs
        groups, cur, tot = [], [], 0
        for blk in self.blocks:
            if tot + blk[1] > 1024:
                groups.append(cur)
                cur, tot = [], 0
            cur.append(blk)
            tot += blk[1]
        if cur:
            groups.append(cur)
        for grp in groups:
            hT = PhT.get()
            subs, off = [], 0
            for (t0, T, is_ctx) in grp:
                v = 1 if is_ctx else 0
                xt = Pxt.get()
                for k in range(KC):
                    b.dma("sp", xt[:, k, 0:T], src_x.ap()[k * 128:(k + 1) * 128, t0:t0 + T])
                sq = Psq.get()
                b.act(sq[:, :, 0:T], xt[:, :, 0:T], AF.Square)
                ps = psum.get()
                for k in range(KC):
                    b.mm(ps[:, 0:T], self.ones_bf[:], sq[:, k, 0:T], k == 0, k == KC - 1)
                rs = Prs.get()
                b.act(rs[:, 0:T], ps[:, 0:T], AF.Sqrt, scale=1.0 / D_MODEL, bias=1e-6)
                b.rcp(rs[:, 0:T], rs[:, 0:T])
                for k in range(KC):
                    b.tt("dve", xt[:, k, 0:T], xt[:, k, 0:T], rs[:, 0:T], ALU.mult)
                    b.ts(("pool", "dve")[k % 2], hT[:, k, off:off + T], xt[:, k, 0:T], mv[:, l, k, 3 * v:3 * v + 1], ALU.mult,
                         mv[:, l, k, 3 * v + 1:3 * v + 2], ALU.add)
                subs.append((t0, T, off))
                off += T
            for ct in range((NFM + 127) // 128):
                wt = Pw.get()
                b.dma("sp", wt[:], self.wbf.ap()[ct])
                for (t0, T, off) in subs:
                    ps = psum.get()
                    for k in range(KC):
                        b.mm(ps[:, 0:T], wt[:, k, :], hT[:, k, off:off + T], k == 0, k == KC - 1)
                    ev = Pev.get()
                    b.cp(("act", "dve")[ct % 2], ev[:, 0:T], ps[:, 0:T])
                    b.dma("pool", self.pT.ap()[ct * 128:(ct + 1) * 128, t0:t0 + T], ev[:, 0:T])
            for vi in range(2):
                wt = Pw.get()
                b.dma("sp", wt[:], self.wbf.ap()[NFM_PAD // 128 + vi])
                for (t0, T, off) in subs:
                    for sbk in range(T // 128):
                        ps = psum.get()
                        for k in range(KC):
                            b.mm(ps[:, 0:128], hT[:, k, off + sbk * 128:off + (sbk + 1) * 128], wt[:, k, :], k == 0, k == KC - 1)
                        evb = Pevb.get()
                        b.cp("dve", evb[:], ps[:, 0:128])
                        r0 = t0 + sbk * 128
                        b.dma("pool", self.vtm.ap()[vi, r0:r0 + 128, :], evb[:])

    def phase_attn(self, l, mi, need_ctx):
        b, I, NT = self.b, self.I, self.NT
        isC = (mi == 2)
        pre = "C" if isC else "A"
        oq, oqp, ok_, okp, og = (SEG_OFF[pre + x] for x in ("q", "qp", "k", "kp", "g"))
        pT = self.pT.ap()
        NCH = NT // 128
        psS = Pool(b, 4, [128, 512], F32, "psS", psum=True)
        psO = Pool(b, 2, [128, 512], F32, "psO", psum=True)
        psB = Pool(b, 2, [128, 512], F32, "psB", psum=True)
        KT = b.sb([128, 2, NT], BF16, "KT")
        VE = b.sb([128, NCH, 2, 65], BF16, "VE")
        cst = b.sb([128, 16], F32, "acst")
        m3 = b.sb([128, 384], F32, "m3")
        ones_f = b.sb([128, 64], F32, "ones_f")
        oblk = b.sb([128, 128], BF16, "oblk")
        b.ms("dve", ones_f[:], 1.0)
        b.ms("dve", oblk[:], 0.0)
        b.ms("dve", oblk[0:64, 0:64], 1.0)
        b.ms("dve", oblk[64:128, 64:128], 1.0)
        b.dma("sp", m3[:], I["m3"].ap())
        b.dma("sp", cst[:, 0:8], I["a_sink"].ap()[l])
        b.dma("sp", cst[:, 8:12], I["c_qk"].ap()[l])
        b.act(cst[:, 0:8], cst[:, 0:8], AF.Exp)
        b.ms("dve", VE[:, :, :, 64:65], 1.0)
        vsrc = self.vtm.ap()[1 if isC else 0].rearrange("(c p) (g d) -> p c g d", p=128, g=2)
        for c0 in range(0, NCH, 8):
            c1 = min(NCH, c0 + 8)
            for g in range(2):
                b.dma("sp", VE[:, c0:c1, g, 0:64], vsrc[:, c0:c1, g, :])

        ld = Pool(b, 4, [128, 512], F32, "ald")
        tb = Pool(b, 2, [128, 512], F32, "atb")
        tmp = Pool(b, 4, [128, 512], F32, "atmp")
        sqp = Pool(b, 2, [128, 512], BF16, "asq")
        rsp = Pool(b, 2, [128, 512], F32, "ars")

        def rope(dst_ap, rows, rowsp, t0, T, cosb, sinb, nidx, dup):
            x, xp = ld.get(), ld.get()
            if dup:
                for hh in range(2):
                    b.dma("sp", x[hh * 64:(hh + 1) * 64, 0:T], pT[rows:rows + 64, t0:t0 + T])
                    b.dma("sp", xp[hh * 64:(hh + 1) * 64, 0:T], pT[rowsp:rowsp + 64, t0:t0 + T])
            else:
                b.dma("sp", x[:, 0:T], pT[rows:rows + 128, t0:t0 + T])
                b.dma("sp", xp[:, 0:T], pT[rowsp:rowsp + 128, t0:t0 + T])
            t1, t2 = tmp.get(), tmp.get()
            if isC:
                sq = sqp.get()
                b.act(sq[:, 0:T], x[:, 0:T], AF.Square)
                ps = psB.get()
                b.mm(ps[:, 0:T], oblk[:], sq[:, 0:T])
                rs = rsp.get()
                b.act(rs[:, 0:T], ps[:, 0:T], AF.Sqrt, scale=1.0 / 64, bias=1e-6)
                b.rcp(rs[:, 0:T], rs[:, 0:T])
                b.stt(t1[:, 0:T], x[:, 0:T], cst[:, nidx:nidx + 1], cosb[:, 0:T], ALU.mult, ALU.mult)
                b.stt(t2[:, 0:T], xp[:, 0:T], cst[:, nidx + 1:nidx + 2], sinb[:, 0:T], ALU.mult, ALU.mult)
                b.tt("pool", t1[:, 0:T], t1[:, 0:T], t2[:, 0:T], ALU.add)
                b.tt("dve", dst_ap, t1[:, 0:T], rs[:, 0:T], ALU.mult)
            else:
                b.tt("dve", t1[:, 0:T], x[:, 0:T], cosb[:, 0:T], ALU.mult)
                b.tt("pool", t2[:, 0:T], xp[:, 0:T], sinb[:, 0:T], ALU.mult)
                b.tt("dve", dst_ap, t1[:, 0:T], t2[:, 0:T], ALU.add)

        def tables(t0, T):
            cosb, sinb = tb.get(), tb.get()
            b.dma("sp", cosb[:, 0:T], I["ropec"].ap()[:, t0:t0 + T])
            b.dma("sp", sinb[:, 0:T], I["ropes"].ap()[:, t0:t0 + T])
            return cosb, sinb

        for (t0, T, is_ctx) in self.blocks:
            cosb, sinb = tables(t0, T)
            for g in range(2):
                rope(KT[:, g, t0:t0 + T], ok_ + g * 64, okp + g * 64, t0, T, cosb, sinb, 10, True)

        QT = Pool(b, 2, [128, 4, 512], BF16, "QT")
        PTp = Pool(b, 4, [128, 512], BF16, "PT")
        gp = Pool(b, 2, [64, 8, 512], F32, "ag")
        ep = Pool(b, 2, [64, 8, 512], F32, "ae")
        drp = Pool(b, 2, [128, 512], F32, "adr")
        bcp = Pool(b, 2, [64, 512], F32, "abc")
        yp = Pool(b, 3, [64, 512], BF16, "ay")
        nlat_ch = self.n // 128
        for (t0, T, is_ctx) in self.blocks:
            if is_ctx and not need_ctx:
                continue
            cosb, sinb = tables(t0, T)
            qt = QT.get()
            for pr in range(4):
                rope(qt[:, pr, 0:T], oq + pr * 128, oqp + pr * 128, t0, T, cosb, sinb, 8, False)
            chunks = [(0, 0, T, None), (1, 0, T, None)]
            if not is_ctx:
                lb = (t0 - CTX) // 128
                nq = T // 128
                if isC:
                    chunks += [(2 + c, 0, T, None) for c in range(nlat_ch)]
                else:
                    for c in range(max(0, lb - 1), min(nlat_ch, lb + nq + 1)):
                        qlo, qhi = max(lb, c - 1), min(lb + nq - 1, c + 1)
                        chunks.append((2 + c, (qlo - lb) * 128, (qhi - lb + 1) * 128, (qlo - c + 1) * 128))
            gall, sgall = gp.get(), ep.get()
            for h in range(8):
                b.dma("sp", gall[:, h, 0:T], pT[og + h * 64:og + (h + 1) * 64, t0:t0 + T])
            b.act(sgall[:, :, 0:T], gall[:, :, 0:T], AF.Sigmoid)
            b.tt("dve", sgall[:, :, 0:T], sgall[:, :, 0:T], gall[:, :, 0:T], ALU.mult)
            for h in range(8):
                g, pr, po = h // 4, h // 2, (h % 2) * 64
                O = psO.get()
                LA = 2
                Sq = []

                def issue_s(cj):
                    ch_, lo_, hi_, _m = chunks[cj]
                    S_ = psS.get()
                    b.mm(S_[:, lo_:hi_], KT[po:po + 64, g, ch_ * 128:(ch_ + 1) * 128], qt[po:po + 64, pr, lo_:hi_])
                    Sq.append(S_)
                for cj in range(min(LA, len(chunks))):
                    issue_s(cj)
                for ci, (ch, lo, hi, mlo) in enumerate(chunks):
                    if ci + LA < len(chunks):
                        issue_s(ci + LA)
                    S = Sq[ci]
                    pt = PTp.get()
                    b.act(pt[:, lo:hi], S[:, lo:hi], AF.Exp, scale=0.125)
                    if mlo is not None:
                        b.tt("dve", pt[:, lo:hi], pt[:, lo:hi], m3[:, mlo:mlo + hi - lo], ALU.mult)
                    b.mm(O[0:65, lo:hi], VE[:, ch, g, :], pt[:, lo:hi], ci == 0, ci == len(chunks) - 1)
                dr = drp.get()
                if isC:
                    b.rcp(dr[64:65, 0:T], O[64:65, 0:T])
                else:
                    b.ts("dve", dr[64:65, 0:T], O[64:65, 0:T], cst[64:65, h:h + 1], ALU.add)
                    b.rcp(dr[64:65, 0:T], dr[64:65, 0:T])
                B = psB.get()
                b.mm(B[0:64, 0:T], ones_f[64:65, 0:64], dr[64:65, 0:T])
                bc = bcp.get()
                b.cp("act", bc[:, 0:T], B[0:64, 0:T])
                b.tt("dve", bc[:, 0:T], O[0:64, 0:T], bc[:, 0:T], ALU.mult)
                y = yp.get()
                b.tt("dve", y[:, 0:T], bc[:, 0:T], sgall[:, h, 0:T], ALU.mult)
                b.dma("pool", self.yT.ap()[mi, h * 64:(h + 1) * 64, t0:t0 + T], y[:, 0:T])

    def seg_bounds(self, t0):
        return (0, CTX) if t0 < CTX else (CTX, self.NT)

    def phase_b_prep(self, l):
        b, I, NT = self.b, self.I, self.NT
        pT = self.pT.ap()
        oz = SEG_OFF["Bz"]
        psum = Pool(b, 4, [128, 512], F32, "ps", psum=True)
        psB = Pool(b, 2, [128, 512], F32, "psb", psum=True)
        mu = b.sb([128, 13, 2], F32, "mu")
        cmu = b.sb([128, 13], F32, "cmu")
        w0a0 = b.sb([128, 2, 2, 4], F32, "w0a0")
        aw = b.sb([128, 2, 512], F32, "aw")
        vec = b.sb([128, 4, 4], F32, "bvec")
        rkb = b.sb([128, 4, 128], F32, "rkb")
        oblk = b.sb([128, 128], BF16, "oblk")
        onesF = b.sb([128, 512], F32, "onesF")
        b.ms("dve", onesF[:], 1.0)
        b.ms("dve", oblk[:], 0.0)
        b.ms("dve", oblk[0:64, 0:64], 1.0)
        b.ms("dve", oblk[64:128, 64:128], 1.0)
        b.dma("sp", mu[:], I["b_mu"].ap()[l])
        b.dma("sp", w0a0[:], I["b_w0a0"].ap()[l])
        b.dma("sp", aw[:], I["b_aw"].ap()[l])
        b.dma("sp", vec[:], I["b_vec"].ap()[l])
        b.dma("sp", rkb[:], I["b_rkblk"].ap()[l])
        b.ts("dve", cmu[:], mu[:, :, 0], -1.0, ALU.mult, 1.0, ALU.add)
        b.tt("dve", cmu[:], cmu[:], mu[:, :, 1], ALU.subtract)
        ptp = Pool(b, 3, [128, 514], F32, "bpt")
        zall = b.sb([128, 13, 512], F32, "zall")
        kkT = b.sb([128, 4, 512], F32, "kkT")
        t2k = Pool(b, 12, [128, 512], F32, "bt")
        sqp = Pool(b, 2, [128, 512], BF16, "bsq")
        Sx = b.sb([128, 513], F32, "Sx")
        b.ms("dve", Sx[:, 0:1], 0.0)
        o7p = Pool(b, 2, [128, 7, 512], F32, "o7")
        gtp = Pool(b, 2, [128, 8], F32, "egt")
        for (t0, T, is_ctx) in self.blocks:
            lo, hi = self.seg_bounds(t0)
            nch = T // 64
            for i in range(13):
                pt = ptp.get()
                a0, a1 = max(t0 - 1, lo), min(t0 + T + 1, hi)
                if a0 > t0 - 1:
                    b.ms("pool", pt[:, 0:1], 0.0)
                if a1 < t0 + T + 1:
                    b.ms("pool", pt[:, T + 1:T + 2], 0.0)
                b.dma("sp", pt[:, a0 - (t0 - 1):a1 - (t0 - 1)], pT[oz + i * 128:oz + (i + 1) * 128, a0:a1])
                z = zall[:, i, 0:T]
                b.ts("dve", z, pt[:, 1:T + 1], cmu[:, i:i + 1], ALU.mult)
                b.stt(z, pt[:, 0:T], mu[:, i, 0:1], z, ALU.mult, ALU.add)
                b.stt(z, pt[:, 2:T + 2], mu[:, i, 1:2], z, ALU.mult, ALU.add)
            b.act(zall[0:64, 12, 0:T], zall[0:64, 12, 0:T], AF.Tanh)
            for pr in range(4):
                kx = t2k.get()
                b.ts("dve", kx[:, 0:T], zall[:, 4 + pr, 0:T], vec[:, 0, pr:pr + 1], ALU.mult)
                sq = sqp.get()
                b.act(sq[:, 0:T], kx[:, 0:T], AF.Square)
                ps = psum.get()
                b.mm(ps[:, 0:T], oblk[:], sq[:, 0:T])
                rn = t2k.get()
                b.act(rn[:, 0:T], ps[:, 0:T], AF.Sqrt, bias=1e-12)
                b.rcp(rn[:, 0:T], rn[:, 0:T])
                b.tt("dve", kkT[:, pr, 0:T], kx[:, 0:T], rn[:, 0:T], ALU.mult)
            for pr in range(4):
                psb = psB.get()
                rT, kT_, vT_ = zall[:, pr, 0:T], zall[:, 4 + pr, 0:T], zall[:, 8 + pr, 0:T]
                for d in range(2):
                    pw = psum.get()
                    b.mm(pw[:, 0:T], aw[0:64, d, pr * 128:(pr + 1) * 128], zall[0:64, 12, 0:T])
                    lw = t2k.get()
                    b.act(lw[:, 0:T], pw[:, 0:T], AF.Sigmoid, bias=w0a0[:, 0, d, pr:pr + 1])
                    b.ts("pool", lw[:, 0:T], lw[:, 0:T], -0.606531, ALU.mult)
                    pa = psum.get()
                    b.mm(pa[:, 0:T], aw[64:128, d, pr * 128:(pr + 1) * 128], zall[64:128, 12, 0:T])
                    a = t2k.get()
                    b.act(a[:, 0:T], pa[:, 0:T], AF.Sigmoid, bias=w0a0[:, 1, d, pr:pr + 1])
                    kt = t2k.get()
                    b.ts("dve", kt[:, 0:T], a[:, 0:T], -1.0, ALU.add, vec[:, 1, pr:pr + 1], ALU.mult)
                    b.stt(kt[:, 0:T], kt[:, 0:T], 1.0, kT_, ALU.add, ALU.mult)
                    bb = t2k.get()
                    b.tt("pool", bb[:, 0:T], kkT[:, pr, 0:T], a[:, 0:T], ALU.mult)
                    b.I("dve", "tensor_tensor_scan", out=Sx[:, 1:T + 1], data0=onesF[:, 0:T], data1=lw[:, 0:T],
                        initial=0.0, op0=ALU.mult, op1=ALU.add)
                    gc = t2k.get()
                    v3 = lambda ap: ap.rearrange("p (c t) -> p c t", t=64)
                    b.tt("dve", v3(gc[:, 0:T]), v3(Sx[:, 1:T + 1]), v3(Sx[:, 0:T])[:, :, 0:1].broadcast_to([128, nch, 64]),
                         ALU.subtract)
                    gtot_bc = v3(gc[:, 0:T])[:, :, 63:64].broadcast_to([128, nch, 64])
                    ge = t2k.get()
                    if d == 0:
                        gi = gc
                        b.tt("dve", ge[:, 0:T], gc[:, 0:T], lw[:, 0:T], ALU.subtract)
                    else:
                        gi = t2k.get()
                        b.tt("dve", v3(ge[:, 0:T]), gtot_bc, v3(gc[:, 0:T]), ALU.subtract)
                        b.tt("pool", gi[:, 0:T], ge[:, 0:T], lw[:, 0:T], ALU.add)
                    egt = gtp.get()
                    b.act(egt[:, 0:nch], v3(gc[:, 0:T])[:, :, 63], AF.Exp)
                    ege, egi, engi = t2k.get(), t2k.get(), t2k.get()
                    b.act(ege[:, 0:T], ge[:, 0:T], AF.Exp)
                    b.act(egi[:, 0:T], gi[:, 0:T], AF.Exp)
                    b.act(engi[:, 0:T], gi[:, 0:T], AF.Exp, scale=-1.0)
                    o7 = o7p.get()
                    b.tt("dve", o7[:, 0, 0:T], kkT[:, pr, 0:T], ege[:, 0:T], ALU.mult)
                    b.tt("pool", o7[:, 1, 0:T], rT, egi[:, 0:T], ALU.mult)
                    b.tt("dve", o7[:, 2, 0:T], bb[:, 0:T], engi[:, 0:T], ALU.mult)
                    b.tt("pool", o7[:, 3, 0:T], kt[:, 0:T], engi[:, 0:T], ALU.mult)
                    ebc = egt[:, 0:nch].unsqueeze(2).broadcast_to([128, nch, 64])
                    b.tt("dve", v3(o7[:, 4, 0:T]), v3(o7[:, 3, 0:T]), ebc, ALU.mult)
                    b.tt("dve", v3(o7[:, 5, 0:T]), v3(o7[:, 2, 0:T]), ebc, ALU.mult)
                    b.cp("pool", o7[:, 6, 0:T], vT_)
                    b.dma("pool", self.UB.ap()[d, pr, :, :, t0:t0 + T], o7[:, :, 0:T])
                    b.dma("pool", self.GAMB.ap()[d, pr, :, t0 // 64:t0 // 64 + nch], egt[:, 0:nch])
                    rkt = t2k.get()
                    b.tt("dve", rkt[:, 0:T], rT, kt[:, 0:T], ALU.mult)
                    b.mm(psb[:, 0:T], rkb[:, pr, :], rkt[:, 0:T], d == 0, d == 1)
                bo = t2k.get()
                b.tt("dve", bo[:, 0:T], psb[:, 0:T], vT_, ALU.mult)
                b.dma("pool", self.bon.ap()[pr * 128:(pr + 1) * 128, t0:t0 + T], bo[:, 0:T])

    def phase_units(self, l, mi):
        b, I, NT = self.b, self.I, self.NT
        isB = (mi == 1)
        nlev = 5 if isB else 6
        CT = 64 if isB else 128
        NCH = NT // CT
        cch = CTX // CT
        order = {0: list(range(NCH)), 1: list(range(cch - 1, -1, -1)) + list(range(NCH - 1, cch - 1, -1))}
        chains = [(u, d) for u in range(4) for d in range(2)]
        NC = len(chains)
        psG = Pool(b, 2, [128, 512], F32, "psG", psum=True)
        psA = Pool(b, 2, [128, 512], F32, "psA", psum=True)
        sm = Pool(b, 4, [128, 512], F32, "psm", psum=True)
        ident = b.sb([128, 128], F32, "ident")
        b.dma("sp", ident[:], I["ident"].ap())
        rmask = b.sb([128, 4, 128], F32, "rmask")
        b.dma("sp", rmask[:], I["rmask" if isB else "gmask"].ap())
        bd2 = b.sb([128, 2, 64], F32, "bd2")
        b.dma("sp", bd2[:], I["bd2"].ap())
        NB = 16
        XXp = Pool(b, NB, [128, 4, 128], BF16, "XX")
        TMp = Pool(b, NB, [128, 3, 128], BF16, "TM3")
        A3p = Pool(b, NB, [128, 3, 128], BF16, "A3")
        Pmp = Pool(b, 10 if isB else 18, [128, 2, 128], F32, "Pm")
        XTp = Pool(b, 17, [128, 128], F32, "XT")
        XTf = Pool(b, 16, [128, 128], F32, "XTf")
        if isB:
            P2p = Pool(b, 10, [128, 128], F32, "P2f")
            Pbp = Pool(b, 18, [128, 2, 128], BF16, "Pb")
            XTbp = Pool(b, 17, [128, 128], BF16, "XTb")
        AVp = Pool(b, NB, [128, 128], F32, "AVs")
        Wp = Pool(b, 9, [128, 128], F32, "Wt")
        NUp = Pool(b, 9, [128, 128], BF16, "negU")
        Yop = Pool(b, 6, [128, 128], F32, "Yo")
        H = [b.sb([128, 128], F32, "H") for _ in chains]
        Hb = [b.sb([128, 128], BF16, "Hb") for _ in chains]
        for i in range(NC):
            b.ms("dve", H[i][:], 0.0)
            b.ms("pool", Hb[i][:], 0.0)
        gam = [b.sb([128, NCH], F32, "gam") for _ in chains]
        if isB:
            ldp = Pool(b, 10, [128, 7, 64], F32, "ld7")
            F3p = Pool(b, 8, [128, 3, 128], F32, "F3")
            for i, (u, d) in enumerate(chains):
                b.dma("sp", gam[i][:], self.GAMB.ap()[d, u])
        else:
            self.gdn_unit_setup(l, gam)
            self.gdn_gam(gam, sm, chains)

        def pre_stages(step):
            units = []
            st = []

            def s_prep():
                for i, (u, d) in enumerate(chains):
                    c = order[d][step]
                    tk = c * CT
                    U = dict(i=i, u=u, d=d, c=c, tk=tk)
                    XX, TM3 = XXp.get(), TMp.get()
                    if isB:
                        ld = ldp.get()
                        b.dma("sp", ld[:], self.UB.ap()[d, u, :, :, tk:tk + 64])
                        bdb = bd2[:].unsqueeze(1).broadcast_to([128, 4, 2, 64])
                        b.tt("dve", XX[:].rearrange("p k (h t) -> p k h t", h=2),
                             ld[:, 0:4, :].unsqueeze(2).broadcast_to([128, 4, 2, 64]), bdb, ALU.mult)
                        F3 = F3p.get()
                        b.tt("pool", F3[:].rearrange("p k (h t) -> p k h t", h=2),
                             ld[:, 4:7, :].unsqueeze(2).broadcast_to([128, 3, 2, 64]),
                             bd2[:].unsqueeze(1).broadcast_to([128, 3, 2, 64]), ALU.mult)
                        tp = sm.get()
                        for k in range(3):
                            b.tr(tp[:, k * 128:(k + 1) * 128], F3[:, k, :], ident[:])
                        b.cp(("act", "dve")[i % 2], TM3[:].rearrange("p k t -> p (k t)"), tp[:, 0:384])
                        if d == 0:
                            WsT, WiT, Ws = rmask[:, 0, :], rmask[:, 1, :], rmask[:, 2, :]
                        else:
                            WsT, WiT, Ws = rmask[:, 2, :], rmask[:, 3, :], rmask[:, 0, :]
                        U.update(XkH=XX[:, 0, :], XrH=XX[:, 1, :])
                    else:
                        WsT, WiT, Ws, XH = self.gdn_unit_prep(l, U, XX, TM3, sm, rmask, ident)
                        U.update(XkH=XH[:, 0, :], XrH=XH[:, 1, :])
                    U.update(XX=XX, TM3=TM3, gcol=gam[i][:, c:c + 1], W=(WsT, WiT, Ws))
                    units.append(U)

            def s_gram():
                for U in units:
                    XX = U["XX"]
                    WsT, WiT, Ws = U["W"]
                    G = psG.get()
                    xkr = XX[:, 0:2, :].rearrange("p k t -> p (k t)")
                    b.mm(G[:, 0:256], XX[:, 2, :], xkr)
                    b.mm(G[:, 256:512], XX[:, 3, :], xkr)
                    g3 = sm.get()
                    b.mm(g3[:, 0:128], XX[:, 0, :], XX[:, 2, :])
                    Pm = Pmp.get()
                    b.stt(Pm[:, 1, :], G[:, 0:128], -1.0, WsT, ALU.mult, ALU.mult)
                    b.stt(Pm[:, 0, :], g3[:, 0:128], -1.0, Ws, ALU.mult, ALU.mult)
                    A3 = A3p.get()
                    b.tt("dve", A3[:, 0, :], G[:, 128:256], WiT, ALU.mult)
                    b.tt("dve", A3[:, 1, :], G[:, 256:384], WsT, ALU.mult)
                    b.tt("dve", A3[:, 2, :], G[:, 384:512], WiT, ALU.mult)
                    XT = XTp.get()
                    b.tt("pool", XT[:], Pm[:, 1, :], ident[:], ALU.add)
                    U.update(A3=A3, Pm=Pm, XT=XT)

            def mk_lev_a(lev):
                def f():
                    last = lev == nlev - 1
                    for U in units:
                        p2 = sm.get()
                        if not isB:
                            Pm = U["Pm"]
                            b.mm(p2[:, 0:128], Pm[:, 1, :], Pm[:, 0, :])
                            Pn = Pmp.get()
                            if not last:
                                b.mm(p2[:, 128:256], Pm[:, 0, :], Pm[:, 1, :])
                                b.cp(("act", "dve")[U["i"] % 2], Pn[:].rearrange("p k t -> p (k t)"), p2[:, 0:256])
                            else:
                                b.cp("act", Pn[:, 0, :], p2[:, 0:128])
                            U["Pn"] = Pn
                        elif lev == 0:
                            Pm = U["Pm"]
                            b.mm(p2[:, 0:128], Pm[:, 1, :], Pm[:, 0, :])
                            b.mm(p2[:, 128:256], Pm[:, 0, :], Pm[:, 1, :])
                            P2f = P2p.get()
                            Pb = Pbp.get()
                            eng = ("act", "dve")[U["i"] % 2]
                            b.cp(eng, P2f[:], p2[:, 0:128])
                            b.cp(eng, Pb[:].rearrange("p k t -> p (k t)"), p2[:, 0:256])
                            U["P2f"], U["Pb"] = P2f, Pb
                        else:
                            Pb = U["Pb"]
                            b.mm(p2[:, 0:128], Pb[:, 1, :], Pb[:, 0, :])
                            Pn = Pbp.get()
                            if not last:
                                b.mm(p2[:, 128:256], Pb[:, 0, :], Pb[:, 1, :])
                                b.cp(("act", "dve")[U["i"] % 2], Pn[:].rearrange("p k t -> p (k t)"), p2[:, 0:256])
                            else:
                                b.cp("act", Pn[:, 0, :], p2[:, 0:128])
                            U["Pb"] = Pn
                return f

            def mk_lev_b(lev):
                def f():
                    last = lev == nlev - 1
                    for U in units:
                        XT = U["XT"]
                        xu = sm.get()
                        if not isB:
                            b.mm(xu[:, 0:128], U["Pn"][:, 0, :], XT[:])
                            U["Pm"] = U["Pn"]
                        elif lev == 0:
                            b.mm(xu[:, 0:128], U["P2f"][:], XT[:])
                        else:
                            b.mm(xu[:, 0:128], U["Pb"][:, 0, :], U["XTb"][:])
                        XTn = (XTf if last else XTp).get()
                        b.tt("dve", XTn[:], xu[:, 0:128], XT[:], ALU.add)
                        if isB and not last:
                            XTb = XTbp.get()
                            b.cp("pool", XTb[:], XTn[:])
                            U["XTb"] = XTb
                        U["XT"] = XTn
                return f

            def s_av():
                for U in units:
                    av = sm.get()
                    b.mm(av[:, 0:128], U["A3"][:, 1, :], U["TM3"][:, 2, :])
                    AVs = AVp.get()
                    b.cp("act", AVs[:], av[:, 0:128])
                    U["AVs"] = AVs

            st.append(s_prep)
            st.append(s_gram)
            for lev in range(nlev):
                st.append(mk_lev_a(lev))
                st.append(mk_lev_b(lev))
            st.append(s_av)
            return st, units

        def seq_stages(units):
            def s5():
                for U in units:
                    kh = sm.get()
                    b.mm(kh[:, 0:128], U["XkH"], Hb[U["i"]][:])
                    Wt = Wp.get()
                    b.tt("dve", Wt[:], kh[:, 0:128], U["AVs"][:], ALU.add)
                    U["Wt"] = Wt

            def s6():
                for U in units:
                    uu = sm.get()
                    b.mm(uu[:, 0:128], U["XT"][:], U["Wt"][:])
                    nu = NUp.get()
                    b.act(nu[:], uu[:, 0:128], AF.Copy, scale=-1.0)
                    U["nu"] = nu

            def s7():
                for U in units:
                    i = U["i"]
                    Y = psA.get()
                    b.mm(Y[:, 0:128], U["XrH"], Hb[i][:], True, False)
                    b.mm(Y[:, 0:128], U["A3"][:, 0, :], U["nu"][:], False, False)
                    b.mm(Y[:, 0:128], U["A3"][:, 2, :], U["TM3"][:, 2, :], False, True)
                    Yo = Yop.get()
                    b.cp("act", Yo[:], Y[:, 0:128])
                    dst = self.ytm.ap()[U["d"]]
                    if isB:
                        for hh in range(2):
                            b.dma("pool", dst[U["tk"]:U["tk"] + 64, (2 * U["u"] + hh) * 64:(2 * U["u"] + hh + 1) * 64],
                                  Yo[hh * 64:(hh + 1) * 64, hh * 64:(hh + 1) * 64])
                    else:
                        b.dma("pool", dst[U["tk"]:U["tk"] + 128, U["u"] * 128:(U["u"] + 1) * 128], Yo[:])

            def s8():
                for U in units:
                    i = U["i"]
                    Hn = psA.get()
                    b.mm(Hn[:, 0:128], U["TM3"][:, 1, :], U["nu"][:], True, False)
                    b.mm(Hn[:, 0:128], U["TM3"][:, 0, :], U["TM3"][:, 2, :], False, True)
                    b.stt(H[i][:], H[i][:], U["gcol"], Hn[:, 0:128], ALU.mult, ALU.add)
                    b.cp("pool", Hb[i][:], H[i][:])
            return [s5, s6, s7, s8]

        st, units = pre_stages(0)
        for f in st:
            f()
        for step in range(NCH):
            seq = seq_stages(units)
            if step + 1 < NCH:
                pre, nunits = pre_stages(step + 1)
            else:
                pre, nunits = [], None
            npre = len(pre)
            marks = {((k + 1) * npre) // 5: k for k in range(4)} if npre else {}
            if self.no_inter:
                for k in range(4):
                    seq[k]()
                marks = {}
                done_all = True
            else:
                done_all = False
            done = set(range(4)) if done_all else set()
            for j, f in enumerate(pre):
                if j in marks and marks[j] not in done:
                    seq[marks[j]]()
                    done.add(marks[j])
                f()
            for k in range(4):
                if k not in done:
                    seq[k]()
                    done.add(k)
            units = nunits

    def phase_post(self, l, mi, need_ctx):
        b, I, NT = self.b, self.I, self.NT
        isB = (mi == 1)
        pT = self.pT.ap()
        og = SEG_OFF["Bg" if isB else "Dg"]
        psum = Pool(b, 3, [128, 512], F32, "ps", psum=True)
        ident = b.sb([128, 128], F32, "ident")
        b.dma("sp", ident[:], I["ident"].ap())
        vec = b.sb([128, 4, 4], F32, "pvec")
        b.dma("sp", vec[:], I["b_vec" if isB else "d_vec"].ap()[l])
        yp = Pool(b, 4, [128, 512], F32, "py")
        stp = Pool(b, 8, [128, 8], F32, "pst")
        gp = Pool(b, 4, [128, 128], F32, "pg")
        g4p = Pool(b, 2, [128, 4, 128], F32, "pg4")
        e4p = Pool(b, 2, [128, 4, 128], F32, "pe4")
        bp = Pool(b, 4, [128, 128], F32, "pb")
        op_ = Pool(b, 4, [128, 128], BF16, "po")
        NH, HD = (8, 64) if isB else (4, 128)
        for t0 in range(0 if need_ctx else CTX, NT, 128):
            y0, y1 = yp.get(), yp.get()
            b.dma("sp", y0[:], self.ytm.ap()[0, t0:t0 + 128, :])
            b.dma("sp", y1[:], self.ytm.ap()[1, t0:t0 + 128, :])
            b.tt("pool", y0[:], y0[:], y1[:], ALU.add)
            y3 = y0[:].rearrange("p (h d) -> p h d", h=NH)
            st = stp.get()
            if isB:
                b.I("dve", "tensor_reduce", out=st[:, 0:NH], in_=y3, axis=mybir.AxisListType.X, op=ALU.add)
                b.ts("dve", st[:, 0:NH], st[:, 0:NH], 1.0 / HD, ALU.mult)
                b.tt("dve", y3, y3, st[:, 0:NH].unsqueeze(2).broadcast_to([128, NH, HD]), ALU.subtract)
            sq = yp.get()
            b.tt("pool", sq[:], y0[:], y0[:], ALU.mult)
            s2 = stp.get()
            b.I("dve", "tensor_reduce", out=s2[:, 0:NH], in_=sq[:].rearrange("p (h d) -> p h d", h=NH),
                axis=mybir.AxisListType.X, op=ALU.add)
            b.act(s2[:, 0:NH], s2[:, 0:NH], AF.Sqrt, scale=1.0 / HD, bias=(64e-5 if isB else 1e-6))
            b.rcp(s2[:, 0:NH], s2[:, 0:NH])
            b.tt("dve", y3, y3, s2[:, 0:NH].unsqueeze(2).broadcast_to([128, NH, HD]), ALU.mult)
            tpb = psum.get()
            for ct in range(4):
                b.tr(tpb[:, ct * 128:(ct + 1) * 128], y0[:, ct * 128:(ct + 1) * 128], ident[:])
            g4, e4 = g4p.get(), e4p.get()
            b.dma("sp", g4[:], pT[og:og + 512, t0:t0 + 128].rearrange("(k p) t -> p k t", p=128))
            b.act(e4[:], g4[:], AF.Sigmoid)
            b.tt("pool", e4[:], e4[:], g4[:], ALU.mult)
            for ct in range(4):
                tp = _View(tpb.h, ct * 128, 128)
                e = _View3(e4.h, ct)
                o = op_.get()
                if isB:
                    bo = bp.get()
                    b.dma("sp", bo[:], self.bon.ap()[ct * 128:(ct + 1) * 128, t0:t0 + 128])
                    t = gp.get()
                    b.ts("dve", t[:], tp[:], vec[:, 2, ct:ct + 1], ALU.mult, vec[:, 3, ct:ct + 1], ALU.add)
                    b.tt("pool", t[:], t[:], bo[:], ALU.add)
                    b.tt("dve", o[:], t[:], e[:], ALU.mult)
                else:
                    b.stt(o[:], tp[:], vec[:, 0, ct:ct + 1], e[:], ALU.mult, ALU.mult)
                b.dma("pool", self.yT.ap()[mi, ct * 128:(ct + 1) * 128, t0:t0 + 128], o[:])


    def phase_d_prep(self, l):
        b, I, NT = self.b, self.I, self.NT
        pT = self.pT.ap()
        oq, oab = SEG_OFF["Dqkv"], SEG_OFF["Dab"]
        psum = Pool(b, 4, [128, 512], F32, "ps", psum=True)
        cw = b.sb([128, 12, 5], F32, "cw")
        b.dma("sp", cw[:], I["d_conv"].ap()[l])
        abp = b.sb([8, 4], F32, "abp")
        b.dma("sp", abp[:], I["d_ab"].ap()[l])
        nexpA = b.sb([8, 1], F32, "nexpA")
        b.act(nexpA[:], abp[:, 1:2], AF.Exp)
        b.ts("dve", nexpA[:], nexpA[:], -1.0, ALU.mult)
        onesF = b.sb([8, 512], F32, "onesF")
        b.ms("dve", onesF[:], 1.0)
        Sx = b.sb([8, 513], F32, "Sx")
        b.ms("dve", Sx[:, 0:1], 0.0)
        ptp = Pool(b, 3, [128, 516], F32, "dpt")
        tp = Pool(b, 6, [128, 512], F32, "dt")
        sqp = Pool(b, 2, [128, 512], BF16, "dsq")
        o3p = Pool(b, 2, [128, 3, 512], F32, "o3")
        rp = Pool(b, 12, [8, 512], F32, "dr")
        o6p = Pool(b, 2, [8, 6, 512], F32, "o6")
        egp = Pool(b, 2, [8, 4], F32, "deg")
        for (t0, T, is_ctx) in self.blocks:
            lo, hi = self.seg_bounds(t0)
            nch = T // 128
            for h in range(4):
                o3 = o3p.get()
                for kind in range(3):
                    i = kind * 4 + h
                    pt = ptp.get()
                    a0, a1 = max(t0 - 2, lo), min(t0 + T + 2, hi)
                    if a0 > t0 - 2:
                        b.ms("pool", pt[:, 0:2], 0.0)
                    if a1 < t0 + T + 2:
                        b.ms("pool", pt[:, T + 2:T + 4], 0.0)
                    b.dma("sp", pt[:, a0 - (t0 - 2):a1 - (t0 - 2)], pT[oq + i * 128:oq + (i + 1) * 128, a0:a1])
                    acc = tp.get()
                    b.ts("dve", acc[:, 0:T], pt[:, 0:T], cw[:, i, 0:1], ALU.mult)
                    for j in range(1, 5):
                        b.stt(acc[:, 0:T], pt[:, j:j + T], cw[:, i, j:j + 1], acc[:, 0:T], ALU.mult, ALU.add)
                    e = tp.get()
                    b.act(e[:, 0:T], acc[:, 0:T], AF.Sigmoid)
                    if kind == 2:
                        b.tt("pool", o3[:, 2, 0:T], acc[:, 0:T], e[:, 0:T], ALU.mult)
                        continue
                    b.tt("pool", acc[:, 0:T], acc[:, 0:T], e[:, 0:T], ALU.mult)
                    sq = sqp.get()
                    b.act(sq[:, 0:T], acc[:, 0:T], AF.Square)
                    ps = psum.get()
                    b.mm(ps[:, 0:T], self.ones_bf[:], sq[:, 0:T])
                    rn = tp.get()
                    b.act(rn[:, 0:T], ps[:, 0:T], AF.Sqrt, bias=1e-12)
                    b.rcp(rn[:, 0:T], rn[:, 0:T])
                    if kind == 0:
                        b.stt(o3[:, 0, 0:T], acc[:, 0:T], 128.0 ** -0.5, rn[:, 0:T], ALU.mult, ALU.mult)
                    else:
                        b.tt("dve", o3[:, 1, 0:T], acc[:, 0:T], rn[:, 0:T], ALU.mult)
                b.dma("pool", self.UD.ap()[h, :, :, t0:t0 + T], o3[:, :, 0:T])
            lgr, br = rp.get(), rp.get()
            for d in range(2):
                b.dma("sp", lgr[d * 4:(d + 1) * 4, 0:T], pT[oab + d * 8:oab + d * 8 + 4, t0:t0 + T])
                b.dma("sp", br[d * 4:(d + 1) * 4, 0:T], pT[oab + d * 8 + 4:oab + d * 8 + 8, t0:t0 + T])
            lg = rp.get()
            b.act(lg[:, 0:T], lgr[:, 0:T], AF.Exp, bias=abp[:, 0:1])
            b.act(lg[:, 0:T], lg[:, 0:T], AF.Ln, bias=1.0)
            b.ts("dve", lg[:, 0:T], lg[:, 0:T], nexpA[:, 0:1], ALU.mult)
            beta = rp.get()
            b.act(beta[:, 0:T], br[:, 0:T], AF.Sigmoid)
            b.I("dve", "tensor_tensor_scan", out=Sx[:, 1:T + 1], data0=onesF[:, 0:T], data1=lg[:, 0:T],
                initial=0.0, op0=ALU.mult, op1=ALU.add)
            v3 = lambda ap: ap.rearrange("p (c t) -> p c t", t=128)
            gc = rp.get()
            b.tt("dve", v3(gc[:, 0:T]), v3(Sx[:, 1:T + 1]), v3(Sx[:, 0:T])[:, :, 0:1].broadcast_to([8, nch, 128]), ALU.subtract)
            gtot_bc = v3(gc[:, 0:T])[:, :, 127:128].broadcast_to([8, nch, 128])
            o6 = o6p.get()
            t1 = rp.get()
            b.ts("dve", t1[:, 0:T], gc[:, 0:T], abp[:, 2:3], ALU.mult)
            b.stt(v3(t1[:, 0:T]), gtot_bc, abp[:, 3:4], v3(t1[:, 0:T]), ALU.mult, ALU.add)
            b.stt(o6[:, 3, 0:T], lg[:, 0:T], abp[:, 3:4], t1[:, 0:T], ALU.mult, ALU.add)
            b.tt("dve", o6[:, 2, 0:T], o6[:, 3, 0:T], lg[:, 0:T], ALU.subtract)
            elg = rp.get()
            b.act(elg[:, 0:T], lg[:, 0:T], AF.Exp)
            b.tt("dve", o6[:, 0, 0:T], beta[:, 0:T], elg[:, 0:T], ALU.mult)
            b.cp("dve", o6[:, 1, 0:T], beta[:, 0:T])
            t2 = rp.get()
            b.tt("dve", v3(t2[:, 0:T]), gtot_bc, v3(o6[:, 3, 0:T]), ALU.subtract)
            b.act(t2[:, 0:T], t2[:, 0:T], AF.Exp)
            b.tt("dve", o6[:, 4, 0:T], beta[:, 0:T], t2[:, 0:T], ALU.mult)
            b.tt("dve", o6[:, 5, 0:T], o6[:, 4, 0:T], elg[:, 0:T], ALU.mult)
            eg = egp.get()
            b.act(eg[:, 0:nch], v3(gc[:, 0:T])[:, :, 127], AF.Exp)
            b.dma("pool", self.RD.ap()[:, :, t0:t0 + T], o6[:, :, 0:T])
            b.dma("pool", self.EGTD.ap()[:, t0 // 128:t0 // 128 + nch], eg[:, 0:nch])

    def gdn_unit_setup(self, l, gam):
        b, I, NT = self.b, self.I, self.NT
        NCH = NT // 128
        sel8 = b.sb([8, 8, 128], F32, "sel8")
        b.dma("sp", sel8[:], I["sel8"].ap())
        egt = b.sb([8, NCH + (NCH % 2)], F32, "egtall")
        b.ms("dve", egt[:], 0.0)
        b.dma("sp", egt[:, 0:NCH], self.EGTD.ap())
        self._gd = dict(
            sel8=sel8,
            ld3=Pool(b, 12, [128, 3, 128], F32, "ld3"),
            rows=Pool(b, 4, [8, 6, 128], F32, "grow"),
            cols=Pool(b, 4, [128, 4, 8], F32, "gcol"),
            Wt=Pool(b, 12, [128, 3, 128], F32, "gW"),
            xa=Pool(b, 6, [128, 128], F32, "gxa"),
            eb=Pool(b, 4, [128, 256], F32, "geb"),
            bcS=Pool(b, 10, [128, 4, 128], F32, "gbc"),
            XH=Pool(b, 17, [128, 2, 128], BF16, "gXH"),
            cache={},
        )
        self._gd_egt = egt

    def gdn_gam(self, gam, sm, chains):
        b = self.b
        NCH = self.NT // 128
        n2 = NCH + (NCH % 2)
        for i, (u, d) in enumerate(chains):
            p = sm.get()
            b.mm(p[:, 0:n2], self._gd["sel8"][:, d * 4 + u, :], self._gd_egt[:, 0:n2])
            b.cp("act", gam[i][:], p[:, 0:NCH])

    def gdn_unit_prep(self, l, U, XX, TM3, sm, gmask, ident):
        b = self.b
        G = self._gd
        h, d, c, tk = U["u"], U["d"], U["c"], U["tk"]
        r = d * 4 + h
        key = (c, d)
        if key not in G["cache"]:
            rows = G["rows"].get()
            b.dma("sp", rows[:], self.RD.ap()[:, :, tk:tk + 128])
            tp = sm.get()
            for j, kind in enumerate((3, 2, 4, 5)):
                b.tr(tp[:, j * 8:(j + 1) * 8], rows[:, kind, :], ident[0:8, 0:8])
            cols = G["cols"].get()
            b.cp("act", cols[:].rearrange("p k r -> p (k r)"), tp[:, 0:32])
            G["cache"] = {kk: vv for kk, vv in G["cache"].items() if kk[1] != d}
            G["cache"][key] = (rows, cols)
        rows, cols = G["cache"][key]
        ld = G["ld3"].get()
        b.dma("sp", ld[:], self.UD.ap()[h, :, :, tk:tk + 128])
        bc = G["bcS"].get()
        b.dma("sp", bc[:], self.RD.ap()[r:r + 1, 0:4, tk:tk + 128].broadcast_to([128, 4, 128]))
        b.cp("act", XX[:, 0, :], ld[:, 1, :])
        b.cp("pool", XX[:, 1, :], ld[:, 0, :])
        b.tt("dve", XX[:, 2, :], bc[:, 0, :], ld[:, 1, :], ALU.mult)
        b.tt("pool", XX[:, 3, :], bc[:, 1, :], ld[:, 1, :], ALU.mult)
        tp = sm.get()
        b.tr(tp[:, 0:128], ld[:, 1, :], ident[:])
        b.tr(tp[:, 128:256], ld[:, 2, :], ident[:])
        b.ts("dve", TM3[:, 0, :], tp[:, 0:128], cols[:, 2, r:r + 1], ALU.mult)
        b.ts("dve", TM3[:, 1, :], tp[:, 0:128], cols[:, 3, r:r + 1], ALU.mult)
        b.cp("dve", TM3[:, 2, :], tp[:, 128:256])
        Wt = G["Wt"].get()
        mk = (0, 1, 2) if d == 0 else (2, 3, 0)
        xa = G["xa"].get()
        b.stt(xa[:], bc[:, 2, :], cols[:, 0, r:r + 1], gmask[:, mk[0], :], ALU.subtract, ALU.add)
        b.act(Wt[:, 0, :], xa[:], AF.Exp)
        xb = G["xa"].get()
        b.stt(xb[:], bc[:, 3, :], cols[:, 0, r:r + 1], gmask[:, mk[1], :], ALU.subtract, ALU.add)
        b.act(Wt[:, 1, :], xb[:], AF.Exp)
        xc = G["xa"].get()
        b.stt(xc[:], bc[:, 3, :], cols[:, 1, r:r + 1], gmask[:, mk[2], :], ALU.subtract, ALU.subtract)
        b.act(Wt[:, 2, :], xc[:], AF.Exp, scale=-1.0)
        eb = G["eb"].get()
        b.act(eb[:], bc[:, 2:4, :].rearrange("p k t -> p (k t)"), AF.Exp)
        XH = G["XH"].get()
        b.tt("dve", XH[:, 0, :], eb[:, 0:128], ld[:, 1, :], ALU.mult)
        b.tt("pool", XH[:, 1, :], eb[:, 128:256], ld[:, 0, :], ALU.mult)
        return Wt[:, 0, :], Wt[:, 1, :], Wt[:, 2, :], XH

    def phase_mcast(self, l):
        b, I = self.b, self.I
        fa = Pool(b, 2, [128, 24, 128], F32, "mcf")
        ba = Pool(b, 2, [128, 24, 128], BF16, "mcb")
        for dt in range(16):
            f, g = fa.get(), ba.get()
            b.dma("sp", f[:], I["wm"].ap()[l, dt])
            b.cp(("dve", "pool")[dt % 2], g[:], f[:])
            b.dma("sp", self.wmb.ap()[dt], g[:])
            f, g = fa.get(), ba.get()
            b.dma("sp", f[:, 0:16, :], I["wo"].ap()[l, dt])
            b.cp(("pool", "dve")[dt % 2], g[:, 0:16, :], f[:, 0:16, :])
            b.dma("sp", self.wob.ap()[dt], g[:, 0:16, :])

    def phase_merge(self, l, need_ctx):
        b, I, NT = self.b, self.I, self.NT
        mv = self.modv
        pT = self.pT.ap()
        opm = SEG_OFF["pm"]
        psG = Pool(b, 2, [128, 512], F32, "psg", psum=True)
        psB = Pool(b, 3, [128, 512], F32, "psb", psum=True)
        psO = Pool(b, 2, [128, 512], F32, "pso", psum=True)
        gb = b.sb([128, 4, 16], F32, "gb")
        b.dma("sp", gb[:], I["g_b"].ap()[l])
        pmf = Pool(b, 1, [128, 2, 512], F32, "pmf")
        pmb = Pool(b, 2, [128, 2, 512], BF16, "pmb")
        ybp = Pool(b, 2, [128, 16, 512], BF16, "yb")
        accT = Pool(b, 2, [128, 16, 512], BF16, "accT")
        wmp = Pool(b, 2, [128, 24, 128], BF16, "wmt")
        wop = Pool(b, 2, [128, 16, 128], BF16, "wot")
        gtp = Pool(b, 3, [128, 512], F32, "mg")
        acp = Pool(b, 2, [128, 512], F32, "macc")
        tmp = Pool(b, 3, [128, 512], F32, "mtmp")
        xtp = Pool(b, 3, [128, 512], F32, "mx")
        src_x = I["xT"] if l == 0 else self.xs
        for (t0, T, is_ctx) in self.blocks:
            if is_ctx and not need_ctx:
                continue
            v = 1 if is_ctx else 0
            pf, pb = pmf.get(), pmb.get()
            b.dma("sp", pf[:, :, 0:T], pT[opm:opm + 256, t0:t0 + T].rearrange("(k p) t -> p k t", p=128))
            b.cp("pool", pb[:, :, 0:T], pf[:, :, 0:T])
            yb = ybp.get()
            for mi in range(4):
                b.dma("sp", yb[:, mi * 4:(mi + 1) * 4, 0:T], self.yT.ap()[mi, :, t0:t0 + T].rearrange("(k p) t -> p k t", p=128))
            aT = accT.get()
            for dt in range(16):
                wt = wmp.get()
                b.dma("sp", wt[:], self.wmb.ap()[dt])
                acc = acp.get()
                for i in range(4):
                    pg = psG.get()
                    for rc in range(2):
                        b.mm(pg[:, 0:T], wt[:, i * 2 + rc, :], pb[:, rc, 0:T], rc == 0, rc == 1)
                    gt = gtp.get()
                    b.act(gt[:, 0:T], pg[:, 0:T], AF.Sigmoid, bias=gb[:, i, dt:dt + 1])
                    pbr = psB.get()
                    for cc in range(4):
                        b.mm(pbr[:, 0:T], wt[:, 8 + i * 4 + cc, :], yb[:, i * 4 + cc, 0:T], cc == 0, cc == 3)
                    if i == 0:
                        b.tt("dve", acc[:, 0:T], pbr[:, 0:T], gt[:, 0:T], ALU.mult)
                    else:
                        tm = tmp.get()
                        b.tt("dve", tm[:, 0:T], pbr[:, 0:T], gt[:, 0:T], ALU.mult)
                        b.tt("pool", acc[:, 0:T], acc[:, 0:T], tm[:, 0:T], ALU.add)
                b.cp("act", aT[:, dt, 0:T], acc[:, 0:T])
            for dt in range(16):
                wo = wop.get()
                b.dma("sp", wo[:], self.wob.ap()[dt])
                po = psO.get()
                for k in range(16):
                    b.mm(po[:, 0:T], wo[:, k, :], aT[:, k, 0:T], k == 0, k == 15)
                xt = xtp.get()
                b.dma("sp", xt[:, 0:T], src_x.ap()[dt * 128:(dt + 1) * 128, t0:t0 + T])
                b.stt(xt[:, 0:T], po[:, 0:T], mv[:, l, dt, 3 * v + 2:3 * v + 3], xt[:, 0:T], ALU.mult, ALU.add)
                b.dma("pool", self.xs.ap()[dt * 128:(dt + 1) * 128, t0:t0 + T], xt[:, 0:T])

    def phase_final(self):
        b, I = self.b, self.I
        psum = Pool(b, 2, [128, 512], F32, "ps", psum=True)
        fg = b.sb([128, KC], F32, "fg")
        b.dma("sp", fg[:], I["final_g"].ap())
        Pxt = Pool(b, 2, [128, KC, 512], F32, "xt")
        Psq = Pool(b, 1, [128, KC, 512], BF16, "sq")
        Prs = Pool(b, 2, [128, 512], F32, "rs")
        for (t0, T, is_ctx) in self.blocks:
            if is_ctx:
                continue
            xt = Pxt.get()
            for k in range(KC):
                b.dma("sp", xt[:, k, 0:T], self.xs.ap()[k * 128:(k + 1) * 128, t0:t0 + T])
            sq = Psq.get()
            b.act(sq[:, :, 0:T], xt[:, :, 0:T], AF.Square)
            ps = psum.get()
            for k in range(KC):
                b.mm(ps[:, 0:T], self.ones_bf[:], sq[:, k, 0:T], k == 0, k == KC - 1)
            rs = Prs.get()
            b.act(rs[:, 0:T], ps[:, 0:T], AF.Sqrt, scale=1.0 / D_MODEL, bias=1e-6)
            b.rcp(rs[:, 0:T], rs[:, 0:T])
            for k in range(KC):
                b.stt(xt[:, k, 0:T], xt[:, k, 0:T], fg[:, k:k + 1], rs[:, 0:T], ALU.mult, ALU.mult)
                b.dma("pool", self.out.ap()[k * 128:(k + 1) * 128, t0 - CTX:t0 - CTX + T], xt[:, k, 0:T])


def _fm(v):
    v = np.asarray(v)
    c = v.shape[-1]
    return np.ascontiguousarray(np.swapaxes(v.reshape(v.shape[:-1] + (c // 128, 128)), -1, -2))


def host_inputs(inp, bi, n_lat, depth):
    L = depth
    d = {}
    xcat = np.concatenate([inp["ctx"][bi], inp["x"][bi][:n_lat]], axis=0)
    d["xT"] = np.ascontiguousarray(xcat.T)
    d["cc"] = np.ascontiguousarray(np.stack([_fm(inp["c"][bi]), _fm(inp["c_ctx"])], axis=-1))
    d["norm_g"] = _fm(inp["norm_g"][:L])
    d["w_mod"] = np.ascontiguousarray(inp["w_mod"][:L])
    d["b_mod"] = _fm(inp["b_mod"][:L])
    cols = w_in_columns()
    w = inp["w_in"][:L][:, :, cols]
    w = w.reshape(L, KC, 128, NCT, 128).transpose(0, 3, 2, 1, 4)
    d["w_in"] = np.ascontiguousarray(w)
    d["final_g"] = _fm(inp["final_g"])
    NT = CTX + n_lat
    p = np.arange(128)
    dd = p % 64
    half, r = dd // 32, dd % 32
    inv = 10000.0 ** (-np.arange(0, 32, 2, dtype=np.float32) / np.float32(32))
    f = inv[r % 16].astype(np.float32)
    t = np.arange(n_lat)
    pos = np.where(half[:, None] == 0, (t // 64)[None, :], (t % 64)[None, :]).astype(np.float32)
    ang = (pos * f[:, None]).astype(np.float32)
    sign = np.where(r < 16, -1.0, 1.0).astype(np.float32)
    rc = np.ones((128, NT), np.float32)
    rsn = np.zeros((128, NT), np.float32)
    rc[:, CTX:] = np.cos(ang)
    rsn[:, CTX:] = np.sin(ang) * sign[:, None]
    d["ropec"], d["ropes"] = rc, rsn
    j = np.arange(128)[:, None]
    i = np.arange(128)[None, :]
    d["m3"] = np.concatenate([(j <= i), np.ones((128, 128), bool), (i <= j)], axis=1).astype(np.float32)
    d["ident"] = np.eye(128, dtype=np.float32)
    d["a_sink"] = np.ascontiguousarray(np.broadcast_to(inp["a_sink"][:L, None, :], (L, 128, 8)))
    pm = _perm64()
    qn, kn = inp["c_qn"][:L], inp["c_kn"][:L]
    cq = np.stack([qn, qn[:, pm], kn, kn[:, pm]], axis=-1)
    d["c_qk"] = np.ascontiguousarray(np.concatenate([cq, cq], axis=1))
    pp = np.arange(128)
    d["bd2"] = np.ascontiguousarray(np.broadcast_to((pp[:, None] // 64 == np.arange(2)[None, :])[:, :, None], (128, 2, 64))).astype(np.float32)
    row, col = pp[:, None], pp[None, :]
    same = (row // 64) == (col // 64)
    tri = np.stack([row < col, row <= col, row > col, row >= col], axis=1)
    d["rmask"] = (tri & same[:, None, :]).astype(np.float32)
    d["gmask"] = np.where(tri, 0.0, -1.0e4).astype(np.float32)
    d["b_mu"] = np.ascontiguousarray(_fm(inp["b_mu"][:L]).transpose(0, 2, 3, 1))
    w0 = _fm(inp["b_w0"][:L]).transpose(0, 2, 1, 3)
    a0 = _fm(inp["b_a0"][:L]).transpose(0, 2, 1, 3)
    d["b_w0a0"] = np.ascontiguousarray(np.stack([w0, a0], axis=2))
    d["b_aw"] = np.ascontiguousarray(np.concatenate([inp["b_wup"][:L], inp["b_aup"][:L]], axis=2).transpose(0, 2, 1, 3))
    d["b_vec"] = np.ascontiguousarray(np.stack([_fm(inp[k][:L]) for k in ("b_kk", "b_ka", "b_lng", "b_lnb")], axis=2))
    rk = inp["b_rk"][:L]
    blk = np.zeros((L, 128, 4, 128), np.float32)
    for pr in range(4):
        for hh in range(2):
            blk[:, hh * 64:(hh + 1) * 64, pr, hh * 64:(hh + 1) * 64] = rk[:, 2 * pr + hh, :, None]
    d["b_rkblk"] = blk
    sel = np.zeros((8, 8, 128), np.float32)
    for r_ in range(8):
        sel[r_, r_, :] = 1.0
    d["sel8"] = sel
    d["d_conv"] = np.ascontiguousarray(_fm(inp["d_conv"][:L]).transpose(0, 2, 3, 1))
    ab = np.zeros((L, 8, 4), np.float32)
    ab[:, :, 0] = inp["d_dtb"][:L].reshape(L, 8)
    ab[:, :, 1] = inp["d_alog"][:L].reshape(L, 8)
    ab[:, 0:4, 2], ab[:, 4:8, 2] = 1.0, -1.0
    ab[:, 4:8, 3] = 1.0
    d["d_ab"] = ab
    dv = np.zeros((L, 128, 4, 4), np.float32)
    dv[:, :, 0, :] = inp["d_norm"][:L][:, :, None]
    d["d_vec"] = dv
    gu = inp["g_up"][:L].reshape(L, 4, 2, 128, 16, 128)
    wb = inp["w_br"][:L].reshape(L, 4, 4, 128, 16, 128)
    wm = np.concatenate([gu.transpose(0, 4, 3, 1, 2, 5).reshape(L, 16, 128, 8, 128),
                         wb.transpose(0, 4, 3, 1, 2, 5).reshape(L, 16, 128, 16, 128)], axis=3)
    d["wm"] = np.ascontiguousarray(wm)
    wo = inp["w_out"][:L].reshape(L, 16, 128, 16, 128)
    d["wo"] = np.ascontiguousarray(wo.transpose(0, 3, 2, 1, 4))
    d["g_b"] = np.ascontiguousarray(_fm(inp["g_b"][:L]).transpose(0, 2, 1, 3))
    return d


N_CORES = 8
_PROG_CACHE = {}


def run_model(inp, n_lat, depth):
    inp = {k: np.asarray(v) for k, v in inp.items()}
    B = inp["x"].shape[0]
    key = (n_lat, depth)
    if key not in _PROG_CACHE:
        _PROG_CACHE[key] = Prog(n_lat, depth).build()
    nc = _PROG_CACHE[key]
    per_b = [host_inputs(inp, bi, n_lat, depth) for bi in range(B)]
    in_maps = [per_b[i % B] for i in range(N_CORES)]
    res = run_bass_kernel_spmd(nc, in_maps, core_ids=list(range(N_CORES)))
    out = np.stack([np.ascontiguousarray(res.results[bi]["outT"].T) for bi in range(B)], axis=0)
    return out.astype(np.float32)


def kernel(**inputs):
    return run_model(inputs, 8192, 4)
```

```python
import math
from contextlib import ExitStack

import numpy as np
import concourse.bass as bass
import concourse.mybir as mybir
from concourse.bass_utils import run_bass_kernel_spmd

F32 = mybir.dt.float32
BF16 = mybir.dt.bfloat16
AF = mybir.ActivationFunctionType
ALU = mybir.AluOpType

D_MODEL = 2048
CTX = 256
W = 512
KC = D_MODEL // 128

OFF_A, OFF_B, OFF_C, OFF_D, OFF_G = 0, 1280, 3456, 4736, 6800
SEGS = [
    ("Aq", OFF_A + 0, 512, False), ("Aqp", OFF_A + 0, 512, True),
    ("Ak", OFF_A + 512, 128, False), ("Akp", OFF_A + 512, 128, True),
    ("Ag", OFF_A + 768, 512, False),
    ("Bz", OFF_B + 0, 1664, False), ("Bg", OFF_B + 1664, 512, False),
    ("Cq", OFF_C + 0, 512, False), ("Cqp", OFF_C + 0, 512, True),
    ("Ck", OFF_C + 512, 128, False), ("Ckp", OFF_C + 512, 128, True),
    ("Cg", OFF_C + 768, 512, False),
    ("Dqkv", OFF_D + 0, 1536, False), ("Dg", OFF_D + 1552, 512, False),
    ("pm", OFF_G, 256, False),
    ("Dab", OFF_D + 1536, 16, False),
]
SEG_OFF = {}
_o = 0
for _n, _s, _c, _p in SEGS:
    SEG_OFF[_n] = _o
    _o += _c
NFM = _o
NFM_PAD = 8192
VSEGS = [("Av", OFF_A + 640, 128), ("Cv", OFF_C + 640, 128)]
NCT = NFM_PAD // 128 + len(VSEGS)


def _perm64():
    p = np.arange(64)
    blk, r = p // 32, p % 32
    return blk * 32 + (r + 16) % 32


def w_in_columns():
    cols = []
    for n, s, c, perm in SEGS:
        idx = np.arange(s, s + c)
        if perm:
            idx = idx.reshape(-1, 64)[:, _perm64()].reshape(-1)
        cols.append(idx)
    cols = np.concatenate(cols)
    pad = np.zeros(NFM_PAD - NFM, dtype=np.int64)
    vcols = np.concatenate([np.arange(s, s + c) for _, s, c in VSEGS])
    return np.concatenate([cols, pad, vcols])


class Tile:
    __slots__ = ("h", "w", "r", "name")

    def __init__(self, h, name):
        self.h = h
        self.w = None
        self.r = []
        self.name = name

    def __getitem__(self, idx):
        return self.h[idx]


_WRITE_KW = ("out", "ap", "accum_out")


class _RowSplit:
    def __init__(self, a, b, half):
        self.a, self.b, self.half = a, b, half

    def ap(self):
        return self

    def __getitem__(self, idx):
        r, c = idx
        if r.start >= self.half:
            return self.b.ap()[r.start - self.half:r.stop - self.half, c]
        assert r.stop <= self.half
        return self.a.ap()[r, c]


class _View3:
    def __init__(self, h, k):
        self.h, self.k = h, k

    def __getitem__(self, idx):
        return self.h[:, self.k, :]


class _View:
    def __init__(self, h, lo, w):
        self.h, self.lo, self.w = h, lo, w

    def __getitem__(self, idx):
        if not isinstance(idx, tuple):
            idx = (idx, slice(None))
        p, c = idx[0], idx[1]
        a, bnd, _ = c.indices(self.w)
        return self.h[p, self.lo + a:self.lo + bnd]


class Builder:
    ENGS = ("pe", "dve", "act", "pool", "sp")

    def __init__(self, nc, es):
        self.nc = nc
        self.es = es
        self.ops = {e: [] for e in self.ENGS}
        self.cnt = {e: 0 for e in self.ENGS}
        self.sem = {}
        for e in ("pe", "dve", "act", "pool"):
            self.sem[e] = es.enter_context(nc.semaphore("s_" + e))
        self.waited = {e: {} for e in self.ENGS}
        NQ = 12
        self.dsem = {}
        self.dnext = {}
        for q in ("sp", "pool", "act"):
            self.dsem[q] = [es.enter_context(nc.semaphore("d_%s%d" % (q, i))) for i in range(NQ)]
            self.dnext[q] = 0
        self.dcum = {}
        self.n_tiles = 0
        self.reg = {}

    def scope(self):
        b = self

        class _S:
            def __enter__(s2):
                s2.old = b.es
                s2.st = ExitStack()
                b.es = s2.st
                return s2

            def __exit__(s2, *a):
                b.barrier()
                b.es = s2.old
                s2.st.close()
                return False
        return _S()

    def sb(self, shape, dtype, name=None, psum=False):
        self.n_tiles += 1
        name = "%s_%d" % (name or "t", self.n_tiles)
        mk = self.nc.psum_tensor if psum else self.nc.sbuf_tensor
        h = self.es.enter_context(mk(name, list(shape), dtype))
        t = Tile(h, name)
        self.reg[h.name] = t
        return t

    def ps(self, shape, dtype=F32, name=None):
        return self.sb(shape, dtype, name, psum=True)

    def _need(self, eng, rec, waits):
        if rec is None:
            return
        if rec[0] == "E":
            key, idx = rec[1], rec[2]
            if key == eng and eng == "pe":
                return
        else:
            key, idx = rec[1], rec[2]
        if self.waited[eng].get(key, 0) >= idx:
            return
        self.waited[eng][key] = idx
        waits.append((key, idx))

    def _deps(self, eng, reads, writes):
        waits = []
        for t in reads:
            self._need(eng, t.w, waits)
        for t in writes:
            self._need(eng, t.w, waits)
            for r in t.r:
                self._need(eng, r, waits)
        return waits

    def _split(self, kw):
        reads, writes = [], []
        for k, v in kw.items():
            if hasattr(v, "tensor") and hasattr(v, "ap"):
                t = self.reg.get(v.tensor.name)
                if isinstance(t, tuple):
                    t = t[1][(v.offset % 512) // t[0]]
                if t is not None:
                    (writes if k in _WRITE_KW else reads).append(t)
        return reads, writes

    def subtiles(self, tile, width):
        subs = []
        for j in range(512 // width):
            st = Tile(_View(tile.h, j * width, width), "%s_s%d" % (tile.name, j))
            subs.append(st)
        self.reg[tile.h.name] = (width, subs)
        return subs

    def _mark(self, rec, reads, writes):
        for t in reads:
            t.r.append(rec)
        for t in writes:
            t.w = rec
            t.r = []

    def I(self, eng, method, **kw):
        reads, writes = self._split(kw)
        waits = self._deps(eng, reads, writes)
        self.cnt[eng] += 1
        idx = self.cnt[eng]
        self.ops[eng].append((waits, method, kw, None))
        self._mark(("E", eng, idx), reads, writes)

    def dma(self, q, out, in_, **kw):
        reads, writes = self._split(dict(out=out, in_=in_))
        waits = self._deps(q, reads, writes)
        i = self.dnext[q]
        self.dnext[q] = (i + 1) % len(self.dsem[q])
        key = (q, i)
        prev = self.dcum.get(key, 0)
        if prev:
            self._need(q, ("D", key, prev), waits)
        val = prev + 16
        self.dcum[key] = val
        kw = dict(kw, out=out, in_=in_)
        self.ops[q].append((waits, "dma_start", kw, self.dsem[q][i]))
        self._mark(("D", key, val), reads, writes)

    def barrier(self):
        for e in self.ENGS:
            waits = []
            for x in ("pe", "dve", "act", "pool"):
                if self.cnt[x] and not (x == e and e == "pe"):
                    self._need(e, ("E", x, self.cnt[x]), waits)
            for key, val in self.dcum.items():
                self._need(e, ("D", key, val), waits)
            if waits:
                self.ops[e].append((waits, None, None, None))

    def _semof(self, key):
        if isinstance(key, tuple):
            return self.dsem[key[0]][key[1]]
        return self.sem[key]

    def emit(self):
        nc = self.nc
        with nc.Block() as block:
            def mk(ename):
                def body(e):
                    own = self.sem.get(ename)
                    for waits, method, kw, dsem in self.ops[ename]:
                        for key, val in waits:
                            e.wait_ge(self._semof(key), val)
                        if method is None:
                            continue
                        ins = getattr(e, method)(**kw)
                        if dsem is not None:
                            ins.then_inc(dsem, 16)
                        else:
                            ins.then_inc(own, 1)
                return body
            block.tensor(mk("pe"))
            block.vector(mk("dve"))
            block.scalar(mk("act"))
            block.gpsimd(mk("pool"))
            block.sync(mk("sp"))

    def mm(self, out, lhsT, rhs, start=True, stop=True):
        self.I("pe", "matmul", out=out, lhsT=lhsT, rhs=rhs, start=start, stop=stop)

    def tr(self, out, in_, identity):
        self.I("pe", "transpose", out=out, in_=in_, identity=identity)

    def act(self, out, in_, func, scale=1.0, bias=0.0):
        self.I("act", "activation", out=out, in_=in_, func=func, scale=scale, bias=bias)

    def tt(self, eng, out, in0, in1, op):
        self.I(eng, "tensor_tensor", out=out, in0=in0, in1=in1, op=op)

    def ts(self, eng, out, in0, s1, op0, s2=None, op1=None):
        if op1 is None:
            self.I(eng, "tensor_scalar", out=out, in0=in0, scalar1=s1, scalar2=None, op0=op0)
        else:
            self.I(eng, "tensor_scalar", out=out, in0=in0, scalar1=s1, scalar2=s2, op0=op0, op1=op1)

    def stt(self, out, in0, scalar, in1, op0, op1):
        self.I("dve", "scalar_tensor_tensor", out=out, in0=in0, scalar=scalar, in1=in1, op0=op0, op1=op1)

    def cp(self, eng, out, in_):
        if eng == "act":
            self.I("act", "activation", out=out, in_=in_, func=AF.Copy)
        else:
            self.I(eng, "tensor_copy", out=out, in_=in_)

    def rcp(self, out, in_):
        self.I("dve", "reciprocal", out=out, in_=in_)

    def ms(self, eng, ap, val):
        self.I(eng, "memset", ap=ap, constant=val)


class Pool:
    def __init__(self, b, n, shape, dtype, name, psum=False):
        self.t = [b.sb(shape, dtype, name, psum=psum) for _ in range(n)]
        self.i = 0

    def get(self):
        t = self.t[self.i]
        self.i = (self.i + 1) % len(self.t)
        return t


class Prog:
    def __init__(self, n_lat, depth, debug=(), mixers=(0, 1, 2, 3)):
        self.n = n_lat
        self.NT = CTX + n_lat
        self.depth = depth
        self.debug = set(debug)
        self.mixers = mixers
        self.stages = ("prep", "units", "post")
        self.cut = 0
        self.do_merge = True
        self.no_inter = False
        self.blocks = [(0, CTX, True)]
        t = CTX
        while t < self.NT:
            s = min(512, self.NT - t)
            self.blocks.append((t, s, False))
            t += s

    def build(self):
        nc = bass.Bass("TRN2", target_bir_lowering=False)
        self.nc = nc
        L, NT = self.depth, self.NT
        with ExitStack() as es:
            b = Builder(nc, es)
            self.b = b
            dk = lambda name: ("ExternalOutput" if name in self.debug else "Internal")
            I = {}

            def inp(name, shape, dt=F32):
                I[name] = nc.dram_tensor(name, list(shape), dt, kind="ExternalInput")
            inp("xT", [D_MODEL, NT])
            inp("cc", [128, KC, 2])
            inp("norm_g", [L, 128, KC])
            inp("w_mod", [L, D_MODEL, 3 * D_MODEL])
            inp("b_mod", [L, 128, 48])
            inp("w_in", [L, NCT, 128, KC, 128])
            inp("final_g", [128, KC])
            inp("ropec", [128, NT])
            inp("ropes", [128, NT])
            inp("m3", [128, 384])
            inp("ident", [128, 128])
            inp("a_sink", [L, 128, 8])
            inp("c_qk", [L, 128, 4])
            inp("bd2", [128, 2, 64])
            inp("rmask", [128, 4, 128])
            inp("b_mu", [L, 128, 13, 2])
            inp("b_w0a0", [L, 128, 2, 2, 4])
            inp("b_aw", [L, 128, 2, 512])
            inp("b_vec", [L, 128, 4, 4])
            inp("b_rkblk", [L, 128, 4, 128])
            inp("gmask", [128, 4, 128])
            inp("sel8", [8, 8, 128])
            inp("d_conv", [L, 128, 12, 5])
            inp("d_ab", [L, 8, 4])
            inp("d_vec", [L, 128, 4, 4])
            inp("wm", [L, 16, 128, 24, 128])
            inp("wo", [L, 16, 128, 16, 128])
            inp("g_b", [L, 128, 4, 16])
            self.I = I
            self.out = nc.dram_tensor("outT", [D_MODEL, self.n], F32, kind="ExternalOutput")
            self.xs = nc.dram_tensor("xs", [D_MODEL, NT], F32, kind=dk("xs"))
            self.pT = _RowSplit(nc.dram_tensor("pTa", [NFM_PAD // 2, NT], F32, kind=dk("pTa")),
                                nc.dram_tensor("pTb", [NFM_PAD // 2, NT], F32, kind=dk("pTb")), NFM_PAD // 2)
            self.vtm = nc.dram_tensor("vtm", [2, NT, 128], BF16, kind=dk("vtm"))
            self.wbf = nc.dram_tensor("wbf", [NCT, 128, KC, 128], BF16, kind="Internal")
            self.yT = nc.dram_tensor("yT", [4, W, NT], BF16, kind=dk("yT"))
            self.UB = nc.dram_tensor("UB", [2, 4, 128, 7, NT], F32, kind=dk("UB"))
            self.GAMB = nc.dram_tensor("GAMB", [2, 4, 128, NT // 64], F32, kind=dk("GAMB"))
            self.bon = nc.dram_tensor("bon", [W, NT], F32, kind=dk("bon"))
            self.ytm = nc.dram_tensor("ytm", [2, NT, W], F32, kind=dk("ytm"))
            self.UD = nc.dram_tensor("UD", [4, 128, 3, NT], F32, kind=dk("UD"))
            self.RD = nc.dram_tensor("RD", [8, 6, NT], F32, kind=dk("RD"))
            self.EGTD = nc.dram_tensor("EGTD", [8, NT // 128], F32, kind=dk("EGTD"))
            self.wmb = nc.dram_tensor("wmb", [16, 128, 24, 128], BF16, kind="Internal")
            self.wob = nc.dram_tensor("wob", [16, 128, 16, 128], BF16, kind="Internal")
            self.ones_bf = b.sb([128, 128], BF16, "ones")
            b.ms("dve", self.ones_bf[:], 1.0)
            self.modv = b.sb([128, L, KC, 6], F32, "modv")

            with b.scope():
                self.phase_mod()
            for l in range(L):
                with b.scope():
                    self.phase_wcast(l)
                with b.scope():
                    self.phase_inproj(l)
                need_ctx = l < L - 1
                for mi in self.mixers:
                    with b.scope():
                        if mi in (0, 2):
                            self.phase_attn(l, mi, need_ctx)
                        elif mi == 1 and "prep" in self.stages:
                            self.phase_b_prep(l)
                    if mi in (1, 3):
                        if mi == 3 and "prep" in self.stages:
                            with b.scope():
                                self.phase_d_prep(l)
                        if "units" in self.stages:
                            with b.scope():
                                self.phase_units(l, mi)
                        if "post" in self.stages:
                            with b.scope():
                                self.phase_post(l, mi, need_ctx)
                if self.do_merge:
                    with b.scope():
                        self.phase_mcast(l)
                    with b.scope():
                        self.phase_merge(l, need_ctx)
            if self.do_merge:
                with b.scope():
                    self.phase_final()
            b.barrier()
            b.emit()
        return nc

    def phase_mod(self):
        b, I, L = self.b, self.I, self.depth
        psum = Pool(b, 4, [128, 512], F32, "ps", psum=True)
        cc = b.sb([128, KC, 2], F32, "cc")
        b.dma("sp", cc[:], I["cc"].ap())
        sc = b.sb([128, KC, 2], F32, "sc")
        b.act(sc[:], cc[:], AF.Silu)
        wpool = Pool(b, 2, [128, KC, 512], F32, "wmod")
        raw = b.sb([128, 48, 2], F32, "modraw")
        bm = b.sb([128, 48], F32, "bmod")
        ng = b.sb([128, KC], F32, "ng")
        mv = self.modv
        for l in range(L):
            b.dma("sp", bm[:], I["b_mod"].ap()[l])
            b.dma("sp", ng[:], I["norm_g"].ap()[l])
            for cg in range(12):
                wt = wpool.get()
                src = I["w_mod"].ap()[l, :, cg * 512:(cg + 1) * 512].rearrange("(k p) c -> p k c", p=128)
                b.dma("sp", wt[:], src)
                for j in range(4):
                    ct = cg * 4 + j
                    ps = psum.get()
                    for k in range(KC):
                        b.mm(ps[:, 0:2], wt[:, k, j * 128:(j + 1) * 128], sc[:, k, :], k == 0, k == KC - 1)
                    b.ts("dve", raw[:, ct, :], ps[:, 0:2], bm[:, ct:ct + 1], ALU.add)
            for v in range(2):
                b.stt(mv[:, l, :, 3 * v + 0], raw[:, 16:32, v], 1.0, ng[:], ALU.add, ALU.mult)
                b.cp("dve", mv[:, l, :, 3 * v + 1], raw[:, 0:16, v])
                b.cp("dve", mv[:, l, :, 3 * v + 2], raw[:, 32:48, v])

    def phase_wcast(self, l):
        b, I = self.b, self.I
        pf = Pool(b, 3, [128, KC, 128], F32, "wcf")
        pb = Pool(b, 3, [128, KC, 128], BF16, "wcb")
        for ct in range(NCT):
            f, g = pf.get(), pb.get()
            b.dma("sp", f[:], I["w_in"].ap()[l, ct])
            b.cp(("dve", "pool")[ct % 2], g[:], f[:])
            b.dma("sp", self.wbf.ap()[ct], g[:])

    def phase_inproj(self, l):
        b, I = self.b, self.I
        mv = self.modv
        psum = Pool(b, 6, [128, 512], F32, "ps", psum=True)
        Pxt = Pool(b, 2, [128, KC, 512], F32, "xt")
        Psq = Pool(b, 1, [128, KC, 512], BF16, "sq")
        PhT = Pool(b, 2, [128, KC, 1024], BF16, "hT")
        Prs = Pool(b, 2, [128, 512], F32, "rs")
        Pw = Pool(b, 3, [128, KC, 128], BF16, "wip")
        Pev = Pool(b, 4, [128, 512], F32, "ev")
        Pevb = Pool(b, 2, [128, 128], BF16, "evb")
        src_x = I["xT"] if l == 0 else self.xs
        groups, cur, tot = [], [], 0
        for blk in self.blocks:
            if tot + blk[1] > 1024:
                groups.append(cur)
                cur, tot = [], 0
            cur.append(blk)
            tot += blk[1]
        if cur:
            groups.append(cur)
        for grp in groups:
            hT = PhT.get()
            subs, off = [], 0
            for (t0, T, is_ctx) in grp:
                v = 1 if is_ctx else 0
                xt = Pxt.get()
                for k in range(KC):
                    b.dma("sp", xt[:, k, 0:T], src_x.ap()[k * 128:(k + 1) * 128, t0:t0 + T])
                sq = Psq.get()
                b.act(sq[:, :, 0:T], xt[:, :, 0:T], AF.Square)
                ps = psum.get()
                for k in range(KC):
                    b.mm(ps[:, 0:T], self.ones_bf[:], sq[:, k, 0:T], k == 0, k == KC - 1)
                rs = Prs.get()
                b.act(rs[:, 0:T], ps[:, 0:T], AF.Sqrt, scale=1.0 / D_MODEL, bias=1e-6)
                b.rcp(rs[:, 0:T], rs[:, 0:T])
                for k in range(KC):
                    b.tt("dve", xt[:, k, 0:T], xt[:, k, 0:T], rs[:, 0:T], ALU.mult)
                    b.ts(("pool", "dve")[k % 2], hT[:, k, off:off + T], xt[:, k, 0:T], mv[:, l, k, 3 * v:3 * v + 1], ALU.mult,
                         mv[:, l, k, 3 * v + 1:3 * v + 2], ALU.add)
                subs.append((t0, T, off))
                off += T
            for ct in range((NFM + 127) // 128):
                wt = Pw.get()
                b.dma("sp", wt[:], self.wbf.ap()[ct])
                for (t0, T, off) in subs:
                    ps = psum.get()
                    for k in range(KC):
                        b.mm(ps[:, 0:T], wt[:, k, :], hT[:, k, off:off + T], k == 0, k == KC - 1)
                    ev = Pev.get()
                    b.cp(("act", "dve")[ct % 2], ev[:, 0:T], ps[:, 0:T])
                    b.dma("pool", self.pT.ap()[ct * 128:(ct + 1) * 128, t0:t0 + T], ev[:, 0:T])
            for vi in range(2):
                wt = Pw.get()
                b.dma("sp", wt[:], self.wbf.ap()[NFM_PAD // 128 + vi])
                for (t0, T, off) in subs:
                    for sbk in range(T // 128):
                        ps = psum.get()
                        for k in range(KC):
                            b.mm(ps[:, 0:128], hT[:, k, off + sbk * 128:off + (sbk + 1) * 128], wt[:, k, :], k == 0, k == KC - 1)
                        evb = Pevb.get()
                        b.cp("dve", evb[:], ps[:, 0:128])
                        r0 = t0 + sbk * 128
                        b.dma("pool", self.vtm.ap()[vi, r0:r0 + 128, :], evb[:])

    def phase_attn(self, l, mi, need_ctx):
        b, I, NT = self.b, self.I, self.NT
        isC = (mi == 2)
        pre = "C" if isC else "A"
        oq, oqp, ok_, okp, og = (SEG_OFF[pre + x] for x in ("q", "qp", "k", "kp", "g"))
        pT = self.pT.ap()
        NCH = NT // 128
        psS = Pool(b, 4, [128, 512], F32, "psS", psum=True)
        psO = Pool(b, 2, [128, 512], F32, "psO", psum=True)
        psB = Pool(b, 2, [128, 512], F32, "psB", psum=True)
        KT = b.sb([128, 2, NT], BF16, "KT")
        VE = b.sb([128, NCH, 2, 65], BF16, "VE")
        cst = b.sb([128, 16], F32, "acst")
        m3 = b.sb([128, 384], F32, "m3")
        ones_f = b.sb([128, 64], F32, "ones_f")
        oblk = b.sb([128, 128], BF16, "oblk")
        b.ms("dve", ones_f[:], 1.0)
        b.ms("dve", oblk[:], 0.0)
        b.ms("dve", oblk[0:64, 0:64], 1.0)
        b.ms("dve", oblk[64:128, 64:128], 1.0)
        b.dma("sp", m3[:], I["m3"].ap())
        b.dma("sp", cst[:, 0:8], I["a_sink"].ap()[l])
        b.dma("sp", cst[:, 8:12], I["c_qk"].ap()[l])
        b.act(cst[:, 0:8], cst[:, 0:8], AF.Exp)
        b.ms("dve", VE[:, :, :, 64:65], 1.0)
        vsrc = self.vtm.ap()[1 if isC else 0].rearrange("(c p) (g d) -> p c g d", p=128, g=2)
        for c0 in range(0, NCH, 8):
            c1 = min(NCH, c0 + 8)
            for g in range(2):
                b.dma("sp", VE[:, c0:c1, g, 0:64], vsrc[:, c0:c1, g, :])

        ld = Pool(b, 4, [128, 512], F32, "ald")
        tb = Pool(b, 2, [128, 512], F32, "atb")
        tmp = Pool(b, 4, [128, 512], F32, "atmp")
        sqp = Pool(b, 2, [128, 512], BF16, "asq")
        rsp = Pool(b, 2, [128, 512], F32, "ars")

        def rope(dst_ap, rows, rowsp, t0, T, cosb, sinb, nidx, dup):
            x, xp = ld.get(), ld.get()
            if dup:
                for hh in range(2):
                    b.dma("sp", x[hh * 64:(hh + 1) * 64, 0:T], pT[rows:rows + 64, t0:t0 + T])
                    b.dma("sp", xp[hh * 64:(hh + 1) * 64, 0:T], pT[rowsp:rowsp + 64, t0:t0 + T])
            else:
                b.dma("sp", x[:, 0:T], pT[rows:rows + 128, t0:t0 + T])
                b.dma("sp", xp[:, 0:T], pT[rowsp:rowsp + 128, t0:t0 + T])
            t1, t2 = tmp.get(), tmp.get()
            if isC:
                sq = sqp.get()
                b.act(sq[:, 0:T], x[:, 0:T], AF.Square)
                ps = psB.get()
                b.mm(ps[:, 0:T], oblk[:], sq[:, 0:T])
                rs = rsp.get()
                b.act(rs[:, 0:T], ps[:, 0:T], AF.Sqrt, scale=1.0 / 64, bias=1e-6)
                b.rcp(rs[:, 0:T], rs[:, 0:T])
                b.stt(t1[:, 0:T], x[:, 0:T], cst[:, nidx:nidx + 1], cosb[:, 0:T], ALU.mult, ALU.mult)
                b.stt(t2[:, 0:T], xp[:, 0:T], cst[:, nidx + 1:nidx + 2], sinb[:, 0:T], ALU.mult, ALU.mult)
                b.tt("pool", t1[:, 0:T], t1[:, 0:T], t2[:, 0:T], ALU.add)
                b.tt("dve", dst_ap, t1[:, 0:T], rs[:, 0:T], ALU.mult)
            else:
                b.tt("dve", t1[:, 0:T], x[:, 0:T], cosb[:, 0:T], ALU.mult)
                b.tt("pool", t2[:, 0:T], xp[:, 0:T], sinb[:, 0:T], ALU.mult)
                b.tt("dve", dst_ap, t1[:, 0:T], t2[:, 0:T], ALU.add)

        def tables(t0, T):
            cosb, sinb = tb.get(), tb.get()
            b.dma("sp", cosb[:, 0:T], I["ropec"].ap()[:, t0:t0 + T])
            b.dma("sp", sinb[:, 0:T], I["ropes"].ap()[:, t0:t0 + T])
            return cosb, sinb

        for (t0, T, is_ctx) in self.blocks:
            cosb, sinb = tables(t0, T)
            for g in range(2):
                rope(KT[:, g, t0:t0 + T], ok_ + g * 64, okp + g * 64, t0, T, cosb, sinb, 10, True)

        QT = Pool(b, 2, [128, 4, 512], BF16, "QT")
        PTp = Pool(b, 6, [128, 512], BF16, "PT")
        gp = Pool(b, 2, [64, 8, 512], F32, "ag")
        ep = Pool(b, 2, [64, 8, 512], F32, "ae")
        drp = Pool(b, 2, [128, 512], F32, "adr")
        bcp = Pool(b, 2, [64, 512], F32, "abc")
        yp = Pool(b, 3, [64, 512], BF16, "ay")
        nlat_ch = self.n // 128
        for (t0, T, is_ctx) in self.blocks:
            if is_ctx and not need_ctx:
                continue
            cosb, sinb = tables(t0, T)
            qt = QT.get()
            for pr in range(4):
                rope(qt[:, pr, 0:T], oq + pr * 128, oqp + pr * 128, t0, T, cosb, sinb, 8, False)
            chunks = [(0, 0, T, None), (1, 0, T, None)]
            if not is_ctx:
                lb = (t0 - CTX) // 128
                nq = T // 128
                if isC:
                    chunks += [(2 + c, 0, T, None) for c in range(nlat_ch)]
                else:
                    for c in range(max(0, lb - 1), min(nlat_ch, lb + nq + 1)):
                        qlo, qhi = max(lb, c - 1), min(lb + nq - 1, c + 1)
                        chunks.append((2 + c, (qlo - lb) * 128, (qhi - lb + 1) * 128, (qlo - c + 1) * 128))
            gall, sgall = gp.get(), ep.get()
            for h in range(8):
                b.dma("sp", gall[:, h, 0:T], pT[og + h * 64:og + (h + 1) * 64, t0:t0 + T])
            b.act(sgall[:, :, 0:T], gall[:, :, 0:T], AF.Sigmoid)
            b.tt("dve", sgall[:, :, 0:T], sgall[:, :, 0:T], gall[:, :, 0:T], ALU.mult)
            for h in range(8):
                g, pr, po = h // 4, h // 2, (h % 2) * 64
                O = psO.get()
                LA = 3
                Sq = []

                def issue_s(cj):
                    ch_, lo_, hi_, _m = chunks[cj]
                    S_ = psS.get()
                    b.mm(S_[:, lo_:hi_], KT[po:po + 64, g, ch_ * 128:(ch_ + 1) * 128], qt[po:po + 64, pr, lo_:hi_])
                    Sq.append(S_)
                for cj in range(min(LA, len(chunks))):
                    issue_s(cj)
                for ci, (ch, lo, hi, mlo) in enumerate(chunks):
                    if ci + LA < len(chunks):
                        issue_s(ci + LA)
                    S = Sq[ci]
                    pt = PTp.get()
                    b.act(pt[:, lo:hi], S[:, lo:hi], AF.Exp, scale=0.125)
                    if mlo is not None:
                        b.tt("dve", pt[:, lo:hi], pt[:, lo:hi], m3[:, mlo:mlo + hi - lo], ALU.mult)
                    b.mm(O[0:65, lo:hi], VE[:, ch, g, :], pt[:, lo:hi], ci == 0, ci == len(chunks) - 1)
                dr = drp.get()
                if isC:
                    b.rcp(dr[64:65, 0:T], O[64:65, 0:T])
                else:
                    b.ts("dve", dr[64:65, 0:T], O[64:65, 0:T], cst[64:65, h:h + 1], ALU.add)
                    b.rcp(dr[64:65, 0:T], dr[64:65, 0:T])
                B = psB.get()
                b.mm(B[0:64, 0:T], ones_f[64:65, 0:64], dr[64:65, 0:T])
                bc = bcp.get()
                b.cp("act", bc[:, 0:T], B[0:64, 0:T])
                b.tt("dve", bc[:, 0:T], O[0:64, 0:T], bc[:, 0:T], ALU.mult)
                y = yp.get()
                b.tt("dve", y[:, 0:T], bc[:, 0:T], sgall[:, h, 0:T], ALU.mult)
                b.dma("pool", self.yT.ap()[mi, h * 64:(h + 1) * 64, t0:t0 + T], y[:, 0:T])

    def seg_bounds(self, t0):
        return (0, CTX) if t0 < CTX else (CTX, self.NT)

    def phase_b_prep(self, l):
        b, I, NT = self.b, self.I, self.NT
        pT = self.pT.ap()
        oz = SEG_OFF["Bz"]
        psum = Pool(b, 4, [128, 512], F32, "ps", psum=True)
        psB = Pool(b, 2, [128, 512], F32, "psb", psum=True)
        mu = b.sb([128, 13, 2], F32, "mu")
        cmu = b.sb([128, 13], F32, "cmu")
        w0a0 = b.sb([128, 2, 2, 4], F32, "w0a0")
        aw = b.sb([128, 2, 512], F32, "aw")
        vec = b.sb([128, 4, 4], F32, "bvec")
        rkb = b.sb([128, 4, 128], F32, "rkb")
        oblk = b.sb([128, 128], BF16, "oblk")
        onesF = b.sb([128, 512], F32, "onesF")
        b.ms("dve", onesF[:], 1.0)
        b.ms("dve", oblk[:], 0.0)
        b.ms("dve", oblk[0:64, 0:64], 1.0)
        b.ms("dve", oblk[64:128, 64:128], 1.0)
        b.dma("sp", mu[:], I["b_mu"].ap()[l])
        b.dma("sp", w0a0[:], I["b_w0a0"].ap()[l])
        b.dma("sp", aw[:], I["b_aw"].ap()[l])
        b.dma("sp", vec[:], I["b_vec"].ap()[l])
        b.dma("sp", rkb[:], I["b_rkblk"].ap()[l])
        b.ts("dve", cmu[:], mu[:, :, 0], -1.0, ALU.mult, 1.0, ALU.add)
        b.tt("dve", cmu[:], cmu[:], mu[:, :, 1], ALU.subtract)
        ptp = Pool(b, 3, [128, 514], F32, "bpt")
        zall = b.sb([128, 13, 512], F32, "zall")
        kkT = b.sb([128, 4, 512], F32, "kkT")
        t2k = Pool(b, 12, [128, 512], F32, "bt")
        sqp = Pool(b, 2, [128, 512], BF16, "bsq")
        Sx = b.sb([128, 513], F32, "Sx")
        b.ms("dve", Sx[:, 0:1], 0.0)
        o7p = Pool(b, 2, [128, 7, 512], F32, "o7")
        gtp = Pool(b, 2, [128, 8], F32, "egt")
        for (t0, T, is_ctx) in self.blocks:
            lo, hi = self.seg_bounds(t0)
            nch = T // 64
            for i in range(13):
                pt = ptp.get()
                a0, a1 = max(t0 - 1, lo), min(t0 + T + 1, hi)
                if a0 > t0 - 1:
                    b.ms("pool", pt[:, 0:1], 0.0)
                if a1 < t0 + T + 1:
                    b.ms("pool", pt[:, T + 1:T + 2], 0.0)
                b.dma("sp", pt[:, a0 - (t0 - 1):a1 - (t0 - 1)], pT[oz + i * 128:oz + (i + 1) * 128, a0:a1])
                z = zall[:, i, 0:T]
                b.ts("dve", z, pt[:, 1:T + 1], cmu[:, i:i + 1], ALU.mult)
                b.stt(z, pt[:, 0:T], mu[:, i, 0:1], z, ALU.mult, ALU.add)
                b.stt(z, pt[:, 2:T + 2], mu[:, i, 1:2], z, ALU.mult, ALU.add)
            b.act(zall[0:64, 12, 0:T], zall[0:64, 12, 0:T], AF.Tanh)
            for pr in range(4):
                kx = t2k.get()
                b.ts("dve", kx[:, 0:T], zall[:, 4 + pr, 0:T], vec[:, 0, pr:pr + 1], ALU.mult)
                sq = sqp.get()
                b.act(sq[:, 0:T], kx[:, 0:T], AF.Square)
                ps = psum.get()
                b.mm(ps[:, 0:T], oblk[:], sq[:, 0:T])
                rn = t2k.get()
                b.act(rn[:, 0:T], ps[:, 0:T], AF.Sqrt, bias=1e-12)
                b.rcp(rn[:, 0:T], rn[:, 0:T])
                b.tt("dve", kkT[:, pr, 0:T], kx[:, 0:T], rn[:, 0:T], ALU.mult)
            for pr in range(4):
                psb = psB.get()
                rT, kT_, vT_ = zall[:, pr, 0:T], zall[:, 4 + pr, 0:T], zall[:, 8 + pr, 0:T]
                for d in range(2):
                    pw = psum.get()
                    b.mm(pw[:, 0:T], aw[0:64, d, pr * 128:(pr + 1) * 128], zall[0:64, 12, 0:T])
                    lw = t2k.get()
                    b.act(lw[:, 0:T], pw[:, 0:T], AF.Sigmoid, bias=w0a0[:, 0, d, pr:pr + 1])
                    b.ts("dve", lw[:, 0:T], lw[:, 0:T], -0.606531, ALU.mult)
                    pa = psum.get()
                    b.mm(pa[:, 0:T], aw[64:128, d, pr * 128:(pr + 1) * 128], zall[64:128, 12, 0:T])
                    a = t2k.get()
                    b.act(a[:, 0:T], pa[:, 0:T], AF.Sigmoid, bias=w0a0[:, 1, d, pr:pr + 1])
                    kt = t2k.get()
                    b.ts("dve", kt[:, 0:T], a[:, 0:T], -1.0, ALU.add, vec[:, 1, pr:pr + 1], ALU.mult)
                    b.stt(kt[:, 0:T], kt[:, 0:T], 1.0, kT_, ALU.add, ALU.mult)
                    bb = t2k.get()
                    b.tt("pool", bb[:, 0:T], kkT[:, pr, 0:T], a[:, 0:T], ALU.mult)
                    b.I("dve", "tensor_tensor_scan", out=Sx[:, 1:T + 1], data0=onesF[:, 0:T], data1=lw[:, 0:T],
                        initial=0.0, op0=ALU.mult, op1=ALU.add)
                    gc = t2k.get()
                    v3 = lambda ap: ap.rearrange("p (c t) -> p c t", t=64)
                    b.tt("dve", v3(gc[:, 0:T]), v3(Sx[:, 1:T + 1]), v3(Sx[:, 0:T])[:, :, 0:1].broadcast_to([128, nch, 64]),
                         ALU.subtract)
                    gtot_bc = v3(gc[:, 0:T])[:, :, 63:64].broadcast_to([128, nch, 64])
                    ge = t2k.get()
                    if d == 0:
                        gi = gc
                        b.tt("dve", ge[:, 0:T], gc[:, 0:T], lw[:, 0:T], ALU.subtract)
                    else:
                        gi = t2k.get()
                        b.tt("dve", v3(ge[:, 0:T]), gtot_bc, v3(gc[:, 0:T]), ALU.subtract)
                        b.tt("pool", gi[:, 0:T], ge[:, 0:T], lw[:, 0:T], ALU.add)
                    egt = gtp.get()
                    b.act(egt[:, 0:nch], v3(gc[:, 0:T])[:, :, 63], AF.Exp)
                    ege, egi, engi = t2k.get(), t2k.get(), t2k.get()
                    b.act(ege[:, 0:T], ge[:, 0:T], AF.Exp)
                    b.act(egi[:, 0:T], gi[:, 0:T], AF.Exp)
                    b.act(engi[:, 0:T], gi[:, 0:T], AF.Exp, scale=-1.0)
                    o7 = o7p.get()
                    b.tt("dve", o7[:, 0, 0:T], kkT[:, pr, 0:T], ege[:, 0:T], ALU.mult)
                    b.tt("pool", o7[:, 1, 0:T], rT, egi[:, 0:T], ALU.mult)
                    b.tt("dve", o7[:, 2, 0:T], bb[:, 0:T], engi[:, 0:T], ALU.mult)
                    b.tt("pool", o7[:, 3, 0:T], kt[:, 0:T], engi[:, 0:T], ALU.mult)
                    ebc = egt[:, 0:nch].unsqueeze(2).broadcast_to([128, nch, 64])
                    b.tt("dve", v3(o7[:, 4, 0:T]), v3(o7[:, 3, 0:T]), ebc, ALU.mult)
                    b.tt("dve", v3(o7[:, 5, 0:T]), v3(o7[:, 2, 0:T]), ebc, ALU.mult)
                    b.cp("pool", o7[:, 6, 0:T], vT_)
                    b.dma("pool", self.UB.ap()[d, pr, :, :, t0:t0 + T], o7[:, :, 0:T])
                    b.dma("pool", self.GAMB.ap()[d, pr, :, t0 // 64:t0 // 64 + nch], egt[:, 0:nch])
                    rkt = t2k.get()
                    b.tt("dve", rkt[:, 0:T], rT, kt[:, 0:T], ALU.mult)
                    b.mm(psb[:, 0:T], rkb[:, pr, :], rkt[:, 0:T], d == 0, d == 1)
                bo = t2k.get()
                b.tt("dve", bo[:, 0:T], psb[:, 0:T], vT_, ALU.mult)
                b.dma("pool", self.bon.ap()[pr * 128:(pr + 1) * 128, t0:t0 + T], bo[:, 0:T])

    def phase_units(self, l, mi):
        b, I, NT = self.b, self.I, self.NT
        isB = (mi == 1)
        nlev = 5 if isB else 6
        CT = 64 if isB else 128
        NCH = NT // CT
        cch = CTX // CT
        order = {0: list(range(NCH)), 1: list(range(cch - 1, -1, -1)) + list(range(NCH - 1, cch - 1, -1))}
        chains = [(u, d) for u in range(4) for d in range(2)]
        NC = len(chains)
        psG = Pool(b, 2, [128, 512], F32, "psG", psum=True)
        psA = Pool(b, 2, [128, 512], F32, "psA", psum=True)
        sm = Pool(b, 4, [128, 512], F32, "psm", psum=True)
        ident = b.sb([128, 128], F32, "ident")
        b.dma("sp", ident[:], I["ident"].ap())
        rmask = b.sb([128, 4, 128], F32, "rmask")
        b.dma("sp", rmask[:], I["rmask" if isB else "gmask"].ap())
        bd2 = b.sb([128, 2, 64], F32, "bd2")
        b.dma("sp", bd2[:], I["bd2"].ap())
        NB = 16
        XXp = Pool(b, NB, [128, 4, 128], BF16, "XX")
        TMp = Pool(b, NB, [128, 3, 128], BF16, "TM3")
        A3p = Pool(b, NB, [128, 3, 128], BF16, "A3")
        Pmp = Pool(b, 10 if isB else 18, [128, 2, 128], F32, "Pm")
        XTp = Pool(b, 17, [128, 128], F32, "XT")
        XTf = Pool(b, 16, [128, 128], F32, "XTf")
        if isB:
            P2p = Pool(b, 10, [128, 128], F32, "P2f")
            Pbp = Pool(b, 18, [128, 2, 128], BF16, "Pb")
            XTbp = Pool(b, 17, [128, 128], BF16, "XTb")
        AVp = Pool(b, NB, [128, 128], F32, "AVs")
        Wp = Pool(b, 9, [128, 128], F32, "Wt")
        NUp = Pool(b, 9, [128, 128], BF16, "negU")
        Yop = Pool(b, 6, [128, 128], F32, "Yo")
        H = [b.sb([128, 128], F32, "H") for _ in chains]
        Hb = [b.sb([128, 128], BF16, "Hb") for _ in chains]
        for i in range(NC):
            b.ms("dve", H[i][:], 0.0)
            b.ms("pool", Hb[i][:], 0.0)
        gam = [b.sb([128, NCH], F32, "gam") for _ in chains]
        if isB:
            ldp = Pool(b, 10, [128, 7, 64], F32, "ld7")
            F3p = Pool(b, 8, [128, 3, 128], F32, "F3")
            for i, (u, d) in enumerate(chains):
                b.dma("sp", gam[i][:], self.GAMB.ap()[d, u])
        else:
            self.gdn_unit_setup(l, gam)
            self.gdn_gam(gam, sm, chains)

        def pre_stages(step):
            units = []
            st = []

            def s_prep():
                for i, (u, d) in enumerate(chains):
                    c = order[d][step]
                    tk = c * CT
                    U = dict(i=i, u=u, d=d, c=c, tk=tk)
                    XX, TM3 = XXp.get(), TMp.get()
                    if isB:
                        ld = ldp.get()
                        b.dma("sp", ld[:], self.UB.ap()[d, u, :, :, tk:tk + 64])
                        bdb = bd2[:].unsqueeze(1).broadcast_to([128, 4, 2, 64])
                        b.tt("dve", XX[:].rearrange("p k (h t) -> p k h t", h=2),
                             ld[:, 0:4, :].unsqueeze(2).broadcast_to([128, 4, 2, 64]), bdb, ALU.mult)
                        F3 = F3p.get()
                        b.tt("pool", F3[:].rearrange("p k (h t) -> p k h t", h=2),
                             ld[:, 4:7, :].unsqueeze(2).broadcast_to([128, 3, 2, 64]),
                             bd2[:].unsqueeze(1).broadcast_to([128, 3, 2, 64]), ALU.mult)
                        tp = sm.get()
                        for k in range(3):
                            b.tr(tp[:, k * 128:(k + 1) * 128], F3[:, k, :], ident[:])
                        b.cp(("act", "dve")[i % 2], TM3[:].rearrange("p k t -> p (k t)"), tp[:, 0:384])
                        if d == 0:
                            WsT, WiT, Ws = rmask[:, 0, :], rmask[:, 1, :], rmask[:, 2, :]
                        else:
                            WsT, WiT, Ws = rmask[:, 2, :], rmask[:, 3, :], rmask[:, 0, :]
                        U.update(XkH=XX[:, 0, :], XrH=XX[:, 1, :])
                    else:
                        WsT, WiT, Ws, XH = self.gdn_unit_prep(l, U, XX, TM3, sm, rmask, ident)
                        U.update(XkH=XH[:, 0, :], XrH=XH[:, 1, :])
                    U.update(XX=XX, TM3=TM3, gcol=gam[i][:, c:c + 1], W=(WsT, WiT, Ws))
                    units.append(U)

            def s_gram():
                for U in units:
                    XX = U["XX"]
                    WsT, WiT, Ws = U["W"]
                    G = psG.get()
                    xkr = XX[:, 0:2, :].rearrange("p k t -> p (k t)")
                    b.mm(G[:, 0:256], XX[:, 2, :], xkr)
                    b.mm(G[:, 256:512], XX[:, 3, :], xkr)
                    g3 = sm.get()
                    b.mm(g3[:, 0:128], XX[:, 0, :], XX[:, 2, :])
                    Pm = Pmp.get()
                    b.stt(Pm[:, 1, :], G[:, 0:128], -1.0, WsT, ALU.mult, ALU.mult)
                    b.stt(Pm[:, 0, :], g3[:, 0:128], -1.0, Ws, ALU.mult, ALU.mult)
                    A3 = A3p.get()
                    b.tt("dve", A3[:, 0, :], G[:, 128:256], WiT, ALU.mult)
                    b.tt("dve", A3[:, 1, :], G[:, 256:384], WsT, ALU.mult)
                    b.tt("dve", A3[:, 2, :], G[:, 384:512], WiT, ALU.mult)
                    XT = XTp.get()
                    b.tt("pool", XT[:], Pm[:, 1, :], ident[:], ALU.add)
                    U.update(A3=A3, Pm=Pm, XT=XT)

            def mk_lev_a(lev):
                def f():
                    last = lev == nlev - 1
                    for U in units:
                        p2 = sm.get()
                        if not isB:
                            Pm = U["Pm"]
                            b.mm(p2[:, 0:128], Pm[:, 1, :], Pm[:, 0, :])
                            Pn = Pmp.get()
                            if not last:
                                b.mm(p2[:, 128:256], Pm[:, 0, :], Pm[:, 1, :])
                                b.cp(("act", "dve")[U["i"] % 2], Pn[:].rearrange("p k t -> p (k t)"), p2[:, 0:256])
                            else:
                                b.cp("act", Pn[:, 0, :], p2[:, 0:128])
                            U["Pn"] = Pn
                        elif lev == 0:
                            Pm = U["Pm"]
                            b.mm(p2[:, 0:128], Pm[:, 1, :], Pm[:, 0, :])
                            b.mm(p2[:, 128:256], Pm[:, 0, :], Pm[:, 1, :])
                            P2f = P2p.get()
                            Pb = Pbp.get()
                            eng = ("act", "dve")[U["i"] % 2]
                            b.cp(eng, P2f[:], p2[:, 0:128])
                            b.cp(eng, Pb[:].rearrange("p k t -> p (k t)"), p2[:, 0:256])
                            U["P2f"], U["Pb"] = P2f, Pb
                        else:
                            Pb = U["Pb"]
                            b.mm(p2[:, 0:128], Pb[:, 1, :], Pb[:, 0, :])
                            Pn = Pbp.get()
                            if not last:
                                b.mm(p2[:, 128:256], Pb[:, 0, :], Pb[:, 1, :])
                                b.cp(("act", "dve")[U["i"] % 2], Pn[:].rearrange("p k t -> p (k t)"), p2[:, 0:256])
                            else:
                                b.cp("act", Pn[:, 0, :], p2[:, 0:128])
                            U["Pb"] = Pn
                return f

            def mk_lev_b(lev):
                def f():
                    last = lev == nlev - 1
                    for U in units:
                        XT = U["XT"]
                        xu = sm.get()
                        if not isB:
                            b.mm(xu[:, 0:128], U["Pn"][:, 0, :], XT[:])
                            U["Pm"] = U["Pn"]
                        elif lev == 0:
                            b.mm(xu[:, 0:128], U["P2f"][:], XT[:])
                        else:
                            b.mm(xu[:, 0:128], U["Pb"][:, 0, :], U["XTb"][:])
                        XTn = (XTf if last else XTp).get()
                        b.tt("dve", XTn[:], xu[:, 0:128], XT[:], ALU.add)
                        if isB and not last:
                            XTb = XTbp.get()
                            b.cp("pool", XTb[:], XTn[:])
                            U["XTb"] = XTb
                        U["XT"] = XTn
                return f

            def s_av():
                for U in units:
                    av = sm.get()
                    b.mm(av[:, 0:128], U["A3"][:, 1, :], U["TM3"][:, 2, :])
                    AVs = AVp.get()
                    b.cp("act", AVs[:], av[:, 0:128])
                    U["AVs"] = AVs

            st.append(s_prep)
            st.append(s_gram)
            for lev in range(nlev):
                st.append(mk_lev_a(lev))
                st.append(mk_lev_b(lev))
            st.append(s_av)
            return st, units

        def seq_stages(units):
            def s5():
                for U in units:
                    kh = sm.get()
                    b.mm(kh[:, 0:128], U["XkH"], Hb[U["i"]][:])
                    Wt = Wp.get()
                    b.tt("dve", Wt[:], kh[:, 0:128], U["AVs"][:], ALU.add)
                    U["Wt"] = Wt

            def s6():
                for U in units:
                    uu = sm.get()
                    b.mm(uu[:, 0:128], U["XT"][:], U["Wt"][:])
                    nu = NUp.get()
                    b.act(nu[:], uu[:, 0:128], AF.Copy, scale=-1.0)
                    U["nu"] = nu

            def s7():
                for U in units:
                    i = U["i"]
                    Y = psA.get()
                    b.mm(Y[:, 0:128], U["XrH"], Hb[i][:], True, False)
                    b.mm(Y[:, 0:128], U["A3"][:, 0, :], U["nu"][:], False, False)
                    b.mm(Y[:, 0:128], U["A3"][:, 2, :], U["TM3"][:, 2, :], False, True)
                    Yo = Yop.get()
                    b.cp("act", Yo[:], Y[:, 0:128])
                    dst = self.ytm.ap()[U["d"]]
                    if isB:
                        for hh in range(2):
                            b.dma("pool", dst[U["tk"]:U["tk"] + 64, (2 * U["u"] + hh) * 64:(2 * U["u"] + hh + 1) * 64],
                                  Yo[hh * 64:(hh + 1) * 64, hh * 64:(hh + 1) * 64])
                    else:
                        b.dma("pool", dst[U["tk"]:U["tk"] + 128, U["u"] * 128:(U["u"] + 1) * 128], Yo[:])

            def s8():
                for U in units:
                    i = U["i"]
                    Hn = psA.get()
                    b.mm(Hn[:, 0:128], U["TM3"][:, 1, :], U["nu"][:], True, False)
                    b.mm(Hn[:, 0:128], U["TM3"][:, 0, :], U["TM3"][:, 2, :], False, True)
                    b.stt(H[i][:], H[i][:], U["gcol"], Hn[:, 0:128], ALU.mult, ALU.add)
                    b.cp("pool", Hb[i][:], H[i][:])
            return [s5, s6, s7, s8]

        st, units = pre_stages(0)
        for f in st:
            f()
        for step in range(NCH):
            seq = seq_stages(units)
            if step + 1 < NCH:
                pre, nunits = pre_stages(step + 1)
            else:
                pre, nunits = [], None
            npre = len(pre)
            marks = {((k + 1) * npre) // 5: k for k in range(4)} if npre else {}
            if self.no_inter:
                for k in range(4):
                    seq[k]()
                marks = {}
                done_all = True
            else:
                done_all = False
            done = set(range(4)) if done_all else set()
            for j, f in enumerate(pre):
                if j in marks and marks[j] not in done:
                    seq[marks[j]]()
                    done.add(marks[j])
                f()
            for k in range(4):
                if k not in done:
                    seq[k]()
                    done.add(k)
            units = nunits

    def phase_post(self, l, mi, need_ctx):
        b, I, NT = self.b, self.I, self.NT
        isB = (mi == 1)
        pT = self.pT.ap()
        og = SEG_OFF["Bg" if isB else "Dg"]
        psum = Pool(b, 3, [128, 512], F32, "ps", psum=True)
        ident = b.sb([128, 128], F32, "ident")
        b.dma("sp", ident[:], I["ident"].ap())
        vec = b.sb([128, 4, 4], F32, "pvec")
        b.dma("sp", vec[:], I["b_vec" if isB else "d_vec"].ap()[l])
        yp = Pool(b, 4, [128, 512], F32, "py")
        stp = Pool(b, 8, [128, 8], F32, "pst")
        gp = Pool(b, 4, [128, 128], F32, "pg")
        g4p = Pool(b, 2, [128, 4, 128], F32, "pg4")
        e4p = Pool(b, 2, [128, 4, 128], F32, "pe4")
        bp = Pool(b, 4, [128, 128], F32, "pb")
        op_ = Pool(b, 4, [128, 128], BF16, "po")
        NH, HD = (8, 64) if isB else (4, 128)
        for t0 in range(0 if need_ctx else CTX, NT, 128):
            y0, y1 = yp.get(), yp.get()
            b.dma("sp", y0[:], self.ytm.ap()[0, t0:t0 + 128, :])
            b.dma("sp", y1[:], self.ytm.ap()[1, t0:t0 + 128, :])
            b.tt("dve", y0[:], y0[:], y1[:], ALU.add)
            y3 = y0[:].rearrange("p (h d) -> p h d", h=NH)
            st = stp.get()
            if isB:
                b.I("dve", "tensor_reduce", out=st[:, 0:NH], in_=y3, axis=mybir.AxisListType.X, op=ALU.add)
                b.ts("dve", st[:, 0:NH], st[:, 0:NH], 1.0 / HD, ALU.mult)
                b.tt("dve", y3, y3, st[:, 0:NH].unsqueeze(2).broadcast_to([128, NH, HD]), ALU.subtract)
            sq = yp.get()
            b.tt("dve", sq[:], y0[:], y0[:], ALU.mult)
            s2 = stp.get()
            b.I("dve", "tensor_reduce", out=s2[:, 0:NH], in_=sq[:].rearrange("p (h d) -> p h d", h=NH),
                axis=mybir.AxisListType.X, op=ALU.add)
            b.act(s2[:, 0:NH], s2[:, 0:NH], AF.Sqrt, scale=1.0 / HD, bias=(64e-5 if isB else 1e-6))
            b.rcp(s2[:, 0:NH], s2[:, 0:NH])
            b.tt("dve", y3, y3, s2[:, 0:NH].unsqueeze(2).broadcast_to([128, NH, HD]), ALU.mult)
            tpb = psum.get()
            for ct in range(4):
                b.tr(tpb[:, ct * 128:(ct + 1) * 128], y0[:, ct * 128:(ct + 1) * 128], ident[:])
            g4, e4 = g4p.get(), e4p.get()
            b.dma("sp", g4[:], pT[og:og + 512, t0:t0 + 128].rearrange("(k p) t -> p k t", p=128))
            b.act(e4[:], g4[:], AF.Sigmoid)
            b.tt("pool", e4[:], e4[:], g4[:], ALU.mult)
            for ct in range(4):
                tp = _View(tpb.h, ct * 128, 128)
                e = _View3(e4.h, ct)
                o = op_.get()
                if isB:
                    bo = bp.get()
                    b.dma("sp", bo[:], self.bon.ap()[ct * 128:(ct + 1) * 128, t0:t0 + 128])
                    t = gp.get()
                    b.ts("dve", t[:], tp[:], vec[:, 2, ct:ct + 1], ALU.mult, vec[:, 3, ct:ct + 1], ALU.add)
                    b.tt("pool", t[:], t[:], bo[:], ALU.add)
                    b.tt("dve", o[:], t[:], e[:], ALU.mult)
                else:
                    b.stt(o[:], tp[:], vec[:, 0, ct:ct + 1], e[:], ALU.mult, ALU.mult)
                b.dma("pool", self.yT.ap()[mi, ct * 128:(ct + 1) * 128, t0:t0 + 128], o[:])


    def phase_d_prep(self, l):
        b, I, NT = self.b, self.I, self.NT
        pT = self.pT.ap()
        oq, oab = SEG_OFF["Dqkv"], SEG_OFF["Dab"]
        psum = Pool(b, 4, [128, 512], F32, "ps", psum=True)
        cw = b.sb([128, 12, 5], F32, "cw")
        b.dma("sp", cw[:], I["d_conv"].ap()[l])
        abp = b.sb([8, 4], F32, "abp")
        b.dma("sp", abp[:], I["d_ab"].ap()[l])
        nexpA = b.sb([8, 1], F32, "nexpA")
        b.act(nexpA[:], abp[:, 1:2], AF.Exp)
        b.ts("dve", nexpA[:], nexpA[:], -1.0, ALU.mult)
        onesF = b.sb([8, 512], F32, "onesF")
        b.ms("dve", onesF[:], 1.0)
        Sx = b.sb([8, 513], F32, "Sx")
        b.ms("dve", Sx[:, 0:1], 0.0)
        ptp = Pool(b, 3, [128, 516], F32, "dpt")
        tp = Pool(b, 6, [128, 512], F32, "dt")
        sqp = Pool(b, 2, [128, 512], BF16, "dsq")
        o3p = Pool(b, 2, [128, 3, 512], F32, "o3")
        rp = Pool(b, 12, [8, 512], F32, "dr")
        o6p = Pool(b, 2, [8, 6, 512], F32, "o6")
        egp = Pool(b, 2, [8, 4], F32, "deg")
        for (t0, T, is_ctx) in self.blocks:
            lo, hi = self.seg_bounds(t0)
            nch = T // 128
            for h in range(4):
                o3 = o3p.get()
                for kind in range(3):
                    i = kind * 4 + h
                    pt = ptp.get()
                    a0, a1 = max(t0 - 2, lo), min(t0 + T + 2, hi)
                    if a0 > t0 - 2:
                        b.ms("pool", pt[:, 0:2], 0.0)
                    if a1 < t0 + T + 2:
                        b.ms("pool", pt[:, T + 2:T + 4], 0.0)
                    b.dma("sp", pt[:, a0 - (t0 - 2):a1 - (t0 - 2)], pT[oq + i * 128:oq + (i + 1) * 128, a0:a1])
                    acc = tp.get()
                    b.ts("dve", acc[:, 0:T], pt[:, 0:T], cw[:, i, 0:1], ALU.mult)
                    for j in range(1, 5):
                        b.stt(acc[:, 0:T], pt[:, j:j + T], cw[:, i, j:j + 1], acc[:, 0:T], ALU.mult, ALU.add)
                    e = tp.get()
                    b.act(e[:, 0:T], acc[:, 0:T], AF.Sigmoid)
                    if kind == 2:
                        b.tt("pool", o3[:, 2, 0:T], acc[:, 0:T], e[:, 0:T], ALU.mult)
                        continue
                    b.tt("pool", acc[:, 0:T], acc[:, 0:T], e[:, 0:T], ALU.mult)
                    sq = sqp.get()
                    b.act(sq[:, 0:T], acc[:, 0:T], AF.Square)
                    ps = psum.get()
                    b.mm(ps[:, 0:T], self.ones_bf[:], sq[:, 0:T])
                    rn = tp.get()
                    b.act(rn[:, 0:T], ps[:, 0:T], AF.Sqrt, bias=1e-12)
                    b.rcp(rn[:, 0:T], rn[:, 0:T])
                    if kind == 0:
                        b.stt(o3[:, 0, 0:T], acc[:, 0:T], 128.0 ** -0.5, rn[:, 0:T], ALU.mult, ALU.mult)
                    else:
                        b.tt("dve", o3[:, 1, 0:T], acc[:, 0:T], rn[:, 0:T], ALU.mult)
                b.dma("pool", self.UD.ap()[h, :, :, t0:t0 + T], o3[:, :, 0:T])
            lgr, br = rp.get(), rp.get()
            for d in range(2):
                b.dma("sp", lgr[d * 4:(d + 1) * 4, 0:T], pT[oab + d * 8:oab + d * 8 + 4, t0:t0 + T])
                b.dma("sp", br[d * 4:(d + 1) * 4, 0:T], pT[oab + d * 8 + 4:oab + d * 8 + 8, t0:t0 + T])
            lg = rp.get()
            b.act(lg[:, 0:T], lgr[:, 0:T], AF.Exp, bias=abp[:, 0:1])
            b.act(lg[:, 0:T], lg[:, 0:T], AF.Ln, bias=1.0)
            b.ts("dve", lg[:, 0:T], lg[:, 0:T], nexpA[:, 0:1], ALU.mult)
            beta = rp.get()
            b.act(beta[:, 0:T], br[:, 0:T], AF.Sigmoid)
            b.I("dve", "tensor_tensor_scan", out=Sx[:, 1:T + 1], data0=onesF[:, 0:T], data1=lg[:, 0:T],
                initial=0.0, op0=ALU.mult, op1=ALU.add)
            v3 = lambda ap: ap.rearrange("p (c t) -> p c t", t=128)
            gc = rp.get()
            b.tt("dve", v3(gc[:, 0:T]), v3(Sx[:, 1:T + 1]), v3(Sx[:, 0:T])[:, :, 0:1].broadcast_to([8, nch, 128]), ALU.subtract)
            gtot_bc = v3(gc[:, 0:T])[:, :, 127:128].broadcast_to([8, nch, 128])
            o6 = o6p.get()
            t1 = rp.get()
            b.ts("dve", t1[:, 0:T], gc[:, 0:T], abp[:, 2:3], ALU.mult)
            b.stt(v3(t1[:, 0:T]), gtot_bc, abp[:, 3:4], v3(t1[:, 0:T]), ALU.mult, ALU.add)
            b.stt(o6[:, 3, 0:T], lg[:, 0:T], abp[:, 3:4], t1[:, 0:T], ALU.mult, ALU.add)
            b.tt("dve", o6[:, 2, 0:T], o6[:, 3, 0:T], lg[:, 0:T], ALU.subtract)
            elg = rp.get()
            b.act(elg[:, 0:T], lg[:, 0:T], AF.Exp)
            b.tt("dve", o6[:, 0, 0:T], beta[:, 0:T], elg[:, 0:T], ALU.mult)
            b.cp("dve", o6[:, 1, 0:T], beta[:, 0:T])
            t2 = rp.get()
            b.tt("dve", v3(t2[:, 0:T]), gtot_bc, v3(o6[:, 3, 0:T]), ALU.subtract)
            b.act(t2[:, 0:T], t2[:, 0:T], AF.Exp)
            b.tt("dve", o6[:, 4, 0:T], beta[:, 0:T], t2[:, 0:T], ALU.mult)
            b.tt("dve", o6[:, 5, 0:T], o6[:, 4, 0:T], elg[:, 0:T], ALU.mult)
            eg = egp.get()
            b.act(eg[:, 0:nch], v3(gc[:, 0:T])[:, :, 127], AF.Exp)
            b.dma("pool", self.RD.ap()[:, :, t0:t0 + T], o6[:, :, 0:T])
            b.dma("pool", self.EGTD.ap()[:, t0 // 128:t0 // 128 + nch], eg[:, 0:nch])

    def gdn_unit_setup(self, l, gam):
        b, I, NT = self.b, self.I, self.NT
        NCH = NT // 128
        sel8 = b.sb([8, 8, 128], F32, "sel8")
        b.dma("sp", sel8[:], I["sel8"].ap())
        egt = b.sb([8, NCH + (NCH % 2)], F32, "egtall")
        b.ms("dve", egt[:], 0.0)
        b.dma("sp", egt[:, 0:NCH], self.EGTD.ap())
        self._gd = dict(
            sel8=sel8,
            ld3=Pool(b, 12, [128, 3, 128], F32, "ld3"),
            rows=Pool(b, 4, [8, 6, 128], F32, "grow"),
            cols=Pool(b, 4, [128, 4, 8], F32, "gcol"),
            Wt=Pool(b, 12, [128, 3, 128], F32, "gW"),
            xa=Pool(b, 6, [128, 128], F32, "gxa"),
            eb=Pool(b, 4, [128, 256], F32, "geb"),
            bcS=Pool(b, 10, [128, 4, 128], F32, "gbc"),
            XH=Pool(b, 17, [128, 2, 128], BF16, "gXH"),
            cache={},
        )
        self._gd_egt = egt

    def gdn_gam(self, gam, sm, chains):
        b = self.b
        NCH = self.NT // 128
        n2 = NCH + (NCH % 2)
        for i, (u, d) in enumerate(chains):
            p = sm.get()
            b.mm(p[:, 0:n2], self._gd["sel8"][:, d * 4 + u, :], self._gd_egt[:, 0:n2])
            b.cp("act", gam[i][:], p[:, 0:NCH])

    def gdn_unit_prep(self, l, U, XX, TM3, sm, gmask, ident):
        b = self.b
        G = self._gd
        h, d, c, tk = U["u"], U["d"], U["c"], U["tk"]
        r = d * 4 + h
        key = (c, d)
        if key not in G["cache"]:
            rows = G["rows"].get()
            b.dma("sp", rows[:], self.RD.ap()[:, :, tk:tk + 128])
            tp = sm.get()
            for j, kind in enumerate((3, 2, 4, 5)):
                b.tr(tp[:, j * 8:(j + 1) * 8], rows[:, kind, :], ident[0:8, 0:8])
            cols = G["cols"].get()
            b.cp("act", cols[:].rearrange("p k r -> p (k r)"), tp[:, 0:32])
            G["cache"] = {kk: vv for kk, vv in G["cache"].items() if kk[1] != d}
            G["cache"][key] = (rows, cols)
        rows, cols = G["cache"][key]
        ld = G["ld3"].get()
        b.dma("sp", ld[:], self.UD.ap()[h, :, :, tk:tk + 128])
        bc = G["bcS"].get()
        b.dma("sp", bc[:], self.RD.ap()[r:r + 1, 0:4, tk:tk + 128].broadcast_to([128, 4, 128]))
        b.cp("act", XX[:, 0, :], ld[:, 1, :])
        b.cp("pool", XX[:, 1, :], ld[:, 0, :])
        b.tt("dve", XX[:, 2, :], bc[:, 0, :], ld[:, 1, :], ALU.mult)
        b.tt("pool", XX[:, 3, :], bc[:, 1, :], ld[:, 1, :], ALU.mult)
        tp = sm.get()
        b.tr(tp[:, 0:128], ld[:, 1, :], ident[:])
        b.tr(tp[:, 128:256], ld[:, 2, :], ident[:])
        b.ts("dve", TM3[:, 0, :], tp[:, 0:128], cols[:, 2, r:r + 1], ALU.mult)
        b.ts("dve", TM3[:, 1, :], tp[:, 0:128], cols[:, 3, r:r + 1], ALU.mult)
        b.cp("dve", TM3[:, 2, :], tp[:, 128:256])
        Wt = G["Wt"].get()
        mk = (0, 1, 2) if d == 0 else (2, 3, 0)
        xa = G["xa"].get()
        b.stt(xa[:], bc[:, 2, :], cols[:, 0, r:r + 1], gmask[:, mk[0], :], ALU.subtract, ALU.add)
        b.act(Wt[:, 0, :], xa[:], AF.Exp)
        xb = G["xa"].get()
        b.stt(xb[:], bc[:, 3, :], cols[:, 0, r:r + 1], gmask[:, mk[1], :], ALU.subtract, ALU.add)
        b.act(Wt[:, 1, :], xb[:], AF.Exp)
        xc = G["xa"].get()
        b.stt(xc[:], bc[:, 3, :], cols[:, 1, r:r + 1], gmask[:, mk[2], :], ALU.subtract, ALU.subtract)
        b.act(Wt[:, 2, :], xc[:], AF.Exp, scale=-1.0)
        eb = G["eb"].get()
        b.act(eb[:], bc[:, 2:4, :].rearrange("p k t -> p (k t)"), AF.Exp)
        XH = G["XH"].get()
        b.tt("dve", XH[:, 0, :], eb[:, 0:128], ld[:, 1, :], ALU.mult)
        b.tt("pool", XH[:, 1, :], eb[:, 128:256], ld[:, 0, :], ALU.mult)
        return Wt[:, 0, :], Wt[:, 1, :], Wt[:, 2, :], XH

    def phase_mcast(self, l):
        b, I = self.b, self.I
        fa = Pool(b, 2, [128, 24, 128], F32, "mcf")
        ba = Pool(b, 2, [128, 24, 128], BF16, "mcb")
        for dt in range(16):
            f, g = fa.get(), ba.get()
            b.dma("sp", f[:], I["wm"].ap()[l, dt])
            b.cp(("dve", "pool")[dt % 2], g[:], f[:])
            b.dma("sp", self.wmb.ap()[dt], g[:])
            f, g = fa.get(), ba.get()
            b.dma("sp", f[:, 0:16, :], I["wo"].ap()[l, dt])
            b.cp(("pool", "dve")[dt % 2], g[:, 0:16, :], f[:, 0:16, :])
            b.dma("sp", self.wob.ap()[dt], g[:, 0:16, :])

    def phase_merge(self, l, need_ctx):
        b, I, NT = self.b, self.I, self.NT
        mv = self.modv
        pT = self.pT.ap()
        opm = SEG_OFF["pm"]
        psG = Pool(b, 2, [128, 512], F32, "psg", psum=True)
        psB = Pool(b, 3, [128, 512], F32, "psb", psum=True)
        psO = Pool(b, 2, [128, 512], F32, "pso", psum=True)
        gb = b.sb([128, 4, 16], F32, "gb")
        b.dma("sp", gb[:], I["g_b"].ap()[l])
        pmf = Pool(b, 1, [128, 2, 512], F32, "pmf")
        pmb = Pool(b, 2, [128, 2, 512], BF16, "pmb")
        ybp = Pool(b, 2, [128, 16, 512], BF16, "yb")
        accT = Pool(b, 2, [128, 16, 512], BF16, "accT")
        wmp = Pool(b, 2, [128, 24, 128], BF16, "wmt")
        wop = Pool(b, 2, [128, 16, 128], BF16, "wot")
        gtp = Pool(b, 3, [128, 512], F32, "mg")
        acp = Pool(b, 2, [128, 512], F32, "macc")
        tmp = Pool(b, 3, [128, 512], F32, "mtmp")
        xtp = Pool(b, 3, [128, 512], F32, "mx")
        src_x = I["xT"] if l == 0 else self.xs
        for (t0, T, is_ctx) in self.blocks:
            if is_ctx and not need_ctx:
                continue
            v = 1 if is_ctx else 0
            pf, pb = pmf.get(), pmb.get()
            b.dma("sp", pf[:, :, 0:T], pT[opm:opm + 256, t0:t0 + T].rearrange("(k p) t -> p k t", p=128))
            b.cp("pool", pb[:, :, 0:T], pf[:, :, 0:T])
            yb = ybp.get()
            for mi in range(4):
                b.dma("sp", yb[:, mi * 4:(mi + 1) * 4, 0:T], self.yT.ap()[mi, :, t0:t0 + T].rearrange("(k p) t -> p k t", p=128))
            aT = accT.get()
            for dt in range(16):
                wt = wmp.get()
                b.dma("sp", wt[:], self.wmb.ap()[dt])
                acc = acp.get()
                for i in range(4):
                    pg = psG.get()
                    for rc in range(2):
                        b.mm(pg[:, 0:T], wt[:, i * 2 + rc, :], pb[:, rc, 0:T], rc == 0, rc == 1)
                    gt = gtp.get()
                    b.act(gt[:, 0:T], pg[:, 0:T], AF.Sigmoid, bias=gb[:, i, dt:dt + 1])
                    pbr = psB.get()
                    for cc in range(4):
                        b.mm(pbr[:, 0:T], wt[:, 8 + i * 4 + cc, :], yb[:, i * 4 + cc, 0:T], cc == 0, cc == 3)
                    if i == 0:
                        b.tt("dve", acc[:, 0:T], pbr[:, 0:T], gt[:, 0:T], ALU.mult)
                    else:
                        tm = tmp.get()
                        b.tt("dve", tm[:, 0:T], pbr[:, 0:T], gt[:, 0:T], ALU.mult)
                        b.tt("pool", acc[:, 0:T], acc[:, 0:T], tm[:, 0:T], ALU.add)
                b.cp("act", aT[:, dt, 0:T], acc[:, 0:T])
            for dt in range(16):
                wo = wop.get()
                b.dma("sp", wo[:], self.wob.ap()[dt])
                po = psO.get()
                for k in range(16):
                    b.mm(po[:, 0:T], wo[:, k, :], aT[:, k, 0:T], k == 0, k == 15)
                xt = xtp.get()
                b.dma("sp", xt[:, 0:T], src_x.ap()[dt * 128:(dt + 1) * 128, t0:t0 + T])
                b.stt(xt[:, 0:T], po[:, 0:T], mv[:, l, dt, 3 * v + 2:3 * v + 3], xt[:, 0:T], ALU.mult, ALU.add)
                b.dma("pool", self.xs.ap()[dt * 128:(dt + 1) * 128, t0:t0 + T], xt[:, 0:T])

    def phase_final(self):
        b, I = self.b, self.I
        psum = Pool(b, 2, [128, 512], F32, "ps", psum=True)
        fg = b.sb([128, KC], F32, "fg")
        b.dma("sp", fg[:], I["final_g"].ap())
        Pxt = Pool(b, 2, [128, KC, 512], F32, "xt")
        Psq = Pool(b, 1, [128, KC, 512], BF16, "sq")
        Prs = Pool(b, 2, [128, 512], F32, "rs")
        for (t0, T, is_ctx) in self.blocks:
            if is_ctx:
                continue
            xt = Pxt.get()
            for k in range(KC):
                b.dma("sp", xt[:, k, 0:T], self.xs.ap()[k * 128:(k + 1) * 128, t0:t0 + T])
            sq = Psq.get()
            b.act(sq[:, :, 0:T], xt[:, :, 0:T], AF.Square)
            ps = psum.get()
            for k in range(KC):
                b.mm(ps[:, 0:T], self.ones_bf[:], sq[:, k, 0:T], k == 0, k == KC - 1)
            rs = Prs.get()
            b.act(rs[:, 0:T], ps[:, 0:T], AF.Sqrt, scale=1.0 / D_MODEL, bias=1e-6)
            b.rcp(rs[:, 0:T], rs[:, 0:T])
            for k in range(KC):
                b.stt(xt[:, k, 0:T], xt[:, k, 0:T], fg[:, k:k + 1], rs[:, 0:T], ALU.mult, ALU.mult)
                b.dma("pool", self.out.ap()[k * 128:(k + 1) * 128, t0 - CTX:t0 - CTX + T], xt[:, k, 0:T])


def _fm(v):
    v = np.asarray(v)
    c = v.shape[-1]
    return np.ascontiguousarray(np.swapaxes(v.reshape(v.shape[:-1] + (c // 128, 128)), -1, -2))


def host_inputs(inp, bi, n_lat, depth):
    L = depth
    d = {}
    xcat = np.concatenate([inp["ctx"][bi], inp["x"][bi][:n_lat]], axis=0)
    d["xT"] = np.ascontiguousarray(xcat.T)
    d["cc"] = np.ascontiguousarray(np.stack([_fm(inp["c"][bi]), _fm(inp["c_ctx"])], axis=-1))
    d["norm_g"] = _fm(inp["norm_g"][:L])
    d["w_mod"] = np.ascontiguousarray(inp["w_mod"][:L])
    d["b_mod"] = _fm(inp["b_mod"][:L])
    cols = w_in_columns()
    w = inp["w_in"][:L][:, :, cols]
    w = w.reshape(L, KC, 128, NCT, 128).transpose(0, 3, 2, 1, 4)
    d["w_in"] = np.ascontiguousarray(w)
    d["final_g"] = _fm(inp["final_g"])
    NT = CTX + n_lat
    p = np.arange(128)
    dd = p % 64
    half, r = dd // 32, dd % 32
    inv = 10000.0 ** (-np.arange(0, 32, 2, dtype=np.float32) / np.float32(32))
    f = inv[r % 16].astype(np.float32)
    t = np.arange(n_lat)
    pos = np.where(half[:, None] == 0, (t // 64)[None, :], (t % 64)[None, :]).astype(np.float32)
    ang = (pos * f[:, None]).astype(np.float32)
    sign = np.where(r < 16, -1.0, 1.0).astype(np.float32)
    rc = np.ones((128, NT), np.float32)
    rsn = np.zeros((128, NT), np.float32)
    rc[:, CTX:] = np.cos(ang)
    rsn[:, CTX:] = np.sin(ang) * sign[:, None]
    d["ropec"], d["ropes"] = rc, rsn
    j = np.arange(128)[:, None]
    i = np.arange(128)[None, :]
    d["m3"] = np.concatenate([(j <= i), np.ones((128, 128), bool), (i <= j)], axis=1).astype(np.float32)
    d["ident"] = np.eye(128, dtype=np.float32)
    d["a_sink"] = np.ascontiguousarray(np.broadcast_to(inp["a_sink"][:L, None, :], (L, 128, 8)))
    pm = _perm64()
    qn, kn = inp["c_qn"][:L], inp["c_kn"][:L]
    cq = np.stack([qn, qn[:, pm], kn, kn[:, pm]], axis=-1)
    d["c_qk"] = np.ascontiguousarray(np.concatenate([cq, cq], axis=1))
    pp = np.arange(128)
    d["bd2"] = np.ascontiguousarray(np.broadcast_to((pp[:, None] // 64 == np.arange(2)[None, :])[:, :, None], (128, 2, 64))).astype(np.float32)
    row, col = pp[:, None], pp[None, :]
    same = (row // 64) == (col // 64)
    tri = np.stack([row < col, row <= col, row > col, row >= col], axis=1)
    d["rmask"] = (tri & same[:, None, :]).astype(np.float32)
    d["gmask"] = np.where(tri, 0.0, -1.0e4).astype(np.float32)
    d["b_mu"] = np.ascontiguousarray(_fm(inp["b_mu"][:L]).transpose(0, 2, 3, 1))
    w0 = _fm(inp["b_w0"][:L]).transpose(0, 2, 1, 3)
    a0 = _fm(inp["b_a0"][:L]).transpose(0, 2, 1, 3)
    d["b_w0a0"] = np.ascontiguousarray(np.stack([w0, a0], axis=2))
    d["b_aw"] = np.ascontiguousarray(np.concatenate([inp["b_wup"][:L], inp["b_aup"][:L]], axis=2).transpose(0, 2, 1, 3))
    d["b_vec"] = np.ascontiguousarray(np.stack([_fm(inp[k][:L]) for k in ("b_kk", "b_ka", "b_lng", "b_lnb")], axis=2))
    rk = inp["b_rk"][:L]
    blk = np.zeros((L, 128, 4, 128), np.float32)
    for pr in range(4):
        for hh in range(2):
            blk[:, hh * 64:(hh + 1) * 64, pr, hh * 64:(hh + 1) * 64] = rk[:, 2 * pr + hh, :, None]
    d["b_rkblk"] = blk
    sel = np.zeros((8, 8, 128), np.float32)
    for r_ in range(8):
        sel[r_, r_, :] = 1.0
    d["sel8"] = sel
    d["d_conv"] = np.ascontiguousarray(_fm(inp["d_conv"][:L]).transpose(0, 2, 3, 1))
    ab = np.zeros((L, 8, 4), np.float32)
    ab[:, :, 0] = inp["d_dtb"][:L].reshape(L, 8)
    ab[:, :, 1] = inp["d_alog"][:L].reshape(L, 8)
    ab[:, 0:4, 2], ab[:, 4:8, 2] = 1.0, -1.0
    ab[:, 4:8, 3] = 1.0
    d["d_ab"] = ab
    dv = np.zeros((L, 128, 4, 4), np.float32)
    dv[:, :, 0, :] = inp["d_norm"][:L][:, :, None]
    d["d_vec"] = dv
    gu = inp["g_up"][:L].reshape(L, 4, 2, 128, 16, 128)
    wb = inp["w_br"][:L].reshape(L, 4, 4, 128, 16, 128)
    wm = np.concatenate([gu.transpose(0, 4, 3, 1, 2, 5).reshape(L, 16, 128, 8, 128),
                         wb.transpose(0, 4, 3, 1, 2, 5).reshape(L, 16, 128, 16, 128)], axis=3)
    d["wm"] = np.ascontiguousarray(wm)
    wo = inp["w_out"][:L].reshape(L, 16, 128, 16, 128)
    d["wo"] = np.ascontiguousarray(wo.transpose(0, 3, 2, 1, 4))
    d["g_b"] = np.ascontiguousarray(_fm(inp["g_b"][:L]).transpose(0, 2, 1, 3))
    return d


N_CORES = 8
_PROG_CACHE = {}


def run_model(inp, n_lat, depth):
    inp = {k: np.asarray(v) for k, v in inp.items()}
    B = inp["x"].shape[0]
    key = (n_lat, depth)
    if key not in _PROG_CACHE:
        _PROG_CACHE[key] = Prog(n_lat, depth).build()
    nc = _PROG_CACHE[key]
    per_b = [host_inputs(inp, bi, n_lat, depth) for bi in range(B)]
    in_maps = [per_b[i % B] for i in range(N_CORES)]
    res = run_bass_kernel_spmd(nc, in_maps, core_ids=list(range(N_CORES)))
    out = np.stack([np.ascontiguousarray(res.results[bi]["outT"].T) for bi in range(B)], axis=0)
    return out.astype(np.float32)


def kernel(**inputs):
    return run_model(inputs, 8192, 4)
```
